# Optimizing a Trainium2 kernel written in Bass

```python
import math
import jax
import jax.numpy as jnp
from jax import lax
import numpy as np

D_MODEL = 1024
BATCH = 8
SEQ = 2048
DEPTH = 4
DEC_BATCH = 32
DEC_SEQ = 8
PAST_LEN = 8192
PAGE_SIZE = 128

N_MIXERS = 3
N_A = (DEPTH + 2) // 3
N_B = (DEPTH + 1) // 3
N_C = DEPTH // 3

DIL_WINDOWS = (128, 512, 2048)
DIL_RATES = (1, 4, 16)
N_GROUPS = 3
A_SPAN = 128
A_HEADS = 4
A_HEAD_DIM = 128
A_BLOCK = 128

HG_HEADS = 8
HG_DK = 128
HG_DV = 128
HG_WIDTH = HG_HEADS * HG_DK
HG_CHUNK = 32

RW_HEADS = 16
RW_HEAD_DIM = 64
RW_LORA_W = 64
RW_LORA_A = 64
RW_LORA_G = 160
RW_GN_EPS = 64e-5

N_MEM = 256
XA_HEADS = 4
XA_HEAD_DIM = D_MODEL // XA_HEADS

D_FF = 2816
CONV_W = 3
RMS_EPS = 1e-6

kernel_name = 'hybrid_dilated_hgrn2_rwkv7_step'


def rmsnorm(x, g):
    xf = x.astype(jnp.float32)
    y = xf * lax.rsqrt(jnp.mean(xf * xf, axis=-1, keepdims=True) + RMS_EPS)
    return y.astype(x.dtype) * g


def dilated_attn_prompt(q, k, v, dil):
    B, S, H, E = q.shape
    n = S // dil
    blk = math.gcd(A_BLOCK, n)
    nb = n // blk

    def by_residue(a):
        return a.reshape(B, n, dil, H, E).transpose(0, 2, 1, 3, 4)

    qr = by_residue(q).reshape(B, dil, nb, blk, H, E)
    pad = ((0, 0), (0, 0), (A_SPAN, 0), (0, 0), (0, 0))
    kp = jnp.pad(by_residue(k), pad)
    vp = jnp.pad(by_residue(v), pad)
    kidx = jnp.arange(nb)[:, None] * blk + jnp.arange(A_SPAN + blk)[None, :]
    kb = kp[:, :, kidx]
    vb = vp[:, :, kidx]
    s = jnp.einsum('brnqhe,brnkhe->brnhqk', qr, kb, preferred_element_type=jnp.float32) * (A_HEAD_DIM ** -0.5)
    rel = (jnp.arange(blk)[:, None] + A_SPAN) - jnp.arange(A_SPAN + blk)[None, :]
    valid = ((rel >= 0) & (rel <= A_SPAN))[None] & (kidx >= A_SPAN)[:, None, :]
    s = jnp.where(valid[None, None, :, None], s, -jnp.inf)
    lse = jax.nn.logsumexp(s, axis=-1)
    p = jnp.exp(s - lse[..., None])
    o = jnp.einsum('brnhqk,brnkhe->brnqhe', p, vb.astype(jnp.float32))
    o = o.reshape(B, dil, n, H, E).transpose(0, 2, 1, 3, 4).reshape(B, S, H, E)
    lse = lse.transpose(0, 1, 2, 4, 3).reshape(B, dil, n, H).transpose(0, 2, 1, 3).reshape(B, S, H)
    return o, lse


def dilated_attn_sample(q, k, v, L, dil):
    T = q.shape[1]
    kidx = (L + jnp.arange(T))[:, None] - dil * jnp.arange(A_SPAN + 1)[None, :]
    valid = kidx >= 0
    kidx = jnp.maximum(kidx, 0)
    kg = k[:, kidx]
    vg = v[:, kidx]
    s = jnp.einsum('bqhe,bqkhe->bhqk', q, kg, preferred_element_type=jnp.float32) * (A_HEAD_DIM ** -0.5)
    s = jnp.where(valid[None, None], s, -jnp.inf)
    lse = jax.nn.logsumexp(s, axis=-1)
    p = jnp.exp(s - lse[..., None])
    o = jnp.einsum('bhqk,bqkhe->bqhe', p, vg.astype(jnp.float32))
    return o, lse.transpose(0, 2, 1)


def mixer_dilated(h, w_qkv, q_gain, k_gain, w_o, bufs):
    B, T, _ = h.shape
    qkv = (h @ w_qkv).reshape(B, T, 3, N_GROUPS, A_HEADS, A_HEAD_DIM)
    q = rmsnorm(qkv[:, :, 0], q_gain[:, None])
    k = rmsnorm(qkv[:, :, 1], k_gain[:, None])
    v = qkv[:, :, 2]
    outs, lses, new_bufs = [], [], []
    for gi in range(N_GROUPS):
        win, dil = DIL_WINDOWS[gi], DIL_RATES[gi]
        kv_new = jnp.stack([k[:, :, gi], v[:, :, gi]], axis=2)
        if bufs is None:
            o, lse = dilated_attn_prompt(q[:, :, gi], k[:, :, gi], v[:, :, gi], dil)
            new_bufs.append(kv_new[:, -min(win, T):])
        else:
            L = bufs[gi].shape[1]
            kv_all = jnp.concatenate([bufs[gi].astype(kv_new.dtype), kv_new], axis=1)
            o, lse = dilated_attn_sample(q[:, :, gi], kv_all[:, :, 0], kv_all[:, :, 1], L, dil)
            new_bufs.append(kv_all[:, T:])
        outs.append(o)
        lses.append(lse)
    wgt = jax.nn.softmax(jnp.stack(lses, axis=0), axis=0)[..., None]
    o = jnp.sum(wgt * jnp.stack(outs, axis=0), axis=0)
    return o.reshape(B, T, A_HEADS * A_HEAD_DIM).astype(h.dtype) @ w_o, new_bufs


def gla_chunked(q, k, v, log_f, S0):
    B, T, H, K = q.shape
    C = math.gcd(HG_CHUNK, T)
    n = T // C
    mask = jnp.tril(jnp.ones((C, C), dtype=bool))[None, :, :, None, None]

    def blocks(a):
        return a.reshape(B, n, C, H, a.shape[-1]).swapaxes(0, 1)

    def step(S, xs):
        qc, kc, vc, gc = xs
        b = jnp.cumsum(gc, axis=1)
        o_inter = jnp.einsum('bchk,bhkv->bchv', qc * jnp.exp(b), S)
        diff = b[:, :, None] - b[:, None, :]
        decay = jnp.where(mask, jnp.exp(jnp.where(mask, diff, 0.0)), 0.0)
        A = jnp.einsum('bihk,bjhk,bijhk->bhij', qc, kc, decay)
        o_intra = jnp.einsum('bhij,bjhv->bihv', A, vc)
        bC = b[:, -1]
        S_new = jnp.exp(bC)[..., None] * S + jnp.einsum('bjhk,bjhv->bhkv', kc * jnp.exp(bC[:, None] - b), vc)
        return S_new, o_inter + o_intra

    S_T, o = lax.scan(step, S0.astype(jnp.float32), (blocks(q), blocks(k), blocks(v), blocks(log_f)))
    return o.swapaxes(0, 1).reshape(B, T, H, v.shape[-1]), S_T


def mixer_hgrn(h, w_in, lb, out_gain, w_o, S0):
    B, T, _ = h.shape
    q, z, i, g = jnp.split(h @ w_in, 4, axis=-1)
    zf = z.astype(jnp.float32)
    log_f = jnp.logaddexp(jnp.log(lb), jnp.log1p(-lb) + jax.nn.log_sigmoid(zf))
    kg = (1.0 - lb) * jax.nn.sigmoid(-zf)

    def heads(t, e):
        return t.astype(jnp.float32).reshape(B, T, HG_HEADS, e)

    o, S_T = gla_chunked(heads(jax.nn.silu(q), HG_DK), heads(kg, HG_DK), heads(i, HG_DV), heads(log_f, HG_DK), S0)
    o = rmsnorm(o, out_gain) * heads(jax.nn.silu(g), HG_DV)
    return o.reshape(B, T, HG_HEADS * HG_DV).astype(h.dtype) @ w_o, S_T


def mixer_rwkv(h, shift, S0, w, j):
    B, T, D = h.shape
    f32 = jnp.float32
    prev = jnp.concatenate([shift[:, None].astype(h.dtype), h[:, :-1]], axis=1)
    xs = h[:, :, None] + (prev - h)[:, :, None] * w['rw_mu'][j]
    rkv = jnp.einsum('btjd,jde->btje', xs[:, :, :3], w['rw_w_rkv'][j])
    r, k, v = rkv[:, :, 0], rkv[:, :, 1], rkv[:, :, 2]
    xw, xa, xg = xs[:, :, 3], xs[:, :, 4], xs[:, :, 5]
    wlog = -jax.nn.softplus(-(w['rw_w0'][j] + jnp.tanh(xw @ w['rw_w1'][j]) @ w['rw_w2'][j])) - 0.5
    decay = jnp.exp(-jnp.exp(wlog.astype(f32)))
    a = jax.nn.sigmoid(w['rw_a0'][j] + (xa @ w['rw_a1'][j]) @ w['rw_a2'][j])
    g = jax.nn.sigmoid(xg @ w['rw_g1'][j]) @ w['rw_g2'][j]
    kk = k * w['rw_k_k'][j]
    k = k * (1.0 + (a - 1.0) * w['rw_k_a'][j])

    def heads(t):
        return t.astype(f32).reshape(B, T, RW_HEADS, RW_HEAD_DIM)

    r, k, v, a, decay, kk = heads(r), heads(k), heads(v), heads(a), heads(decay), heads(kk)
    kk = kk / jnp.maximum(jnp.sqrt(jnp.sum(kk * kk, axis=-1, keepdims=True)), 1e-12)

    def step(S, inp):
        r_t, d_t, k_t, v_t, kk_t, b_t = inp
        sa = jnp.einsum('bhvk,bhk->bhv', S, -kk_t)
        S = S * d_t[:, :, None, :] + sa[..., None] * b_t[:, :, None, :] + v_t[..., None] * k_t[:, :, None, :]
        return S, jnp.einsum('bhvk,bhk->bhv', S, r_t)

    S_T, y = lax.scan(step, S0.astype(f32), tuple(t.swapaxes(0, 1) for t in (r, decay, k, v, kk, kk * a)))
    y = y.swapaxes(0, 1)
    mean = jnp.mean(y, axis=-1, keepdims=True)
    var = jnp.mean(jnp.square(y - mean), axis=-1, keepdims=True)
    y = ((y - mean) * lax.rsqrt(var + RW_GN_EPS)).reshape(B, T, D) * w['rw_ln_g'][j] + w['rw_ln_b'][j]
    bonus = jnp.sum(r * k * w['rw_r_k'][j], axis=-1, keepdims=True) * v
    y = y + bonus.reshape(B, T, D)
    return (y * g).astype(h.dtype) @ w['rw_w_o'][j], S_T, h[:, -1]


def memory_kv(mem, mem_norm, w_kv, k_gain):
    B = mem.shape[0]
    m = rmsnorm(mem[None], mem_norm[:, None, None])
    kv = jnp.einsum('lbmd,lde->lbme', m, w_kv).reshape(DEPTH, B, N_MEM, 2, XA_HEADS, XA_HEAD_DIM)
    kn = rmsnorm(kv[:, :, :, 0], k_gain[:, None, None, None])
    return jnp.stack([kn, kv[:, :, :, 1]], axis=3)


def mem_attend(h, kv, w_q, q_gain, w_o):
    B, T, _ = h.shape
    q = rmsnorm((h @ w_q).reshape(B, T, XA_HEADS, XA_HEAD_DIM), q_gain)
    s = jnp.einsum('bthe,bmhe->bhtm', q, kv[:, :, 0], preferred_element_type=jnp.float32) * (XA_HEAD_DIM ** -0.5)
    p = jax.nn.softmax(s, axis=-1)
    o = jnp.einsum('bhtm,bmhe->bthe', p.astype(kv.dtype), kv[:, :, 1])
    return o.reshape(B, T, D_MODEL).astype(h.dtype) @ w_o


def conv_ffn(h, buf, w_in, conv_w, conv_b, w_down):
    T = h.shape[1]
    u, gate = jnp.split(h @ w_in, 2, axis=-1)
    uc = jnp.concatenate([buf.astype(u.dtype), u], axis=1)
    c = conv_b
    for jj in range(CONV_W):
        c = c + conv_w[jj] * uc[:, jj:jj + T]
    return (jax.nn.silu(c) * gate) @ w_down, uc[:, T:]


def trunk(x, mem_kv, a_bufs, hg_S, rw_S, rw_shift, ffn_buf, w):
    new_a = ([], [], [])
    new_hg, new_rw, new_sh, new_ffn = [], [], [], []
    for i in range(DEPTH):
        kind, j = i % N_MIXERS, i // N_MIXERS
        h = rmsnorm(x, w['norm_mix'][i])
        if kind == 0:
            bufs = None if a_bufs is None else tuple(b[j] for b in a_bufs)
            y, nb = mixer_dilated(h, w['attn_w_qkv'][j], w['attn_q_gain'][j], w['attn_k_gain'][j], w['attn_w_o'][j], bufs)
            for gi in range(N_GROUPS):
                new_a[gi].append(nb[gi])
        elif kind == 1:
            y, S = mixer_hgrn(h, w['hg_w_in'][j], w['hg_lb'][i], w['hg_out_gain'][j], w['hg_w_o'][j], hg_S[j])
            new_hg.append(S)
        else:
            y, S, sh = mixer_rwkv(h, rw_shift[j], rw_S[j], w, j)
            new_rw.append(S)
            new_sh.append(sh)
        x = x + y
        h = rmsnorm(x, w['norm_mem'][i])
        x = x + mem_attend(h, mem_kv[i], w['xa_w_q'][i], w['xa_q_gain'][i], w['xa_w_o'][i])
        h = rmsnorm(x, w['norm_ffn'][i])
        y, fb = conv_ffn(h, ffn_buf[i], w['ffn_w_in'][i], w['ffn_conv_w'][i], w['ffn_conv_b'][i], w['ffn_w_down'][i])
        new_ffn.append(fb)
        x = x + y
    return (x, tuple(jnp.stack(b) for b in new_a), jnp.stack(new_hg), jnp.stack(new_rw), jnp.stack(new_sh), jnp.stack(new_ffn))


def setup_inputs(seed: int = 0) -> dict:
    key = jax.random.key(seed)
    ks = iter(jax.random.split(key, 64))
    D = D_MODEL

    def nrm(shape, scale=1.0):
        return jax.random.normal(next(ks), shape, jnp.float32) * scale

    def gain(shape):
        return 1.0 + nrm(shape, 0.05)

    def unif(shape, lo, hi):
        return jax.random.uniform(next(ks), shape, jnp.float32, lo, hi)

    return {
        'x_prompt': nrm((BATCH, SEQ, D)),
        'x_sample': nrm((DEC_BATCH, DEC_SEQ, D)),
        'mem_prompt': nrm((BATCH, N_MEM, D)),
        'cache_attn_kv_w128': nrm((N_A, DEC_BATCH, min(DIL_WINDOWS[0], PAST_LEN), 2, A_HEADS, A_HEAD_DIM)),
        'cache_attn_kv_w512': nrm((N_A, DEC_BATCH, min(DIL_WINDOWS[1], PAST_LEN), 2, A_HEADS, A_HEAD_DIM)),
        'cache_attn_kv_w2048': nrm((N_A, DEC_BATCH, min(DIL_WINDOWS[2], PAST_LEN), 2, A_HEADS, A_HEAD_DIM)),
        'state_hgrn': nrm((N_B, DEC_BATCH, HG_HEADS, HG_DK, HG_DV), 0.3),
        'state_rwkv': nrm((N_C, DEC_BATCH, RW_HEADS, RW_HEAD_DIM, RW_HEAD_DIM), 0.3),
        'state_rwkv_shift': nrm((N_C, DEC_BATCH, D)),
        'state_ffn_conv': nrm((DEPTH, DEC_BATCH, CONV_W - 1, D_FF)),
        'cache_mem_kv': nrm((DEPTH, DEC_BATCH, N_MEM, 2, XA_HEADS, XA_HEAD_DIM)),
        'norm_mix': gain((DEPTH, D)),
        'norm_mem': gain((DEPTH, D)),
        'norm_ffn': gain((DEPTH, D)),
        'mem_norm': gain((DEPTH, D)),
        'attn_w_qkv': nrm((N_A, D, 3 * N_GROUPS * A_HEADS * A_HEAD_DIM), D ** -0.5),
        'attn_q_gain': gain((N_A, N_GROUPS, A_HEAD_DIM)),
        'attn_k_gain': gain((N_A, N_GROUPS, A_HEAD_DIM)),
        'attn_w_o': nrm((N_A, A_HEADS * A_HEAD_DIM, D), (A_HEADS * A_HEAD_DIM) ** -0.5),
        'hg_w_in': nrm((N_B, D, 4 * HG_WIDTH), D ** -0.5),
        'hg_lb_logits': nrm((DEPTH, HG_WIDTH), 0.3),
        'hg_out_gain': gain((N_B, HG_DV)),
        'hg_w_o': nrm((N_B, HG_HEADS * HG_DV, D), (HG_HEADS * HG_DV) ** -0.5),
        'rw_mu': unif((N_C, 6, D), 0.0, 1.0),
        'rw_w_rkv': nrm((N_C, 3, D, D), D ** -0.5),
        'rw_w0': unif((N_C, D), -6.0, 1.0),
        'rw_w1': nrm((N_C, D, RW_LORA_W), D ** -0.5),
        'rw_w2': nrm((N_C, RW_LORA_W, D), RW_LORA_W ** -0.5),
        'rw_a0': nrm((N_C, D), 0.1),
        'rw_a1': nrm((N_C, D, RW_LORA_A), D ** -0.5),
        'rw_a2': nrm((N_C, RW_LORA_A, D), RW_LORA_A ** -0.5),
        'rw_g1': nrm((N_C, D, RW_LORA_G), D ** -0.5),
        'rw_g2': nrm((N_C, RW_LORA_G, D), RW_LORA_G ** -0.5),
        'rw_k_k': 0.85 + nrm((N_C, D), 0.05),
        'rw_k_a': gain((N_C, D)),
        'rw_r_k': nrm((N_C, RW_HEADS, RW_HEAD_DIM), 0.1),
        'rw_ln_g': gain((N_C, D)),
        'rw_ln_b': nrm((N_C, D), 0.02),
        'rw_w_o': nrm((N_C, D, D), D ** -0.5),
        'xa_w_q': nrm((DEPTH, D, D), D ** -0.5),
        'xa_w_kv': nrm((DEPTH, D, 2 * D), D ** -0.5),
        'xa_q_gain': gain((DEPTH, XA_HEAD_DIM)),
        'xa_k_gain': gain((DEPTH, XA_HEAD_DIM)),
        'xa_w_o': nrm((DEPTH, D, D), D ** -0.5),
        'ffn_w_in': nrm((DEPTH, D, 2 * D_FF), D ** -0.5),
        'ffn_conv_w': nrm((DEPTH, CONV_W, D_FF), 0.5),
        'ffn_conv_b': nrm((DEPTH, D_FF), 0.02),
        'ffn_w_down': nrm((DEPTH, D_FF, D), D_FF ** -0.5),
    }


def reference(x_prompt, x_sample, mem_prompt, cache_attn_kv_w128, cache_attn_kv_w512, cache_attn_kv_w2048,
              state_hgrn, state_rwkv, state_rwkv_shift, state_ffn_conv, cache_mem_kv,
              norm_mix, norm_mem, norm_ffn, mem_norm,
              attn_w_qkv, attn_q_gain, attn_k_gain, attn_w_o,
              hg_w_in, hg_lb_logits, hg_out_gain, hg_w_o,
              rw_mu, rw_w_rkv, rw_w0, rw_w1, rw_w2, rw_a0, rw_a1, rw_a2, rw_g1, rw_g2,
              rw_k_k, rw_k_a, rw_r_k, rw_ln_g, rw_ln_b, rw_w_o,
              xa_w_q, xa_w_kv, xa_q_gain, xa_k_gain, xa_w_o,
              ffn_w_in, ffn_conv_w, ffn_conv_b, ffn_w_down):
    lb = jnp.cumsum(jax.nn.softmax(hg_lb_logits.astype(jnp.float32), axis=0), axis=0)
    lb = lb - lb[0:1]
    w = {
        'norm_mix': norm_mix, 'norm_mem': norm_mem, 'norm_ffn': norm_ffn,
        'attn_w_qkv': attn_w_qkv, 'attn_q_gain': attn_q_gain, 'attn_k_gain': attn_k_gain, 'attn_w_o': attn_w_o,
        'hg_w_in': hg_w_in, 'hg_lb': lb, 'hg_out_gain': hg_out_gain, 'hg_w_o': hg_w_o,
        'rw_mu': rw_mu, 'rw_w_rkv': rw_w_rkv, 'rw_w0': rw_w0, 'rw_w1': rw_w1, 'rw_w2': rw_w2,
        'rw_a0': rw_a0, 'rw_a1': rw_a1, 'rw_a2': rw_a2, 'rw_g1': rw_g1, 'rw_g2': rw_g2,
        'rw_k_k': rw_k_k, 'rw_k_a': rw_k_a, 'rw_r_k': rw_r_k, 'rw_ln_g': rw_ln_g, 'rw_ln_b': rw_ln_b, 'rw_w_o': rw_w_o,
        'xa_w_q': xa_w_q, 'xa_q_gain': xa_q_gain, 'xa_w_o': xa_w_o,
        'ffn_w_in': ffn_w_in, 'ffn_conv_w': ffn_conv_w, 'ffn_conv_b': ffn_conv_b, 'ffn_w_down': ffn_w_down,
    }
    Bp = x_prompt.shape[0]
    mem_kv_prompt = memory_kv(mem_prompt, mem_norm, xa_w_kv, xa_k_gain)
    y_prompt, a_p, hg_p, rw_p, sh_p, ffn_p = trunk(
        x_prompt, mem_kv_prompt, None,
        jnp.zeros((N_B, Bp, HG_HEADS, HG_DK, HG_DV), jnp.float32),
        jnp.zeros((N_C, Bp, RW_HEADS, RW_HEAD_DIM, RW_HEAD_DIM), jnp.float32),
        jnp.zeros((N_C, Bp, D_MODEL), x_prompt.dtype),
        jnp.zeros((DEPTH, Bp, CONV_W - 1, D_FF), x_prompt.dtype), w)
    y_sample, a_s, hg_s, rw_s, sh_s, ffn_s = trunk(
        x_sample, cache_mem_kv, (cache_attn_kv_w128, cache_attn_kv_w512, cache_attn_kv_w2048),
        state_hgrn, state_rwkv, state_rwkv_shift, state_ffn_conv, w)
    return (y_prompt, y_sample, a_p[0], a_p[1], a_p[2], hg_p, rw_p, sh_p, ffn_p, mem_kv_prompt,
            a_s[0], a_s[1], a_s[2], hg_s, rw_s, sh_s, ffn_s)
```

```python
import contextlib
import numpy as np
import concourse.bass as bass
import concourse.mybir as mybir
from concourse.bass_utils import run_bass_kernel_spmd

F32 = mybir.dt.float32
BF16 = mybir.dt.bfloat16
AF = mybir.ActivationFunctionType
ALU = mybir.AluOpType
AX = mybir.AxisListType
ENGS = ("pe", "act", "dve", "pool", "sp")
NDSEM = 48
NHW = 32

NT = 2080
TB = [(0, 512), (512, 512), (1024, 512), (1536, 512), (2048, 32)]
D = 1024
DFF = 2816
NJ = 22
EPS = 1e-6
DILS = (1, 4, 16)
WINS = (128, 512, 2048)
NWB = 6
STAGES = {"attn": True, "hgrn": True, "rwkv": True, "mem": True, "ffn": True, "layers": 4, "a_d2d": 1, "a_kvout": 1, "a_sout": 1, "a_pu": 1, "a_su": 1, "a_groups": 3}


class Op:
    __slots__ = ("eng", "fn", "reads", "writes", "dma", "seq", "waits", "signal", "cnt", "dsem", "dval", "snap")

    def __init__(self, eng, fn, reads, writes, dma):
        self.eng, self.fn, self.reads, self.writes, self.dma = eng, fn, tuple(reads), tuple(writes), dma
        self.waits = []
        self.signal = False
        self.cnt = 0
        self.dsem = -1
        self.dval = 0
        self.snap = None


class Prog:
    def __init__(self, nc):
        self.nc = nc
        self.ops = []

    def add(self, eng, fn, reads=(), writes=(), dma=False):
        self.ops.append(Op(eng, fn, reads, writes, dma))

    def barrier(self):
        self.ops.append(None)

    def analyse(self):
        ops = self.ops
        last_w = {}
        readers = {}
        seqc = {e: 0 for e in ENGS}
        known = {e: {x: 0 for x in ENGS} for e in ENGS}
        kd = {e: set() for e in ENGS}
        dsem_last = [None] * NDSEM
        dsem_cnt = [0] * NDSEM
        nd = 0
        nds = 0
        pend = {e: [] for e in ENGS}
        last_op = {e: None for e in ENGS}
        for i, op in enumerate(ops):
            if op is None:
                for E in ENGS:
                    wl = []
                    for E2 in ENGS:
                        j = last_op[E2]
                        if j is not None and known[E][E2] < ops[j].seq:
                            ops[j].signal = True
                            wl.append(("e", E2, j))
                            known[E][E2] = ops[j].seq
                    for s_ in range(NDSEM):
                        if dsem_cnt[s_] > 0:
                            wl.append(("d", s_, 16 * dsem_cnt[s_]))
                    pend[E] = pend[E] + wl
                last_w.clear()
                readers.clear()
                dsem_last = [None] * NDSEM
                continue
            E = op.eng
            if pend[E]:
                op.waits.extend(pend[E])
                pend[E] = []
            seqc[E] += 1
            op.seq = seqc[E]
            own = op.seq - 1 if E in ("pe", "sp") else 0
            if own > known[E][E]:
                known[E][E] = own
            deps = set()
            for k in op.reads:
                j = last_w.get(k)
                if j is not None:
                    deps.add(j)
            for k in op.writes:
                j = last_w.get(k)
                if j is not None and (ops[j].dma or op.dma or ops[j].eng != E or E != "pe"):
                    deps.add(j)
                rd = readers.get(k)
                if rd:
                    for j in rd.values():
                        if ops[j].dma or op.dma or ops[j].eng != E or E != "pe":
                            deps.add(j)
            deps.discard(i)
            if op.dma:
                if E == "pool":
                    s = NHW + (nds % (NDSEM - NHW))
                    nds += 1
                else:
                    s = nd % NHW
                    nd += 1
                if dsem_last[s] is not None:
                    deps.add(dsem_last[s])
                dsem_cnt[s] += 1
                op.dsem, op.dval = s, 16 * dsem_cnt[s]
                dsem_last[s] = i
            for j in sorted(deps):
                p = ops[j]
                if p.dma:
                    if j in kd[E]:
                        continue
                    op.waits.append(("d", p.dsem, p.dval))
                    kd[E].add(j)
                    for x in ENGS:
                        if p.snap[x] > known[E][x]:
                            known[E][x] = p.snap[x]
                else:
                    if known[E][p.eng] >= p.seq:
                        continue
                    p.signal = True
                    op.waits.append(("e", p.eng, j))
                    known[E][p.eng] = p.seq
                    for x in ENGS:
                        if p.snap[x] > known[E][x]:
                            known[E][x] = p.snap[x]
            op.snap = dict(known[E])
            if not op.dma:
                last_op[E] = i
            for k in op.writes:
                last_w[k] = i
                readers[k] = {}
            for k in op.reads:
                readers.setdefault(k, {})[("dma", i) if op.dma else E] = i
        c = {e: 0 for e in ENGS}
        for op in ops:
            if op is not None and op.signal:
                c[op.eng] += 1
                op.cnt = c[op.eng]
        self.sig_tot = c
        fin = {}
        for op in ops:
            if op is not None and op.dma:
                fin[op.dsem] = max(fin.get(op.dsem, 0), op.dval)
        self.final = fin

    def emit(self):
        nc = self.nc
        self.analyse()
        ops = self.ops
        with contextlib.ExitStack() as st:
            psem = {e: st.enter_context(nc.semaphore("prog_" + e)) for e in ENGS}
            dsem = [st.enter_context(nc.semaphore("dmas%d" % i)) for i in range(NDSEM)]
            block = st.enter_context(nc.Block())

            def stream(ename):
                def body(eng):
                    for op in ops:
                        if op is None or op.eng != ename:
                            continue
                        for w in op.waits:
                            if w[0] == "d":
                                eng.wait_ge(dsem[w[1]], w[2])
                            else:
                                eng.wait_ge(psem[w[1]], ops[w[2]].cnt)
                        ins = op.fn(eng)
                        if op.dma:
                            ins.then_inc(dsem[op.dsem], 16)
                        elif op.signal:
                            ins.then_inc(psem[ename], 1)
                    if ename == "sp":
                        for s, v in self.final.items():
                            eng.wait_ge(dsem[s], v)
                        for e2 in ENGS:
                            if e2 != "sp" and self.sig_tot[e2] > 0:
                                eng.wait_ge(psem[e2], self.sig_tot[e2])
                return body

            block.tensor(stream("pe"))
            block.scalar(stream("act"))
            block.vector(stream("dve"))
            block.gpsimd(stream("pool"))
            block.sync(stream("sp"))


class ColAlloc:
    def __init__(self):
        self.n = 0
        self.cols = {}

    def add(self, name, ncols):
        self.cols[name] = self.n
        self.n += ncols
        return self.cols[name]


def fm_vec(v):
    v = np.asarray(v, np.float32).reshape(-1)
    nc_ = v.size // 128
    return np.ascontiguousarray(v.reshape(nc_, 128).T)


def pv_layout():
    ca = ColAlloc()
    for l in range(4):
        for nm in ("norm_mix", "norm_mem", "norm_ffn", "mem_norm"):
            ca.add("%s%d" % (nm, l), 8)
        ca.add("xa_q_gain%d" % l, 2)
        ca.add("xa_k_gain%d" % l, 2)
        for t in range(3):
            ca.add("conv_w%d_%d" % (l, t), NJ)
        ca.add("conv_b%d" % l, NJ)
    for ja in range(2):
        for g in range(3):
            ca.add("aq_gain%d_%d" % (ja, g), 1)
            ca.add("ak_gain%d_%d" % (ja, g), 1)
    for l in range(4):
        ca.add("hg_lb%d" % l, 8)
    ca.add("hg_out_gain", 1)
    for j in range(6):
        ca.add("rw_mu%d" % j, 8)
    for nm in ("rw_w0", "rw_a0", "rw_k_k", "rw_k_a", "rw_r_k", "rw_ln_g", "rw_ln_b"):
        ca.add(nm, 8)
    return ca


def build_pv(inp):
    ca = pv_layout()
    pv = np.zeros((128, ca.n), np.float32)

    def put(name, v):
        a = fm_vec(v)
        pv[:, ca.cols[name]:ca.cols[name] + a.shape[1]] = a

    for l in range(4):
        for nm in ("norm_mix", "norm_mem", "norm_ffn", "mem_norm"):
            put("%s%d" % (nm, l), inp[nm][l])
        put("xa_q_gain%d" % l, inp["xa_q_gain"][l])
        put("xa_k_gain%d" % l, inp["xa_k_gain"][l])
        for t in range(3):
            put("conv_w%d_%d" % (l, t), inp["ffn_conv_w"][l, t])
        put("conv_b%d" % l, inp["ffn_conv_b"][l])
        put("hg_lb%d" % l, inp["hg_lb_logits"][l])
    for ja in range(2):
        for g in range(3):
            put("aq_gain%d_%d" % (ja, g), inp["attn_q_gain"][ja, g])
            put("ak_gain%d_%d" % (ja, g), inp["attn_k_gain"][ja, g])
    put("hg_out_gain", inp["hg_out_gain"][0])
    for j in range(6):
        put("rw_mu%d" % j, inp["rw_mu"][0, j])
    for nm in ("rw_w0", "rw_a0", "rw_k_k", "rw_k_a", "rw_r_k", "rw_ln_g", "rw_ln_b"):
        put(nm, inp[nm][0])
    return pv, ca


def cst_layout():
    ca = ColAlloc()
    ca.add("ident", 128)
    ca.add("ones", 128)
    ca.add("m_own", 128)
    ca.add("m_prev", 128)
    ca.add("blk64", 128)
    ca.add("own_s", 96)
    ca.add("bd_su", 128)
    ca.add("bd_sl", 128)
    ca.add("bd_iu", 128)
    return ca


def build_cst():
    ca = cst_layout()
    c = np.zeros((128, ca.n), np.float32)
    j = np.arange(128)[:, None]
    i = np.arange(128)[None, :]
    c[:, ca.cols["ident"]:ca.cols["ident"] + 128] = (j == i)
    c[:, ca.cols["ones"]:ca.cols["ones"] + 128] = 1.0
    c[:, ca.cols["m_own"]:ca.cols["m_own"] + 128] = (j <= i)
    c[:, ca.cols["m_prev"]:ca.cols["m_prev"] + 128] = (j >= i)
    c[:, ca.cols["blk64"]:ca.cols["blk64"] + 128] = ((j // 64) == (i // 64))
    col = ca.cols["own_s"]
    for g in range(3):
        R = min(DILS[g], 8)
        nq = 8 // R
        for s in range(4):
            for r in range(R):
                for u in range(nq):
                    for ip in range(8):
                        if ip % R == r and ip <= r + R * u:
                            c[8 * s + ip, col + u] = 1.0
                col += nq
    same = ((j // 64) == (i // 64))
    c[:, ca.cols["bd_su"]:ca.cols["bd_su"] + 128] = same & ((j % 64) < (i % 64))
    c[:, ca.cols["bd_sl"]:ca.cols["bd_sl"] + 128] = same & ((j % 64) > (i % 64))
    c[:, ca.cols["bd_iu"]:ca.cols["bd_iu"] + 128] = same & ((j % 64) <= (i % 64))
    return c, ca


def own_s_col(ca, g, s, r):
    col = ca.cols["own_s"]
    for gg in range(3):
        R = min(DILS[gg], 8)
        nq = 8 // R
        if gg == g:
            return col + (s * R + r) * nq
        col += 4 * R * nq
    raise ValueError


def tile_w(w, bw=128):
    K, N = w.shape
    return np.ascontiguousarray(w.reshape(K // 128, 128, N // bw, bw).transpose(2, 1, 0, 3))


class Builder:
    def __init__(self, nc, dr, pvca, cca):
        self.nc, self.dr, self.pvca, self.cca = nc, dr, pvca, cca
        self.P = Prog(nc)
        self.psi = 0
        self.wi = 0
        self.scri = 0
        self.off = 0

    def alloc(self, cols):
        a = self.arena[:, self.off:self.off + cols]
        self.off += cols
        assert self.off <= self.acols, ("SBUF arena overflow", self.off, self.acols)
        return a

    def alloc16(self, cols):
        return self.alloc((cols + 1) // 2).bitcast(BF16)[:, 0:cols]

    def mark(self):
        return self.off

    def release(self, m):
        self.P.barrier()
        self.off = m

    def newps(self):
        i = self.psi % 8
        self.psi += 1
        return self.ps[i], "ps%d" % i

    def scr(self):
        i = self.scri % 4
        self.scri += 1
        return self.scrb[i], "scr%d" % i

    def mm(self, out, lhsT, rhs, start, stop, r, w):
        self.P.add("pe", lambda e: e.matmul(out, lhsT, rhs, start=start, stop=stop), r, w)

    def tr(self, out, in_, ident, r, w):
        self.P.add("pe", lambda e: e.transpose(out, in_, ident), r, w)

    def act(self, out, in_, func, r, w, bias=None, scale=None, accum=None):
        kw = {}
        if bias is not None:
            kw["bias"] = bias
        if scale is not None:
            kw["scale"] = scale
        if accum is not None:
            kw["accum_out"] = accum
        self.P.add("act", lambda e: e.activation(out, in_, func, **kw), r, w)

    def cp(self, eng, out, in_, r, w):
        if eng == "act":
            self.P.add("act", lambda e: e.copy(out, in_), r, w)
        else:
            self.P.add(eng, lambda e: e.tensor_copy(out, in_), r, w)

    def tt(self, out, a, b, op, r, w, eng="dve"):
        self.P.add(eng, lambda e: e.tensor_tensor(out, a, b, op), r, w)

    def ts(self, out, a, s1, s2, op0, op1, r, w, eng="dve"):
        if s2 is None:
            self.P.add(eng, lambda e: e.tensor_scalar(out, a, s1, None, op0), r, w)
        else:
            self.P.add(eng, lambda e: e.tensor_scalar(out, a, s1, s2, op0, op1), r, w)

    def stt(self, out, a, s, b, op0, op1, r, w, eng="dve"):
        self.P.add(eng, lambda e: e.scalar_tensor_tensor(out, a, s, b, op0, op1), r, w)

    def recip(self, out, in_, r, w):
        self.P.add("dve", lambda e: e.reciprocal(out, in_), r, w)

    def memset(self, out, val, w, eng="dve"):
        self.P.add(eng, lambda e: e.memset(out, val), (), w)

    def dma(self, out, in_, r, w, q="sp", slow=False):
        if slow:
            self.P.add(q, lambda e: e.dma_start(out=out, in_=in_, allow_slow_non_contiguous=True), r, w, dma=True)
        else:
            self.P.add(q, lambda e: e.dma_start(out=out, in_=in_), r, w, dma=True)

    def pvc(self, name, k=0, n=1):
        c = self.pvca.cols[name] + k
        return self.pv[:, c:c + n]

    def cst(self, name, rows=128, n=None, k=0):
        c = self.cca.cols[name] + k
        if n is None:
            n = 128
        return self.c32[:rows, c:c + n]

    def cst16(self, name, rows=128, n=None, k=0):
        c = self.cca.cols[name] + k
        if n is None:
            n = 128
        return self.c16[:rows, c:c + n]

    def load_w(self, wdram, KC, rows=128, ncol=128):
        i = self.wi % NWB
        self.wi += 1
        wt = self.wb[i]
        key = "wb%d" % i
        self.dma(wt[:rows, :KC, :ncol], wdram, (), [key], q="pool")
        return wt, key

    def linear(self, wdram, KC, rhs_fn, rkeys_fn, cons, blocks=(0, 1, 2, 3, 4), rows=128, ncol=128):
        wt, wk = self.load_w(wdram, KC, rows, ncol)
        for bi in blocks:
            c0, n = TB[bi]
            ps, pk = self.newps()
            for kc in range(KC):
                self.mm(ps[:ncol, :n], wt[:rows, kc, :ncol], rhs_fn(kc, c0, n), kc == 0, kc == KC - 1,
                        [wk] + rkeys_fn(kc, bi), [pk])
            cons(bi, c0, n, ps, pk)

    def h_rhs(self, kc, c0, n):
        return self.hT[:, kc, c0:c0 + n]

    def h_keys(self, kc, bi):
        return ["h%d.%d" % (kc, bi)]

    def rstd_block(self, srcs, skeys, n, dim, eps, ones=None):
        if ones is None:
            ones = self.cst("ones")
        ps, pk = self.newps()
        for ci, (ap, k) in enumerate(zip(srcs, skeys)):
            sq, sk = self.scr()
            self.act(sq[:, :n], ap, AF.Square, [k], [sk])
            self.mm(ps[:, :n], ones, sq[:, :n], ci == 0, ci == len(srcs) - 1, [sk, "cst"], [pk])
        rs, rk = self.scr()
        self.act(rs[:, :n], ps[:, :n], AF.Sqrt, [pk], [rk], scale=1.0 / dim, bias=self.epsc[:, 0:1] if eps == EPS else eps)
        self.recip(rs[:, :n], rs[:, :n], [rk], [rk])
        return rs, rk

    def dnorm(self, gname):
        for bi, (c0, n) in enumerate(TB):
            rs, rk = self.rstd_block([self.xT[:, c, c0:c0 + n] for c in range(8)],
                                     ["x%d.%d" % (c, bi) for c in range(8)], n, D, EPS)
            for c in range(8):
                self.stt(self.hT[:, c, c0:c0 + n], self.xT[:, c, c0:c0 + n], self.pvc(gname, c), rs[:, :n],
                         ALU.mult, ALU.mult, ["x%d.%d" % (c, bi), rk, "pv"], ["h%d.%d" % (c, bi)])

    def add_resid(self, nb):
        def cons(bi, c0, n, ps, pk):
            k = "x%d.%d" % (nb, bi)
            self.tt(self.xT[:, nb, c0:c0 + n], ps[:, :n], self.xT[:, nb, c0:c0 + n], ALU.add, [pk, k], [k])
        return cons

    def setup(self, st):
        nc = self.nc
        self.acols = 52800
        self.arena = st.enter_context(nc.sbuf_tensor("arena", [128, self.acols], F32))
        self.ps = [st.enter_context(nc.psum_tensor("psb%d" % i, [128, 512], F32)) for i in range(8)]
        self.xT = self.alloc(8 * NT).rearrange("p (c t) -> p c t", t=NT)
        self.hT = self.alloc16(8 * NT).rearrange("p (c t) -> p c t", t=NT)
        self.pv = self.alloc(self.pvca.n)
        self.c32 = self.alloc(self.cca.n)
        self.c16 = self.alloc16(self.cca.n)
        self.epsc = self.alloc(2)
        self.wb = [self.alloc16(8 * 128).rearrange("p (k n) -> p k n", n=128) for _ in range(NWB)]
        self.scrb = [self.alloc(512) for _ in range(4)]
        self.dma(self.pv, self.dr["pv"], (), ["pv"])
        self.dma(self.c32, self.dr["cst"], (), ["cst"])
        self.cp("dve", self.c16, self.c32, ["cst"], ["cst16"])
        self.memset(self.epsc[:, 0:1], EPS, ["epsc"])
        self.memset(self.epsc[:, 1:2], 64e-5, ["epsc"])
        self.P.barrier()

    def load_x(self):
        m = self.mark()
        stg = [self.alloc(1024) for _ in range(2)]
        ident = self.cst("ident")
        for tt_ in range(17):
            sb = stg[tt_ % 2]
            sk = "xstg%d" % (tt_ % 2)
            if tt_ < 16:
                rows, c0, src = 128, tt_ * 128, self.dr["xp"][tt_ * 128:(tt_ + 1) * 128, :]
            else:
                rows, c0, src = 32, 2048, self.dr["xs"]
            self.dma(sb[:rows, :], src, (), [sk])
            bi = min(c0 // 512, 4)
            for half in range(2):
                ps, pk = self.newps()
                for q in range(4):
                    c = half * 4 + q
                    self.tr(ps[:, q * 128:q * 128 + rows], sb[:rows, c * 128:(c + 1) * 128], ident[:rows, :rows],
                            [sk, "cst"], [pk])
                self.cp("act" if half == 0 else "dve",
                        self.xT[:, half * 4:half * 4 + 4, c0:c0 + rows],
                        ps[:, :].rearrange("p (q t) -> p q t", t=128)[:, :, :rows],
                        [pk], ["x%d.%d" % (half * 4 + q, bi) for q in range(4)])
        self.release(m)

    def store_y(self):
        m = self.mark()
        stg = [self.alloc(1024) for _ in range(2)]
        ident = self.cst("ident")
        for tt_ in range(17):
            sb = stg[tt_ % 2]
            sk = "ystg%d" % (tt_ % 2)
            if tt_ < 16:
                rows, c0, dst = 128, tt_ * 128, self.dr["y_p"][tt_ * 128:(tt_ + 1) * 128, :]
            else:
                rows, c0, dst = 32, 2048, self.dr["y_s"]
            bi = min(c0 // 512, 4)
            for half in range(2):
                ps, pk = self.newps()
                for q in range(4):
                    c = half * 4 + q
                    self.tr(ps[:rows, q * 128:(q + 1) * 128], self.xT[:, c, c0:c0 + rows], ident,
                            ["x%d.%d" % (c, bi), "cst"], [pk])
                self.cp("act" if half == 0 else "dve", sb[:rows, half * 512:(half + 1) * 512], ps[:rows, :], [pk], [sk])
            self.dma(dst, sb[:rows, :], [sk], ())
        self.release(m)

    def ffn(self, l):
        self.dnorm("norm_ffn%d" % l)
        m = self.mark()
        UW = 2050 + 40
        ub = [self.alloc(UW) for _ in range(2)]
        cb = [self.alloc(NT) for _ in range(2)]
        sl = cb
        GS = 8
        aT = self.alloc16(GS * NT).rearrange("p (g t) -> p g t", t=NT)
        fcst = self.alloc(NJ * 8).rearrange("p (j s t) -> p j s t", s=4, t=2)
        fco = self.alloc(NJ * 10).rearrange("p (j s t) -> p j s t", s=5, t=2)
        self.dma(fcst, self.dr["st_fc"][l], (), ["fcst"])
        for i in range(2):
            self.memset(ub[i][:, 0:2], 0.0, ["u%d" % i])
        win = self.dr["w_fin"][l]
        wdn = self.dr["w_fdn"][l]
        groups = [list(range(a, min(a + GS, NJ))) for a in range(0, NJ, GS)]
        cnt = 0
        for grp in groups:
            for gi, j in enumerate(grp):
                u = ub[cnt % 2]
                uk = "u%d" % (cnt % 2)
                c_ = cb[cnt % 2]
                ck = "c%d" % (cnt % 2)
                s_ = sl[cnt % 2]
                sk = ck
                cnt += 1
                us = u[:, 2050:2090].rearrange("p (s k) -> p s k", k=10)

                def cons_u(bi, c0, n, ps, pk, u=u, uk=uk, us=us):
                    if bi < 4:
                        self.cp("act", u[:, 2 + c0:2 + c0 + n], ps[:, :n], [pk], [uk])
                    else:
                        self.cp("act", us[:, :, 2:10], ps[:, :32].rearrange("p (s i) -> p s i", i=8), [pk], [uk])
                self.linear(win[j], 8, self.h_rhs, self.h_keys, cons_u)
                self.cp("dve", us[:, :, 0:2], fcst[:, j, :, :], ["fcst", uk], [uk])
                w0, w1, w2, bb = (self.pvc("conv_w%d_0" % l, j), self.pvc("conv_w%d_1" % l, j),
                                  self.pvc("conv_w%d_2" % l, j), self.pvc("conv_b%d" % l, j))
                self.ts(c_[:, 0:2048], u[:, 2:2050], w2, bb, ALU.mult, ALU.add, [uk, "pv"], [ck])
                self.stt(c_[:, 0:2048], u[:, 1:2049], w1, c_[:, 0:2048], ALU.mult, ALU.add, [uk, ck, "pv"], [ck])
                self.stt(c_[:, 0:2048], u[:, 0:2048], w0, c_[:, 0:2048], ALU.mult, ALU.add, [uk, ck, "pv"], [ck])
                cs = c_[:, 2048:2080].rearrange("p (s i) -> p s i", i=8)
                self.ts(cs, us[:, :, 2:10], w2, bb, ALU.mult, ALU.add, [uk, "pv"], [ck])
                self.stt(cs, us[:, :, 1:9], w1, cs, ALU.mult, ALU.add, [uk, ck, "pv"], [ck])
                self.stt(cs, us[:, :, 0:8], w0, cs, ALU.mult, ALU.add, [uk, ck, "pv"], [ck])
                self.act(s_[:, :], c_[:, :], AF.Silu, [ck], [sk])
                self.cp("act", fco[:, j, 0, :], u[:, 2048:2050], [uk], ["fco"])
                self.cp("act", fco[:, j, 1:5, :], us[:, :, 8:10], [uk], ["fco"])

                def cons_g(bi, c0, n, ps, pk, gi=gi, s_=s_, sk=sk):
                    self.tt(aT[:, gi, c0:c0 + n], ps[:, :n], s_[:, c0:c0 + n], ALU.mult, [pk, sk], ["a%d.%d" % (gi, bi)])
                self.linear(win[NJ + j], 8, self.h_rhs, self.h_keys, cons_g)
            g0, gl = grp[0], len(grp)
            for nb in range(8):
                self.linear(wdn[nb][:, g0:g0 + gl, :], gl, lambda kc, c0, n: aT[:, kc, c0:c0 + n],
                            lambda kc, bi: ["a%d.%d" % (kc, bi)], self.add_resid(nb))
        self.dma(self.dr["fc_o"][l], fco, ["fco"], ())
        self.release(m)

    def mem_prep(self):
        self.memTn = self.alloc(8 * 256).rearrange("p (c t) -> p c t", t=256)
        m = self.mark()
        stg = [self.alloc(1024) for _ in range(2)]
        raw = self.alloc(8 * 256).rearrange("p (c t) -> p c t", t=256)
        ident = self.cst("ident")
        for mb in range(2):
            self.dma(stg[mb], self.dr["mem"][mb * 128:(mb + 1) * 128, :], (), ["mstg%d" % mb])
            for half in range(2):
                ps, pk = self.newps()
                for q in range(4):
                    c = half * 4 + q
                    self.tr(ps[:, q * 128:(q + 1) * 128], stg[mb][:, c * 128:(c + 1) * 128], ident, ["mstg%d" % mb, "cst"], [pk])
                self.cp("act", raw[:, half * 4:half * 4 + 4, mb * 128:(mb + 1) * 128],
                        ps[:, :].rearrange("p (q t) -> p q t", t=128), [pk], ["mraw"])
        rs, rk = self.rstd_block([raw[:, c, :] for c in range(8)], ["mraw"] * 8, 256, D, EPS)
        for c in range(8):
            self.tt(self.memTn[:, c, :], raw[:, c, :], rs[:, :256], ALU.mult, ["mraw", rk], ["memTn"])
        self.release(m)

    def mem_kv(self, l, KTm, Vm):
        m = self.mark()
        ml = self.alloc16(8 * 256).rearrange("p (c t) -> p c t", t=256)
        kvraw = self.alloc(16 * 256).rearrange("p (b t) -> p b t", t=256)
        stage = self.alloc(2 * 2048).rearrange("p (mb c) -> p mb c", c=2048)
        for c in range(8):
            self.ts(ml[:, c, :], self.memTn[:, c, :], self.pvc("mem_norm%d" % l, c), None, ALU.mult, None,
                    ["memTn", "pv"], ["ml"])
        wkv = self.dr["w_xkv"][l]
        for blk in range(16):
            wt, wk = self.load_w(wkv[blk], 8)
            ps, pk = self.newps()
            for kc in range(8):
                self.mm(ps[:, :256], wt[:, kc, :], ml[:, kc, :], kc == 0, kc == 7, [wk, "ml"], [pk])
            self.cp("act", kvraw[:, blk, :], ps[:, :256], [pk], ["kvraw%d" % blk])
        for h in range(4):
            rs, rk = self.rstd_block([kvraw[:, 2 * h + e, :] for e in range(2)], ["kvraw%d" % (2 * h + e) for e in range(2)],
                                     256, 256, EPS)
            for e in range(2):
                b_ = 2 * h + e
                self.stt(kvraw[:, b_, :], kvraw[:, b_, :], self.pvc("xa_k_gain%d" % l, e), rs[:, :256], ALU.mult, ALU.mult,
                         ["kvraw%d" % b_, rk, "pv"], ["kvraw%d" % b_])
                self.cp("act", KTm[:, b_, :], kvraw[:, b_, :], ["kvraw%d" % b_], ["KTm"])
        ident = self.cst("ident")
        for mb in range(2):
            for q4 in range(4):
                ps, pk = self.newps()
                for q in range(4):
                    blk = q4 * 4 + q
                    self.tr(ps[:, q * 128:(q + 1) * 128], kvraw[:, blk, mb * 128:(mb + 1) * 128], ident,
                            ["kvraw%d" % blk, "cst"], [pk])
                self.cp("act" if q4 % 2 == 0 else "dve", stage[:, mb, q4 * 512:(q4 + 1) * 512], ps[:, :], [pk], ["mstage"])
        self.dma(self.dr["mkv_o"][l].rearrange("(mb m) c -> m mb c", m=128), stage, ["mstage"], ())
        self.cp("dve", Vm, stage[:, :, 1024:2048], ["mstage"], ["Vm"])
        self.release(m)

    def mem_attend(self, l):
        self.dnorm("norm_mem%d" % l)
        m0 = self.mark()
        KTm = self.alloc16(8 * 256).rearrange("p (b t) -> p b t", t=256)
        Vm = self.alloc16(2 * 1024).rearrange("p (mb c) -> p mb c", c=1024)
        self.mem_kv(l, KTm, Vm)
        qraw = self.alloc(2 * NT).rearrange("p (e t) -> p e t", t=NT)
        q16 = self.alloc16(2 * NT).rearrange("p (e t) -> p e t", t=NT)
        oT = self.alloc16(2 * NT).rearrange("p (b t) -> p b t", t=NT)
        pT = [self.alloc16(2 * 512).rearrange("p (mb t) -> p mb t", t=512) for _ in range(2)]
        rden = [self.alloc(512) for _ in range(2)]
        ckv = [self.alloc(2 * 2 * 256).rearrange("p (mb t e) -> p mb t e", t=2, e=256) for _ in range(2)]
        kTs = [self.alloc16(4 * 128).rearrange("p (q t) -> p q t", t=128) for _ in range(2)]
        vs16 = [self.alloc16(2 * 256).rearrange("p (mb e) -> p mb e", e=256) for _ in range(2)]
        pTs = [self.alloc16(16) for _ in range(2)]
        gsc = self.alloc(2)
        self.ts(gsc, self.pvc("xa_q_gain%d" % l, 0, 2), 256 ** -0.5, None, ALU.mult, None, ["pv"], ["gsc"])
        ones16 = self.cst16("ones")
        ident = self.cst("ident")
        wq = self.dr["w_xq"][l]
        it = 0
        for h in range(4):
            for e in range(2):
                def cons_q(bi, c0, n, ps, pk, e=e):
                    self.cp("act", qraw[:, e, c0:c0 + n], ps[:, :n], [pk], ["qraw%d.%d" % (e, bi)])
                self.linear(wq[2 * h + e], 8, self.h_rhs, self.h_keys, cons_q)
            for bi, (c0, n) in enumerate(TB):
                rs, rk = self.rstd_block([qraw[:, e, c0:c0 + n] for e in range(2)], ["qraw%d.%d" % (e, bi) for e in range(2)],
                                         n, 256, EPS)
                for e in range(2):
                    self.stt(q16[:, e, c0:c0 + n], qraw[:, e, c0:c0 + n], gsc[:, e:e + 1], rs[:, :n], ALU.mult, ALU.mult,
                             ["qraw%d.%d" % (e, bi), rk, "gsc"], ["q16.%d.%d" % (e, bi)])
            for bi in range(4):
                c0, n = TB[bi]
                p_ = pT[it % 2]
                pk_ = "pT%d" % (it % 2)
                rd = rden[it % 2]
                rdk = "rden%d" % (it % 2)
                it += 1
                for mb in range(2):
                    ps, pk = self.newps()
                    for e in range(2):
                        self.mm(ps[:, :n], KTm[:, 2 * h + e, mb * 128:(mb + 1) * 128], q16[:, e, c0:c0 + n], e == 0, e == 1,
                                ["KTm", "q16.%d.%d" % (e, bi)], [pk])
                    self.act(p_[:, mb, :n], ps[:, :n], AF.Exp, [pk], [pk_ + ".%d" % mb])
                psd, pkd = self.newps()
                for mb in range(2):
                    self.mm(psd[:, :n], ones16, p_[:, mb, :n], mb == 0, mb == 1, ["cst16", pk_ + ".%d" % mb], [pkd])
                self.recip(rd[:, :n], psd[:, :n], [pkd], [rdk])
                for e in range(2):
                    pso, pko = self.newps()
                    for mb in range(2):
                        self.mm(pso[:, :n], Vm[:, mb, h * 256 + e * 128:h * 256 + (e + 1) * 128], p_[:, mb, :n], mb == 0, mb == 1,
                                ["Vm", pk_ + ".%d" % mb], [pko])
                    self.tt(oT[:, e, c0:c0 + n], pso[:, :n], rd[:, :n], ALU.mult, [pko, rdk], ["oT%d.%d" % (e, bi)])
            for s in range(4):
                ck = ckv[s % 2]
                ckk = "ckv%d" % (s % 2)
                kt = kTs[s % 2]
                ktk = "kTs%d" % (s % 2)
                v16 = vs16[s % 2]
                vk = "vs16%d" % (s % 2)
                pts = pTs[s % 2]
                ptk = "pTs%d" % (s % 2)
                for mb in range(2):
                    self.dma(ck[:, mb, :, :], self.dr["cmem"][l, s, mb * 128:(mb + 1) * 128, :, h, :], (), [ckk])
                ps, pk = self.newps()
                for e in range(2):
                    for mb in range(2):
                        q = e * 2 + mb
                        self.tr(ps[:, q * 128:(q + 1) * 128], ck[:, mb, 0, e * 128:(e + 1) * 128], ident, [ckk, "cst"], [pk])
                self.cp("act", kt, ps[:, :].rearrange("p (q t) -> p q t", t=128), [pk], [ktk])
                self.cp("dve", v16, ck[:, :, 1, :], [ckk], [vk])
                q0 = 2048 + 8 * s
                ps, pk = self.newps()
                for mb in range(2):
                    for e in range(2):
                        self.mm(ps[:, mb * 8:mb * 8 + 8], kt[:, e * 2 + mb, :], q16[:, e, q0:q0 + 8], e == 0, e == 1,
                                [ktk, "q16.%d.4" % e], [pk])
                self.act(pts[:, 0:16], ps[:, 0:16], AF.Exp, [pk], [ptk])
                pso, pko = self.newps()
                for e in range(2):
                    for mb in range(2):
                        self.mm(pso[:, e * 8:e * 8 + 8], v16[:, mb, e * 128:(e + 1) * 128], pts[:, mb * 8:mb * 8 + 8], mb == 0, mb == 1,
                                [vk, ptk], [pko])
                for mb in range(2):
                    self.mm(pso[:, 16:24], ones16, pts[:, mb * 8:mb * 8 + 8], mb == 0, mb == 1, ["cst16", ptk], [pko])
                rd, rdk = self.scr()
                self.recip(rd[:, 0:8], pso[:, 16:24], [pko], [rdk])
                for e in range(2):
                    self.tt(oT[:, e, q0:q0 + 8], pso[:, e * 8:e * 8 + 8], rd[:, 0:8], ALU.mult, [pko, rdk],
                            ["oT%d.4" % e])
            wo = self.dr["w_xo"][l]
            for nb in range(8):
                self.linear(wo[nb][:, 2 * h:2 * h + 2, :], 2, lambda kc, c0, n: oT[:, kc, c0:c0 + n],
                            lambda kc, bi: ["oT%d.%d" % (kc, bi)], self.add_resid(nb))
        self.release(m0)

    def attn_unit(self, q_ap, qkeys, nq, blocks, acc, acc_cols, bufs):
        pT, ptk = bufs
        ps, pk = self.newps()
        off = 0
        offs = []
        for (kT, vt, mk, nk, keys) in blocks:
            self.mm(ps[:nk, off:off + nq], kT, q_ap, True, True, keys + qkeys, [pk])
            offs.append(off)
            off += nq
        if len(blocks) == 2 and blocks[0][3] == 128 and blocks[1][3] == 128 and nq == 128:
            self.act(pT[:, 0:256], ps[:, 0:256], AF.Exp, [pk], [ptk])
            self.tt(pT[:, 0:256], pT[:, 0:256], self.mboth, ALU.mult, [ptk, "mboth"], [ptk])
        else:
            for bi_, (kT, vt, mk, nk, keys) in enumerate(blocks):
                o_ = offs[bi_]
                self.act(pT[:nk, o_:o_ + nq], ps[:nk, o_:o_ + nq], AF.Exp, [pk], [ptk])
                self.tt(pT[:nk, o_:o_ + nq], pT[:nk, o_:o_ + nq], mk, ALU.mult, [ptk, "cst16"], [ptk])
        pso, pko = self.newps()
        nb_ = len(blocks)
        for bi_, (kT, vt, mk, nk, keys) in enumerate(blocks):
            o_ = offs[bi_]
            self.mm(pso[:, 0:nq], vt, pT[:nk, o_:o_ + nq], bi_ == 0, bi_ == nb_ - 1, keys + [ptk], [pko])
        ones16 = self.cst16("ones")
        for bi_, (kT, vt, mk, nk, keys) in enumerate(blocks):
            o_ = offs[bi_]
            self.mm(pso[:, 128:128 + nq], ones16[:nk, :], pT[:nk, o_:o_ + nq], bi_ == 0, bi_ == nb_ - 1, ["cst16", ptk], [pko])
        src = pso[:, 0:256].rearrange("p (a b) -> p a b", b=128)[:, :, 0:nq]
        self.tt(acc_cols, src, acc_cols, ALU.add, [pko, "acc"], ["acc"])

    def attn(self, l, ja):
        self.dnorm("norm_mix%d" % l)
        m0 = self.mark()
        raw = [self.alloc(NT) for _ in range(2)]
        QT = self.alloc16(NT)
        KT = self.alloc16(NT)
        tok32 = [self.alloc(4 * 128).rearrange("p (b e) -> p b e", e=128) for _ in range(2)]
        vtok = self.alloc16(16 * 128).rearrange("p (b e) -> p b e", e=128)
        acc = self.alloc(2 * NT).rearrange("p (a t) -> p a t", t=NT)
        oT = self.alloc16(NT)
        pTb = [self.alloc16(256) for _ in range(3)]
        self.mboth = self.alloc16(256)
        cb_ = self.alloc(8 * 2 * 128).rearrange("p (r t e) -> p r t e", t=2, e=128)
        cv_ = self.alloc16(8 * 128).rearrange("p (r e) -> p r e", e=128)
        kcT = [self.alloc16(128) for _ in range(2)]
        sstg = self.alloc(2 * 128).rearrange("p (t e) -> p t e", e=128)
        vs16 = self.alloc16(128)
        gq = self.alloc(3)
        self.cp("dve", self.mboth[:, 0:128], self.cst16("m_own"), ["cst16"], ["mboth"])
        self.cp("dve", self.mboth[:, 128:256], self.cst16("m_prev"), ["cst16"], ["mboth"])
        for g in range(3):
            self.ts(gq[:, g:g + 1], self.pvc("aq_gain%d_%d" % (ja, g)), 128 ** -0.5, None, ALU.mult, None, ["pv"], ["gq"])
        ident = self.cst("ident")
        wqkv = self.dr["w_qkv"][ja]
        wo = self.dr["w_ao"][ja]
        caches = (self.dr["c128"], self.dr["c512"], self.dr["c2048"])
        outs_p = (self.dr["kv128_p"], self.dr["kv512_p"], self.dr["kv2048_p"])
        outs_s = (self.dr["kv128_s"], self.dr["kv512_s"], self.dr["kv2048_s"])
        for g in range(3):
            L = WINS[g]
            for s in range(4):
                if STAGES["a_d2d"]:
                    self.dma(outs_s[g][ja, s, 0:L - 8], caches[g][ja, s, 8:L], (), ())
        ui = 0
        ti = 0
        for h in range(4):
            self.memset(acc[:, :, :], 0.0, ["acc"])
            for g in range(STAGES["a_groups"]):
                dil = DILS[g]
                L = WINS[g]
                nkb = (2048 // dil) // 128

                def proj(si, dst, dkey):
                    blk = (si * 3 + g) * 4 + h

                    def cons_p(bi, c0, n, ps, pk):
                        self.cp("act", dst[:, c0:c0 + n], ps[:, :n], [pk], ["%s.%d" % (dkey, bi)])
                    self.linear(wqkv[blk], 8, self.h_rhs, self.h_keys, cons_p)
                proj(0, raw[0], "raw0")
                proj(1, raw[1], "raw1")
                for bi, (c0, n) in enumerate(TB):
                    rs, rk = self.rstd_block([raw[0][:, c0:c0 + n]], ["raw0.%d" % bi], n, 128, EPS)
                    self.stt(QT[:, c0:c0 + n], raw[0][:, c0:c0 + n], gq[:, g:g + 1], rs[:, :n], ALU.mult, ALU.mult,
                             ["raw0.%d" % bi, rk, "gq"], ["QT.%d" % bi])
                    rs, rk = self.rstd_block([raw[1][:, c0:c0 + n]], ["raw1.%d" % bi], n, 128, EPS)
                    self.stt(raw[1][:, c0:c0 + n], raw[1][:, c0:c0 + n], self.pvc("ak_gain%d_%d" % (ja, g)), rs[:, :n],
                             ALU.mult, ALU.mult, ["raw1.%d" % bi, rk, "pv"], ["raw1.%d" % bi])
                    self.cp("act", KT[:, c0:c0 + n], raw[1][:, c0:c0 + n], ["raw1.%d" % bi], ["KT.%d" % bi])
                proj(2, raw[0], "raw0")
                allq = ["QT.%d" % b for b in range(5)]
                allk = ["KT.%d" % b for b in range(5)]
                o_ = outs_p[g][ja]
                for si, rsrc, rkn in ((1, raw[1], "raw1"), (2, raw[0], "raw0")):
                    rkeys = ["%s.%d" % (rkn, b) for b in range(4)]
                    for b4 in range(4):
                        tk = tok32[ti % 2]
                        tkk = "tok32_%d" % (ti % 2)
                        ti += 1
                        ps, pk = self.newps()
                        for q in range(4):
                            idx = b4 * 4 + q
                            r_, kb = idx // nkb, idx % nkb
                            st_ = r_ + dil * 128 * kb
                            self.tr(ps[:, q * 128:(q + 1) * 128], rsrc[:, st_:st_ + dil * 127 + 1:dil], ident, rkeys + ["cst"], [pk])
                        self.cp("act", tk[:, :, :], ps[:, :].rearrange("p (q e) -> p q e", e=128), [pk], [tkk])
                        if si == 2:
                            self.cp("dve", vtok[:, b4 * 4:b4 * 4 + 4, :], tk[:, :, :], [tkk], ["vtok"])
                        if not STAGES["a_kvout"]:
                            pass
                        elif g == 0:
                            if b4 == 3:
                                self.dma(o_[:, si - 1, h, :], tk[:, 3, :], [tkk], ())
                        elif g == 1:
                            dst = o_.rearrange("(j r) t h e -> j r t h e", r=dil)[:, b4, si - 1, h, :]
                            self.dma(dst, tk[:, 3, :], [tkk], ())
                        else:
                            dst = o_.rearrange("(j r) t h e -> j r t h e", r=dil)[:, b4 * 4:b4 * 4 + 4, si - 1, h, :]
                            self.dma(dst, tk[:, :, :], [tkk], ())
                    ps, pk = self.newps()
                    self.tr(ps[:32, 0:128], rsrc[:, 2048:2080], ident, ["%s.4" % rkn, "cst"], [pk])
                    self.cp("act", sstg[:32, si - 1, :], ps[:32, 0:128], [pk], ["sstg"])
                    if si == 2:
                        self.cp("dve", vs16[:32, :], sstg[:32, 1, :], ["sstg"], ["vs16"])
                for s in range(4):
                    if STAGES["a_sout"]:
                        self.dma(outs_s[g][ja, s, L - 8:L, :, h, :], sstg[8 * s:8 * s + 8, :, :], ["sstg"], ())
                for r_ in range(dil if STAGES["a_pu"] else 0):
                    for qb in range(nkb):
                        st_ = r_ + dil * 128 * qb
                        sl_ = slice(st_, st_ + dil * 127 + 1, dil)
                        blocks = [(KT[:, sl_], vtok[:, r_ * nkb + qb, :], self.cst16("m_own"), 128, allk + ["vtok"])]
                        if qb > 0:
                            sp_ = r_ + dil * 128 * (qb - 1)
                            blocks.append((KT[:, sp_:sp_ + dil * 127 + 1:dil], vtok[:, r_ * nkb + qb - 1, :], self.cst16("m_prev"),
                                           128, allk + ["vtok"]))
                        self.attn_unit(QT[:, sl_], allq, 128, blocks, acc, acc[:, :, sl_], (pTb[ui % 3], "pTb%d" % (ui % 3)))
                        ui += 1
                R = min(dil, 8)
                nq = 8 // R
                for s in range(4 if STAGES["a_su"] else 0):
                    src = caches[g][ja, s].rearrange("(j r) t h e -> j r t h e", r=dil)[:, 0:R, :, h, :]
                    self.dma(cb_[:, 0:R, :, :], src, (), ["cb"])
                    self.cp("dve", cv_[:, 0:R, :], cb_[:, 0:R, 1, :], ["cb"], ["cv"])
                    for r_ in range(R):
                        kc_ = kcT[ui % 2]
                        kck = "kcT%d" % (ui % 2)
                        ps, pk = self.newps()
                        self.tr(ps[:, 0:128], cb_[:, r_, 0, :], ident, ["cb", "cst"], [pk])
                        self.cp("act", kc_[:, :], ps[:, 0:128], [pk], [kck])
                        q0 = 2048 + 8 * s + r_
                        sl_ = slice(q0, q0 + R * (nq - 1) + 1, R)
                        oc = own_s_col(self.cca, g, s, r_) - self.cca.cols["own_s"]
                        blocks = [(kc_[:, :], cv_[:, r_, :], self.cst16("m_prev", 128, nq), 128, [kck, "cv"]),
                                  (KT[:, 2048:2080], vs16[:32, :], self.cst16("own_s", 32, nq, oc), 32, allk + ["vs16"])]
                        self.attn_unit(QT[:, sl_], allq, nq, blocks, acc, acc[:, :, sl_], (pTb[ui % 3], "pTb%d" % (ui % 3)))
                        ui += 1
            self.recip(acc[:, 1, :], acc[:, 1, :], ["acc"], ["acc"])
            for bi, (c0, n) in enumerate(TB):
                self.tt(oT[:, c0:c0 + n], acc[:, 0, c0:c0 + n], acc[:, 1, c0:c0 + n], ALU.mult, ["acc"], ["ao.%d" % bi])
            for nb in range(8):
                self.linear(wo[nb][:, h:h + 1, :], 1, lambda kc, c0, n: oT[:, c0:c0 + n], lambda kc, bi: ["ao.%d" % bi],
                            self.add_resid(nb))
        self.release(m0)

    def hgrn(self, l):
        self.dnorm("norm_mix%d" % l)
        m0 = self.mark()
        Bq, Bz, Bk, Bm, Bd, Bv, Bb = [self.alloc(NT) for _ in range(7)]
        Sb = [self.alloc(128) for _ in range(2)]
        ktok = [self.alloc(128) for _ in range(2)]
        vtk = [self.alloc(128) for _ in range(2)]
        ATb = [self.alloc(64) for _ in range(2)]
        ebC = self.alloc(36)
        ex = self.alloc(32).rearrange("p (l c) -> p l c", c=8)
        lbv = self.alloc(8)
        omlb = self.alloc(8)
        tot = self.alloc(8)
        oTh = self.alloc16(NT)

        def K(n):
            return ["%s.%d" % (n, b) for b in range(5)]
        c_lb = self.pvca.cols["hg_lb0"]
        self.act(ex[:, :, :], self.pv[:, c_lb:c_lb + 32].rearrange("p (l c) -> p l c", c=8), AF.Exp, ["pv"], ["hex"])
        self.tt(tot, ex[:, 0, :], ex[:, 1, :], ALU.add, ["hex"], ["htot"])
        self.tt(tot, tot, ex[:, 2, :], ALU.add, ["hex", "htot"], ["htot"])
        self.tt(tot, tot, ex[:, 3, :], ALU.add, ["hex", "htot"], ["htot"])
        self.cp("dve", lbv, ex[:, 1, :], ["hex"], ["hlb"])
        for l2 in range(2, l + 1):
            self.tt(lbv, lbv, ex[:, l2, :], ALU.add, ["hex", "hlb"], ["hlb"])
        self.recip(tot, tot, ["htot"], ["htot"])
        self.tt(lbv, lbv, tot, ALU.mult, ["hlb", "htot"], ["hlb"])
        self.ts(omlb, lbv, -1.0, 1.0, ALU.mult, ALU.add, ["hlb"], ["homlb"])
        ident = self.cst("ident")
        m_own = self.cst("m_own")
        win = self.dr["w_hgin"]
        wo = self.dr["w_hgo"]
        ci = 0
        si = 0
        for h in range(8):
            def proj(blk, dst, dkey):
                def cons_p(bi, c0, n, ps, pk):
                    self.cp("act", dst[:, c0:c0 + n], ps[:, :n], [pk], ["%s.%d" % (dkey, bi)])
                self.linear(win[blk], 8, self.h_rhs, self.h_keys, cons_p)
            proj(h, Bq, "hq")
            proj(8 + h, Bz, "hz")
            proj(16 + h, Bv, "hv")
            self.act(Bz[:, :], Bz[:, :], AF.Sigmoid, K("hz"), K("hz"))
            self.ts(Bz[:, :], Bz[:, :], omlb[:, h:h + 1], lbv[:, h:h + 1], ALU.mult, ALU.add, K("hz") + ["hlb", "homlb"], K("hz"))
            self.ts(Bk[:, :], Bz[:, :], -1.0, 1.0, ALU.mult, ALU.add, K("hz"), K("hk"))
            self.act(Bz[:, :], Bz[:, :], AF.Ln, K("hz"), K("hz"))
            self.memset(Bm[:, :], 1.0, K("hm"))
            self.memset(Bm[:, 0:2048:64], 0.0, K("hm"))
            self.memset(Bm[:, 2048:2080:8], 0.0, K("hm"))
            self.P.add("dve", lambda e: e.tensor_tensor_scan(Bb[:, :], Bm[:, :], Bz[:, :], 0.0, ALU.mult, ALU.add),
                       K("hm") + K("hz"), K("hb"))
            self.act(ebC[:, 0:32], Bb[:, 63:2048:64], AF.Exp, K("hb"), ["hebc"])
            self.act(ebC[:, 32:36], Bb[:, 2055:2080:8], AF.Exp, K("hb"), ["hebc"])
            self.act(Bq[:, :], Bq[:, :], AF.Silu, K("hq"), K("hq"))
            self.act(Bz[:, :], Bb[:, :], AF.Exp, K("hb") + K("hz"), K("hz"))
            self.tt(Bq[:, :], Bq[:, :], Bz[:, :], ALU.mult, K("hq") + K("hz"), K("hq"))
            self.act(Bm[:, :], Bb[:, :], AF.Exp, K("hb") + K("hm"), K("hm"), scale=-1.0)
            self.tt(Bm[:, :], Bm[:, :], Bk[:, :], ALU.mult, K("hm") + K("hk"), K("hm"))
            bp = Bb[:, 0:2048].rearrange("p (n c) -> p n c", c=64)
            self.tt(Bd[:, 0:2048].rearrange("p (n c) -> p n c", c=64), bp[:, :, 63:64].to_broadcast([128, 32, 64]), bp, ALU.subtract,
                    K("hb"), K("hd"))
            bs = Bb[:, 2048:2080].rearrange("p (n c) -> p n c", c=8)
            self.tt(Bd[:, 2048:2080].rearrange("p (n c) -> p n c", c=8), bs[:, :, 7:8].to_broadcast([128, 4, 8]), bs, ALU.subtract,
                    K("hb"), K("hd"))
            self.act(Bd[:, :], Bd[:, :], AF.Exp, K("hd"), K("hd"))
            self.tt(Bd[:, :], Bd[:, :], Bk[:, :], ALU.mult, K("hd") + K("hk"), K("hd"))

            def chunk(c0, C, S, Sk, ecol, bi):
                nonlocal ci
                kt, ktk = ktok[ci % 2], "hkt%d" % (ci % 2)
                vt, vtkk = vtk[ci % 2], "hvt%d" % (ci % 2)
                AT, atk = ATb[ci % 2], "hat%d" % (ci % 2)
                ci += 1
                ps, pk = self.newps()
                self.tr(ps[:C, 0:128], Bd[:, c0:c0 + C], ident, ["hd.%d" % bi, "cst"], [pk])
                self.cp("act", kt[:C, :], ps[:C, 0:128], [pk], [ktk])
                psb, pkb = self.newps()
                self.tr(psb[:C, 0:128], Bv[:, c0:c0 + C], ident, ["hv.%d" % bi, "cst"], [pkb])
                self.cp("dve", vt[:C, :], psb[:C, 0:128], [pkb], [vtkk])
                ps2, pk2 = self.newps()
                self.mm(ps2[:C, :C], Bm[:, c0:c0 + C], Bq[:, c0:c0 + C], True, True, ["hm.%d" % bi, "hq.%d" % bi], [pk2])
                self.tt(AT[:C, :C], ps2[:C, :C], m_own[:C, :C], ALU.mult, [pk2, "cst"], [atk])
                ps3, pk3 = self.newps()
                self.mm(ps3[:, :C], S[:, :], Bq[:, c0:c0 + C], True, False, [Sk, "hq.%d" % bi], [pk3])
                self.mm(ps3[:, :C], vt[:C, :], AT[:C, :C], False, True, [vtkk, atk], [pk3])
                self.cp("act", Bk[:, c0:c0 + C], ps3[:, :C], [pk3], ["hk.%d" % bi])
                ps4, pk4 = self.newps()
                self.mm(ps4[:, 0:128], kt[:C, :], vt[:C, :], True, True, [ktk, vtkk], [pk4])
                self.stt(S[:, :], S[:, :], ebC[:, ecol:ecol + 1], ps4[:, 0:128], ALU.mult, ALU.add, [Sk, "hebc", pk4], [Sk])
            S, Sk = Sb[si % 2], "hS%d" % (si % 2)
            si += 1
            self.memset(S[:, :], 0.0, [Sk])
            for n_ in range(32):
                chunk(64 * n_, 64, S, Sk, n_, n_ // 8)
            self.dma(self.dr["hg_p"][h], S[:, :], [Sk], ())
            for s in range(4):
                S, Sk = Sb[si % 2], "hS%d" % (si % 2)
                si += 1
                self.dma(S[:, :], self.dr["st_hg"][s, h], (), [Sk])
                chunk(2048 + 8 * s, 8, S, Sk, 32 + s, 4)
                self.dma(self.dr["hg_s"][s, h], S[:, :], [Sk], ())
            proj(24 + h, Bq, "hq")
            self.act(Bq[:, :], Bq[:, :], AF.Silu, K("hq"), K("hq"))
            for bi, (c0, n) in enumerate(TB):
                rs, rk = self.rstd_block([Bk[:, c0:c0 + n]], ["hk.%d" % bi], n, 128, EPS)
                self.stt(Bk[:, c0:c0 + n], Bk[:, c0:c0 + n], self.pvc("hg_out_gain"), rs[:, :n], ALU.mult, ALU.mult,
                         ["hk.%d" % bi, rk, "pv"], ["hk.%d" % bi])
                self.tt(oTh[:, c0:c0 + n], Bk[:, c0:c0 + n], Bq[:, c0:c0 + n], ALU.mult, ["hk.%d" % bi, "hq.%d" % bi], ["hoT.%d" % bi])
            for nb in range(8):
                self.linear(wo[nb][:, h:h + 1, :], 1, lambda kc, c0, n: oTh[:, c0:c0 + n], lambda kc, bi: ["hoT.%d" % bi],
                            self.add_resid(nb))
        self.release(m0)

    def rwkv(self, l):
        self.dnorm("norm_mix%d" % l)
        m0 = self.mark()
        NCH = 4
        ident = self.cst("ident")
        blk64 = self.cst("blk64")
        pvc = self.pvc

        def hk2(c0, n, kc):
            b0 = min(max(c0 - 1, 0) // 512, 4)
            b1 = min((c0 + n - 1) // 512, 4)
            return ["h%d.%d" % (kc, b) for b in sorted({b0, b1})]
        xl = self.alloc(40).rearrange("p (c t) -> p c t", t=5)
        self.cp("dve", xl[:, :, 0:1], self.xT[:, :, 2047:2048], ["x%d.3" % c for c in range(8)], ["xl"])
        self.cp("dve", xl[:, :, 1:5], self.xT[:, :, 2055:2080:8], ["x%d.4" % c for c in range(8)], ["xl"])
        rs, rk = self.rstd_block([xl[:, c, :] for c in range(8)], ["xl"] * 8, 5, D, EPS)
        for c in range(8):
            self.stt(xl[:, c, :], xl[:, c, :], pvc("norm_mix%d" % l, c), rs[:, 0:5], ALU.mult, ALU.mult, ["xl", rk, "pv"], ["xl"])
        self.dma(self.dr["sh_o"], xl, ["xl"], ())
        hsp = self.alloc16(8 * 32).rearrange("p (c t) -> p c t", t=32)
        shs = self.alloc(32).rearrange("p (c s) -> p c s", s=4)
        self.dma(shs, self.dr["st_sh"], (), ["shs"])
        self.cp("dve", hsp[:, :, 0:32:8], shs, ["shs"], ["hsp"])
        for c in range(8):
            self.cp("dve", hsp[:, c, :].rearrange("p (s i) -> p s i", i=8)[:, :, 1:8],
                    self.hT[:, c, 2048:2080].rearrange("p (s i) -> p s i", i=8)[:, :, 0:7], ["h%d.4" % c], ["hsp"])
        omu = self.alloc(48)
        cmu = self.pvca.cols["rw_mu0"]
        mu = self.pv[:, cmu:cmu + 48]
        self.ts(omu, mu, -1.0, 1.0, ALU.mult, ALU.add, ["pv"], ["omu"])
        omka = self.alloc(8)
        self.ts(omka, pvc("rw_k_a", 0, 8), -1.0, 1.0, ALU.mult, ALU.add, ["pv"], ["omka"])
        wst = self.alloc(8 * 160)

        def proj(ps, pk, m, wA, wB, wkeys, c0, n, sample):
            for kc in range(8):
                self.mm(ps[:m, :n], wA(kc), self.hT[:, kc, c0:c0 + n], kc == 0, False, wkeys + hk2(c0, n, kc), [pk])
            for kc in range(8):
                last = kc == 7
                if sample:
                    self.mm(ps[:m, :n], wB(kc), hsp[:, kc, :], False, last, wkeys + ["hsp"], [pk])
                elif c0 == 0:
                    self.mm(ps[:m, 1:n], wB(kc), self.hT[:, kc, 0:n - 1], False, last, wkeys + hk2(c0, n, kc), [pk])
                else:
                    self.mm(ps[:m, :n], wB(kc), self.hT[:, kc, c0 - 1:c0 - 1 + n], False, last, wkeys + hk2(c0, n, kc), [pk])

        def scaled(dstA, dstB, src, ncol, j, keyA, keyB, skey):
            mj = mu[:, 8 * j:8 * j + 8].unsqueeze(2).to_broadcast([128, 8, ncol])
            oj = omu[:, 8 * j:8 * j + 8].unsqueeze(2).to_broadcast([128, 8, ncol])
            self.tt(dstA, src, oj, ALU.mult, [skey, "omu"], [keyA])
            self.tt(dstB, src, mj, ALU.mult, [skey, "pv"], [keyB])
        TWA = self.alloc16(NT)
        TG1 = self.alloc16(NT)
        TG2 = self.alloc16(NT)
        l1A = self.alloc16(8 * 160).rearrange("p (k n) -> p k n", n=160)
        l1B = self.alloc16(8 * 160).rearrange("p (k n) -> p k n", n=160)
        w64 = wst[:, 0:8 * 64].rearrange("p (k n) -> p k n", n=64)
        for (nm, j, lo) in (("rw1", 3, 0), ("ra1", 4, 64)):
            self.dma(w64, self.dr[nm], (), ["wst"])
            scaled(l1A[:, :, lo:lo + 64], l1B[:, :, lo:lo + 64], w64, 64, j, "l1A", "l1B", "wst")
        for bi, (c0, n) in enumerate(TB):
            ps, pk = self.newps()
            proj(ps, pk, 128, lambda kc: l1A[:, kc, 0:128], lambda kc: l1B[:, kc, 0:128], ["l1A", "l1B"], c0, n, bi == 4)
            self.act(TWA[0:64, c0:c0 + n], ps[0:64, :n], AF.Tanh, [pk], ["twa.%d" % bi])
            self.cp("act", TWA[64:128, c0:c0 + n], ps[64:128, :n], [pk], ["twa.%d" % bi])
        w160 = wst[:, 0:8 * 160].rearrange("p (k n) -> p k n", n=160)
        self.dma(w160, self.dr["rg1"], (), ["wst"])
        scaled(l1A[:, :, :], l1B[:, :, :], w160, 160, 5, "l1A", "l1B", "wst")
        for bi, (c0, n) in enumerate(TB):
            ps, pk = self.newps()
            proj(ps, pk, 128, lambda kc: l1A[:, kc, 0:128], lambda kc: l1B[:, kc, 0:128], ["l1A", "l1B"], c0, n, bi == 4)
            self.act(TG1[:, c0:c0 + n], ps[:, :n], AF.Sigmoid, [pk], ["tg.%d" % bi])
            ps, pk = self.newps()
            proj(ps, pk, 32, lambda kc: l1A[:, kc, 128:160], lambda kc: l1B[:, kc, 128:160], ["l1A", "l1B"], c0, n, bi == 4)
            self.act(TG2[0:32, c0:c0 + n], ps[0:32, :n], AF.Sigmoid, [pk], ["tg.%d" % bi])
        W2A = self.alloc16(128)
        G2a = self.alloc16(128)
        G2b = self.alloc16(128)
        yT = self.alloc16(NT)
        NB_ = 256
        Rr, Rk, Rv, Rld, Ra, Rg, Rkk, Rk2, Rcum, Rt1, Rt2, Rbon, Ry = [self.alloc(NB_) for _ in range(13)]
        blk = [self.alloc(NCH * 128).rearrange("p (j t) -> p j t", t=128) for _ in range(5)]
        Ablk, Bblk, Kblk, Rblk, Vblk = blk
        for b_ in blk:
            self.memset(b_[:, :, :], 0.0, ["blk"])
        WS = [[self.alloc(128) for _ in range(9)] for _ in range(NCH)]
        STb = [self.alloc(128) for _ in range(2)]
        DC = self.alloc(NCH)
        maskp = self.alloc(NB_)
        masks = self.alloc(32)
        self.memset(maskp, 1.0, ["maskp"])
        self.memset(maskp[:, 0:NB_:64], 0.0, ["maskp"])
        self.memset(masks, 1.0, ["masks"])
        self.memset(masks[:, 0:32:8], 0.0, ["masks"])
        bd_su, bd_sl, bd_iu = self.cst("bd_su"), self.cst("bd_sl"), self.cst("bd_iu")
        RB = [(NB_ * i, NB_, False) for i in range(2048 // NB_)] + [(2048, 32, True)]
        sti = 0
        wo = self.dr["w_rwo"]
        for p in range(8):
            w128 = wst[:, 0:1024].rearrange("p (k n) -> p k n", n=128)
            for j in range(3):
                self.dma(w128, self.dr["w_rkv"][j, p], (), ["wst"])
                scaled(self.wb[2 * j][:, :, :], self.wb[2 * j + 1][:, :, :], w128, 128, j, "wb%d" % (2 * j), "wb%d" % (2 * j + 1), "wst")
            self.dma(W2A[0:64, :], self.dr["rw2"][:, p * 128:(p + 1) * 128], (), ["w2a"], q="pool")
            self.dma(W2A[64:128, :], self.dr["ra2"][:, p * 128:(p + 1) * 128], (), ["w2a"], q="pool")
            self.dma(G2a[:, :], self.dr["rg2"][0:128, p * 128:(p + 1) * 128], (), ["g2"], q="pool")
            self.dma(G2b[0:32, :], self.dr["rg2"][128:160, p * 128:(p + 1) * 128], (), ["g2"], q="pool")
            ST, STk = STb[sti % 2], "rST%d" % (sti % 2)
            sti += 1
            self.memset(ST[:, :], 0.0, [STk])
            for (c0, n, smp) in RB:
                bi = min(c0 // 512, 4)
                C = 8 if smp else 64
                nch = 4
                for j, dst, dk in ((0, Rr, "Rr"), (1, Rk, "Rk"), (2, Rv, "Rv")):
                    ps, pk = self.newps()
                    proj(ps, pk, 128, lambda kc, j=j: self.wb[2 * j][:, kc, :], lambda kc, j=j: self.wb[2 * j + 1][:, kc, :],
                         ["wb%d" % (2 * j), "wb%d" % (2 * j + 1)], c0, n, smp)
                    self.cp("act", dst[:, :n], ps[:, :n], [pk], [dk])
                ps, pk = self.newps()
                self.mm(ps[:, :n], W2A[0:64, :], TWA[0:64, c0:c0 + n], True, True, ["w2a", "twa.%d" % bi], [pk])
                self.act(Rld[:, :n], ps[:, :n], AF.Sigmoid, [pk, "pv"], ["Rld"], bias=pvc("rw_w0", p))
                self.ts(Rld[:, :n], Rld[:, :n], -0.6065306597126334, None, ALU.mult, None, ["Rld"], ["Rld"])
                ps, pk = self.newps()
                self.mm(ps[:, :n], W2A[64:128, :], TWA[64:128, c0:c0 + n], True, True, ["w2a", "twa.%d" % bi], [pk])
                self.act(Ra[:, :n], ps[:, :n], AF.Sigmoid, [pk, "pv"], ["Ra"], bias=pvc("rw_a0", p))
                ps, pk = self.newps()
                self.mm(ps[:, :n], G2a[:, :], TG1[:, c0:c0 + n], True, False, ["g2", "tg.%d" % bi], [pk])
                self.mm(ps[:, :n], G2b[0:32, :], TG2[0:32, c0:c0 + n], False, True, ["g2", "tg.%d" % bi], [pk])
                self.cp("act", Rg[:, :n], ps[:, :n], [pk], ["Rg"])
                self.ts(Rkk[:, :n], Rk[:, :n], pvc("rw_k_k", p), None, ALU.mult, None, ["Rk", "pv"], ["Rkk"])
                self.act(Rt1[:, :n], Rkk[:, :n], AF.Square, ["Rkk"], ["Rt1"])
                ps, pk = self.newps()
                self.mm(ps[:, :n], blk64, Rt1[:, :n], True, True, ["cst", "Rt1"], [pk])
                self.act(Rt2[:, :n], ps[:, :n], AF.Sqrt, [pk], ["Rt2"])
                self.ts(Rt2[:, :n], Rt2[:, :n], 1e-12, None, ALU.max, None, ["Rt2"], ["Rt2"])
                self.recip(Rt2[:, :n], Rt2[:, :n], ["Rt2"], ["Rt2"])
                self.tt(Rkk[:, :n], Rkk[:, :n], Rt2[:, :n], ALU.mult, ["Rkk", "Rt2"], ["Rkk"])
                self.ts(Rt1[:, :n], Ra[:, :n], pvc("rw_k_a", p), omka[:, p:p + 1], ALU.mult, ALU.add, ["Ra", "pv", "omka", "Rt1"], ["Rt1"])
                self.tt(Rk2[:, :n], Rk[:, :n], Rt1[:, :n], ALU.mult, ["Rk", "Rt1"], ["Rk2"])
                self.tt(Rt1[:, :n], Rr[:, :n], Rk2[:, :n], ALU.mult, ["Rr", "Rk2", "Rt1"], ["Rt1"])
                self.ts(Rt1[:, :n], Rt1[:, :n], pvc("rw_r_k", p), None, ALU.mult, None, ["Rt1", "pv"], ["Rt1"])
                ps, pk = self.newps()
                self.mm(ps[:, :n], blk64, Rt1[:, :n], True, True, ["cst", "Rt1"], [pk])
                self.tt(Rbon[:, :n], ps[:, :n], Rv[:, :n], ALU.mult, [pk, "Rv"], ["Rbon"])
                mk_ = masks if smp else maskp
                mkk = "masks" if smp else "maskp"
                self.P.add("dve", lambda e, n=n, mk_=mk_: e.tensor_tensor_scan(Rcum[:, :n], mk_[:, :n], Rld[:, :n], 0.0, ALU.mult, ALU.add),
                           [mkk, "Rld"], ["Rcum"])
                self.act(DC[:, 0:nch], Rcum[:, C - 1:n:C], AF.Exp, ["Rcum"], ["DC"])
                if smp:
                    for b_ in blk:
                        self.memset(b_[:, :, :], 0.0, ["blk"])

                def toblk(dst, fn, rkeys):
                    for hh in range(2):
                        rows = slice(64 * hh, 64 * hh + 64)
                        ov = dst[rows, 0:nch, 64 * hh:64 * hh + C]
                        fn(ov, rows, lambda x: x[rows, 0:n].rearrange("p (j s) -> p j s", s=C))
                self.act(Rt1[:, :n], Rcum[:, :n], AF.Exp, ["Rcum", "Rt1"], ["Rt1"])
                toblk(Rblk, lambda ov, rows, V: self.tt(ov, V(Rr), V(Rt1), ALU.mult, ["Rr", "Rt1"], ["blk"]), None)
                self.act(Rt1[:, :n], Rcum[:, :n], AF.Exp, ["Rcum", "Rt1", "blk"], ["Rt1"], scale=-1.0)
                self.tt(Rt2[:, :n], Rkk[:, :n], Ra[:, :n], ALU.mult, ["Rkk", "Ra", "Rt2"], ["Rt2"])
                toblk(Bblk, lambda ov, rows, V: self.tt(ov, V(Rt2), V(Rt1), ALU.mult, ["Rt2", "Rt1"], ["blk"]), None)
                toblk(Kblk, lambda ov, rows, V: self.tt(ov, V(Rk2), V(Rt1), ALU.mult, ["Rk2", "Rt1"], ["blk"]), None)
                self.tt(Rt2[:, :n], Rcum[:, :n], Rld[:, :n], ALU.subtract, ["Rcum", "Rld", "Rt2", "blk"], ["Rt2"])
                self.act(Rt2[:, :n], Rt2[:, :n], AF.Exp, ["Rt2"], ["Rt2"])
                toblk(Ablk, lambda ov, rows, V: self.stt(ov, V(Rkk), -1.0, V(Rt2), ALU.mult, ALU.mult, ["Rkk", "Rt2"], ["blk"]), None)
                toblk(Vblk, lambda ov, rows, V: self.cp("act", ov, V(Rv), ["Rv"], ["blk"]), None)
                M_ = [WS[j][0] for j in range(nch)]
                MT_ = [WS[j][1] for j in range(nch)]
                M2_ = [WS[j][2] for j in range(nch)]
                M2T_ = [WS[j][3] for j in range(nch)]
                P_ = [WS[j][4] for j in range(nch)]

                def wk(j, i):
                    return "rws%d.%d" % (j, i)
                for j in range(nch):
                    ps, pk = self.newps()
                    self.mm(ps[:, 0:128], Bblk[:, j, :], Ablk[:, j, :], True, True, ["blk"], [pk])
                    self.mm(ps[:, 128:256], Ablk[:, j, :], Bblk[:, j, :], True, True, ["blk"], [pk])
                    self.tt(M_[j][:, :], ps[:, 0:128], bd_su, ALU.mult, [pk, "cst"], [wk(j, 0)])
                    self.tt(MT_[j][:, :], ps[:, 128:256], bd_sl, ALU.mult, [pk, "cst"], [wk(j, 1)])
                    self.tt(P_[j][:, :], M_[j][:, :], ident, ALU.add, [wk(j, 0), "cst"], [wk(j, 4)], eng="pool")
                cur = (M_, MT_, 0, 1)
                nxt = (M2_, M2T_, 2, 3)
                for step in range(5):
                    A_, AT_, ia, iat = cur
                    N_, NT_, in_, int_ = nxt
                    pss = []
                    for j in range(nch):
                        ps, pk = self.newps()
                        pss.append((ps, pk))
                        self.mm(ps[:, 128:256], A_[j][:, :], AT_[j][:, :], True, True, [wk(j, ia), wk(j, iat)], [pk])
                        if step < 4:
                            self.mm(ps[:, 0:128], AT_[j][:, :], A_[j][:, :], True, True, [wk(j, ia), wk(j, iat)], [pk])
                    for j in range(nch):
                        ps, pk = pss[j]
                        self.cp("act", NT_[j][:, :], ps[:, 128:256], [pk], [wk(j, int_)])
                        if step < 4:
                            self.cp("act", N_[j][:, :], ps[:, 0:128], [pk], [wk(j, in_)])
                    pss = []
                    for j in range(nch):
                        ps, pk = self.newps()
                        pss.append((ps, pk))
                        lt = AT_[j] if step == 0 else None
                        self.mm(ps[:, 0:128], NT_[j][:, :], P_[j][:, :], True, True, [wk(j, int_), wk(j, 4)], [pk])
                    for j in range(nch):
                        ps, pk = pss[j]
                        self.tt(P_[j][:, :], ps[:, 0:128], P_[j][:, :], ALU.add, [pk, wk(j, 4)], [wk(j, 4)])
                    cur, nxt = nxt, cur
                Mk_ = [WS[j][0] for j in range(nch)]
                Nb_ = [WS[j][1] for j in range(nch)]
                Nk_ = [WS[j][2] for j in range(nch)]
                Vt_ = [WS[j][3] for j in range(nch)]
                Bt_ = [WS[j][5] for j in range(nch)]
                Kt_ = [WS[j][6] for j in range(nch)]
                XT_ = [WS[j][7] for j in range(nch)]
                UT_ = [WS[j][8] for j in range(nch)]
                for j in range(nch):
                    ps, pk = self.newps()
                    self.mm(ps[:, 0:128], Kblk[:, j, :], Ablk[:, j, :], True, True, ["blk"], [pk])
                    self.mm(ps[:, 128:256], Bblk[:, j, :], Rblk[:, j, :], True, True, ["blk"], [pk])
                    self.mm(ps[:, 256:384], Kblk[:, j, :], Rblk[:, j, :], True, True, ["blk"], [pk])
                    self.tt(Mk_[j][:, :], ps[:, 0:128], bd_su, ALU.mult, [pk, "cst"], [wk(j, 0)])
                    self.tt(Nb_[j][:, :], ps[:, 128:256], bd_iu, ALU.mult, [pk, "cst"], [wk(j, 1)])
                    self.tt(Nk_[j][:, :], ps[:, 256:384], bd_iu, ALU.mult, [pk, "cst"], [wk(j, 2)])
                    ps, pk = self.newps()
                    self.tr(ps[:, 0:128], Vblk[:, j, :], ident, ["blk", "cst"], [pk])
                    self.tr(ps[:, 128:256], Bblk[:, j, :], ident, ["blk", "cst"], [pk])
                    self.tr(ps[:, 256:384], Kblk[:, j, :], ident, ["blk", "cst"], [pk])
                    self.cp("act", Vt_[j][:, :], ps[:, 0:128], [pk], [wk(j, 3)])
                    self.cp("act", Bt_[j][:, :], ps[:, 128:256], [pk], [wk(j, 5)])
                    self.cp("act", Kt_[j][:, :], ps[:, 256:384], [pk], [wk(j, 6)])
                for j in range(nch):
                    if smp:
                        ST, STk = STb[sti % 2], "rST%d" % (sti % 2)
                        sti += 1
                        self.memset(ST[:, :], 0.0, [STk])
                        for hh in range(2):
                            self.dma(ST[64 * hh:64 * hh + 64, 64 * hh:64 * hh + 64], self.dr["st_rw"][j, 2 * p + hh], (), [STk])
                    ps, pk = self.newps()
                    self.mm(ps[:, 0:128], Ablk[:, j, :], ST[:, :], True, False, ["blk", STk], [pk])
                    self.mm(ps[:, 0:128], Mk_[j][:, :], Vt_[j][:, :], False, True, [wk(j, 0), wk(j, 3)], [pk])
                    self.cp("act", XT_[j][:, :], ps[:, 0:128], [pk], [wk(j, 7)])
                    ps, pk = self.newps()
                    self.mm(ps[:, 0:128], P_[j][:, :], XT_[j][:, :], True, True, [wk(j, 4), wk(j, 7)], [pk])
                    self.cp("dve", UT_[j][:, :], ps[:, 0:128], [pk], [wk(j, 8)])
                    ps, pk = self.newps()
                    self.mm(ps[:, 0:128], ST[:, :], Rblk[:, j, :], True, False, [STk, "blk"], [pk])
                    self.mm(ps[:, 0:128], UT_[j][:, :], Nb_[j][:, :], False, False, [wk(j, 8), wk(j, 1)], [pk])
                    self.mm(ps[:, 0:128], Vt_[j][:, :], Nk_[j][:, :], False, True, [wk(j, 3), wk(j, 2)], [pk])
                    for hh in range(2):
                        self.cp("act", Ry[64 * hh:64 * hh + 64, j * C:(j + 1) * C], ps[64 * hh:64 * hh + 64, 64 * hh:64 * hh + C], [pk], ["Ry"])
                    ps, pk = self.newps()
                    self.mm(ps[:, 0:128], Bt_[j][:, :], UT_[j][:, :], True, False, [wk(j, 5), wk(j, 8)], [pk])
                    self.mm(ps[:, 0:128], Kt_[j][:, :], Vt_[j][:, :], False, True, [wk(j, 6), wk(j, 3)], [pk])
                    self.tt(ST[:, :], ST[:, :], ps[:, 0:128], ALU.add, [STk, pk], [STk])
                    self.act(ST[:, :], ST[:, :], AF.Identity, [STk, "DC"], [STk], scale=DC[:, j:j + 1])
                    if smp:
                        for hh in range(2):
                            self.dma(self.dr["rw_s"][j, 2 * p + hh], ST[64 * hh:64 * hh + 64, 64 * hh:64 * hh + 64], [STk], ())
                if (not smp) and c0 + n == 2048:
                    for hh in range(2):
                        self.dma(self.dr["rw_p"][2 * p + hh], ST[64 * hh:64 * hh + 64, 64 * hh:64 * hh + 64], [STk], ())
                ps, pk = self.newps()
                self.mm(ps[:, :n], blk64, Ry[:, :n], True, True, ["cst", "Ry"], [pk])
                self.stt(Rt1[:, :n], ps[:, :n], -1.0 / 64, Ry[:, :n], ALU.mult, ALU.add, [pk, "Ry", "Rt1"], ["Rt1"])
                self.act(Rt2[:, :n], Rt1[:, :n], AF.Square, ["Rt1", "Rt2"], ["Rt2"])
                ps, pk = self.newps()
                self.mm(ps[:, :n], blk64, Rt2[:, :n], True, True, ["cst", "Rt2"], [pk])
                self.act(Rt2[:, :n], ps[:, :n], AF.Sqrt, [pk, "epsc"], ["Rt2"], scale=1.0 / 64, bias=self.epsc[:, 1:2])
                self.recip(Rt2[:, :n], Rt2[:, :n], ["Rt2"], ["Rt2"])
                self.tt(Rt1[:, :n], Rt1[:, :n], Rt2[:, :n], ALU.mult, ["Rt1", "Rt2"], ["Rt1"])
                self.ts(Rt1[:, :n], Rt1[:, :n], pvc("rw_ln_g", p), pvc("rw_ln_b", p), ALU.mult, ALU.add, ["Rt1", "pv"], ["Rt1"])
                self.tt(Rt1[:, :n], Rt1[:, :n], Rbon[:, :n], ALU.add, ["Rt1", "Rbon"], ["Rt1"])
                self.tt(yT[:, c0:c0 + n], Rt1[:, :n], Rg[:, :n], ALU.mult, ["Rt1", "Rg"], ["ryT.%d" % bi])
            for nb in range(8):
                self.linear(wo[nb][:, p:p + 1, :], 1, lambda kc, c0, n: yT[:, c0:c0 + n], lambda kc, bi: ["ryT.%d" % bi],
                            self.add_resid(nb))
        self.release(m0)

    def build(self):
        with contextlib.ExitStack() as st:
            self.setup(st)
            self.load_x()
            self.mem_prep()
            for l in range(STAGES["layers"]):
                kind = l % 3
                if kind == 0:
                    if STAGES["attn"]:
                        self.attn(l, l // 3)
                elif kind == 1:
                    if STAGES["hgrn"]:
                        self.hgrn(l)
                else:
                    if STAGES["rwkv"]:
                        self.rwkv(l)
                if STAGES["mem"]:
                    self.mem_attend(l)
                if STAGES["ffn"]:
                    self.ffn(l)
            self.store_y()
            self.P.emit()


IN_SPECS = [
    ("xp", [2048, 1024]), ("xs", [32, 1024]), ("mem", [256, 1024]),
    ("c128", [2, 4, 128, 2, 4, 128]), ("c512", [2, 4, 512, 2, 4, 128]), ("c2048", [2, 4, 2048, 2, 4, 128]),
    ("st_hg", [4, 8, 128, 128]), ("st_rw", [4, 16, 64, 64]), ("st_sh", [128, 8, 4]),
    ("st_fc", [4, 128, NJ, 4, 2]), ("cmem", [4, 4, 256, 2, 4, 256]),
    ("w_qkv", [2, 36, 128, 8, 128]), ("w_ao", [2, 8, 128, 4, 128]),
    ("w_hgin", [32, 128, 8, 128]), ("w_hgo", [8, 128, 8, 128]),
    ("w_rkv", [3, 8, 128, 8, 128]), ("rw1", [128, 8, 64]), ("ra1", [128, 8, 64]), ("rg1", [128, 8, 160]),
    ("rw2", [64, 1024]), ("ra2", [64, 1024]), ("rg2", [160, 1024]), ("w_rwo", [8, 128, 8, 128]),
    ("w_xq", [4, 8, 128, 8, 128]), ("w_xkv", [4, 16, 128, 8, 128]), ("w_xo", [4, 8, 128, 8, 128]),
    ("w_fin", [4, 2 * NJ, 128, 8, 128]), ("w_fdn", [4, 8, 128, NJ, 128]),
]
OUT_SPECS = [
    ("y_p", [2048, 1024]), ("y_s", [32, 1024]),
    ("kv128_p", [2, 128, 2, 4, 128]), ("kv512_p", [2, 512, 2, 4, 128]), ("kv2048_p", [2, 2048, 2, 4, 128]),
    ("hg_p", [8, 128, 128]), ("rw_p", [16, 64, 64]), ("sh_o", [128, 8, 5]), ("fc_o", [4, 128, NJ, 5, 2]),
    ("mkv_o", [4, 256, 2048]),
    ("kv128_s", [2, 4, 128, 2, 4, 128]), ("kv512_s", [2, 4, 512, 2, 4, 128]), ("kv2048_s", [2, 4, 2048, 2, 4, 128]),
    ("hg_s", [4, 8, 128, 128]), ("rw_s", [4, 16, 64, 64]),
]


def build_nc(pvca, cca):
    nc = bass.Bass("TRN2", target_bir_lowering=False)
    dr = {}
    for name, shape in IN_SPECS + [("pv", [128, pvca.n]), ("cst", [128, cca.n])]:
        dr[name] = nc.dram_tensor(name, shape, F32, kind="ExternalInput").ap()
    for name, shape in OUT_SPECS:
        dr[name] = nc.dram_tensor(name, shape, F32, kind="ExternalOutput").ap()
    b = Builder(nc, dr, pvca, cca)
    b.build()
    return nc


def kernel(**inp):
    inp = {k: np.asarray(v) for k, v in inp.items()}
    f = np.float32
    pv, pvca = build_pv(inp)
    cst, cca = build_cst()
    nc = build_nc(pvca, cca)
    shared = {
        "pv": pv, "cst": cst,
        "w_qkv": np.stack([tile_w(inp["attn_w_qkv"][j]) for j in range(2)]),
        "w_ao": np.stack([tile_w(inp["attn_w_o"][j]) for j in range(2)]),
        "w_hgin": tile_w(inp["hg_w_in"][0]), "w_hgo": tile_w(inp["hg_w_o"][0]),
        "w_rkv": np.stack([tile_w(inp["rw_w_rkv"][0, j]) for j in range(3)]),
        "rw1": np.ascontiguousarray(inp["rw_w1"][0].reshape(8, 128, 64).transpose(1, 0, 2)),
        "ra1": np.ascontiguousarray(inp["rw_a1"][0].reshape(8, 128, 64).transpose(1, 0, 2)),
        "rg1": np.ascontiguousarray(inp["rw_g1"][0].reshape(8, 128, 160).transpose(1, 0, 2)),
        "rw2": np.ascontiguousarray(inp["rw_w2"][0]), "ra2": np.ascontiguousarray(inp["rw_a2"][0]),
        "rg2": np.ascontiguousarray(inp["rw_g2"][0]),
        "w_rwo": tile_w(inp["rw_w_o"][0]),
        "w_xq": np.stack([tile_w(inp["xa_w_q"][l]) for l in range(4)]),
        "w_xkv": np.stack([tile_w(inp["xa_w_kv"][l]) for l in range(4)]),
        "w_xo": np.stack([tile_w(inp["xa_w_o"][l]) for l in range(4)]),
        "w_fin": np.stack([tile_w(inp["ffn_w_in"][l]) for l in range(4)]),
        "w_fdn": np.stack([tile_w(inp["ffn_w_down"][l]) for l in range(4)]),
    }
    in_maps = []
    for c in range(8):
        sl = slice(4 * c, 4 * c + 4)
        m = dict(shared)
        m["xp"] = np.ascontiguousarray(inp["x_prompt"][c])
        m["xs"] = np.ascontiguousarray(inp["x_sample"][sl].reshape(32, 1024))
        m["mem"] = np.ascontiguousarray(inp["mem_prompt"][c])
        m["c128"] = np.ascontiguousarray(inp["cache_attn_kv_w128"][:, sl])
        m["c512"] = np.ascontiguousarray(inp["cache_attn_kv_w512"][:, sl])
        m["c2048"] = np.ascontiguousarray(inp["cache_attn_kv_w2048"][:, sl])
        m["st_hg"] = np.ascontiguousarray(inp["state_hgrn"][0, sl])
        m["st_rw"] = np.ascontiguousarray(inp["state_rwkv"][0, sl].transpose(0, 1, 3, 2))
        m["st_sh"] = np.ascontiguousarray(inp["state_rwkv_shift"][0, sl].reshape(4, 8, 128).transpose(2, 1, 0))
        m["st_fc"] = np.ascontiguousarray(inp["state_ffn_conv"][:, sl].reshape(4, 4, 2, NJ, 128).transpose(0, 4, 3, 1, 2))
        m["cmem"] = np.ascontiguousarray(inp["cache_mem_kv"][:, sl])
        in_maps.append({k: np.ascontiguousarray(v, dtype=f) for k, v in m.items()})
    res = run_bass_kernel_spmd(nc, in_maps, core_ids=list(range(8)))
    R = res.results

    def cat(name, axis=0, stack=False):
        arrs = [np.asarray(R[c][name]) for c in range(8)]
        return np.stack(arrs, axis) if stack else np.concatenate(arrs, axis)

    y_p = cat("y_p", 0, True)
    y_s = cat("y_s", 0, True).reshape(32, 8, 1024)
    kvp = [cat(n, 1, True) for n in ("kv128_p", "kv512_p", "kv2048_p")]
    hg_p = cat("hg_p", 0, True)[None]
    rw_p = cat("rw_p", 0, True).transpose(0, 1, 3, 2)[None]
    sh = cat("sh_o", 0, True)
    sh = sh.transpose(0, 3, 2, 1).reshape(8, 5, 1024)
    sh_p = sh[:, 0][None]
    sh_s = sh[:, 1:5].reshape(32, 1024)[None]
    fc = cat("fc_o", 0, True)
    fc = fc.transpose(1, 0, 4, 5, 3, 2).reshape(4, 8, 5, 2, DFF)
    fc_p = np.ascontiguousarray(fc[:, :, 0])
    fc_s = np.ascontiguousarray(fc[:, :, 1:5].reshape(4, 32, 2, DFF))
    mkv = cat("mkv_o", 1, True).reshape(4, 8, 256, 2, 4, 256)
    kvs = [cat(n, 1) for n in ("kv128_s", "kv512_s", "kv2048_s")]
    hg_s = cat("hg_s", 0)[None]
    rw_s = cat("rw_s", 0).transpose(0, 1, 3, 2)[None]
    outs = (y_p, y_s, kvp[0], kvp[1], kvp[2], hg_p, rw_p, sh_p, fc_p, mkv, kvs[0], kvs[1], kvs[2], hg_s, rw_s, sh_s, fc_s)
    return tuple(np.ascontiguousarray(o, dtype=np.float32) for o in outs)
```

```python
import contextlib
import numpy as np
import concourse.bass as bass
import concourse.mybir as mybir
from concourse.bass_utils import run_bass_kernel_spmd

F32 = mybir.dt.float32
BF16 = mybir.dt.bfloat16
AF = mybir.ActivationFunctionType
ALU = mybir.AluOpType
AX = mybir.AxisListType
ENGS = ("pe", "act", "dve", "pool", "sp")
NDSEM = 48
NHW = 32

NT = 2080
TB = [(0, 512), (512, 512), (1024, 512), (1536, 512), (2048, 32)]
D = 1024
DFF = 2816
NJ = 22
EPS = 1e-6
DILS = (1, 4, 16)
WINS = (128, 512, 2048)
NWB = 6
STRICT_SAME_ENGINE = False
STAGES = {"attn": True, "hgrn": True, "rwkv": True, "mem": True, "ffn": True, "layers": 4, "a_d2d": 1, "a_kvout": 1, "a_sout": 1, "a_pu": 1, "a_su": 1, "a_groups": 3}


class Op:
    __slots__ = ("eng", "fn", "reads", "writes", "dma", "seq", "waits", "signal", "cnt", "dsem", "dval", "snap")

    def __init__(self, eng, fn, reads, writes, dma):
        self.eng, self.fn, self.reads, self.writes, self.dma = eng, fn, tuple(reads), tuple(writes), dma
        self.waits = []
        self.signal = False
        self.cnt = 0
        self.dsem = -1
        self.dval = 0
        self.snap = None


class Prog:
    def __init__(self, nc):
        self.nc = nc
        self.ops = []

    def add(self, eng, fn, reads=(), writes=(), dma=False):
        self.ops.append(Op(eng, fn, reads, writes, dma))

    def barrier(self):
        self.ops.append(None)

    def analyse(self):
        ops = self.ops
        last_w = {}
        readers = {}
        seqc = {e: 0 for e in ENGS}
        known = {e: {x: 0 for x in ENGS} for e in ENGS}
        kd = {e: set() for e in ENGS}
        dsem_last = [None] * NDSEM
        dsem_cnt = [0] * NDSEM
        nd = 0
        nds = 0
        pend = {e: [] for e in ENGS}
        last_op = {e: None for e in ENGS}
        for i, op in enumerate(ops):
            if op is None:
                for E in ENGS:
                    wl = []
                    for E2 in ENGS:
                        j = last_op[E2]
                        if j is not None and known[E][E2] < ops[j].seq:
                            ops[j].signal = True
                            wl.append(("e", E2, j))
                            known[E][E2] = ops[j].seq
                    for s_ in range(NDSEM):
                        if dsem_cnt[s_] > 0:
                            wl.append(("d", s_, 16 * dsem_cnt[s_]))
                    pend[E] = pend[E] + wl
                last_w.clear()
                readers.clear()
                dsem_last = [None] * NDSEM
                continue
            E = op.eng
            if pend[E]:
                op.waits.extend(pend[E])
                pend[E] = []
            seqc[E] += 1
            op.seq = seqc[E]
            own = op.seq - 1 if E in ("pe", "sp") else 0
            if own > known[E][E]:
                known[E][E] = own
            deps = set()
            for k in op.reads:
                j = last_w.get(k)
                if j is not None:
                    deps.add(j)
            for k in op.writes:
                j = last_w.get(k)
                if j is not None and (ops[j].dma or op.dma or ops[j].eng != E or STRICT_SAME_ENGINE):
                    deps.add(j)
                rd = readers.get(k)
                if rd:
                    for j in rd.values():
                        if ops[j].dma or op.dma or ops[j].eng != E or STRICT_SAME_ENGINE:
                            deps.add(j)
            deps.discard(i)
            if op.dma:
                if E == "pool":
                    s = NHW + (nds % (NDSEM - NHW))
                    nds += 1
                else:
                    s = nd % NHW
                    nd += 1
                if dsem_last[s] is not None:
                    deps.add(dsem_last[s])
                dsem_cnt[s] += 1
                op.dsem, op.dval = s, 16 * dsem_cnt[s]
                dsem_last[s] = i
            for j in sorted(deps):
                p = ops[j]
                if p.dma:
                    if j in kd[E]:
                        continue
                    op.waits.append(("d", p.dsem, p.dval))
                    kd[E].add(j)
                    for x in ENGS:
                        if p.snap[x] > known[E][x]:
                            known[E][x] = p.snap[x]
                else:
                    if known[E][p.eng] >= p.seq:
                        continue
                    p.signal = True
                    op.waits.append(("e", p.eng, j))
                    known[E][p.eng] = p.seq
                    for x in ENGS:
                        if p.snap[x] > known[E][x]:
                            known[E][x] = p.snap[x]
            op.snap = dict(known[E])
            if not op.dma:
                last_op[E] = i
            for k in op.writes:
                last_w[k] = i
                readers[k] = {}
            for k in op.reads:
                readers.setdefault(k, {})[("dma", i) if op.dma else E] = i
        c = {e: 0 for e in ENGS}
        for op in ops:
            if op is not None and op.signal:
                c[op.eng] += 1
                op.cnt = c[op.eng]
        self.sig_tot = c
        fin = {}
        for op in ops:
            if op is not None and op.dma:
                fin[op.dsem] = max(fin.get(op.dsem, 0), op.dval)
        self.final = fin

    def emit(self):
        nc = self.nc
        self.analyse()
        ops = self.ops
        with contextlib.ExitStack() as st:
            psem = {e: st.enter_context(nc.semaphore("prog_" + e)) for e in ENGS}
            dsem = [st.enter_context(nc.semaphore("dmas%d" % i)) for i in range(NDSEM)]
            block = st.enter_context(nc.Block())

            def stream(ename):
                def body(eng):
                    for op in ops:
                        if op is None or op.eng != ename:
                            continue
                        for w in op.waits:
                            if w[0] == "d":
                                eng.wait_ge(dsem[w[1]], w[2])
                            else:
                                eng.wait_ge(psem[w[1]], ops[w[2]].cnt)
                        ins = op.fn(eng)
                        if op.dma:
                            ins.then_inc(dsem[op.dsem], 16)
                        elif op.signal:
                            ins.then_inc(psem[ename], 1)
                    if ename == "sp":
                        for s, v in self.final.items():
                            eng.wait_ge(dsem[s], v)
                        for e2 in ENGS:
                            if e2 != "sp" and self.sig_tot[e2] > 0:
                                eng.wait_ge(psem[e2], self.sig_tot[e2])
                return body

            block.tensor(stream("pe"))
            block.scalar(stream("act"))
            block.vector(stream("dve"))
            block.gpsimd(stream("pool"))
            block.sync(stream("sp"))


class ColAlloc:
    def __init__(self):
        self.n = 0
        self.cols = {}

    def add(self, name, ncols):
        self.cols[name] = self.n
        self.n += ncols
        return self.cols[name]


def fm_vec(v):
    v = np.asarray(v, np.float32).reshape(-1)
    nc_ = v.size // 128
    return np.ascontiguousarray(v.reshape(nc_, 128).T)


def pv_layout():
    ca = ColAlloc()
    for l in range(4):
        for nm in ("norm_mix", "norm_mem", "norm_ffn", "mem_norm"):
            ca.add("%s%d" % (nm, l), 8)
        ca.add("xa_q_gain%d" % l, 2)
        ca.add("xa_k_gain%d" % l, 2)
        for t in range(3):
            ca.add("conv_w%d_%d" % (l, t), NJ)
        ca.add("conv_b%d" % l, NJ)
    for ja in range(2):
        for g in range(3):
            ca.add("aq_gain%d_%d" % (ja, g), 1)
            ca.add("ak_gain%d_%d" % (ja, g), 1)
    for l in range(4):
        ca.add("hg_lb%d" % l, 8)
    ca.add("hg_out_gain", 1)
    for j in range(6):
        ca.add("rw_mu%d" % j, 8)
    for nm in ("rw_w0", "rw_a0", "rw_k_k", "rw_k_a", "rw_r_k", "rw_ln_g", "rw_ln_b"):
        ca.add(nm, 8)
    return ca


def build_pv(inp):
    ca = pv_layout()
    pv = np.zeros((128, ca.n), np.float32)

    def put(name, v):
        a = fm_vec(v)
        pv[:, ca.cols[name]:ca.cols[name] + a.shape[1]] = a

    for l in range(4):
        for nm in ("norm_mix", "norm_mem", "norm_ffn", "mem_norm"):
            put("%s%d" % (nm, l), inp[nm][l])
        put("xa_q_gain%d" % l, inp["xa_q_gain"][l])
        put("xa_k_gain%d" % l, inp["xa_k_gain"][l])
        for t in range(3):
            put("conv_w%d_%d" % (l, t), inp["ffn_conv_w"][l, t])
        put("conv_b%d" % l, inp["ffn_conv_b"][l])
        put("hg_lb%d" % l, inp["hg_lb_logits"][l])
    for ja in range(2):
        for g in range(3):
            put("aq_gain%d_%d" % (ja, g), inp["attn_q_gain"][ja, g])
            put("ak_gain%d_%d" % (ja, g), inp["attn_k_gain"][ja, g])
    put("hg_out_gain", inp["hg_out_gain"][0])
    for j in range(6):
        put("rw_mu%d" % j, inp["rw_mu"][0, j])
    for nm in ("rw_w0", "rw_a0", "rw_k_k", "rw_k_a", "rw_r_k", "rw_ln_g", "rw_ln_b"):
        put(nm, inp[nm][0])
    return pv, ca


def cst_layout():
    ca = ColAlloc()
    ca.add("ident", 128)
    ca.add("ones", 128)
    ca.add("m_own", 128)
    ca.add("m_prev", 128)
    ca.add("blk64", 128)
    ca.add("own_s", 96)
    ca.add("bd_su", 128)
    ca.add("bd_sl", 128)
    ca.add("bd_iu", 128)
    return ca


def build_cst():
    ca = cst_layout()
    c = np.zeros((128, ca.n), np.float32)
    j = np.arange(128)[:, None]
    i = np.arange(128)[None, :]
    c[:, ca.cols["ident"]:ca.cols["ident"] + 128] = (j == i)
    c[:, ca.cols["ones"]:ca.cols["ones"] + 128] = 1.0
    c[:, ca.cols["m_own"]:ca.cols["m_own"] + 128] = (j <= i)
    c[:, ca.cols["m_prev"]:ca.cols["m_prev"] + 128] = (j >= i)
    c[:, ca.cols["blk64"]:ca.cols["blk64"] + 128] = ((j // 64) == (i // 64))
    col = ca.cols["own_s"]
    for g in range(3):
        R = min(DILS[g], 8)
        nq = 8 // R
        for s in range(4):
            for r in range(R):
                for u in range(nq):
                    for ip in range(8):
                        if ip % R == r and ip <= r + R * u:
                            c[8 * s + ip, col + u] = 1.0
                col += nq
    same = ((j // 64) == (i // 64))
    c[:, ca.cols["bd_su"]:ca.cols["bd_su"] + 128] = same & ((j % 64) < (i % 64))
    c[:, ca.cols["bd_sl"]:ca.cols["bd_sl"] + 128] = same & ((j % 64) > (i % 64))
    c[:, ca.cols["bd_iu"]:ca.cols["bd_iu"] + 128] = same & ((j % 64) <= (i % 64))
    return c, ca


def own_s_col(ca, g, s, r):
    col = ca.cols["own_s"]
    for gg in range(3):
        R = min(DILS[gg], 8)
        nq = 8 // R
        if gg == g:
            return col + (s * R + r) * nq
        col += 4 * R * nq
    raise ValueError


def tile_w(w, bw=128):
    K, N = w.shape
    return np.ascontiguousarray(w.reshape(K // 128, 128, N // bw, bw).transpose(2, 1, 0, 3))


class Builder:
    def __init__(self, nc, dr, pvca, cca):
        self.nc, self.dr, self.pvca, self.cca = nc, dr, pvca, cca
        self.P = Prog(nc)
        self.psi = 0
        self.wi = 0
        self.scri = 0
        self.off = 0

    def alloc(self, cols):
        a = self.arena[:, self.off:self.off + cols]
        self.off += cols
        assert self.off <= self.acols, ("SBUF arena overflow", self.off, self.acols)
        return a

    def alloc16(self, cols):
        return self.alloc((cols + 1) // 2).bitcast(BF16)[:, 0:cols]

    def mark(self):
        return self.off

    def release(self, m):
        self.P.barrier()
        self.off = m

    def newps(self):
        i = self.psi % 8
        self.psi += 1
        return self.ps[i], "ps%d" % i

    def scr(self):
        i = self.scri % 4
        self.scri += 1
        return self.scrb[i], "scr%d" % i

    def mm(self, out, lhsT, rhs, start, stop, r, w):
        self.P.add("pe", lambda e: e.matmul(out, lhsT, rhs, start=start, stop=stop), r, w)

    def tr(self, out, in_, ident, r, w):
        self.P.add("pe", lambda e: e.transpose(out, in_, ident), r, w)

    def act(self, out, in_, func, r, w, bias=None, scale=None, accum=None):
        kw = {}
        if bias is not None:
            kw["bias"] = bias
        if scale is not None:
            kw["scale"] = scale
        if accum is not None:
            kw["accum_out"] = accum
        self.P.add("act", lambda e: e.activation(out, in_, func, **kw), r, w)

    def cp(self, eng, out, in_, r, w):
        if eng == "act":
            self.P.add("act", lambda e: e.copy(out, in_), r, w)
        else:
            self.P.add(eng, lambda e: e.tensor_copy(out, in_), r, w)

    def tt(self, out, a, b, op, r, w, eng="dve"):
        self.P.add(eng, lambda e: e.tensor_tensor(out, a, b, op), r, w)

    def ts(self, out, a, s1, s2, op0, op1, r, w, eng="dve"):
        if s2 is None:
            self.P.add(eng, lambda e: e.tensor_scalar(out, a, s1, None, op0), r, w)
        else:
            self.P.add(eng, lambda e: e.tensor_scalar(out, a, s1, s2, op0, op1), r, w)

    def stt(self, out, a, s, b, op0, op1, r, w, eng="dve"):
        self.P.add(eng, lambda e: e.scalar_tensor_tensor(out, a, s, b, op0, op1), r, w)

    def recip(self, out, in_, r, w):
        self.P.add("dve", lambda e: e.reciprocal(out, in_), r, w)

    def memset(self, out, val, w, eng="dve"):
        self.P.add(eng, lambda e: e.memset(out, val), (), w)

    def dma(self, out, in_, r, w, q="sp", slow=False):
        if slow:
            self.P.add(q, lambda e: e.dma_start(out=out, in_=in_, allow_slow_non_contiguous=True), r, w, dma=True)
        else:
            self.P.add(q, lambda e: e.dma_start(out=out, in_=in_), r, w, dma=True)

    def pvc(self, name, k=0, n=1):
        c = self.pvca.cols[name] + k
        return self.pv[:, c:c + n]

    def cst(self, name, rows=128, n=None, k=0):
        c = self.cca.cols[name] + k
        if n is None:
            n = 128
        return self.c32[:rows, c:c + n]

    def cst16(self, name, rows=128, n=None, k=0):
        c = self.cca.cols[name] + k
        if n is None:
            n = 128
        return self.c16[:rows, c:c + n]

    def load_w(self, wdram, KC, rows=128, ncol=128):
        i = self.wi % NWB
        self.wi += 1
        wt = self.wb[i]
        key = "wb%d" % i
        self.dma(wt[:rows, :KC, :ncol], wdram, (), [key], q="pool")
        return wt, key

    def linear(self, wdram, KC, rhs_fn, rkeys_fn, cons, blocks=(0, 1, 2, 3, 4), rows=128, ncol=128):
        wt, wk = self.load_w(wdram, KC, rows, ncol)
        for bi in blocks:
            c0, n = TB[bi]
            ps, pk = self.newps()
            for kc in range(KC):
                self.mm(ps[:ncol, :n], wt[:rows, kc, :ncol], rhs_fn(kc, c0, n), kc == 0, kc == KC - 1,
                        [wk] + rkeys_fn(kc, bi), [pk])
            cons(bi, c0, n, ps, pk)

    def h_rhs(self, kc, c0, n):
        return self.hT[:, kc, c0:c0 + n]

    def h_keys(self, kc, bi):
        return ["h%d.%d" % (kc, bi)]

    def rstd_block(self, srcs, skeys, n, dim, eps, ones=None):
        if ones is None:
            ones = self.cst16("ones")
        ps, pk = self.newps()
        for ci, (ap, k) in enumerate(zip(srcs, skeys)):
            sq, sk = self.scr()
            sq16 = sq.bitcast(BF16)
            self.act(sq16[:, :n], ap, AF.Square, [k], [sk])
            self.mm(ps[:, :n], ones, sq16[:, :n], ci == 0, ci == len(srcs) - 1, [sk, "cst16"], [pk])
        rs, rk = self.scr()
        self.act(rs[:, :n], ps[:, :n], AF.Sqrt, [pk], [rk], scale=1.0 / dim, bias=self.epsc[:, 0:1] if eps == EPS else eps)
        self.recip(rs[:, :n], rs[:, :n], [rk], [rk])
        return rs, rk

    def dnorm(self, gname):
        for bi, (c0, n) in enumerate(TB):
            rs, rk = self.rstd_block([self.xT[:, c, c0:c0 + n] for c in range(8)],
                                     ["x%d.%d" % (c, bi) for c in range(8)], n, D, EPS)
            for c in range(8):
                self.stt(self.hT[:, c, c0:c0 + n], self.xT[:, c, c0:c0 + n], self.pvc(gname, c), rs[:, :n],
                         ALU.mult, ALU.mult, ["x%d.%d" % (c, bi), rk, "pv"], ["h%d.%d" % (c, bi)])

    def add_resid(self, nb):
        def cons(bi, c0, n, ps, pk):
            k = "x%d.%d" % (nb, bi)
            self.tt(self.xT[:, nb, c0:c0 + n], ps[:, :n], self.xT[:, nb, c0:c0 + n], ALU.add, [pk, k], [k])
        return cons

    def setup(self, st):
        nc = self.nc
        self.acols = 52800
        self.arena = st.enter_context(nc.sbuf_tensor("arena", [128, self.acols], F32))
        self.ps = [st.enter_context(nc.psum_tensor("psb%d" % i, [128, 512], F32)) for i in range(8)]
        self.xT = self.alloc(8 * NT).rearrange("p (c t) -> p c t", t=NT)
        self.hT = self.alloc16(8 * NT).rearrange("p (c t) -> p c t", t=NT)
        self.pv = self.alloc(self.pvca.n)
        self.c32 = self.alloc(self.cca.n)
        self.c16 = self.alloc16(self.cca.n)
        self.epsc = self.alloc(2)
        self.wb = [self.alloc16(8 * 128).rearrange("p (k n) -> p k n", n=128) for _ in range(NWB)]
        self.scrb = [self.alloc(512) for _ in range(4)]
        self.dma(self.pv, self.dr["pv"], (), ["pv"])
        self.dma(self.c32, self.dr["cst"], (), ["cst"])
        self.cp("dve", self.c16, self.c32, ["cst"], ["cst16"])
        self.memset(self.epsc[:, 0:1], EPS, ["epsc"])
        self.memset(self.epsc[:, 1:2], 64e-5, ["epsc"])
        self.P.barrier()

    def load_x(self):
        m = self.mark()
        stg = [self.alloc(1024) for _ in range(2)]
        ident = self.cst("ident")
        for tt_ in range(17):
            sb = stg[tt_ % 2]
            sk = "xstg%d" % (tt_ % 2)
            if tt_ < 16:
                rows, c0, src = 128, tt_ * 128, self.dr["xp"][tt_ * 128:(tt_ + 1) * 128, :]
            else:
                rows, c0, src = 32, 2048, self.dr["xs"]
            self.dma(sb[:rows, :], src, (), [sk])
            bi = min(c0 // 512, 4)
            for half in range(2):
                ps, pk = self.newps()
                for q in range(4):
                    c = half * 4 + q
                    self.tr(ps[:, q * 128:q * 128 + rows], sb[:rows, c * 128:(c + 1) * 128], ident[:rows, :rows],
                            [sk, "cst"], [pk])
                self.cp("act" if half == 0 else "dve",
                        self.xT[:, half * 4:half * 4 + 4, c0:c0 + rows],
                        ps[:, :].rearrange("p (q t) -> p q t", t=128)[:, :, :rows],
                        [pk], ["x%d.%d" % (half * 4 + q, bi) for q in range(4)])
        self.release(m)

    def store_y(self):
        m = self.mark()
        stg = [self.alloc(1024) for _ in range(2)]
        ident = self.cst("ident")
        for tt_ in range(17):
            sb = stg[tt_ % 2]
            sk = "ystg%d" % (tt_ % 2)
            if tt_ < 16:
                rows, c0, dst = 128, tt_ * 128, self.dr["y_p"][tt_ * 128:(tt_ + 1) * 128, :]
            else:
                rows, c0, dst = 32, 2048, self.dr["y_s"]
            bi = min(c0 // 512, 4)
            for half in range(2):
                ps, pk = self.newps()
                for q in range(4):
                    c = half * 4 + q
                    self.tr(ps[:rows, q * 128:(q + 1) * 128], self.xT[:, c, c0:c0 + rows], ident,
                            ["x%d.%d" % (c, bi), "cst"], [pk])
                self.cp("act" if half == 0 else "dve", sb[:rows, half * 512:(half + 1) * 512], ps[:rows, :], [pk], [sk])
            self.dma(dst, sb[:rows, :], [sk], ())
        self.release(m)

    def ffn(self, l):
        self.dnorm("norm_ffn%d" % l)
        m = self.mark()
        UW = 2050 + 40
        ub = [self.alloc(UW) for _ in range(2)]
        cb = [self.alloc(NT) for _ in range(2)]
        sl = cb
        GS = 8
        aT = self.alloc16(GS * NT).rearrange("p (g t) -> p g t", t=NT)
        fcst = self.alloc(NJ * 8).rearrange("p (j s t) -> p j s t", s=4, t=2)
        fco = self.alloc(NJ * 10).rearrange("p (j s t) -> p j s t", s=5, t=2)
        self.dma(fcst, self.dr["st_fc"][l], (), ["fcst"])
        for i in range(2):
            self.memset(ub[i][:, 0:2], 0.0, ["u%d" % i])
        win = self.dr["w_fin"][l]
        wdn = self.dr["w_fdn"][l]
        groups = [list(range(a, min(a + GS, NJ))) for a in range(0, NJ, GS)]
        cnt = 0
        for grp in groups:
            for gi, j in enumerate(grp):
                u = ub[cnt % 2]
                uk = "u%d" % (cnt % 2)
                c_ = cb[cnt % 2]
                ck = "c%d" % (cnt % 2)
                s_ = sl[cnt % 2]
                sk = ck
                cnt += 1
                us = u[:, 2050:2090].rearrange("p (s k) -> p s k", k=10)

                def cons_u(bi, c0, n, ps, pk, u=u, uk=uk, us=us):
                    if bi < 4:
                        self.cp("act", u[:, 2 + c0:2 + c0 + n], ps[:, :n], [pk], [uk])
                    else:
                        self.cp("act", us[:, :, 2:10], ps[:, :32].rearrange("p (s i) -> p s i", i=8), [pk], [uk])
                self.linear(win[j], 8, self.h_rhs, self.h_keys, cons_u)
                self.cp("dve", us[:, :, 0:2], fcst[:, j, :, :], ["fcst", uk], [uk])
                w0, w1, w2, bb = (self.pvc("conv_w%d_0" % l, j), self.pvc("conv_w%d_1" % l, j),
                                  self.pvc("conv_w%d_2" % l, j), self.pvc("conv_b%d" % l, j))
                self.ts(c_[:, 0:2048], u[:, 2:2050], w2, bb, ALU.mult, ALU.add, [uk, "pv"], [ck])
                self.stt(c_[:, 0:2048], u[:, 1:2049], w1, c_[:, 0:2048], ALU.mult, ALU.add, [uk, ck, "pv"], [ck])
                self.stt(c_[:, 0:2048], u[:, 0:2048], w0, c_[:, 0:2048], ALU.mult, ALU.add, [uk, ck, "pv"], [ck])
                cs = c_[:, 2048:2080].rearrange("p (s i) -> p s i", i=8)
                self.ts(cs, us[:, :, 2:10], w2, bb, ALU.mult, ALU.add, [uk, "pv"], [ck])
                self.stt(cs, us[:, :, 1:9], w1, cs, ALU.mult, ALU.add, [uk, ck, "pv"], [ck])
                self.stt(cs, us[:, :, 0:8], w0, cs, ALU.mult, ALU.add, [uk, ck, "pv"], [ck])
                self.act(s_[:, :], c_[:, :], AF.Silu, [ck], [sk])
                self.cp("act", fco[:, j, 0, :], u[:, 2048:2050], [uk], ["fco"])
                self.cp("act", fco[:, j, 1:5, :], us[:, :, 8:10], [uk], ["fco"])

                def cons_g(bi, c0, n, ps, pk, gi=gi, s_=s_, sk=sk):
                    self.tt(aT[:, gi, c0:c0 + n], ps[:, :n], s_[:, c0:c0 + n], ALU.mult, [pk, sk], ["a%d.%d" % (gi, bi)])
                self.linear(win[NJ + j], 8, self.h_rhs, self.h_keys, cons_g)
            g0, gl = grp[0], len(grp)
            for nb in range(8):
                self.linear(wdn[nb][:, g0:g0 + gl, :], gl, lambda kc, c0, n: aT[:, kc, c0:c0 + n],
                            lambda kc, bi: ["a%d.%d" % (kc, bi)], self.add_resid(nb))
        self.dma(self.dr["fc_o"][l], fco, ["fco"], ())
        self.release(m)

    def mem_prep(self):
        self.memTn = self.alloc(8 * 256).rearrange("p (c t) -> p c t", t=256)
        m = self.mark()
        stg = [self.alloc(1024) for _ in range(2)]
        raw = self.alloc(8 * 256).rearrange("p (c t) -> p c t", t=256)
        ident = self.cst("ident")
        for mb in range(2):
            self.dma(stg[mb], self.dr["mem"][mb * 128:(mb + 1) * 128, :], (), ["mstg%d" % mb])
            for half in range(2):
                ps, pk = self.newps()
                for q in range(4):
                    c = half * 4 + q
                    self.tr(ps[:, q * 128:(q + 1) * 128], stg[mb][:, c * 128:(c + 1) * 128], ident, ["mstg%d" % mb, "cst"], [pk])
                self.cp("act", raw[:, half * 4:half * 4 + 4, mb * 128:(mb + 1) * 128],
                        ps[:, :].rearrange("p (q t) -> p q t", t=128), [pk], ["mraw"])
        rs, rk = self.rstd_block([raw[:, c, :] for c in range(8)], ["mraw"] * 8, 256, D, EPS)
        for c in range(8):
            self.tt(self.memTn[:, c, :], raw[:, c, :], rs[:, :256], ALU.mult, ["mraw", rk], ["memTn"])
        self.release(m)

    def mem_kv(self, l, KTm, Vm):
        m = self.mark()
        ml = self.alloc16(8 * 256).rearrange("p (c t) -> p c t", t=256)
        kvraw = self.alloc(16 * 256).rearrange("p (b t) -> p b t", t=256)
        stage = self.alloc(2 * 2048).rearrange("p (mb c) -> p mb c", c=2048)
        for c in range(8):
            self.ts(ml[:, c, :], self.memTn[:, c, :], self.pvc("mem_norm%d" % l, c), None, ALU.mult, None,
                    ["memTn", "pv"], ["ml"])
        wkv = self.dr["w_xkv"][l]
        for blk in range(16):
            wt, wk = self.load_w(wkv[blk], 8)
            ps, pk = self.newps()
            for kc in range(8):
                self.mm(ps[:, :256], wt[:, kc, :], ml[:, kc, :], kc == 0, kc == 7, [wk, "ml"], [pk])
            self.cp("act", kvraw[:, blk, :], ps[:, :256], [pk], ["kvraw%d" % blk])
        for h in range(4):
            rs, rk = self.rstd_block([kvraw[:, 2 * h + e, :] for e in range(2)], ["kvraw%d" % (2 * h + e) for e in range(2)],
                                     256, 256, EPS)
            for e in range(2):
                b_ = 2 * h + e
                self.stt(kvraw[:, b_, :], kvraw[:, b_, :], self.pvc("xa_k_gain%d" % l, e), rs[:, :256], ALU.mult, ALU.mult,
                         ["kvraw%d" % b_, rk, "pv"], ["kvraw%d" % b_])
                self.cp("act", KTm[:, b_, :], kvraw[:, b_, :], ["kvraw%d" % b_], ["KTm"])
        ident = self.cst("ident")
        for mb in range(2):
            for q4 in range(4):
                ps, pk = self.newps()
                for q in range(4):
                    blk = q4 * 4 + q
                    self.tr(ps[:, q * 128:(q + 1) * 128], kvraw[:, blk, mb * 128:(mb + 1) * 128], ident,
                            ["kvraw%d" % blk, "cst"], [pk])
                self.cp("act" if q4 % 2 == 0 else "dve", stage[:, mb, q4 * 512:(q4 + 1) * 512], ps[:, :], [pk], ["mstage"])
        self.dma(self.dr["mkv_o"][l].rearrange("(mb m) c -> m mb c", m=128), stage, ["mstage"], ())
        self.cp("dve", Vm, stage[:, :, 1024:2048], ["mstage"], ["Vm"])
        self.release(m)

    def mem_attend(self, l):
        self.dnorm("norm_mem%d" % l)
        m0 = self.mark()
        KTm = self.alloc16(8 * 256).rearrange("p (b t) -> p b t", t=256)
        Vm = self.alloc16(2 * 1024).rearrange("p (mb c) -> p mb c", c=1024)
        self.mem_kv(l, KTm, Vm)
        qraw = self.alloc(2 * NT).rearrange("p (e t) -> p e t", t=NT)
        q16 = self.alloc16(2 * NT).rearrange("p (e t) -> p e t", t=NT)
        oT = self.alloc16(2 * NT).rearrange("p (b t) -> p b t", t=NT)
        pT = [self.alloc16(2 * 512).rearrange("p (mb t) -> p mb t", t=512) for _ in range(2)]
        rden = [self.alloc(512) for _ in range(2)]
        ckv = [self.alloc(2 * 2 * 256).rearrange("p (mb t e) -> p mb t e", t=2, e=256) for _ in range(2)]
        kTs = [self.alloc16(4 * 128).rearrange("p (q t) -> p q t", t=128) for _ in range(2)]
        vs16 = [self.alloc16(2 * 256).rearrange("p (mb e) -> p mb e", e=256) for _ in range(2)]
        pTs = [self.alloc16(16) for _ in range(2)]
        gsc = self.alloc(2)
        self.ts(gsc, self.pvc("xa_q_gain%d" % l, 0, 2), 256 ** -0.5, None, ALU.mult, None, ["pv"], ["gsc"])
        ones16 = self.cst16("ones")
        ident = self.cst("ident")
        wq = self.dr["w_xq"][l]
        it = 0
        for h in range(4):
            for e in range(2):
                def cons_q(bi, c0, n, ps, pk, e=e):
                    self.cp("act", qraw[:, e, c0:c0 + n], ps[:, :n], [pk], ["qraw%d.%d" % (e, bi)])
                self.linear(wq[2 * h + e], 8, self.h_rhs, self.h_keys, cons_q)
            for bi, (c0, n) in enumerate(TB):
                rs, rk = self.rstd_block([qraw[:, e, c0:c0 + n] for e in range(2)], ["qraw%d.%d" % (e, bi) for e in range(2)],
                                         n, 256, EPS)
                for e in range(2):
                    self.stt(q16[:, e, c0:c0 + n], qraw[:, e, c0:c0 + n], gsc[:, e:e + 1], rs[:, :n], ALU.mult, ALU.mult,
                             ["qraw%d.%d" % (e, bi), rk, "gsc"], ["q16.%d.%d" % (e, bi)])
            for bi in range(4):
                c0, n = TB[bi]
                p_ = pT[it % 2]
                pk_ = "pT%d" % (it % 2)
                rd = rden[it % 2]
                rdk = "rden%d" % (it % 2)
                it += 1
                for mb in range(2):
                    ps, pk = self.newps()
                    for e in range(2):
                        self.mm(ps[:, :n], KTm[:, 2 * h + e, mb * 128:(mb + 1) * 128], q16[:, e, c0:c0 + n], e == 0, e == 1,
                                ["KTm", "q16.%d.%d" % (e, bi)], [pk])
                    self.act(p_[:, mb, :n], ps[:, :n], AF.Exp, [pk], [pk_ + ".%d" % mb])
                psd, pkd = self.newps()
                for mb in range(2):
                    self.mm(psd[:, :n], ones16, p_[:, mb, :n], mb == 0, mb == 1, ["cst16", pk_ + ".%d" % mb], [pkd])
                self.recip(rd[:, :n], psd[:, :n], [pkd], [rdk])
                for e in range(2):
                    pso, pko = self.newps()
                    for mb in range(2):
                        self.mm(pso[:, :n], Vm[:, mb, h * 256 + e * 128:h * 256 + (e + 1) * 128], p_[:, mb, :n], mb == 0, mb == 1,
                                ["Vm", pk_ + ".%d" % mb], [pko])
                    self.tt(oT[:, e, c0:c0 + n], pso[:, :n], rd[:, :n], ALU.mult, [pko, rdk], ["oT%d.%d" % (e, bi)])
            for s in range(4):
                ck = ckv[s % 2]
                ckk = "ckv%d" % (s % 2)
                kt = kTs[s % 2]
                ktk = "kTs%d" % (s % 2)
                v16 = vs16[s % 2]
                vk = "vs16%d" % (s % 2)
                pts = pTs[s % 2]
                ptk = "pTs%d" % (s % 2)
                for mb in range(2):
                    self.dma(ck[:, mb, :, :], self.dr["cmem"][l, s, mb * 128:(mb + 1) * 128, :, h, :], (), [ckk])
                ps, pk = self.newps()
                for e in range(2):
                    for mb in range(2):
                        q = e * 2 + mb
                        self.tr(ps[:, q * 128:(q + 1) * 128], ck[:, mb, 0, e * 128:(e + 1) * 128], ident, [ckk, "cst"], [pk])
                self.cp("act", kt, ps[:, :].rearrange("p (q t) -> p q t", t=128), [pk], [ktk])
                self.cp("dve", v16, ck[:, :, 1, :], [ckk], [vk])
                q0 = 2048 + 8 * s
                ps, pk = self.newps()
                for mb in range(2):
                    for e in range(2):
                        self.mm(ps[:, mb * 8:mb * 8 + 8], kt[:, e * 2 + mb, :], q16[:, e, q0:q0 + 8], e == 0, e == 1,
                                [ktk, "q16.%d.4" % e], [pk])
                self.act(pts[:, 0:16], ps[:, 0:16], AF.Exp, [pk], [ptk])
                pso, pko = self.newps()
                for e in range(2):
                    for mb in range(2):
                        self.mm(pso[:, e * 8:e * 8 + 8], v16[:, mb, e * 128:(e + 1) * 128], pts[:, mb * 8:mb * 8 + 8], mb == 0, mb == 1,
                                [vk, ptk], [pko])
                for mb in range(2):
                    self.mm(pso[:, 16:24], ones16, pts[:, mb * 8:mb * 8 + 8], mb == 0, mb == 1, ["cst16", ptk], [pko])
                rd, rdk = self.scr()
                self.recip(rd[:, 0:8], pso[:, 16:24], [pko], [rdk])
                for e in range(2):
                    self.tt(oT[:, e, q0:q0 + 8], pso[:, e * 8:e * 8 + 8], rd[:, 0:8], ALU.mult, [pko, rdk],
                            ["oT%d.4" % e])
            wo = self.dr["w_xo"][l]
            for nb in range(8):
                self.linear(wo[nb][:, 2 * h:2 * h + 2, :], 2, lambda kc, c0, n: oT[:, kc, c0:c0 + n],
                            lambda kc, bi: ["oT%d.%d" % (kc, bi)], self.add_resid(nb))
        self.release(m0)

    def attn_unit_a(self, q_ap, qkeys, nq, blocks, acc_cols, bufs):
        pT, ptk = bufs
        ps, pk = self.newps()
        off = 0
        offs = []
        for (kT, vt, mk, nk, keys) in blocks:
            self.mm(ps[:nk, off:off + nq], kT, q_ap, True, True, keys + qkeys, [pk])
            offs.append(off)
            off += nq
        if len(blocks) == 2 and blocks[0][3] == 128 and blocks[1][3] == 128 and nq == 128:
            self.act(pT[:, 0:256], ps[:, 0:256], AF.Exp, [pk], [ptk])
            self.tt(pT[:, 0:256], pT[:, 0:256], self.mboth, ALU.mult, [ptk, "mboth"], [ptk])
        else:
            for bi_, (kT, vt, mk, nk, keys) in enumerate(blocks):
                o_ = offs[bi_]
                self.act(pT[:nk, o_:o_ + nq], ps[:nk, o_:o_ + nq], AF.Exp, [pk], [ptk])
                self.tt(pT[:nk, o_:o_ + nq], pT[:nk, o_:o_ + nq], mk, ALU.mult, [ptk, "cst16"], [ptk])
        return (nq, blocks, offs, pT, ptk, acc_cols)

    def attn_unit_b(self, ctx):
        nq, blocks, offs, pT, ptk, acc_cols = ctx
        pso, pko = self.newps()
        nb_ = len(blocks)
        for bi_, (kT, vt, mk, nk, keys) in enumerate(blocks):
            o_ = offs[bi_]
            self.mm(pso[:, 0:nq], vt, pT[:nk, o_:o_ + nq], bi_ == 0, bi_ == nb_ - 1, keys + [ptk], [pko])
        ones16 = self.cst16("ones")
        for bi_, (kT, vt, mk, nk, keys) in enumerate(blocks):
            o_ = offs[bi_]
            self.mm(pso[:, 128:128 + nq], ones16[:nk, :], pT[:nk, o_:o_ + nq], bi_ == 0, bi_ == nb_ - 1, ["cst16", ptk], [pko])
        src = pso[:, 0:256].rearrange("p (a b) -> p a b", b=128)[:, :, 0:nq]
        self.tt(acc_cols, src, acc_cols, ALU.add, [pko, "acc"], ["acc"])

    def attn(self, l, ja):
        self.dnorm("norm_mix%d" % l)
        m0 = self.mark()
        raw = [self.alloc(NT) for _ in range(2)]
        QT = self.alloc16(NT)
        KT = self.alloc16(NT)
        tok32 = [self.alloc(4 * 128).rearrange("p (b e) -> p b e", e=128) for _ in range(4)]
        vtok = self.alloc16(16 * 128).rearrange("p (b e) -> p b e", e=128)
        acc = self.alloc(2 * NT).rearrange("p (a t) -> p a t", t=NT)
        oT = self.alloc16(NT)
        NPT = 4
        DEPTH = 2
        pTb = [self.alloc16(256) for _ in range(NPT)]
        pend = []

        def unit(q_ap, qkeys, nq, blocks, acc_cols, ui):
            pend.append(self.attn_unit_a(q_ap, qkeys, nq, blocks, acc_cols, (pTb[ui % NPT], "pTb%d" % (ui % NPT))))
            if len(pend) > DEPTH:
                self.attn_unit_b(pend.pop(0))

        def flush():
            while pend:
                self.attn_unit_b(pend.pop(0))
        self.mboth = self.alloc16(256)
        cb_ = self.alloc(8 * 2 * 128).rearrange("p (r t e) -> p r t e", t=2, e=128)
        cv_ = self.alloc16(8 * 128).rearrange("p (r e) -> p r e", e=128)
        kcT = [self.alloc16(128) for _ in range(2)]
        sstg = self.alloc(2 * 128).rearrange("p (t e) -> p t e", e=128)
        vs16 = self.alloc16(128)
        gq = self.alloc(3)
        self.cp("dve", self.mboth[:, 0:128], self.cst16("m_own"), ["cst16"], ["mboth"])
        self.cp("dve", self.mboth[:, 128:256], self.cst16("m_prev"), ["cst16"], ["mboth"])
        for g in range(3):
            self.ts(gq[:, g:g + 1], self.pvc("aq_gain%d_%d" % (ja, g)), 128 ** -0.5, None, ALU.mult, None, ["pv"], ["gq"])
        ident = self.cst("ident")
        wqkv = self.dr["w_qkv"][ja]
        wo = self.dr["w_ao"][ja]
        caches = (self.dr["c128"], self.dr["c512"], self.dr["c2048"])
        outs_p = (self.dr["kv128_p"], self.dr["kv512_p"], self.dr["kv2048_p"])
        outs_s = (self.dr["kv128_s"], self.dr["kv512_s"], self.dr["kv2048_s"])
        for g in range(3):
            L = WINS[g]
            for s in range(4):
                if STAGES["a_d2d"]:
                    self.dma(outs_s[g][ja, s, 0:L - 8], caches[g][ja, s, 8:L], (), ())
        ui = 0
        ti = 0
        for h in range(4):
            self.memset(acc[:, :, :], 0.0, ["acc"])
            for g in range(STAGES["a_groups"]):
                dil = DILS[g]
                L = WINS[g]
                nkb = (2048 // dil) // 128

                def proj(si, dst, dkey):
                    blk = (si * 3 + g) * 4 + h

                    def cons_p(bi, c0, n, ps, pk):
                        self.cp("act", dst[:, c0:c0 + n], ps[:, :n], [pk], ["%s.%d" % (dkey, bi)])
                    self.linear(wqkv[blk], 8, self.h_rhs, self.h_keys, cons_p)
                proj(0, raw[0], "raw0")
                proj(1, raw[1], "raw1")
                for bi, (c0, n) in enumerate(TB):
                    rs, rk = self.rstd_block([raw[0][:, c0:c0 + n]], ["raw0.%d" % bi], n, 128, EPS)
                    self.stt(QT[:, c0:c0 + n], raw[0][:, c0:c0 + n], gq[:, g:g + 1], rs[:, :n], ALU.mult, ALU.mult,
                             ["raw0.%d" % bi, rk, "gq"], ["QT.%d" % bi])
                    rs, rk = self.rstd_block([raw[1][:, c0:c0 + n]], ["raw1.%d" % bi], n, 128, EPS)
                    self.stt(raw[1][:, c0:c0 + n], raw[1][:, c0:c0 + n], self.pvc("ak_gain%d_%d" % (ja, g)), rs[:, :n],
                             ALU.mult, ALU.mult, ["raw1.%d" % bi, rk, "pv"], ["raw1.%d" % bi])
                    self.cp("act", KT[:, c0:c0 + n], raw[1][:, c0:c0 + n], ["raw1.%d" % bi], ["KT.%d" % bi])
                proj(2, raw[0], "raw0")
                allq = ["QT.%d" % b for b in range(5)]
                allk = ["KT.%d" % b for b in range(5)]
                o_ = outs_p[g][ja]
                for si, rsrc, rkn in ((1, raw[1], "raw1"), (2, raw[0], "raw0")):
                    rkeys = ["%s.%d" % (rkn, b) for b in range(4)]
                    for b4 in range(4):
                        tk = tok32[ti % 4]
                        tkk = "tok32_%d" % (ti % 4)
                        ti += 1
                        ps, pk = self.newps()
                        for q in range(4):
                            idx = b4 * 4 + q
                            r_, kb = idx // nkb, idx % nkb
                            st_ = r_ + dil * 128 * kb
                            self.tr(ps[:, q * 128:(q + 1) * 128], rsrc[:, st_:st_ + dil * 127 + 1:dil], ident, rkeys + ["cst"], [pk])
                        self.cp("act", tk[:, :, :], ps[:, :].rearrange("p (q e) -> p q e", e=128), [pk], [tkk])
                        if si == 2:
                            self.cp("dve", vtok[:, b4 * 4:b4 * 4 + 4, :], tk[:, :, :], [tkk], ["vtok"])
                        if not STAGES["a_kvout"]:
                            pass
                        elif g == 0:
                            if b4 == 3:
                                self.dma(o_[:, si - 1, h, :], tk[:, 3, :], [tkk], ())
                        elif g == 1:
                            dst = o_.rearrange("(j r) t h e -> j r t h e", r=dil)[:, b4, si - 1, h, :]
                            self.dma(dst, tk[:, 3, :], [tkk], ())
                        else:
                            dst = o_.rearrange("(j r) t h e -> j r t h e", r=dil)[:, b4 * 4:b4 * 4 + 4, si - 1, h, :]
                            self.dma(dst, tk[:, :, :], [tkk], ())
                    ps, pk = self.newps()
                    self.tr(ps[:32, 0:128], rsrc[:, 2048:2080], ident, ["%s.4" % rkn, "cst"], [pk])
                    self.cp("act", sstg[:32, si - 1, :], ps[:32, 0:128], [pk], ["sstg"])
                    if si == 2:
                        self.cp("dve", vs16[:32, :], sstg[:32, 1, :], ["sstg"], ["vs16"])
                for s in range(4):
                    if STAGES["a_sout"]:
                        self.dma(outs_s[g][ja, s, L - 8:L, :, h, :], sstg[8 * s:8 * s + 8, :, :], ["sstg"], ())
                for r_ in range(dil if STAGES["a_pu"] else 0):
                    for qb in range(nkb):
                        st_ = r_ + dil * 128 * qb
                        sl_ = slice(st_, st_ + dil * 127 + 1, dil)
                        blocks = [(KT[:, sl_], vtok[:, r_ * nkb + qb, :], self.cst16("m_own"), 128, allk + ["vtok"])]
                        if qb > 0:
                            sp_ = r_ + dil * 128 * (qb - 1)
                            blocks.append((KT[:, sp_:sp_ + dil * 127 + 1:dil], vtok[:, r_ * nkb + qb - 1, :], self.cst16("m_prev"),
                                           128, allk + ["vtok"]))
                        unit(QT[:, sl_], allq, 128, blocks, acc[:, :, sl_], ui)
                        ui += 1
                R = min(dil, 8)
                nq = 8 // R
                for s in range(4 if STAGES["a_su"] else 0):
                    flush()
                    src = caches[g][ja, s].rearrange("(j r) t h e -> j r t h e", r=dil)[:, 0:R, :, h, :]
                    self.dma(cb_[:, 0:R, :, :], src, (), ["cb"])
                    self.cp("dve", cv_[:, 0:R, :], cb_[:, 0:R, 1, :], ["cb"], ["cv"])
                    for r_ in range(R):
                        kc_ = kcT[ui % 2]
                        kck = "kcT%d" % (ui % 2)
                        ps, pk = self.newps()
                        self.tr(ps[:, 0:128], cb_[:, r_, 0, :], ident, ["cb", "cst"], [pk])
                        self.cp("act", kc_[:, :], ps[:, 0:128], [pk], [kck])
                        q0 = 2048 + 8 * s + r_
                        sl_ = slice(q0, q0 + R * (nq - 1) + 1, R)
                        oc = own_s_col(self.cca, g, s, r_) - self.cca.cols["own_s"]
                        blocks = [(kc_[:, :], cv_[:, r_, :], self.cst16("m_prev", 128, nq), 128, [kck, "cv"]),
                                  (KT[:, 2048:2080], vs16[:32, :], self.cst16("own_s", 32, nq, oc), 32, allk + ["vs16"])]
                        unit(QT[:, sl_], allq, nq, blocks, acc[:, :, sl_], ui)
                        ui += 1
                flush()
            self.recip(acc[:, 1, :], acc[:, 1, :], ["acc"], ["acc"])
            for bi, (c0, n) in enumerate(TB):
                self.tt(oT[:, c0:c0 + n], acc[:, 0, c0:c0 + n], acc[:, 1, c0:c0 + n], ALU.mult, ["acc"], ["ao.%d" % bi])
            for nb in range(8):
                self.linear(wo[nb][:, h:h + 1, :], 1, lambda kc, c0, n: oT[:, c0:c0 + n], lambda kc, bi: ["ao.%d" % bi],
                            self.add_resid(nb))
        self.release(m0)

    def hgrn(self, l):
        self.dnorm("norm_mix%d" % l)
        m0 = self.mark()
        Bq, Bz, Bk, Bm, Bd, Bv, Bb = [self.alloc(NT) for _ in range(7)]
        Sb = [self.alloc(128) for _ in range(2)]
        NLA = 4
        ktok = [self.alloc(128) for _ in range(NLA)]
        vtk = [self.alloc(128) for _ in range(NLA)]
        ATb = [self.alloc(64) for _ in range(NLA)]
        kvb_ = [self.alloc(128) for _ in range(NLA)]
        ebC = self.alloc(36)
        ex = self.alloc(32).rearrange("p (l c) -> p l c", c=8)
        lbv = self.alloc(8)
        omlb = self.alloc(8)
        tot = self.alloc(8)
        oTh = self.alloc16(NT)

        def K(n):
            return ["%s.%d" % (n, b) for b in range(5)]
        c_lb = self.pvca.cols["hg_lb0"]
        self.act(ex[:, :, :], self.pv[:, c_lb:c_lb + 32].rearrange("p (l c) -> p l c", c=8), AF.Exp, ["pv"], ["hex"])
        self.tt(tot, ex[:, 0, :], ex[:, 1, :], ALU.add, ["hex"], ["htot"])
        self.tt(tot, tot, ex[:, 2, :], ALU.add, ["hex", "htot"], ["htot"])
        self.tt(tot, tot, ex[:, 3, :], ALU.add, ["hex", "htot"], ["htot"])
        self.cp("dve", lbv, ex[:, 1, :], ["hex"], ["hlb"])
        for l2 in range(2, l + 1):
            self.tt(lbv, lbv, ex[:, l2, :], ALU.add, ["hex", "hlb"], ["hlb"])
        self.recip(tot, tot, ["htot"], ["htot"])
        self.tt(lbv, lbv, tot, ALU.mult, ["hlb", "htot"], ["hlb"])
        self.ts(omlb, lbv, -1.0, 1.0, ALU.mult, ALU.add, ["hlb"], ["homlb"])
        ident = self.cst("ident")
        m_own = self.cst("m_own")
        win = self.dr["w_hgin"]
        wo = self.dr["w_hgo"]
        ci = 0
        si = 0
        for h in range(8):
            def proj(blk, dst, dkey):
                def cons_p(bi, c0, n, ps, pk):
                    self.cp("act", dst[:, c0:c0 + n], ps[:, :n], [pk], ["%s.%d" % (dkey, bi)])
                self.linear(win[blk], 8, self.h_rhs, self.h_keys, cons_p)
            proj(h, Bq, "hq")
            proj(8 + h, Bz, "hz")
            proj(16 + h, Bv, "hv")
            self.act(Bz[:, :], Bz[:, :], AF.Sigmoid, K("hz"), K("hz"))
            self.ts(Bz[:, :], Bz[:, :], omlb[:, h:h + 1], lbv[:, h:h + 1], ALU.mult, ALU.add, K("hz") + ["hlb", "homlb"], K("hz"))
            self.ts(Bk[:, :], Bz[:, :], -1.0, 1.0, ALU.mult, ALU.add, K("hz"), K("hk"))
            self.act(Bz[:, :], Bz[:, :], AF.Ln, K("hz"), K("hz"))
            self.memset(Bm[:, :], 1.0, K("hm"))
            self.memset(Bm[:, 0:2048:64], 0.0, K("hm"))
            self.memset(Bm[:, 2048:2080:8], 0.0, K("hm"))
            self.P.add("dve", lambda e: e.tensor_tensor_scan(Bb[:, :], Bm[:, :], Bz[:, :], 0.0, ALU.mult, ALU.add),
                       K("hm") + K("hz"), K("hb"))
            self.act(ebC[:, 0:32], Bb[:, 63:2048:64], AF.Exp, K("hb"), ["hebc"])
            self.act(ebC[:, 32:36], Bb[:, 2055:2080:8], AF.Exp, K("hb"), ["hebc"])
            self.act(Bq[:, :], Bq[:, :], AF.Silu, K("hq"), K("hq"))
            self.act(Bz[:, :], Bb[:, :], AF.Exp, K("hb") + K("hz"), K("hz"))
            self.tt(Bq[:, :], Bq[:, :], Bz[:, :], ALU.mult, K("hq") + K("hz"), K("hq"))
            self.act(Bm[:, :], Bb[:, :], AF.Exp, K("hb") + K("hm"), K("hm"), scale=-1.0)
            self.tt(Bm[:, :], Bm[:, :], Bk[:, :], ALU.mult, K("hm") + K("hk"), K("hm"))
            bp = Bb[:, 0:2048].rearrange("p (n c) -> p n c", c=64)
            self.tt(Bd[:, 0:2048].rearrange("p (n c) -> p n c", c=64), bp[:, :, 63:64].to_broadcast([128, 32, 64]), bp, ALU.subtract,
                    K("hb"), K("hd"))
            bs = Bb[:, 2048:2080].rearrange("p (n c) -> p n c", c=8)
            self.tt(Bd[:, 2048:2080].rearrange("p (n c) -> p n c", c=8), bs[:, :, 7:8].to_broadcast([128, 4, 8]), bs, ALU.subtract,
                    K("hb"), K("hd"))
            self.act(Bd[:, :], Bd[:, :], AF.Exp, K("hd"), K("hd"))
            self.tt(Bd[:, :], Bd[:, :], Bk[:, :], ALU.mult, K("hd") + K("hk"), K("hd"))

            def chunk_a(c0, C, bi):
                nonlocal ci
                kt, ktk = ktok[ci % NLA], "hkt%d" % (ci % NLA)
                vt, vtkk = vtk[ci % NLA], "hvt%d" % (ci % NLA)
                AT, atk = ATb[ci % NLA], "hat%d" % (ci % NLA)
                ci += 1
                ps, pk = self.newps()
                self.tr(ps[:C, 0:128], Bd[:, c0:c0 + C], ident, ["hd.%d" % bi, "cst"], [pk])
                self.cp("act", kt[:C, :], ps[:C, 0:128], [pk], [ktk])
                psb, pkb = self.newps()
                self.tr(psb[:C, 0:128], Bv[:, c0:c0 + C], ident, ["hv.%d" % bi, "cst"], [pkb])
                self.cp("act", vt[:C, :], psb[:C, 0:128], [pkb], [vtkk])
                ps2, pk2 = self.newps()
                self.mm(ps2[:C, :C], Bm[:, c0:c0 + C], Bq[:, c0:c0 + C], True, True, ["hm.%d" % bi, "hq.%d" % bi], [pk2])
                self.tt(AT[:C, :C], ps2[:C, :C], m_own[:C, :C], ALU.mult, [pk2, "cst"], [atk])
                ps4, pk4 = self.newps()
                self.mm(ps4[:, 0:128], kt[:C, :], vt[:C, :], True, True, [ktk, vtkk], [pk4])
                kvb, kvk = kvb_[(ci - 1) % NLA], "hkv%d" % ((ci - 1) % NLA)
                self.cp("act", kvb[:, :], ps4[:, 0:128], [pk4], [kvk])
                return (c0, C, bi, vt, vtkk, AT, atk, kvb, kvk)

            def chunk_b(ctx, S, Sk, ecol):
                c0, C, bi, vt, vtkk, AT, atk, kvb, kvk = ctx
                ps3, pk3 = self.newps()
                self.mm(ps3[:, :C], S[:, :], Bq[:, c0:c0 + C], True, False, [Sk, "hq.%d" % bi], [pk3])
                self.mm(ps3[:, :C], vt[:C, :], AT[:C, :C], False, True, [vtkk, atk], [pk3])
                self.cp("act", Bk[:, c0:c0 + C], ps3[:, :C], [pk3], ["hk.%d" % bi])
                self.stt(S[:, :], S[:, :], ebC[:, ecol:ecol + 1], kvb[:, :], ALU.mult, ALU.add, [Sk, "hebc", kvk], [Sk])
            S, Sk = Sb[si % 2], "hS%d" % (si % 2)
            si += 1
            self.memset(S[:, :], 0.0, [Sk])
            LOOK = 2
            ctxs = {}
            for n_ in range(32 + LOOK):
                if n_ < 32:
                    ctxs[n_] = chunk_a(64 * n_, 64, n_ // 8)
                if n_ >= LOOK:
                    chunk_b(ctxs.pop(n_ - LOOK), S, Sk, n_ - LOOK)
            self.dma(self.dr["hg_p"][h], S[:, :], [Sk], ())
            sctx = [chunk_a(2048 + 8 * s, 8, 4) for s in range(2)]
            for s in range(4):
                S, Sk = Sb[si % 2], "hS%d" % (si % 2)
                si += 1
                self.dma(S[:, :], self.dr["st_hg"][s, h], (), [Sk])
                chunk_b(sctx.pop(0), S, Sk, 32 + s)
                if s + 2 < 4:
                    sctx.append(chunk_a(2048 + 8 * (s + 2), 8, 4))
                self.dma(self.dr["hg_s"][s, h], S[:, :], [Sk], ())
            proj(24 + h, Bq, "hq")
            self.act(Bq[:, :], Bq[:, :], AF.Silu, K("hq"), K("hq"))
            for bi, (c0, n) in enumerate(TB):
                rs, rk = self.rstd_block([Bk[:, c0:c0 + n]], ["hk.%d" % bi], n, 128, EPS)
                self.stt(Bk[:, c0:c0 + n], Bk[:, c0:c0 + n], self.pvc("hg_out_gain"), rs[:, :n], ALU.mult, ALU.mult,
                         ["hk.%d" % bi, rk, "pv"], ["hk.%d" % bi])
                self.tt(oTh[:, c0:c0 + n], Bk[:, c0:c0 + n], Bq[:, c0:c0 + n], ALU.mult, ["hk.%d" % bi, "hq.%d" % bi], ["hoT.%d" % bi])
            for nb in range(8):
                self.linear(wo[nb][:, h:h + 1, :], 1, lambda kc, c0, n: oTh[:, c0:c0 + n], lambda kc, bi: ["hoT.%d" % bi],
                            self.add_resid(nb))
        self.release(m0)

    def rwkv(self, l):
        self.dnorm("norm_mix%d" % l)
        m0 = self.mark()
        NCH = 4
        ident = self.cst("ident")
        blk64 = self.cst("blk64")
        pvc = self.pvc

        def hk2(c0, n, kc):
            b0 = min(max(c0 - 1, 0) // 512, 4)
            b1 = min((c0 + n - 1) // 512, 4)
            return ["h%d.%d" % (kc, b) for b in sorted({b0, b1})]
        xl = self.alloc(40).rearrange("p (c t) -> p c t", t=5)
        self.cp("dve", xl[:, :, 0:1], self.xT[:, :, 2047:2048], ["x%d.3" % c for c in range(8)], ["xl"])
        self.cp("dve", xl[:, :, 1:5], self.xT[:, :, 2055:2080:8], ["x%d.4" % c for c in range(8)], ["xl"])
        rs, rk = self.rstd_block([xl[:, c, :] for c in range(8)], ["xl"] * 8, 5, D, EPS)
        for c in range(8):
            self.stt(xl[:, c, :], xl[:, c, :], pvc("norm_mix%d" % l, c), rs[:, 0:5], ALU.mult, ALU.mult, ["xl", rk, "pv"], ["xl"])
        self.dma(self.dr["sh_o"], xl, ["xl"], ())
        hsp = self.alloc16(8 * 32).rearrange("p (c t) -> p c t", t=32)
        shs = self.alloc(32).rearrange("p (c s) -> p c s", s=4)
        self.dma(shs, self.dr["st_sh"], (), ["shs"])
        self.cp("dve", hsp[:, :, 0:32:8], shs, ["shs"], ["hsp"])
        for c in range(8):
            self.cp("dve", hsp[:, c, :].rearrange("p (s i) -> p s i", i=8)[:, :, 1:8],
                    self.hT[:, c, 2048:2080].rearrange("p (s i) -> p s i", i=8)[:, :, 0:7], ["h%d.4" % c], ["hsp"])
        omu = self.alloc(48)
        cmu = self.pvca.cols["rw_mu0"]
        mu = self.pv[:, cmu:cmu + 48]
        self.ts(omu, mu, -1.0, 1.0, ALU.mult, ALU.add, ["pv"], ["omu"])
        omka = self.alloc(8)
        self.ts(omka, pvc("rw_k_a", 0, 8), -1.0, 1.0, ALU.mult, ALU.add, ["pv"], ["omka"])
        wst = self.alloc(8 * 160)

        def proj(ps, pk, m, wA, wB, wkeys, c0, n, sample):
            for kc in range(8):
                self.mm(ps[:m, :n], wA(kc), self.hT[:, kc, c0:c0 + n], kc == 0, False, wkeys + hk2(c0, n, kc), [pk])
            for kc in range(8):
                last = kc == 7
                if sample:
                    self.mm(ps[:m, :n], wB(kc), hsp[:, kc, :], False, last, wkeys + ["hsp"], [pk])
                elif c0 == 0:
                    self.mm(ps[:m, 1:n], wB(kc), self.hT[:, kc, 0:n - 1], False, last, wkeys + hk2(c0, n, kc), [pk])
                else:
                    self.mm(ps[:m, :n], wB(kc), self.hT[:, kc, c0 - 1:c0 - 1 + n], False, last, wkeys + hk2(c0, n, kc), [pk])

        def scaled(dstA, dstB, src, ncol, j, keyA, keyB, skey):
            mj = mu[:, 8 * j:8 * j + 8].unsqueeze(2).to_broadcast([128, 8, ncol])
            oj = omu[:, 8 * j:8 * j + 8].unsqueeze(2).to_broadcast([128, 8, ncol])
            self.tt(dstA, src, oj, ALU.mult, [skey, "omu"], [keyA])
            self.tt(dstB, src, mj, ALU.mult, [skey, "pv"], [keyB])
        TWA = self.alloc16(NT)
        TG1 = self.alloc16(NT)
        TG2 = self.alloc16(NT)
        l1A = self.alloc16(8 * 160).rearrange("p (k n) -> p k n", n=160)
        l1B = self.alloc16(8 * 160).rearrange("p (k n) -> p k n", n=160)
        w64 = wst[:, 0:8 * 64].rearrange("p (k n) -> p k n", n=64)
        for (nm, j, lo) in (("rw1", 3, 0), ("ra1", 4, 64)):
            self.dma(w64, self.dr[nm], (), ["wst"])
            scaled(l1A[:, :, lo:lo + 64], l1B[:, :, lo:lo + 64], w64, 64, j, "l1A", "l1B", "wst")
        for bi, (c0, n) in enumerate(TB):
            ps, pk = self.newps()
            proj(ps, pk, 128, lambda kc: l1A[:, kc, 0:128], lambda kc: l1B[:, kc, 0:128], ["l1A", "l1B"], c0, n, bi == 4)
            self.act(TWA[0:64, c0:c0 + n], ps[0:64, :n], AF.Tanh, [pk], ["twa.%d" % bi])
            self.cp("act", TWA[64:128, c0:c0 + n], ps[64:128, :n], [pk], ["twa.%d" % bi])
        w160 = wst[:, 0:8 * 160].rearrange("p (k n) -> p k n", n=160)
        self.dma(w160, self.dr["rg1"], (), ["wst"])
        scaled(l1A[:, :, :], l1B[:, :, :], w160, 160, 5, "l1A", "l1B", "wst")
        for bi, (c0, n) in enumerate(TB):
            ps, pk = self.newps()
            proj(ps, pk, 128, lambda kc: l1A[:, kc, 0:128], lambda kc: l1B[:, kc, 0:128], ["l1A", "l1B"], c0, n, bi == 4)
            self.act(TG1[:, c0:c0 + n], ps[:, :n], AF.Sigmoid, [pk], ["tg.%d" % bi])
            ps, pk = self.newps()
            proj(ps, pk, 32, lambda kc: l1A[:, kc, 128:160], lambda kc: l1B[:, kc, 128:160], ["l1A", "l1B"], c0, n, bi == 4)
            self.act(TG2[0:32, c0:c0 + n], ps[0:32, :n], AF.Sigmoid, [pk], ["tg.%d" % bi])
        W2A = self.alloc16(128)
        G2a = self.alloc16(128)
        G2b = self.alloc16(128)
        yT = self.alloc16(NT)
        NB_ = 256
        Rr, Rk, Rv, Rld, Ra, Rg, Rkk, Rk2, Rcum, Rt1, Rt2, Rbon, Ry = [self.alloc(NB_) for _ in range(13)]
        blk = [self.alloc(NCH * 128).rearrange("p (j t) -> p j t", t=128) for _ in range(5)]
        Ablk, Bblk, Kblk, Rblk, Vblk = blk
        for b_ in blk:
            self.memset(b_[:, :, :], 0.0, ["blk"])
        WS = [[self.alloc(128) for _ in range(9)] for _ in range(NCH)]
        STb = [self.alloc(128) for _ in range(2)]
        DC = self.alloc(NCH)
        maskp = self.alloc(NB_)
        masks = self.alloc(32)
        self.memset(maskp, 1.0, ["maskp"])
        self.memset(maskp[:, 0:NB_:64], 0.0, ["maskp"])
        self.memset(masks, 1.0, ["masks"])
        self.memset(masks[:, 0:32:8], 0.0, ["masks"])
        bd_su, bd_sl, bd_iu = self.cst("bd_su"), self.cst("bd_sl"), self.cst("bd_iu")
        RB = [(NB_ * i, NB_, False) for i in range(2048 // NB_)] + [(2048, 32, True)]
        sti = 0
        wo = self.dr["w_rwo"]
        for p in range(8):
            w128 = wst[:, 0:1024].rearrange("p (k n) -> p k n", n=128)
            for j in range(3):
                self.dma(w128, self.dr["w_rkv"][j, p], (), ["wst"])
                scaled(self.wb[2 * j][:, :, :], self.wb[2 * j + 1][:, :, :], w128, 128, j, "wb%d" % (2 * j), "wb%d" % (2 * j + 1), "wst")
            self.dma(W2A[0:64, :], self.dr["rw2"][:, p * 128:(p + 1) * 128], (), ["w2a"], q="pool")
            self.dma(W2A[64:128, :], self.dr["ra2"][:, p * 128:(p + 1) * 128], (), ["w2a"], q="pool")
            self.dma(G2a[:, :], self.dr["rg2"][0:128, p * 128:(p + 1) * 128], (), ["g2"], q="pool")
            self.dma(G2b[0:32, :], self.dr["rg2"][128:160, p * 128:(p + 1) * 128], (), ["g2"], q="pool")
            ST, STk = STb[sti % 2], "rST%d" % (sti % 2)
            sti += 1
            self.memset(ST[:, :], 0.0, [STk])
            for (c0, n, smp) in RB:
                bi = min(c0 // 512, 4)
                C = 8 if smp else 64
                nch = 4
                for j, dst, dk in ((0, Rr, "Rr"), (1, Rk, "Rk"), (2, Rv, "Rv")):
                    ps, pk = self.newps()
                    proj(ps, pk, 128, lambda kc, j=j: self.wb[2 * j][:, kc, :], lambda kc, j=j: self.wb[2 * j + 1][:, kc, :],
                         ["wb%d" % (2 * j), "wb%d" % (2 * j + 1)], c0, n, smp)
                    self.cp("act", dst[:, :n], ps[:, :n], [pk], [dk])
                ps, pk = self.newps()
                self.mm(ps[:, :n], W2A[0:64, :], TWA[0:64, c0:c0 + n], True, True, ["w2a", "twa.%d" % bi], [pk])
                self.act(Rld[:, :n], ps[:, :n], AF.Sigmoid, [pk, "pv"], ["Rld"], bias=pvc("rw_w0", p))
                self.ts(Rld[:, :n], Rld[:, :n], -0.6065306597126334, None, ALU.mult, None, ["Rld"], ["Rld"])
                ps, pk = self.newps()
                self.mm(ps[:, :n], W2A[64:128, :], TWA[64:128, c0:c0 + n], True, True, ["w2a", "twa.%d" % bi], [pk])
                self.act(Ra[:, :n], ps[:, :n], AF.Sigmoid, [pk, "pv"], ["Ra"], bias=pvc("rw_a0", p))
                ps, pk = self.newps()
                self.mm(ps[:, :n], G2a[:, :], TG1[:, c0:c0 + n], True, False, ["g2", "tg.%d" % bi], [pk])
                self.mm(ps[:, :n], G2b[0:32, :], TG2[0:32, c0:c0 + n], False, True, ["g2", "tg.%d" % bi], [pk])
                self.cp("act", Rg[:, :n], ps[:, :n], [pk], ["Rg"])
                self.ts(Rkk[:, :n], Rk[:, :n], pvc("rw_k_k", p), None, ALU.mult, None, ["Rk", "pv"], ["Rkk"])
                self.act(Rt1[:, :n], Rkk[:, :n], AF.Square, ["Rkk"], ["Rt1"])
                ps, pk = self.newps()
                self.mm(ps[:, :n], blk64, Rt1[:, :n], True, True, ["cst", "Rt1"], [pk])
                self.act(Rt2[:, :n], ps[:, :n], AF.Sqrt, [pk], ["Rt2"])
                self.ts(Rt2[:, :n], Rt2[:, :n], 1e-12, None, ALU.max, None, ["Rt2"], ["Rt2"])
                self.recip(Rt2[:, :n], Rt2[:, :n], ["Rt2"], ["Rt2"])
                self.tt(Rkk[:, :n], Rkk[:, :n], Rt2[:, :n], ALU.mult, ["Rkk", "Rt2"], ["Rkk"])
                self.ts(Rt1[:, :n], Ra[:, :n], pvc("rw_k_a", p), omka[:, p:p + 1], ALU.mult, ALU.add, ["Ra", "pv", "omka", "Rt1"], ["Rt1"])
                self.tt(Rk2[:, :n], Rk[:, :n], Rt1[:, :n], ALU.mult, ["Rk", "Rt1"], ["Rk2"])
                self.tt(Rt1[:, :n], Rr[:, :n], Rk2[:, :n], ALU.mult, ["Rr", "Rk2", "Rt1"], ["Rt1"])
                self.ts(Rt1[:, :n], Rt1[:, :n], pvc("rw_r_k", p), None, ALU.mult, None, ["Rt1", "pv"], ["Rt1"])
                ps, pk = self.newps()
                self.mm(ps[:, :n], blk64, Rt1[:, :n], True, True, ["cst", "Rt1"], [pk])
                self.tt(Rbon[:, :n], ps[:, :n], Rv[:, :n], ALU.mult, [pk, "Rv"], ["Rbon"])
                mk_ = masks if smp else maskp
                mkk = "masks" if smp else "maskp"
                self.P.add("dve", lambda e, n=n, mk_=mk_: e.tensor_tensor_scan(Rcum[:, :n], mk_[:, :n], Rld[:, :n], 0.0, ALU.mult, ALU.add),
                           [mkk, "Rld"], ["Rcum"])
                self.act(DC[:, 0:nch], Rcum[:, C - 1:n:C], AF.Exp, ["Rcum"], ["DC"])
                if smp:
                    for b_ in blk:
                        self.memset(b_[:, :, :], 0.0, ["blk"])

                def toblk(dst, fn, rkeys):
                    for hh in range(2):
                        rows = slice(64 * hh, 64 * hh + 64)
                        ov = dst[rows, 0:nch, 64 * hh:64 * hh + C]
                        fn(ov, rows, lambda x: x[rows, 0:n].rearrange("p (j s) -> p j s", s=C))
                self.act(Rt1[:, :n], Rcum[:, :n], AF.Exp, ["Rcum", "Rt1"], ["Rt1"])
                toblk(Rblk, lambda ov, rows, V: self.tt(ov, V(Rr), V(Rt1), ALU.mult, ["Rr", "Rt1"], ["blk"]), None)
                self.act(Rt1[:, :n], Rcum[:, :n], AF.Exp, ["Rcum", "Rt1", "blk"], ["Rt1"], scale=-1.0)
                self.tt(Rt2[:, :n], Rkk[:, :n], Ra[:, :n], ALU.mult, ["Rkk", "Ra", "Rt2"], ["Rt2"])
                toblk(Bblk, lambda ov, rows, V: self.tt(ov, V(Rt2), V(Rt1), ALU.mult, ["Rt2", "Rt1"], ["blk"]), None)
                toblk(Kblk, lambda ov, rows, V: self.tt(ov, V(Rk2), V(Rt1), ALU.mult, ["Rk2", "Rt1"], ["blk"]), None)
                self.tt(Rt2[:, :n], Rcum[:, :n], Rld[:, :n], ALU.subtract, ["Rcum", "Rld", "Rt2", "blk"], ["Rt2"])
                self.act(Rt2[:, :n], Rt2[:, :n], AF.Exp, ["Rt2"], ["Rt2"])
                toblk(Ablk, lambda ov, rows, V: self.stt(ov, V(Rkk), -1.0, V(Rt2), ALU.mult, ALU.mult, ["Rkk", "Rt2"], ["blk"]), None)
                toblk(Vblk, lambda ov, rows, V: self.cp("act", ov, V(Rv), ["Rv"], ["blk"]), None)
                M_ = [WS[j][0] for j in range(nch)]
                MT_ = [WS[j][1] for j in range(nch)]
                M2_ = [WS[j][2] for j in range(nch)]
                M2T_ = [WS[j][3] for j in range(nch)]
                P_ = [WS[j][4] for j in range(nch)]

                def wk(j, i):
                    return "rws%d.%d" % (j, i)
                for j in range(nch):
                    ps, pk = self.newps()
                    self.mm(ps[:, 0:128], Bblk[:, j, :], Ablk[:, j, :], True, True, ["blk"], [pk])
                    self.mm(ps[:, 128:256], Ablk[:, j, :], Bblk[:, j, :], True, True, ["blk"], [pk])
                    self.tt(M_[j][:, :], ps[:, 0:128], bd_su, ALU.mult, [pk, "cst"], [wk(j, 0)])
                    self.tt(MT_[j][:, :], ps[:, 128:256], bd_sl, ALU.mult, [pk, "cst"], [wk(j, 1)])
                    self.tt(P_[j][:, :], M_[j][:, :], ident, ALU.add, [wk(j, 0), "cst"], [wk(j, 4)], eng="pool")
                cur = (M_, MT_, 0, 1)
                nxt = (M2_, M2T_, 2, 3)
                for step in range(5):
                    A_, AT_, ia, iat = cur
                    N_, NT_, in_, int_ = nxt
                    pss = []
                    for j in range(nch):
                        ps, pk = self.newps()
                        pss.append((ps, pk))
                        self.mm(ps[:, 128:256], A_[j][:, :], AT_[j][:, :], True, True, [wk(j, ia), wk(j, iat)], [pk])
                        if step < 4:
                            self.mm(ps[:, 0:128], AT_[j][:, :], A_[j][:, :], True, True, [wk(j, ia), wk(j, iat)], [pk])
                    for j in range(nch):
                        ps, pk = pss[j]
                        self.cp("act", NT_[j][:, :], ps[:, 128:256], [pk], [wk(j, int_)])
                        if step < 4:
                            self.cp("act", N_[j][:, :], ps[:, 0:128], [pk], [wk(j, in_)])
                    pss = []
                    for j in range(nch):
                        ps, pk = self.newps()
                        pss.append((ps, pk))
                        lt = AT_[j] if step == 0 else None
                        self.mm(ps[:, 0:128], NT_[j][:, :], P_[j][:, :], True, True, [wk(j, int_), wk(j, 4)], [pk])
                    for j in range(nch):
                        ps, pk = pss[j]
                        self.tt(P_[j][:, :], ps[:, 0:128], P_[j][:, :], ALU.add, [pk, wk(j, 4)], [wk(j, 4)])
                    cur, nxt = nxt, cur
                Mk_ = [WS[j][0] for j in range(nch)]
                Nb_ = [WS[j][1] for j in range(nch)]
                Nk_ = [WS[j][2] for j in range(nch)]
                Vt_ = [WS[j][3] for j in range(nch)]
                Bt_ = [WS[j][5] for j in range(nch)]
                Kt_ = [WS[j][6] for j in range(nch)]
                XT_ = [WS[j][7] for j in range(nch)]
                UT_ = [WS[j][8] for j in range(nch)]
                for j in range(nch):
                    ps, pk = self.newps()
                    self.mm(ps[:, 0:128], Kblk[:, j, :], Ablk[:, j, :], True, True, ["blk"], [pk])
                    self.mm(ps[:, 128:256], Bblk[:, j, :], Rblk[:, j, :], True, True, ["blk"], [pk])
                    self.mm(ps[:, 256:384], Kblk[:, j, :], Rblk[:, j, :], True, True, ["blk"], [pk])
                    self.tt(Mk_[j][:, :], ps[:, 0:128], bd_su, ALU.mult, [pk, "cst"], [wk(j, 0)])
                    self.tt(Nb_[j][:, :], ps[:, 128:256], bd_iu, ALU.mult, [pk, "cst"], [wk(j, 1)])
                    self.tt(Nk_[j][:, :], ps[:, 256:384], bd_iu, ALU.mult, [pk, "cst"], [wk(j, 2)])
                    ps, pk = self.newps()
                    self.tr(ps[:, 0:128], Vblk[:, j, :], ident, ["blk", "cst"], [pk])
                    self.tr(ps[:, 128:256], Bblk[:, j, :], ident, ["blk", "cst"], [pk])
                    self.tr(ps[:, 256:384], Kblk[:, j, :], ident, ["blk", "cst"], [pk])
                    self.cp("act", Vt_[j][:, :], ps[:, 0:128], [pk], [wk(j, 3)])
                    self.cp("act", Bt_[j][:, :], ps[:, 128:256], [pk], [wk(j, 5)])
                    self.cp("act", Kt_[j][:, :], ps[:, 256:384], [pk], [wk(j, 6)])
                for j in range(nch):
                    if smp:
                        ST, STk = STb[sti % 2], "rST%d" % (sti % 2)
                        sti += 1
                        self.memset(ST[:, :], 0.0, [STk])
                        for hh in range(2):
                            self.dma(ST[64 * hh:64 * hh + 64, 64 * hh:64 * hh + 64], self.dr["st_rw"][j, 2 * p + hh], (), [STk])
                    ps, pk = self.newps()
                    self.mm(ps[:, 0:128], Ablk[:, j, :], ST[:, :], True, False, ["blk", STk], [pk])
                    self.mm(ps[:, 0:128], Mk_[j][:, :], Vt_[j][:, :], False, True, [wk(j, 0), wk(j, 3)], [pk])
                    self.cp("act", XT_[j][:, :], ps[:, 0:128], [pk], [wk(j, 7)])
                    ps, pk = self.newps()
                    self.mm(ps[:, 0:128], P_[j][:, :], XT_[j][:, :], True, True, [wk(j, 4), wk(j, 7)], [pk])
                    self.cp("dve", UT_[j][:, :], ps[:, 0:128], [pk], [wk(j, 8)])
                    ps, pk = self.newps()
                    self.mm(ps[:, 0:128], ST[:, :], Rblk[:, j, :], True, False, [STk, "blk"], [pk])
                    self.mm(ps[:, 0:128], UT_[j][:, :], Nb_[j][:, :], False, False, [wk(j, 8), wk(j, 1)], [pk])
                    self.mm(ps[:, 0:128], Vt_[j][:, :], Nk_[j][:, :], False, True, [wk(j, 3), wk(j, 2)], [pk])
                    for hh in range(2):
                        self.cp("act", Ry[64 * hh:64 * hh + 64, j * C:(j + 1) * C], ps[64 * hh:64 * hh + 64, 64 * hh:64 * hh + C], [pk], ["Ry"])
                    ps, pk = self.newps()
                    self.mm(ps[:, 0:128], Bt_[j][:, :], UT_[j][:, :], True, False, [wk(j, 5), wk(j, 8)], [pk])
                    self.mm(ps[:, 0:128], Kt_[j][:, :], Vt_[j][:, :], False, True, [wk(j, 6), wk(j, 3)], [pk])
                    self.tt(ST[:, :], ST[:, :], ps[:, 0:128], ALU.add, [STk, pk], [STk])
                    self.act(ST[:, :], ST[:, :], AF.Identity, [STk, "DC"], [STk], scale=DC[:, j:j + 1])
                    if smp:
                        for hh in range(2):
                            self.dma(self.dr["rw_s"][j, 2 * p + hh], ST[64 * hh:64 * hh + 64, 64 * hh:64 * hh + 64], [STk], ())
                if (not smp) and c0 + n == 2048:
                    for hh in range(2):
                        self.dma(self.dr["rw_p"][2 * p + hh], ST[64 * hh:64 * hh + 64, 64 * hh:64 * hh + 64], [STk], ())
                ps, pk = self.newps()
                self.mm(ps[:, :n], blk64, Ry[:, :n], True, True, ["cst", "Ry"], [pk])
                self.stt(Rt1[:, :n], ps[:, :n], -1.0 / 64, Ry[:, :n], ALU.mult, ALU.add, [pk, "Ry", "Rt1"], ["Rt1"])
                self.act(Rt2[:, :n], Rt1[:, :n], AF.Square, ["Rt1", "Rt2"], ["Rt2"])
                ps, pk = self.newps()
                self.mm(ps[:, :n], blk64, Rt2[:, :n], True, True, ["cst", "Rt2"], [pk])
                self.act(Rt2[:, :n], ps[:, :n], AF.Sqrt, [pk, "epsc"], ["Rt2"], scale=1.0 / 64, bias=self.epsc[:, 1:2])
                self.recip(Rt2[:, :n], Rt2[:, :n], ["Rt2"], ["Rt2"])
                self.tt(Rt1[:, :n], Rt1[:, :n], Rt2[:, :n], ALU.mult, ["Rt1", "Rt2"], ["Rt1"])
                self.ts(Rt1[:, :n], Rt1[:, :n], pvc("rw_ln_g", p), pvc("rw_ln_b", p), ALU.mult, ALU.add, ["Rt1", "pv"], ["Rt1"])
                self.tt(Rt1[:, :n], Rt1[:, :n], Rbon[:, :n], ALU.add, ["Rt1", "Rbon"], ["Rt1"])
                self.tt(yT[:, c0:c0 + n], Rt1[:, :n], Rg[:, :n], ALU.mult, ["Rt1", "Rg"], ["ryT.%d" % bi])
            for nb in range(8):
                self.linear(wo[nb][:, p:p + 1, :], 1, lambda kc, c0, n: yT[:, c0:c0 + n], lambda kc, bi: ["ryT.%d" % bi],
                            self.add_resid(nb))
        self.release(m0)

    def build(self):
        with contextlib.ExitStack() as st:
            self.setup(st)
            self.load_x()
            self.mem_prep()
            for l in range(STAGES["layers"]):
                kind = l % 3
                if kind == 0:
                    if STAGES["attn"]:
                        self.attn(l, l // 3)
                elif kind == 1:
                    if STAGES["hgrn"]:
                        self.hgrn(l)
                else:
                    if STAGES["rwkv"]:
                        self.rwkv(l)
                if STAGES["mem"]:
                    self.mem_attend(l)
                if STAGES["ffn"]:
                    self.ffn(l)
            self.store_y()
            self.P.emit()


IN_SPECS = [
    ("xp", [2048, 1024]), ("xs", [32, 1024]), ("mem", [256, 1024]),
    ("c128", [2, 4, 128, 2, 4, 128]), ("c512", [2, 4, 512, 2, 4, 128]), ("c2048", [2, 4, 2048, 2, 4, 128]),
    ("st_hg", [4, 8, 128, 128]), ("st_rw", [4, 16, 64, 64]), ("st_sh", [128, 8, 4]),
    ("st_fc", [4, 128, NJ, 4, 2]), ("cmem", [4, 4, 256, 2, 4, 256]),
    ("w_qkv", [2, 36, 128, 8, 128]), ("w_ao", [2, 8, 128, 4, 128]),
    ("w_hgin", [32, 128, 8, 128]), ("w_hgo", [8, 128, 8, 128]),
    ("w_rkv", [3, 8, 128, 8, 128]), ("rw1", [128, 8, 64]), ("ra1", [128, 8, 64]), ("rg1", [128, 8, 160]),
    ("rw2", [64, 1024]), ("ra2", [64, 1024]), ("rg2", [160, 1024]), ("w_rwo", [8, 128, 8, 128]),
    ("w_xq", [4, 8, 128, 8, 128]), ("w_xkv", [4, 16, 128, 8, 128]), ("w_xo", [4, 8, 128, 8, 128]),
    ("w_fin", [4, 2 * NJ, 128, 8, 128]), ("w_fdn", [4, 8, 128, NJ, 128]),
]
OUT_SPECS = [
    ("y_p", [2048, 1024]), ("y_s", [32, 1024]),
    ("kv128_p", [2, 128, 2, 4, 128]), ("kv512_p", [2, 512, 2, 4, 128]), ("kv2048_p", [2, 2048, 2, 4, 128]),
    ("hg_p", [8, 128, 128]), ("rw_p", [16, 64, 64]), ("sh_o", [128, 8, 5]), ("fc_o", [4, 128, NJ, 5, 2]),
    ("mkv_o", [4, 256, 2048]),
    ("kv128_s", [2, 4, 128, 2, 4, 128]), ("kv512_s", [2, 4, 512, 2, 4, 128]), ("kv2048_s", [2, 4, 2048, 2, 4, 128]),
    ("hg_s", [4, 8, 128, 128]), ("rw_s", [4, 16, 64, 64]),
]


def build_nc(pvca, cca):
    nc = bass.Bass("TRN2", target_bir_lowering=False)
    dr = {}
    for name, shape in IN_SPECS + [("pv", [128, pvca.n]), ("cst", [128, cca.n])]:
        dr[name] = nc.dram_tensor(name, shape, F32, kind="ExternalInput").ap()
    for name, shape in OUT_SPECS:
        dr[name] = nc.dram_tensor(name, shape, F32, kind="ExternalOutput").ap()
    b = Builder(nc, dr, pvca, cca)
    b.build()
    return nc


def kernel(**inp):
    inp = {k: np.asarray(v) for k, v in inp.items()}
    f = np.float32
    pv, pvca = build_pv(inp)
    cst, cca = build_cst()
    nc = build_nc(pvca, cca)
    shared = {
        "pv": pv, "cst": cst,
        "w_qkv": np.stack([tile_w(inp["attn_w_qkv"][j]) for j in range(2)]),
        "w_ao": np.stack([tile_w(inp["attn_w_o"][j]) for j in range(2)]),
        "w_hgin": tile_w(inp["hg_w_in"][0]), "w_hgo": tile_w(inp["hg_w_o"][0]),
        "w_rkv": np.stack([tile_w(inp["rw_w_rkv"][0, j]) for j in range(3)]),
        "rw1": np.ascontiguousarray(inp["rw_w1"][0].reshape(8, 128, 64).transpose(1, 0, 2)),
        "ra1": np.ascontiguousarray(inp["rw_a1"][0].reshape(8, 128, 64).transpose(1, 0, 2)),
        "rg1": np.ascontiguousarray(inp["rw_g1"][0].reshape(8, 128, 160).transpose(1, 0, 2)),
        "rw2": np.ascontiguousarray(inp["rw_w2"][0]), "ra2": np.ascontiguousarray(inp["rw_a2"][0]),
        "rg2": np.ascontiguousarray(inp["rw_g2"][0]),
        "w_rwo": tile_w(inp["rw_w_o"][0]),
        "w_xq": np.stack([tile_w(inp["xa_w_q"][l]) for l in range(4)]),
        "w_xkv": np.stack([tile_w(inp["xa_w_kv"][l]) for l in range(4)]),
        "w_xo": np.stack([tile_w(inp["xa_w_o"][l]) for l in range(4)]),
        "w_fin": np.stack([tile_w(inp["ffn_w_in"][l]) for l in range(4)]),
        "w_fdn": np.stack([tile_w(inp["ffn_w_down"][l]) for l in range(4)]),
    }
    in_maps = []
    for c in range(8):
        sl = slice(4 * c, 4 * c + 4)
        m = dict(shared)
        m["xp"] = np.ascontiguousarray(inp["x_prompt"][c])
        m["xs"] = np.ascontiguousarray(inp["x_sample"][sl].reshape(32, 1024))
        m["mem"] = np.ascontiguousarray(inp["mem_prompt"][c])
        m["c128"] = np.ascontiguousarray(inp["cache_attn_kv_w128"][:, sl])
        m["c512"] = np.ascontiguousarray(inp["cache_attn_kv_w512"][:, sl])
        m["c2048"] = np.ascontiguousarray(inp["cache_attn_kv_w2048"][:, sl])
        m["st_hg"] = np.ascontiguousarray(inp["state_hgrn"][0, sl])
        m["st_rw"] = np.ascontiguousarray(inp["state_rwkv"][0, sl].transpose(0, 1, 3, 2))
        m["st_sh"] = np.ascontiguousarray(inp["state_rwkv_shift"][0, sl].reshape(4, 8, 128).transpose(2, 1, 0))
        m["st_fc"] = np.ascontiguousarray(inp["state_ffn_conv"][:, sl].reshape(4, 4, 2, NJ, 128).transpose(0, 4, 3, 1, 2))
        m["cmem"] = np.ascontiguousarray(inp["cache_mem_kv"][:, sl])
        in_maps.append({k: np.ascontiguousarray(v, dtype=f) for k, v in m.items()})
    res = run_bass_kernel_spmd(nc, in_maps, core_ids=list(range(8)))
    R = res.results

    def cat(name, axis=0, stack=False):
        arrs = [np.asarray(R[c][name]) for c in range(8)]
        return np.stack(arrs, axis) if stack else np.concatenate(arrs, axis)

    y_p = cat("y_p", 0, True)
    y_s = cat("y_s", 0, True).reshape(32, 8, 1024)
    kvp = [cat(n, 1, True) for n in ("kv128_p", "kv512_p", "kv2048_p")]
    hg_p = cat("hg_p", 0, True)[None]
    rw_p = cat("rw_p", 0, True).transpose(0, 1, 3, 2)[None]
    sh = cat("sh_o", 0, True)
    sh = sh.transpose(0, 3, 2, 1).reshape(8, 5, 1024)
    sh_p = sh[:, 0][None]
    sh_s = sh[:, 1:5].reshape(32, 1024)[None]
    fc = cat("fc_o", 0, True)
    fc = fc.transpose(1, 0, 4, 5, 3, 2).reshape(4, 8, 5, 2, DFF)
    fc_p = np.ascontiguousarray(fc[:, :, 0])
    fc_s = np.ascontiguousarray(fc[:, :, 1:5].reshape(4, 32, 2, DFF))
    mkv = cat("mkv_o", 1, True).reshape(4, 8, 256, 2, 4, 256)
    kvs = [cat(n, 1) for n in ("kv128_s", "kv512_s", "kv2048_s")]
    hg_s = cat("hg_s", 0)[None]
    rw_s = cat("rw_s", 0).transpose(0, 1, 3, 2)[None]
    outs = (y_p, y_s, kvp[0], kvp[1], kvp[2], hg_p, rw_p, sh_p, fc_p, mkv, kvs[0], kvs[1], kvs[2], hg_s, rw_s, sh_s, fc_s)
    return tuple(np.ascontiguousarray(o, dtype=np.float32) for o in outs)
```

```python
import contextlib
import numpy as np
import concourse.bass as bass
import concourse.mybir as mybir
from concourse.bass_utils import run_bass_kernel_spmd

F32 = mybir.dt.float32
BF16 = mybir.dt.bfloat16
AF = mybir.ActivationFunctionType
ALU = mybir.AluOpType
AX = mybir.AxisListType
ENGS = ("pe", "act", "dve", "pool", "sp")
NDSEM = 48
NHW = 32

NT = 2080
TB = [(0, 512), (512, 512), (1024, 512), (1536, 512), (2048, 32)]
D = 1024
DFF = 2816
NJ = 22
EPS = 1e-6
DILS = (1, 4, 16)
WINS = (128, 512, 2048)
NWB = 6
STRICT_SAME_ENGINE = False
STAGES = {"attn": True, "hgrn": True, "rwkv": True, "mem": True, "ffn": True, "layers": 4, "a_d2d": 1, "a_kvout": 1, "a_sout": 1, "a_pu": 1, "a_su": 1, "a_groups": 3}


class Op:
    __slots__ = ("eng", "fn", "reads", "writes", "dma", "seq", "waits", "signal", "cnt", "dsem", "dval", "snap")

    def __init__(self, eng, fn, reads, writes, dma):
        self.eng, self.fn, self.reads, self.writes, self.dma = eng, fn, tuple(reads), tuple(writes), dma
        self.waits = []
        self.signal = False
        self.cnt = 0
        self.dsem = -1
        self.dval = 0
        self.snap = None


class Prog:
    def __init__(self, nc):
        self.nc = nc
        self.ops = []

    def add(self, eng, fn, reads=(), writes=(), dma=False):
        self.ops.append(Op(eng, fn, reads, writes, dma))

    def barrier(self):
        self.ops.append(None)

    def analyse(self):
        ops = self.ops
        last_w = {}
        readers = {}
        seqc = {e: 0 for e in ENGS}
        known = {e: {x: 0 for x in ENGS} for e in ENGS}
        kd = {e: set() for e in ENGS}
        dsem_last = [None] * NDSEM
        dsem_cnt = [0] * NDSEM
        nd = 0
        nds = 0
        pend = {e: [] for e in ENGS}
        last_op = {e: None for e in ENGS}
        for i, op in enumerate(ops):
            if op is None:
                for E in ENGS:
                    wl = []
                    for E2 in ENGS:
                        j = last_op[E2]
                        if j is not None and known[E][E2] < ops[j].seq:
                            ops[j].signal = True
                            wl.append(("e", E2, j))
                            known[E][E2] = ops[j].seq
                    for s_ in range(NDSEM):
                        if dsem_cnt[s_] > 0:
                            wl.append(("d", s_, 16 * dsem_cnt[s_]))
                    pend[E] = pend[E] + wl
                last_w.clear()
                readers.clear()
                dsem_last = [None] * NDSEM
                continue
            E = op.eng
            if pend[E]:
                op.waits.extend(pend[E])
                pend[E] = []
            seqc[E] += 1
            op.seq = seqc[E]
            own = op.seq - 1 if E in ("pe", "sp") else 0
            if own > known[E][E]:
                known[E][E] = own
            deps = set()
            for k in op.reads:
                j = last_w.get(k)
                if j is not None:
                    deps.add(j)
            for k in op.writes:
                j = last_w.get(k)
                if j is not None and (ops[j].dma or op.dma or ops[j].eng != E or STRICT_SAME_ENGINE):
                    deps.add(j)
                rd = readers.get(k)
                if rd:
                    for j in rd.values():
                        if ops[j].dma or op.dma or ops[j].eng != E or STRICT_SAME_ENGINE:
                            deps.add(j)
            deps.discard(i)
            if op.dma:
                if E == "pool":
                    s = NHW + (nds % (NDSEM - NHW))
                    nds += 1
                else:
                    s = nd % NHW
                    nd += 1
                if dsem_last[s] is not None:
                    deps.add(dsem_last[s])
                dsem_cnt[s] += 1
                op.dsem, op.dval = s, 16 * dsem_cnt[s]
                dsem_last[s] = i
            for j in sorted(deps):
                p = ops[j]
                if p.dma:
                    if j in kd[E]:
                        continue
                    op.waits.append(("d", p.dsem, p.dval))
                    kd[E].add(j)
                    for x in ENGS:
                        if p.snap[x] > known[E][x]:
                            known[E][x] = p.snap[x]
                else:
                    if known[E][p.eng] >= p.seq:
                        continue
                    p.signal = True
                    op.waits.append(("e", p.eng, j))
                    known[E][p.eng] = p.seq
                    for x in ENGS:
                        if p.snap[x] > known[E][x]:
                            known[E][x] = p.snap[x]
            op.snap = dict(known[E])
            if not op.dma:
                last_op[E] = i
            for k in op.writes:
                last_w[k] = i
                readers[k] = {}
            for k in op.reads:
                readers.setdefault(k, {})[("dma", i) if op.dma else E] = i
        c = {e: 0 for e in ENGS}
        for op in ops:
            if op is not None and op.signal:
                c[op.eng] += 1
                op.cnt = c[op.eng]
        self.sig_tot = c
        fin = {}
        for op in ops:
            if op is not None and op.dma:
                fin[op.dsem] = max(fin.get(op.dsem, 0), op.dval)
        self.final = fin

    def emit(self):
        nc = self.nc
        self.analyse()
        ops = self.ops
        with contextlib.ExitStack() as st:
            psem = {e: st.enter_context(nc.semaphore("prog_" + e)) for e in ENGS}
            dsem = [st.enter_context(nc.semaphore("dmas%d" % i)) for i in range(NDSEM)]
            block = st.enter_context(nc.Block())

            def stream(ename):
                def body(eng):
                    for op in ops:
                        if op is None or op.eng != ename:
                            continue
                        for w in op.waits:
                            if w[0] == "d":
                                eng.wait_ge(dsem[w[1]], w[2])
                            else:
                                eng.wait_ge(psem[w[1]], ops[w[2]].cnt)
                        ins = op.fn(eng)
                        if op.dma:
                            ins.then_inc(dsem[op.dsem], 16)
                        elif op.signal:
                            ins.then_inc(psem[ename], 1)
                    if ename == "sp":
                        for s, v in self.final.items():
                            eng.wait_ge(dsem[s], v)
                        for e2 in ENGS:
                            if e2 != "sp" and self.sig_tot[e2] > 0:
                                eng.wait_ge(psem[e2], self.sig_tot[e2])
                return body

            block.tensor(stream("pe"))
            block.scalar(stream("act"))
            block.vector(stream("dve"))
            block.gpsimd(stream("pool"))
            block.sync(stream("sp"))


class ColAlloc:
    def __init__(self):
        self.n = 0
        self.cols = {}

    def add(self, name, ncols):
        self.cols[name] = self.n
        self.n += ncols
        return self.cols[name]


def fm_vec(v):
    v = np.asarray(v, np.float32).reshape(-1)
    nc_ = v.size // 128
    return np.ascontiguousarray(v.reshape(nc_, 128).T)


def pv_layout():
    ca = ColAlloc()
    for l in range(4):
        for nm in ("norm_mix", "norm_mem", "norm_ffn", "mem_norm"):
            ca.add("%s%d" % (nm, l), 8)
        ca.add("xa_q_gain%d" % l, 2)
        ca.add("xa_k_gain%d" % l, 2)
        for t in range(3):
            ca.add("conv_w%d_%d" % (l, t), NJ)
        ca.add("conv_b%d" % l, NJ)
    for ja in range(2):
        for g in range(3):
            ca.add("aq_gain%d_%d" % (ja, g), 1)
            ca.add("ak_gain%d_%d" % (ja, g), 1)
    for l in range(4):
        ca.add("hg_lb%d" % l, 8)
    ca.add("hg_out_gain", 1)
    for j in range(6):
        ca.add("rw_mu%d" % j, 8)
    for nm in ("rw_w0", "rw_a0", "rw_k_k", "rw_k_a", "rw_r_k", "rw_ln_g", "rw_ln_b"):
        ca.add(nm, 8)
    return ca


def build_pv(inp):
    ca = pv_layout()
    pv = np.zeros((128, ca.n), np.float32)

    def put(name, v):
        a = fm_vec(v)
        pv[:, ca.cols[name]:ca.cols[name] + a.shape[1]] = a

    for l in range(4):
        for nm in ("norm_mix", "norm_mem", "norm_ffn", "mem_norm"):
            put("%s%d" % (nm, l), inp[nm][l])
        put("xa_q_gain%d" % l, inp["xa_q_gain"][l])
        put("xa_k_gain%d" % l, inp["xa_k_gain"][l])
        for t in range(3):
            put("conv_w%d_%d" % (l, t), inp["ffn_conv_w"][l, t])
        put("conv_b%d" % l, inp["ffn_conv_b"][l])
        put("hg_lb%d" % l, inp["hg_lb_logits"][l])
    for ja in range(2):
        for g in range(3):
            put("aq_gain%d_%d" % (ja, g), inp["attn_q_gain"][ja, g])
            put("ak_gain%d_%d" % (ja, g), inp["attn_k_gain"][ja, g])
    put("hg_out_gain", inp["hg_out_gain"][0])
    for j in range(6):
        put("rw_mu%d" % j, inp["rw_mu"][0, j])
    for nm in ("rw_w0", "rw_a0", "rw_k_k", "rw_k_a", "rw_r_k", "rw_ln_g", "rw_ln_b"):
        put(nm, inp[nm][0])
    return pv, ca


def cst_layout():
    ca = ColAlloc()
    ca.add("ident", 128)
    ca.add("ones", 128)
    ca.add("m_own", 128)
    ca.add("m_prev", 128)
    ca.add("blk64", 128)
    ca.add("own_s", 96)
    ca.add("bd_su", 128)
    ca.add("bd_sl", 128)
    ca.add("bd_iu", 128)
    return ca


def build_cst():
    ca = cst_layout()
    c = np.zeros((128, ca.n), np.float32)
    j = np.arange(128)[:, None]
    i = np.arange(128)[None, :]
    c[:, ca.cols["ident"]:ca.cols["ident"] + 128] = (j == i)
    c[:, ca.cols["ones"]:ca.cols["ones"] + 128] = 1.0
    c[:, ca.cols["m_own"]:ca.cols["m_own"] + 128] = (j <= i)
    c[:, ca.cols["m_prev"]:ca.cols["m_prev"] + 128] = (j >= i)
    c[:, ca.cols["blk64"]:ca.cols["blk64"] + 128] = ((j // 64) == (i // 64))
    col = ca.cols["own_s"]
    for g in range(3):
        R = min(DILS[g], 8)
        nq = 8 // R
        for s in range(4):
            for r in range(R):
                for u in range(nq):
                    for ip in range(8):
                        if ip % R == r and ip <= r + R * u:
                            c[8 * s + ip, col + u] = 1.0
                col += nq
    same = ((j // 64) == (i // 64))
    c[:, ca.cols["bd_su"]:ca.cols["bd_su"] + 128] = same & ((j % 64) < (i % 64))
    c[:, ca.cols["bd_sl"]:ca.cols["bd_sl"] + 128] = same & ((j % 64) > (i % 64))
    c[:, ca.cols["bd_iu"]:ca.cols["bd_iu"] + 128] = same & ((j % 64) <= (i % 64))
    return c, ca


def own_s_col(ca, g, s, r):
    col = ca.cols["own_s"]
    for gg in range(3):
        R = min(DILS[gg], 8)
        nq = 8 // R
        if gg == g:
            return col + (s * R + r) * nq
        col += 4 * R * nq
    raise ValueError


def tile_w(w, bw=128):
    K, N = w.shape
    return np.ascontiguousarray(w.reshape(K // 128, 128, N // bw, bw).transpose(2, 1, 0, 3))


class Builder:
    def __init__(self, nc, dr, pvca, cca):
        self.nc, self.dr, self.pvca, self.cca = nc, dr, pvca, cca
        self.P = Prog(nc)
        self.psi = 0
        self.wi = 0
        self.scri = 0
        self.off = 0

    def alloc(self, cols):
        a = self.arena[:, self.off:self.off + cols]
        self.off += cols
        assert self.off <= self.acols, ("SBUF arena overflow", self.off, self.acols)
        return a

    def alloc16(self, cols):
        return self.alloc((cols + 1) // 2).bitcast(BF16)[:, 0:cols]

    def mark(self):
        return self.off

    def release(self, m):
        self.P.barrier()
        self.off = m

    def newps(self):
        i = self.psi % 8
        self.psi += 1
        return self.ps[i], "ps%d" % i

    def scr(self):
        i = self.scri % 4
        self.scri += 1
        return self.scrb[i], "scr%d" % i

    def mm(self, out, lhsT, rhs, start, stop, r, w):
        self.P.add("pe", lambda e: e.matmul(out, lhsT, rhs, start=start, stop=stop), r, w)

    def tr(self, out, in_, ident, r, w):
        self.P.add("pe", lambda e: e.transpose(out, in_, ident), r, w)

    def act(self, out, in_, func, r, w, bias=None, scale=None, accum=None):
        kw = {}
        if bias is not None:
            kw["bias"] = bias
        if scale is not None:
            kw["scale"] = scale
        if accum is not None:
            kw["accum_out"] = accum
        self.P.add("act", lambda e: e.activation(out, in_, func, **kw), r, w)

    def cp(self, eng, out, in_, r, w):
        if eng == "act":
            self.P.add("act", lambda e: e.copy(out, in_), r, w)
        else:
            self.P.add(eng, lambda e: e.tensor_copy(out, in_), r, w)

    def tt(self, out, a, b, op, r, w, eng="dve"):
        self.P.add(eng, lambda e: e.tensor_tensor(out, a, b, op), r, w)

    def ts(self, out, a, s1, s2, op0, op1, r, w, eng="dve"):
        if s2 is None:
            self.P.add(eng, lambda e: e.tensor_scalar(out, a, s1, None, op0), r, w)
        else:
            self.P.add(eng, lambda e: e.tensor_scalar(out, a, s1, s2, op0, op1), r, w)

    def stt(self, out, a, s, b, op0, op1, r, w, eng="dve"):
        self.P.add(eng, lambda e: e.scalar_tensor_tensor(out, a, s, b, op0, op1), r, w)

    def recip(self, out, in_, r, w):
        self.P.add("dve", lambda e: e.reciprocal(out, in_), r, w)

    def memset(self, out, val, w, eng="dve"):
        self.P.add(eng, lambda e: e.memset(out, val), (), w)

    def dma(self, out, in_, r, w, q="sp", slow=False):
        if slow:
            self.P.add(q, lambda e: e.dma_start(out=out, in_=in_, allow_slow_non_contiguous=True), r, w, dma=True)
        else:
            self.P.add(q, lambda e: e.dma_start(out=out, in_=in_), r, w, dma=True)

    def pvc(self, name, k=0, n=1):
        c = self.pvca.cols[name] + k
        return self.pv[:, c:c + n]

    def cst(self, name, rows=128, n=None, k=0):
        c = self.cca.cols[name] + k
        if n is None:
            n = 128
        return self.c32[:rows, c:c + n]

    def cst16(self, name, rows=128, n=None, k=0):
        c = self.cca.cols[name] + k
        if n is None:
            n = 128
        return self.c16[:rows, c:c + n]

    def load_w(self, wdram, KC, rows=128, ncol=128):
        i = self.wi % NWB
        self.wi += 1
        wt = self.wb[i]
        key = "wb%d" % i
        self.dma(wt[:rows, :KC, :ncol], wdram, (), [key], q="pool")
        return wt, key

    def linear(self, wdram, KC, rhs_fn, rkeys_fn, cons, blocks=(0, 1, 2, 3, 4), rows=128, ncol=128):
        wt, wk = self.load_w(wdram, KC, rows, ncol)
        for bi in blocks:
            c0, n = TB[bi]
            ps, pk = self.newps()
            for kc in range(KC):
                self.mm(ps[:ncol, :n], wt[:rows, kc, :ncol], rhs_fn(kc, c0, n), kc == 0, kc == KC - 1,
                        [wk] + rkeys_fn(kc, bi), [pk])
            cons(bi, c0, n, ps, pk)

    def h_rhs(self, kc, c0, n):
        return self.hT[:, kc, c0:c0 + n]

    def h_keys(self, kc, bi):
        return ["h%d.%d" % (kc, bi)]

    def rstd_block(self, srcs, skeys, n, dim, eps, ones=None):
        if ones is None:
            ones = self.cst16("ones")
        ps, pk = self.newps()
        for ci, (ap, k) in enumerate(zip(srcs, skeys)):
            sq, sk = self.scr()
            sq16 = sq.bitcast(BF16)
            self.act(sq16[:, :n], ap, AF.Square, [k], [sk])
            self.mm(ps[:, :n], ones, sq16[:, :n], ci == 0, ci == len(srcs) - 1, [sk, "cst16"], [pk])
        rs, rk = self.scr()
        self.act(rs[:, :n], ps[:, :n], AF.Ln, [pk], [rk], scale=1.0 / dim, bias=self.epsc[:, 0:1] if eps == EPS else eps)
        self.act(rs[:, :n], rs[:, :n], AF.Exp, [rk], [rk], scale=-0.5)
        return rs, rk

    def dnorm(self, gname):
        for bi, (c0, n) in enumerate(TB):
            rs, rk = self.rstd_block([self.xT[:, c, c0:c0 + n] for c in range(8)],
                                     ["x%d.%d" % (c, bi) for c in range(8)], n, D, EPS)
            for c in range(8):
                self.stt(self.hT[:, c, c0:c0 + n], self.xT[:, c, c0:c0 + n], self.pvc(gname, c), rs[:, :n],
                         ALU.mult, ALU.mult, ["x%d.%d" % (c, bi), rk, "pv"], ["h%d.%d" % (c, bi)])

    def add_resid(self, nb):
        def cons(bi, c0, n, ps, pk):
            k = "x%d.%d" % (nb, bi)
            self.tt(self.xT[:, nb, c0:c0 + n], ps[:, :n], self.xT[:, nb, c0:c0 + n], ALU.add, [pk, k], [k])
        return cons

    def setup(self, st):
        nc = self.nc
        self.acols = 52800
        self.arena = st.enter_context(nc.sbuf_tensor("arena", [128, self.acols], F32))
        self.ps = [st.enter_context(nc.psum_tensor("psb%d" % i, [128, 512], F32)) for i in range(8)]
        self.xT = self.alloc(8 * NT).rearrange("p (c t) -> p c t", t=NT)
        self.hT = self.alloc16(8 * NT).rearrange("p (c t) -> p c t", t=NT)
        self.pv = self.alloc(self.pvca.n)
        self.c32 = self.alloc(self.cca.n)
        self.c16 = self.alloc16(self.cca.n)
        self.epsc = self.alloc(2)
        self.wb = [self.alloc16(8 * 128).rearrange("p (k n) -> p k n", n=128) for _ in range(NWB)]
        self.scrb = [self.alloc(512) for _ in range(4)]
        self.dma(self.pv, self.dr["pv"], (), ["pv"])
        self.dma(self.c32, self.dr["cst"], (), ["cst"])
        self.cp("dve", self.c16, self.c32, ["cst"], ["cst16"])
        self.memset(self.epsc[:, 0:1], EPS, ["epsc"])
        self.memset(self.epsc[:, 1:2], 64e-5, ["epsc"])
        self.P.barrier()

    def load_x(self):
        m = self.mark()
        stg = [self.alloc(1024) for _ in range(2)]
        ident = self.cst("ident")
        for tt_ in range(17):
            sb = stg[tt_ % 2]
            sk = "xstg%d" % (tt_ % 2)
            if tt_ < 16:
                rows, c0, src = 128, tt_ * 128, self.dr["xp"][tt_ * 128:(tt_ + 1) * 128, :]
            else:
                rows, c0, src = 32, 2048, self.dr["xs"]
            self.dma(sb[:rows, :], src, (), [sk])
            bi = min(c0 // 512, 4)
            for half in range(2):
                ps, pk = self.newps()
                for q in range(4):
                    c = half * 4 + q
                    self.tr(ps[:, q * 128:q * 128 + rows], sb[:rows, c * 128:(c + 1) * 128], ident[:rows, :rows],
                            [sk, "cst"], [pk])
                self.cp("act" if half == 0 else "dve",
                        self.xT[:, half * 4:half * 4 + 4, c0:c0 + rows],
                        ps[:, :].rearrange("p (q t) -> p q t", t=128)[:, :, :rows],
                        [pk], ["x%d.%d" % (half * 4 + q, bi) for q in range(4)])
        self.release(m)

    def store_y(self):
        m = self.mark()
        stg = [self.alloc(1024) for _ in range(2)]
        ident = self.cst("ident")
        for tt_ in range(17):
            sb = stg[tt_ % 2]
            sk = "ystg%d" % (tt_ % 2)
            if tt_ < 16:
                rows, c0, dst = 128, tt_ * 128, self.dr["y_p"][tt_ * 128:(tt_ + 1) * 128, :]
            else:
                rows, c0, dst = 32, 2048, self.dr["y_s"]
            bi = min(c0 // 512, 4)
            for half in range(2):
                ps, pk = self.newps()
                for q in range(4):
                    c = half * 4 + q
                    self.tr(ps[:rows, q * 128:(q + 1) * 128], self.xT[:, c, c0:c0 + rows], ident,
                            ["x%d.%d" % (c, bi), "cst"], [pk])
                self.cp("act" if half == 0 else "dve", sb[:rows, half * 512:(half + 1) * 512], ps[:rows, :], [pk], [sk])
            self.dma(dst, sb[:rows, :], [sk], ())
        self.release(m)

    def ffn(self, l):
        self.dnorm("norm_ffn%d" % l)
        m = self.mark()
        UW = 2050 + 40
        ub = [self.alloc(UW) for _ in range(2)]
        cb = [self.alloc(NT) for _ in range(2)]
        sl = cb
        GS = 8
        aT = self.alloc16(GS * NT).rearrange("p (g t) -> p g t", t=NT)
        fcst = self.alloc(NJ * 8).rearrange("p (j s t) -> p j s t", s=4, t=2)
        fco = self.alloc(NJ * 10).rearrange("p (j s t) -> p j s t", s=5, t=2)
        self.dma(fcst, self.dr["st_fc"][l], (), ["fcst"])
        for i in range(2):
            self.memset(ub[i][:, 0:2], 0.0, ["u%d" % i])
        win = self.dr["w_fin"][l]
        wdn = self.dr["w_fdn"][l]
        groups = [list(range(a, min(a + GS, NJ))) for a in range(0, NJ, GS)]
        cnt = 0
        for grp in groups:
            for gi, j in enumerate(grp):
                u = ub[cnt % 2]
                uk = "u%d" % (cnt % 2)
                c_ = cb[cnt % 2]
                ck = "c%d" % (cnt % 2)
                s_ = sl[cnt % 2]
                sk = ck
                cnt += 1
                us = u[:, 2050:2090].rearrange("p (s k) -> p s k", k=10)

                def cons_u(bi, c0, n, ps, pk, u=u, uk=uk, us=us):
                    if bi < 4:
                        self.cp("act", u[:, 2 + c0:2 + c0 + n], ps[:, :n], [pk], [uk])
                    else:
                        self.cp("act", us[:, :, 2:10], ps[:, :32].rearrange("p (s i) -> p s i", i=8), [pk], [uk])
                self.linear(win[j], 8, self.h_rhs, self.h_keys, cons_u)
                self.cp("dve", us[:, :, 0:2], fcst[:, j, :, :], ["fcst", uk], [uk])
                w0, w1, w2, bb = (self.pvc("conv_w%d_0" % l, j), self.pvc("conv_w%d_1" % l, j),
                                  self.pvc("conv_w%d_2" % l, j), self.pvc("conv_b%d" % l, j))
                self.ts(c_[:, 0:2048], u[:, 2:2050], w2, bb, ALU.mult, ALU.add, [uk, "pv"], [ck])
                self.stt(c_[:, 0:2048], u[:, 1:2049], w1, c_[:, 0:2048], ALU.mult, ALU.add, [uk, ck, "pv"], [ck])
                self.stt(c_[:, 0:2048], u[:, 0:2048], w0, c_[:, 0:2048], ALU.mult, ALU.add, [uk, ck, "pv"], [ck])
                cs = c_[:, 2048:2080].rearrange("p (s i) -> p s i", i=8)
                self.ts(cs, us[:, :, 2:10], w2, bb, ALU.mult, ALU.add, [uk, "pv"], [ck])
                self.stt(cs, us[:, :, 1:9], w1, cs, ALU.mult, ALU.add, [uk, ck, "pv"], [ck])
                self.stt(cs, us[:, :, 0:8], w0, cs, ALU.mult, ALU.add, [uk, ck, "pv"], [ck])
                self.act(s_[:, :], c_[:, :], AF.Silu, [ck], [sk])
                self.cp("act", fco[:, j, 0, :], u[:, 2048:2050], [uk], ["fco"])
                self.cp("act", fco[:, j, 1:5, :], us[:, :, 8:10], [uk], ["fco"])

                def cons_g(bi, c0, n, ps, pk, gi=gi, s_=s_, sk=sk):
                    self.tt(aT[:, gi, c0:c0 + n], ps[:, :n], s_[:, c0:c0 + n], ALU.mult, [pk, sk], ["a%d.%d" % (gi, bi)])
                self.linear(win[NJ + j], 8, self.h_rhs, self.h_keys, cons_g)
            g0, gl = grp[0], len(grp)
            for nb in range(8):
                self.linear(wdn[nb][:, g0:g0 + gl, :], gl, lambda kc, c0, n: aT[:, kc, c0:c0 + n],
                            lambda kc, bi: ["a%d.%d" % (kc, bi)], self.add_resid(nb))
        self.dma(self.dr["fc_o"][l], fco, ["fco"], ())
        self.release(m)

    def mem_prep(self):
        self.memTn = self.alloc(8 * 256).rearrange("p (c t) -> p c t", t=256)
        m = self.mark()
        stg = [self.alloc(1024) for _ in range(2)]
        raw = self.alloc(8 * 256).rearrange("p (c t) -> p c t", t=256)
        ident = self.cst("ident")
        for mb in range(2):
            self.dma(stg[mb], self.dr["mem"][mb * 128:(mb + 1) * 128, :], (), ["mstg%d" % mb])
            for half in range(2):
                ps, pk = self.newps()
                for q in range(4):
                    c = half * 4 + q
                    self.tr(ps[:, q * 128:(q + 1) * 128], stg[mb][:, c * 128:(c + 1) * 128], ident, ["mstg%d" % mb, "cst"], [pk])
                self.cp("act", raw[:, half * 4:half * 4 + 4, mb * 128:(mb + 1) * 128],
                        ps[:, :].rearrange("p (q t) -> p q t", t=128), [pk], ["mraw"])
        rs, rk = self.rstd_block([raw[:, c, :] for c in range(8)], ["mraw"] * 8, 256, D, EPS)
        for c in range(8):
            self.tt(self.memTn[:, c, :], raw[:, c, :], rs[:, :256], ALU.mult, ["mraw", rk], ["memTn"])
        self.release(m)

    def mem_kv(self, l, KTm, Vm):
        m = self.mark()
        ml = self.alloc16(8 * 256).rearrange("p (c t) -> p c t", t=256)
        kvraw = self.alloc(16 * 256).rearrange("p (b t) -> p b t", t=256)
        stage = self.alloc(2 * 2048).rearrange("p (mb c) -> p mb c", c=2048)
        for c in range(8):
            self.ts(ml[:, c, :], self.memTn[:, c, :], self.pvc("mem_norm%d" % l, c), None, ALU.mult, None,
                    ["memTn", "pv"], ["ml"])
        wkv = self.dr["w_xkv"][l]
        for blk in range(16):
            wt, wk = self.load_w(wkv[blk], 8)
            ps, pk = self.newps()
            for kc in range(8):
                self.mm(ps[:, :256], wt[:, kc, :], ml[:, kc, :], kc == 0, kc == 7, [wk, "ml"], [pk])
            self.cp("act", kvraw[:, blk, :], ps[:, :256], [pk], ["kvraw%d" % blk])
        for h in range(4):
            rs, rk = self.rstd_block([kvraw[:, 2 * h + e, :] for e in range(2)], ["kvraw%d" % (2 * h + e) for e in range(2)],
                                     256, 256, EPS)
            for e in range(2):
                b_ = 2 * h + e
                self.stt(kvraw[:, b_, :], kvraw[:, b_, :], self.pvc("xa_k_gain%d" % l, e), rs[:, :256], ALU.mult, ALU.mult,
                         ["kvraw%d" % b_, rk, "pv"], ["kvraw%d" % b_])
                self.cp("act", KTm[:, b_, :], kvraw[:, b_, :], ["kvraw%d" % b_], ["KTm"])
        ident = self.cst("ident")
        for mb in range(2):
            for q4 in range(4):
                ps, pk = self.newps()
                for q in range(4):
                    blk = q4 * 4 + q
                    self.tr(ps[:, q * 128:(q + 1) * 128], kvraw[:, blk, mb * 128:(mb + 1) * 128], ident,
                            ["kvraw%d" % blk, "cst"], [pk])
                self.cp("act" if q4 % 2 == 0 else "dve", stage[:, mb, q4 * 512:(q4 + 1) * 512], ps[:, :], [pk], ["mstage"])
        self.dma(self.dr["mkv_o"][l].rearrange("(mb m) c -> m mb c", m=128), stage, ["mstage"], ())
        self.cp("dve", Vm, stage[:, :, 1024:2048], ["mstage"], ["Vm"])
        self.release(m)

    def mem_attend(self, l):
        self.dnorm("norm_mem%d" % l)
        m0 = self.mark()
        KTm = self.alloc16(8 * 256).rearrange("p (b t) -> p b t", t=256)
        Vm = self.alloc16(2 * 1024).rearrange("p (mb c) -> p mb c", c=1024)
        self.mem_kv(l, KTm, Vm)
        qraw = self.alloc(2 * NT).rearrange("p (e t) -> p e t", t=NT)
        q16 = self.alloc16(2 * NT).rearrange("p (e t) -> p e t", t=NT)
        oT = self.alloc16(2 * NT).rearrange("p (b t) -> p b t", t=NT)
        pT = [self.alloc16(2 * 512).rearrange("p (mb t) -> p mb t", t=512) for _ in range(2)]
        rden = [self.alloc(512) for _ in range(2)]
        ckv = [self.alloc(2 * 2 * 256).rearrange("p (mb t e) -> p mb t e", t=2, e=256) for _ in range(2)]
        kTs = [self.alloc16(4 * 128).rearrange("p (q t) -> p q t", t=128) for _ in range(2)]
        vs16 = [self.alloc16(2 * 256).rearrange("p (mb e) -> p mb e", e=256) for _ in range(2)]
        pTs = [self.alloc16(16) for _ in range(2)]
        gsc = self.alloc(2)
        self.ts(gsc, self.pvc("xa_q_gain%d" % l, 0, 2), 256 ** -0.5, None, ALU.mult, None, ["pv"], ["gsc"])
        ones16 = self.cst16("ones")
        ident = self.cst("ident")
        wq = self.dr["w_xq"][l]
        it = 0
        for h in range(4):
            for e in range(2):
                def cons_q(bi, c0, n, ps, pk, e=e):
                    self.cp("act", qraw[:, e, c0:c0 + n], ps[:, :n], [pk], ["qraw%d.%d" % (e, bi)])
                self.linear(wq[2 * h + e], 8, self.h_rhs, self.h_keys, cons_q)
            for bi, (c0, n) in enumerate(TB):
                rs, rk = self.rstd_block([qraw[:, e, c0:c0 + n] for e in range(2)], ["qraw%d.%d" % (e, bi) for e in range(2)],
                                         n, 256, EPS)
                for e in range(2):
                    self.stt(q16[:, e, c0:c0 + n], qraw[:, e, c0:c0 + n], gsc[:, e:e + 1], rs[:, :n], ALU.mult, ALU.mult,
                             ["qraw%d.%d" % (e, bi), rk, "gsc"], ["q16.%d.%d" % (e, bi)])
            for bi in range(4):
                c0, n = TB[bi]
                p_ = pT[it % 2]
                pk_ = "pT%d" % (it % 2)
                rd = rden[it % 2]
                rdk = "rden%d" % (it % 2)
                it += 1
                for mb in range(2):
                    ps, pk = self.newps()
                    for e in range(2):
                        self.mm(ps[:, :n], KTm[:, 2 * h + e, mb * 128:(mb + 1) * 128], q16[:, e, c0:c0 + n], e == 0, e == 1,
                                ["KTm", "q16.%d.%d" % (e, bi)], [pk])
                    self.act(p_[:, mb, :n], ps[:, :n], AF.Exp, [pk], [pk_ + ".%d" % mb])
                psd, pkd = self.newps()
                for mb in range(2):
                    self.mm(psd[:, :n], ones16, p_[:, mb, :n], mb == 0, mb == 1, ["cst16", pk_ + ".%d" % mb], [pkd])
                self.act(rd[:, :n], psd[:, :n], AF.Ln, [pkd], [rdk])
                self.act(rd[:, :n], rd[:, :n], AF.Exp, [rdk], [rdk], scale=-1.0)
                for e in range(2):
                    pso, pko = self.newps()
                    for mb in range(2):
                        self.mm(pso[:, :n], Vm[:, mb, h * 256 + e * 128:h * 256 + (e + 1) * 128], p_[:, mb, :n], mb == 0, mb == 1,
                                ["Vm", pk_ + ".%d" % mb], [pko])
                    self.tt(oT[:, e, c0:c0 + n], pso[:, :n], rd[:, :n], ALU.mult, [pko, rdk], ["oT%d.%d" % (e, bi)])
            for s in range(4):
                ck = ckv[s % 2]
                ckk = "ckv%d" % (s % 2)
                kt = kTs[s % 2]
                ktk = "kTs%d" % (s % 2)
                v16 = vs16[s % 2]
                vk = "vs16%d" % (s % 2)
                pts = pTs[s % 2]
                ptk = "pTs%d" % (s % 2)
                for mb in range(2):
                    self.dma(ck[:, mb, :, :], self.dr["cmem"][l, s, mb * 128:(mb + 1) * 128, :, h, :], (), [ckk])
                ps, pk = self.newps()
                for e in range(2):
                    for mb in range(2):
                        q = e * 2 + mb
                        self.tr(ps[:, q * 128:(q + 1) * 128], ck[:, mb, 0, e * 128:(e + 1) * 128], ident, [ckk, "cst"], [pk])
                self.cp("act", kt, ps[:, :].rearrange("p (q t) -> p q t", t=128), [pk], [ktk])
                self.cp("dve", v16, ck[:, :, 1, :], [ckk], [vk])
                q0 = 2048 + 8 * s
                ps, pk = self.newps()
                for mb in range(2):
                    for e in range(2):
                        self.mm(ps[:, mb * 8:mb * 8 + 8], kt[:, e * 2 + mb, :], q16[:, e, q0:q0 + 8], e == 0, e == 1,
                                [ktk, "q16.%d.4" % e], [pk])
                self.act(pts[:, 0:16], ps[:, 0:16], AF.Exp, [pk], [ptk])
                pso, pko = self.newps()
                for e in range(2):
                    for mb in range(2):
                        self.mm(pso[:, e * 8:e * 8 + 8], v16[:, mb, e * 128:(e + 1) * 128], pts[:, mb * 8:mb * 8 + 8], mb == 0, mb == 1,
                                [vk, ptk], [pko])
                for mb in range(2):
                    self.mm(pso[:, 16:24], ones16, pts[:, mb * 8:mb * 8 + 8], mb == 0, mb == 1, ["cst16", ptk], [pko])
                rd, rdk = self.scr()
                self.recip(rd[:, 0:8], pso[:, 16:24], [pko], [rdk])
                for e in range(2):
                    self.tt(oT[:, e, q0:q0 + 8], pso[:, e * 8:e * 8 + 8], rd[:, 0:8], ALU.mult, [pko, rdk],
                            ["oT%d.4" % e])
            wo = self.dr["w_xo"][l]
            for nb in range(8):
                self.linear(wo[nb][:, 2 * h:2 * h + 2, :], 2, lambda kc, c0, n: oT[:, kc, c0:c0 + n],
                            lambda kc, bi: ["oT%d.%d" % (kc, bi)], self.add_resid(nb))
        self.release(m0)

    def attn_unit_a(self, q_ap, qkeys, nq, blocks, acc_cols, bufs):
        pT, ptk = bufs
        ps, pk = self.newps()
        off = 0
        offs = []
        for (kT, vt, mk, nk, keys) in blocks:
            self.mm(ps[:nk, off:off + nq], kT, q_ap, True, True, keys + qkeys, [pk])
            offs.append(off)
            off += nq
        if len(blocks) == 2 and blocks[0][3] == 128 and blocks[1][3] == 128 and nq == 128:
            self.act(pT[:, 0:256], ps[:, 0:256], AF.Exp, [pk], [ptk])
            self.tt(pT[:, 0:256], pT[:, 0:256], self.mboth, ALU.mult, [ptk, "mboth"], [ptk])
        else:
            for bi_, (kT, vt, mk, nk, keys) in enumerate(blocks):
                o_ = offs[bi_]
                self.act(pT[:nk, o_:o_ + nq], ps[:nk, o_:o_ + nq], AF.Exp, [pk], [ptk])
                self.tt(pT[:nk, o_:o_ + nq], pT[:nk, o_:o_ + nq], mk, ALU.mult, [ptk, "cst16"], [ptk])
        return (nq, blocks, offs, pT, ptk, acc_cols)

    def attn_unit_b(self, ctx):
        nq, blocks, offs, pT, ptk, acc_cols = ctx
        pso, pko = self.newps()
        nb_ = len(blocks)
        for bi_, (kT, vt, mk, nk, keys) in enumerate(blocks):
            o_ = offs[bi_]
            self.mm(pso[:, 0:nq], vt, pT[:nk, o_:o_ + nq], bi_ == 0, bi_ == nb_ - 1, keys + [ptk], [pko])
        ones16 = self.cst16("ones")
        for bi_, (kT, vt, mk, nk, keys) in enumerate(blocks):
            o_ = offs[bi_]
            self.mm(pso[:, 128:128 + nq], ones16[:nk, :], pT[:nk, o_:o_ + nq], bi_ == 0, bi_ == nb_ - 1, ["cst16", ptk], [pko])
        src = pso[:, 0:256].rearrange("p (a b) -> p a b", b=128)[:, :, 0:nq]
        self.tt(acc_cols, src, acc_cols, ALU.add, [pko, "acc"], ["acc"])

    def attn(self, l, ja):
        self.dnorm("norm_mix%d" % l)
        m0 = self.mark()
        raw = [self.alloc(NT) for _ in range(2)]
        QT = self.alloc16(NT)
        KT = self.alloc16(NT)
        tok32 = [self.alloc(4 * 128).rearrange("p (b e) -> p b e", e=128) for _ in range(4)]
        vtok = self.alloc16(16 * 128).rearrange("p (b e) -> p b e", e=128)
        acc = self.alloc(2 * NT).rearrange("p (a t) -> p a t", t=NT)
        oT = self.alloc16(NT)
        NPT = 4
        DEPTH = 2
        pTb = [self.alloc16(256) for _ in range(NPT)]
        pend = []

        def unit(q_ap, qkeys, nq, blocks, acc_cols, ui):
            pend.append(self.attn_unit_a(q_ap, qkeys, nq, blocks, acc_cols, (pTb[ui % NPT], "pTb%d" % (ui % NPT))))
            if len(pend) > DEPTH:
                self.attn_unit_b(pend.pop(0))

        def flush():
            while pend:
                self.attn_unit_b(pend.pop(0))
        self.mboth = self.alloc16(256)
        cb_ = self.alloc(8 * 2 * 128).rearrange("p (r t e) -> p r t e", t=2, e=128)
        cv_ = self.alloc16(8 * 128).rearrange("p (r e) -> p r e", e=128)
        kcT = [self.alloc16(128) for _ in range(2)]
        sstg = self.alloc(2 * 128).rearrange("p (t e) -> p t e", e=128)
        vs16 = self.alloc16(128)
        gq = self.alloc(3)
        self.cp("dve", self.mboth[:, 0:128], self.cst16("m_own"), ["cst16"], ["mboth"])
        self.cp("dve", self.mboth[:, 128:256], self.cst16("m_prev"), ["cst16"], ["mboth"])
        for g in range(3):
            self.ts(gq[:, g:g + 1], self.pvc("aq_gain%d_%d" % (ja, g)), 128 ** -0.5, None, ALU.mult, None, ["pv"], ["gq"])
        ident = self.cst("ident")
        wqkv = self.dr["w_qkv"][ja]
        wo = self.dr["w_ao"][ja]
        caches = (self.dr["c128"], self.dr["c512"], self.dr["c2048"])
        outs_p = (self.dr["kv128_p"], self.dr["kv512_p"], self.dr["kv2048_p"])
        outs_s = (self.dr["kv128_s"], self.dr["kv512_s"], self.dr["kv2048_s"])
        for g in range(3):
            L = WINS[g]
            for s in range(4):
                if STAGES["a_d2d"]:
                    self.dma(outs_s[g][ja, s, 0:L - 8], caches[g][ja, s, 8:L], (), ())
        ui = 0
        ti = 0
        for h in range(4):
            self.memset(acc[:, :, :], 0.0, ["acc"])
            for g in range(STAGES["a_groups"]):
                dil = DILS[g]
                L = WINS[g]
                nkb = (2048 // dil) // 128

                def proj(si, dst, dkey):
                    blk = (si * 3 + g) * 4 + h

                    def cons_p(bi, c0, n, ps, pk):
                        self.cp("act", dst[:, c0:c0 + n], ps[:, :n], [pk], ["%s.%d" % (dkey, bi)])
                    self.linear(wqkv[blk], 8, self.h_rhs, self.h_keys, cons_p)
                proj(0, raw[0], "raw0")
                proj(1, raw[1], "raw1")
                for bi, (c0, n) in enumerate(TB):
                    rs, rk = self.rstd_block([raw[0][:, c0:c0 + n]], ["raw0.%d" % bi], n, 128, EPS)
                    self.stt(QT[:, c0:c0 + n], raw[0][:, c0:c0 + n], gq[:, g:g + 1], rs[:, :n], ALU.mult, ALU.mult,
                             ["raw0.%d" % bi, rk, "gq"], ["QT.%d" % bi])
                    rs, rk = self.rstd_block([raw[1][:, c0:c0 + n]], ["raw1.%d" % bi], n, 128, EPS)
                    self.stt(raw[1][:, c0:c0 + n], raw[1][:, c0:c0 + n], self.pvc("ak_gain%d_%d" % (ja, g)), rs[:, :n],
                             ALU.mult, ALU.mult, ["raw1.%d" % bi, rk, "pv"], ["raw1.%d" % bi])
                    self.cp("act", KT[:, c0:c0 + n], raw[1][:, c0:c0 + n], ["raw1.%d" % bi], ["KT.%d" % bi])
                proj(2, raw[0], "raw0")
                allq = ["QT.%d" % b for b in range(5)]
                allk = ["KT.%d" % b for b in range(5)]
                o_ = outs_p[g][ja]
                for si, rsrc, rkn in ((1, raw[1], "raw1"), (2, raw[0], "raw0")):
                    rkeys = ["%s.%d" % (rkn, b) for b in range(4)]
                    for b4 in range(4):
                        tk = tok32[ti % 4]
                        tkk = "tok32_%d" % (ti % 4)
                        ti += 1
                        ps, pk = self.newps()
                        for q in range(4):
                            idx = b4 * 4 + q
                            r_, kb = idx // nkb, idx % nkb
                            st_ = r_ + dil * 128 * kb
                            self.tr(ps[:, q * 128:(q + 1) * 128], rsrc[:, st_:st_ + dil * 127 + 1:dil], ident, rkeys + ["cst"], [pk])
                        self.cp("act", tk[:, :, :], ps[:, :].rearrange("p (q e) -> p q e", e=128), [pk], [tkk])
                        if si == 2:
                            self.cp("dve", vtok[:, b4 * 4:b4 * 4 + 4, :], tk[:, :, :], [tkk], ["vtok"])
                        if not STAGES["a_kvout"]:
                            pass
                        elif g == 0:
                            if b4 == 3:
                                self.dma(o_[:, si - 1, h, :], tk[:, 3, :], [tkk], ())
                        elif g == 1:
                            dst = o_.rearrange("(j r) t h e -> j r t h e", r=dil)[:, b4, si - 1, h, :]
                            self.dma(dst, tk[:, 3, :], [tkk], ())
                        else:
                            dst = o_.rearrange("(j r) t h e -> j r t h e", r=dil)[:, b4 * 4:b4 * 4 + 4, si - 1, h, :]
                            self.dma(dst, tk[:, :, :], [tkk], ())
                    ps, pk = self.newps()
                    self.tr(ps[:32, 0:128], rsrc[:, 2048:2080], ident, ["%s.4" % rkn, "cst"], [pk])
                    self.cp("act", sstg[:32, si - 1, :], ps[:32, 0:128], [pk], ["sstg"])
                    if si == 2:
                        self.cp("dve", vs16[:32, :], sstg[:32, 1, :], ["sstg"], ["vs16"])
                for s in range(4):
                    if STAGES["a_sout"]:
                        self.dma(outs_s[g][ja, s, L - 8:L, :, h, :], sstg[8 * s:8 * s + 8, :, :], ["sstg"], ())
                for r_ in range(dil if STAGES["a_pu"] else 0):
                    for qb in range(nkb):
                        st_ = r_ + dil * 128 * qb
                        sl_ = slice(st_, st_ + dil * 127 + 1, dil)
                        blocks = [(KT[:, sl_], vtok[:, r_ * nkb + qb, :], self.cst16("m_own"), 128, allk + ["vtok"])]
                        if qb > 0:
                            sp_ = r_ + dil * 128 * (qb - 1)
                            blocks.append((KT[:, sp_:sp_ + dil * 127 + 1:dil], vtok[:, r_ * nkb + qb - 1, :], self.cst16("m_prev"),
                                           128, allk + ["vtok"]))
                        unit(QT[:, sl_], allq, 128, blocks, acc[:, :, sl_], ui)
                        ui += 1
                R = min(dil, 8)
                nq = 8 // R
                for s in range(4 if STAGES["a_su"] else 0):
                    flush()
                    src = caches[g][ja, s].rearrange("(j r) t h e -> j r t h e", r=dil)[:, 0:R, :, h, :]
                    self.dma(cb_[:, 0:R, :, :], src, (), ["cb"])
                    self.cp("dve", cv_[:, 0:R, :], cb_[:, 0:R, 1, :], ["cb"], ["cv"])
                    for r_ in range(R):
                        kc_ = kcT[ui % 2]
                        kck = "kcT%d" % (ui % 2)
                        ps, pk = self.newps()
                        self.tr(ps[:, 0:128], cb_[:, r_, 0, :], ident, ["cb", "cst"], [pk])
                        self.cp("act", kc_[:, :], ps[:, 0:128], [pk], [kck])
                        q0 = 2048 + 8 * s + r_
                        sl_ = slice(q0, q0 + R * (nq - 1) + 1, R)
                        oc = own_s_col(self.cca, g, s, r_) - self.cca.cols["own_s"]
                        blocks = [(kc_[:, :], cv_[:, r_, :], self.cst16("m_prev", 128, nq), 128, [kck, "cv"]),
                                  (KT[:, 2048:2080], vs16[:32, :], self.cst16("own_s", 32, nq, oc), 32, allk + ["vs16"])]
                        unit(QT[:, sl_], allq, nq, blocks, acc[:, :, sl_], ui)
                        ui += 1
                flush()
            self.recip(acc[:, 1, :], acc[:, 1, :], ["acc"], ["acc"])
            for bi, (c0, n) in enumerate(TB):
                self.tt(oT[:, c0:c0 + n], acc[:, 0, c0:c0 + n], acc[:, 1, c0:c0 + n], ALU.mult, ["acc"], ["ao.%d" % bi])
            for nb in range(8):
                self.linear(wo[nb][:, h:h + 1, :], 1, lambda kc, c0, n: oT[:, c0:c0 + n], lambda kc, bi: ["ao.%d" % bi],
                            self.add_resid(nb))
        self.release(m0)

    def hgrn(self, l):
        self.dnorm("norm_mix%d" % l)
        m0 = self.mark()
        Bq, Bz, Bk, Bm, Bd, Bv, Bb = [self.alloc(NT) for _ in range(7)]
        Sb = [self.alloc(128) for _ in range(2)]
        NLA = 4
        ktok = [self.alloc16(128) for _ in range(NLA)]
        vtk = [self.alloc16(128) for _ in range(NLA)]
        ATb = [self.alloc16(64) for _ in range(NLA)]
        kvb_ = [self.alloc(128) for _ in range(NLA)]
        ident16 = self.cst16("ident")
        ebC = self.alloc(36)
        ex = self.alloc(32).rearrange("p (l c) -> p l c", c=8)
        lbv = self.alloc(8)
        omlb = self.alloc(8)
        tot = self.alloc(8)
        oTh = self.alloc16(NT)

        def K(n):
            return ["%s.%d" % (n, b) for b in range(5)]
        c_lb = self.pvca.cols["hg_lb0"]
        self.act(ex[:, :, :], self.pv[:, c_lb:c_lb + 32].rearrange("p (l c) -> p l c", c=8), AF.Exp, ["pv"], ["hex"])
        self.tt(tot, ex[:, 0, :], ex[:, 1, :], ALU.add, ["hex"], ["htot"])
        self.tt(tot, tot, ex[:, 2, :], ALU.add, ["hex", "htot"], ["htot"])
        self.tt(tot, tot, ex[:, 3, :], ALU.add, ["hex", "htot"], ["htot"])
        self.cp("dve", lbv, ex[:, 1, :], ["hex"], ["hlb"])
        for l2 in range(2, l + 1):
            self.tt(lbv, lbv, ex[:, l2, :], ALU.add, ["hex", "hlb"], ["hlb"])
        self.recip(tot, tot, ["htot"], ["htot"])
        self.tt(lbv, lbv, tot, ALU.mult, ["hlb", "htot"], ["hlb"])
        self.ts(omlb, lbv, -1.0, 1.0, ALU.mult, ALU.add, ["hlb"], ["homlb"])
        ident = self.cst("ident")
        m_own = self.cst("m_own")
        win = self.dr["w_hgin"]
        wo = self.dr["w_hgo"]
        ci = 0
        si = 0
        for h in range(8):
            def proj(blk, dst, dkey):
                def cons_p(bi, c0, n, ps, pk):
                    self.cp("act", dst[:, c0:c0 + n], ps[:, :n], [pk], ["%s.%d" % (dkey, bi)])
                self.linear(win[blk], 8, self.h_rhs, self.h_keys, cons_p)
            proj(h, Bq, "hq")
            proj(8 + h, Bz, "hz")
            proj(16 + h, Bv, "hv")
            self.act(Bz[:, :], Bz[:, :], AF.Sigmoid, K("hz"), K("hz"))
            self.ts(Bz[:, :], Bz[:, :], omlb[:, h:h + 1], lbv[:, h:h + 1], ALU.mult, ALU.add, K("hz") + ["hlb", "homlb"], K("hz"))
            self.ts(Bk[:, :], Bz[:, :], -1.0, 1.0, ALU.mult, ALU.add, K("hz"), K("hk"))
            self.act(Bz[:, :], Bz[:, :], AF.Ln, K("hz"), K("hz"))
            self.memset(Bm[:, :], 1.0, K("hm"))
            self.memset(Bm[:, 0:2048:64], 0.0, K("hm"))
            self.memset(Bm[:, 2048:2080:8], 0.0, K("hm"))
            self.P.add("dve", lambda e: e.tensor_tensor_scan(Bb[:, :], Bm[:, :], Bz[:, :], 0.0, ALU.mult, ALU.add),
                       K("hm") + K("hz"), K("hb"))
            self.act(ebC[:, 0:32], Bb[:, 63:2048:64], AF.Exp, K("hb"), ["hebc"])
            self.act(ebC[:, 32:36], Bb[:, 2055:2080:8], AF.Exp, K("hb"), ["hebc"])
            self.act(Bq[:, :], Bq[:, :], AF.Silu, K("hq"), K("hq"))
            self.act(Bz[:, :], Bb[:, :], AF.Exp, K("hb") + K("hz"), K("hz"))
            self.tt(Bq[:, :], Bq[:, :], Bz[:, :], ALU.mult, K("hq") + K("hz"), K("hq"))
            Z16 = Bz[:, :].bitcast(BF16)
            qe16 = Z16[:, 0:NT]
            v16 = Z16[:, NT:2 * NT]
            self.cp("act", qe16, Bq[:, :], K("hq") + K("hz"), K("hz"))
            self.cp("dve", v16, Bv[:, :], K("hv") + K("hz"), K("hz"))
            bp = Bb[:, 0:2048].rearrange("p (n c) -> p n c", c=64)
            self.tt(Bd[:, 0:2048].rearrange("p (n c) -> p n c", c=64), bp[:, :, 63:64].to_broadcast([128, 32, 64]), bp, ALU.subtract,
                    K("hb"), K("hd"))
            bs = Bb[:, 2048:2080].rearrange("p (n c) -> p n c", c=8)
            self.tt(Bd[:, 2048:2080].rearrange("p (n c) -> p n c", c=8), bs[:, :, 7:8].to_broadcast([128, 4, 8]), bs, ALU.subtract,
                    K("hb"), K("hd"))
            self.act(Bd[:, :], Bd[:, :], AF.Exp, K("hd"), K("hd"))
            self.act(Bm[:, :], Bb[:, :], AF.Exp, K("hb") + K("hm"), K("hm"), scale=-1.0)
            B16 = Bb[:, :].bitcast(BF16)
            ke16 = B16[:, 0:NT]
            kd16 = B16[:, NT:2 * NT]
            self.tt(ke16, Bm[:, :], Bk[:, :], ALU.mult, K("hm") + K("hk") + K("hb"), K("hb"))
            self.tt(kd16, Bd[:, :], Bk[:, :], ALU.mult, K("hd") + K("hk") + K("hb"), K("hb"))
            psT = [p_[:, :].bitcast(BF16) for p_ in self.ps]

            def chunk_a(c0, C, bi):
                nonlocal ci
                kt, ktk = ktok[ci % NLA], "hkt%d" % (ci % NLA)
                vt, vtkk = vtk[ci % NLA], "hvt%d" % (ci % NLA)
                AT, atk = ATb[ci % NLA], "hat%d" % (ci % NLA)
                kvb, kvk = kvb_[ci % NLA], "hkv%d" % (ci % NLA)
                ci += 1
                i1 = self.psi % 8
                ps, pk = self.newps()
                self.tr(psT[i1][:C, 0:128], kd16[:, c0:c0 + C], ident16, K("hb") + ["cst16"], [pk])
                self.tr(psT[i1][:C, 128:256], v16[:, c0:c0 + C], ident16, K("hz") + ["cst16"], [pk])
                self.cp("act", kt[:C, :], psT[i1][:C, 0:128], [pk], [ktk])
                self.cp("act", vt[:C, :], psT[i1][:C, 128:256], [pk], [vtkk])
                ps2, pk2 = self.newps()
                self.mm(ps2[:C, :C], ke16[:, c0:c0 + C], qe16[:, c0:c0 + C], True, True, K("hb") + K("hz"), [pk2])
                self.tt(AT[:C, :C], ps2[:C, :C], m_own[:C, :C], ALU.mult, [pk2, "cst"], [atk])
                ps4, pk4 = self.newps()
                self.mm(ps4[:, 0:128], kt[:C, :], vt[:C, :], True, True, [ktk, vtkk], [pk4])
                self.cp("act", kvb[:, :], ps4[:, 0:128], [pk4], [kvk])
                return (c0, C, bi, vt, vtkk, AT, atk, kvb, kvk)

            def chunk_b(ctx, S, Sk, ecol):
                c0, C, bi, vt, vtkk, AT, atk, kvb, kvk = ctx
                ps3, pk3 = self.newps()
                self.mm(ps3[:, :C], S[:, :], Bq[:, c0:c0 + C], True, False, [Sk, "hq.%d" % bi], [pk3])
                self.mm(ps3[:, :C], vt[:C, :], AT[:C, :C], False, True, [vtkk, atk], [pk3])
                self.cp("act", Bk[:, c0:c0 + C], ps3[:, :C], [pk3], ["hk.%d" % bi])
                self.stt(S[:, :], S[:, :], ebC[:, ecol:ecol + 1], kvb[:, :], ALU.mult, ALU.add, [Sk, "hebc", kvk], [Sk])
            S, Sk = Sb[si % 2], "hS%d" % (si % 2)
            si += 1
            self.memset(S[:, :], 0.0, [Sk])
            LOOK = 2
            ctxs = {}
            for n_ in range(32 + LOOK):
                if n_ < 32:
                    ctxs[n_] = chunk_a(64 * n_, 64, n_ // 8)
                if n_ >= LOOK:
                    chunk_b(ctxs.pop(n_ - LOOK), S, Sk, n_ - LOOK)
            self.dma(self.dr["hg_p"][h], S[:, :], [Sk], ())
            sctx = [chunk_a(2048 + 8 * s, 8, 4) for s in range(2)]
            for s in range(4):
                S, Sk = Sb[si % 2], "hS%d" % (si % 2)
                si += 1
                self.dma(S[:, :], self.dr["st_hg"][s, h], (), [Sk])
                chunk_b(sctx.pop(0), S, Sk, 32 + s)
                if s + 2 < 4:
                    sctx.append(chunk_a(2048 + 8 * (s + 2), 8, 4))
                self.dma(self.dr["hg_s"][s, h], S[:, :], [Sk], ())
            proj(24 + h, Bq, "hq")
            self.act(Bq[:, :], Bq[:, :], AF.Silu, K("hq"), K("hq"))
            for bi, (c0, n) in enumerate(TB):
                rs, rk = self.rstd_block([Bk[:, c0:c0 + n]], ["hk.%d" % bi], n, 128, EPS)
                self.stt(Bk[:, c0:c0 + n], Bk[:, c0:c0 + n], self.pvc("hg_out_gain"), rs[:, :n], ALU.mult, ALU.mult,
                         ["hk.%d" % bi, rk, "pv"], ["hk.%d" % bi])
                self.tt(oTh[:, c0:c0 + n], Bk[:, c0:c0 + n], Bq[:, c0:c0 + n], ALU.mult, ["hk.%d" % bi, "hq.%d" % bi], ["hoT.%d" % bi])
            for nb in range(8):
                self.linear(wo[nb][:, h:h + 1, :], 1, lambda kc, c0, n: oTh[:, c0:c0 + n], lambda kc, bi: ["hoT.%d" % bi],
                            self.add_resid(nb))
        self.release(m0)

    def rwkv(self, l):
        self.dnorm("norm_mix%d" % l)
        m0 = self.mark()
        NCH = 4
        ident = self.cst("ident")
        blk64 = self.cst("blk64")
        pvc = self.pvc

        def hk2(c0, n, kc):
            b0 = min(max(c0 - 1, 0) // 512, 4)
            b1 = min((c0 + n - 1) // 512, 4)
            return ["h%d.%d" % (kc, b) for b in sorted({b0, b1})]
        xl = self.alloc(40).rearrange("p (c t) -> p c t", t=5)
        self.cp("dve", xl[:, :, 0:1], self.xT[:, :, 2047:2048], ["x%d.3" % c for c in range(8)], ["xl"])
        self.cp("dve", xl[:, :, 1:5], self.xT[:, :, 2055:2080:8], ["x%d.4" % c for c in range(8)], ["xl"])
        rs, rk = self.rstd_block([xl[:, c, :] for c in range(8)], ["xl"] * 8, 5, D, EPS)
        for c in range(8):
            self.stt(xl[:, c, :], xl[:, c, :], pvc("norm_mix%d" % l, c), rs[:, 0:5], ALU.mult, ALU.mult, ["xl", rk, "pv"], ["xl"])
        self.dma(self.dr["sh_o"], xl, ["xl"], ())
        hsp = self.alloc16(8 * 32).rearrange("p (c t) -> p c t", t=32)
        shs = self.alloc(32).rearrange("p (c s) -> p c s", s=4)
        self.dma(shs, self.dr["st_sh"], (), ["shs"])
        self.cp("dve", hsp[:, :, 0:32:8], shs, ["shs"], ["hsp"])
        for c in range(8):
            self.cp("dve", hsp[:, c, :].rearrange("p (s i) -> p s i", i=8)[:, :, 1:8],
                    self.hT[:, c, 2048:2080].rearrange("p (s i) -> p s i", i=8)[:, :, 0:7], ["h%d.4" % c], ["hsp"])
        omu = self.alloc(48)
        cmu = self.pvca.cols["rw_mu0"]
        mu = self.pv[:, cmu:cmu + 48]
        self.ts(omu, mu, -1.0, 1.0, ALU.mult, ALU.add, ["pv"], ["omu"])
        omka = self.alloc(8)
        self.ts(omka, pvc("rw_k_a", 0, 8), -1.0, 1.0, ALU.mult, ALU.add, ["pv"], ["omka"])
        wst = self.alloc(8 * 160)

        def proj(ps, pk, m, wA, wB, wkeys, c0, n, sample):
            for kc in range(8):
                self.mm(ps[:m, :n], wA(kc), self.hT[:, kc, c0:c0 + n], kc == 0, False, wkeys + hk2(c0, n, kc), [pk])
            for kc in range(8):
                last = kc == 7
                if sample:
                    self.mm(ps[:m, :n], wB(kc), hsp[:, kc, :], False, last, wkeys + ["hsp"], [pk])
                elif c0 == 0:
                    self.mm(ps[:m, 1:n], wB(kc), self.hT[:, kc, 0:n - 1], False, last, wkeys + hk2(c0, n, kc), [pk])
                else:
                    self.mm(ps[:m, :n], wB(kc), self.hT[:, kc, c0 - 1:c0 - 1 + n], False, last, wkeys + hk2(c0, n, kc), [pk])

        def scaled(dstA, dstB, src, ncol, j, keyA, keyB, skey):
            mj = mu[:, 8 * j:8 * j + 8].unsqueeze(2).to_broadcast([128, 8, ncol])
            oj = omu[:, 8 * j:8 * j + 8].unsqueeze(2).to_broadcast([128, 8, ncol])
            self.tt(dstA, src, oj, ALU.mult, [skey, "omu"], [keyA])
            self.tt(dstB, src, mj, ALU.mult, [skey, "pv"], [keyB])
        TWA = self.alloc16(NT)
        TG1 = self.alloc16(NT)
        TG2 = self.alloc16(NT)
        l1A = self.alloc16(8 * 160).rearrange("p (k n) -> p k n", n=160)
        l1B = self.alloc16(8 * 160).rearrange("p (k n) -> p k n", n=160)
        w64 = wst[:, 0:8 * 64].rearrange("p (k n) -> p k n", n=64)
        for (nm, j, lo) in (("rw1", 3, 0), ("ra1", 4, 64)):
            self.dma(w64, self.dr[nm], (), ["wst"])
            scaled(l1A[:, :, lo:lo + 64], l1B[:, :, lo:lo + 64], w64, 64, j, "l1A", "l1B", "wst")
        for bi, (c0, n) in enumerate(TB):
            ps, pk = self.newps()
            proj(ps, pk, 128, lambda kc: l1A[:, kc, 0:128], lambda kc: l1B[:, kc, 0:128], ["l1A", "l1B"], c0, n, bi == 4)
            self.act(TWA[0:64, c0:c0 + n], ps[0:64, :n], AF.Tanh, [pk], ["twa.%d" % bi])
            self.cp("act", TWA[64:128, c0:c0 + n], ps[64:128, :n], [pk], ["twa.%d" % bi])
        w160 = wst[:, 0:8 * 160].rearrange("p (k n) -> p k n", n=160)
        self.dma(w160, self.dr["rg1"], (), ["wst"])
        scaled(l1A[:, :, :], l1B[:, :, :], w160, 160, 5, "l1A", "l1B", "wst")
        for bi, (c0, n) in enumerate(TB):
            ps, pk = self.newps()
            proj(ps, pk, 128, lambda kc: l1A[:, kc, 0:128], lambda kc: l1B[:, kc, 0:128], ["l1A", "l1B"], c0, n, bi == 4)
            self.act(TG1[:, c0:c0 + n], ps[:, :n], AF.Sigmoid, [pk], ["tg.%d" % bi])
            ps, pk = self.newps()
            proj(ps, pk, 32, lambda kc: l1A[:, kc, 128:160], lambda kc: l1B[:, kc, 128:160], ["l1A", "l1B"], c0, n, bi == 4)
            self.act(TG2[0:32, c0:c0 + n], ps[0:32, :n], AF.Sigmoid, [pk], ["tg.%d" % bi])
        W2A = self.alloc16(128)
        G2a = self.alloc16(128)
        G2b = self.alloc16(128)
        yT = self.alloc16(NT)
        NB_ = 256
        Rr, Rk, Rv, Rld, Ra, Rg, Rkk, Rk2, Rcum, Rt1, Rt2, Rbon, Ry = [self.alloc(NB_) for _ in range(13)]
        blk = [self.alloc(NCH * 128).rearrange("p (j t) -> p j t", t=128) for _ in range(5)]
        Ablk, Bblk, Kblk, Rblk, Vblk = blk
        for b_ in blk:
            self.memset(b_[:, :, :], 0.0, ["blk"])
        WSk = [self.alloc(NCH * 128).rearrange("p (j t) -> p j t", t=128) for _ in range(9)]
        STb = [self.alloc(128) for _ in range(2)]
        DC = self.alloc(NCH)
        maskp = self.alloc(NB_)
        masks = self.alloc(32)
        self.memset(maskp, 1.0, ["maskp"])
        self.memset(maskp[:, 0:NB_:64], 0.0, ["maskp"])
        self.memset(masks, 1.0, ["masks"])
        self.memset(masks[:, 0:32:8], 0.0, ["masks"])
        bd_su, bd_sl, bd_iu = self.cst("bd_su"), self.cst("bd_sl"), self.cst("bd_iu")
        RB = [(NB_ * i, NB_, False) for i in range(2048 // NB_)] + [(2048, 32, True)]
        sti = 0
        wo = self.dr["w_rwo"]
        for p in range(8):
            w128 = wst[:, 0:1024].rearrange("p (k n) -> p k n", n=128)
            for j in range(3):
                self.dma(w128, self.dr["w_rkv"][j, p], (), ["wst"])
                scaled(self.wb[2 * j][:, :, :], self.wb[2 * j + 1][:, :, :], w128, 128, j, "wb%d" % (2 * j), "wb%d" % (2 * j + 1), "wst")
            self.dma(W2A[0:64, :], self.dr["rw2"][:, p * 128:(p + 1) * 128], (), ["w2a"], q="pool")
            self.dma(W2A[64:128, :], self.dr["ra2"][:, p * 128:(p + 1) * 128], (), ["w2a"], q="pool")
            self.dma(G2a[:, :], self.dr["rg2"][0:128, p * 128:(p + 1) * 128], (), ["g2"], q="pool")
            self.dma(G2b[0:32, :], self.dr["rg2"][128:160, p * 128:(p + 1) * 128], (), ["g2"], q="pool")
            ST, STk = STb[sti % 2], "rST%d" % (sti % 2)
            sti += 1
            self.memset(ST[:, :], 0.0, [STk])
            for (c0, n, smp) in RB:
                bi = min(c0 // 512, 4)
                C = 8 if smp else 64
                nch = 4
                for j, dst, dk in ((0, Rr, "Rr"), (1, Rk, "Rk"), (2, Rv, "Rv")):
                    ps, pk = self.newps()
                    proj(ps, pk, 128, lambda kc, j=j: self.wb[2 * j][:, kc, :], lambda kc, j=j: self.wb[2 * j + 1][:, kc, :],
                         ["wb%d" % (2 * j), "wb%d" % (2 * j + 1)], c0, n, smp)
                    self.cp("act", dst[:, :n], ps[:, :n], [pk], [dk])
                ps, pk = self.newps()
                self.mm(ps[:, :n], W2A[0:64, :], TWA[0:64, c0:c0 + n], True, True, ["w2a", "twa.%d" % bi], [pk])
                self.act(Rld[:, :n], ps[:, :n], AF.Sigmoid, [pk, "pv"], ["Rld"], bias=pvc("rw_w0", p))
                self.ts(Rld[:, :n], Rld[:, :n], -0.6065306597126334, None, ALU.mult, None, ["Rld"], ["Rld"])
                ps, pk = self.newps()
                self.mm(ps[:, :n], W2A[64:128, :], TWA[64:128, c0:c0 + n], True, True, ["w2a", "twa.%d" % bi], [pk])
                self.act(Ra[:, :n], ps[:, :n], AF.Sigmoid, [pk, "pv"], ["Ra"], bias=pvc("rw_a0", p))
                ps, pk = self.newps()
                self.mm(ps[:, :n], G2a[:, :], TG1[:, c0:c0 + n], True, False, ["g2", "tg.%d" % bi], [pk])
                self.mm(ps[:, :n], G2b[0:32, :], TG2[0:32, c0:c0 + n], False, True, ["g2", "tg.%d" % bi], [pk])
                self.cp("act", Rg[:, :n], ps[:, :n], [pk], ["Rg"])
                self.ts(Rkk[:, :n], Rk[:, :n], pvc("rw_k_k", p), None, ALU.mult, None, ["Rk", "pv"], ["Rkk"])
                self.act(Rt1[:, :n], Rkk[:, :n], AF.Square, ["Rkk"], ["Rt1"])
                ps, pk = self.newps()
                self.mm(ps[:, :n], blk64, Rt1[:, :n], True, True, ["cst", "Rt1"], [pk])
                self.act(Rt2[:, :n], ps[:, :n], AF.Sqrt, [pk], ["Rt2"])
                self.ts(Rt2[:, :n], Rt2[:, :n], 1e-12, None, ALU.max, None, ["Rt2"], ["Rt2"])
                self.recip(Rt2[:, :n], Rt2[:, :n], ["Rt2"], ["Rt2"])
                self.tt(Rkk[:, :n], Rkk[:, :n], Rt2[:, :n], ALU.mult, ["Rkk", "Rt2"], ["Rkk"])
                self.ts(Rt1[:, :n], Ra[:, :n], pvc("rw_k_a", p), omka[:, p:p + 1], ALU.mult, ALU.add, ["Ra", "pv", "omka", "Rt1"], ["Rt1"])
                self.tt(Rk2[:, :n], Rk[:, :n], Rt1[:, :n], ALU.mult, ["Rk", "Rt1"], ["Rk2"])
                self.tt(Rt1[:, :n], Rr[:, :n], Rk2[:, :n], ALU.mult, ["Rr", "Rk2", "Rt1"], ["Rt1"])
                self.ts(Rt1[:, :n], Rt1[:, :n], pvc("rw_r_k", p), None, ALU.mult, None, ["Rt1", "pv"], ["Rt1"])
                ps, pk = self.newps()
                self.mm(ps[:, :n], blk64, Rt1[:, :n], True, True, ["cst", "Rt1"], [pk])
                self.tt(Rbon[:, :n], ps[:, :n], Rv[:, :n], ALU.mult, [pk, "Rv"], ["Rbon"])
                mk_ = masks if smp else maskp
                mkk = "masks" if smp else "maskp"
                self.P.add("dve", lambda e, n=n, mk_=mk_: e.tensor_tensor_scan(Rcum[:, :n], mk_[:, :n], Rld[:, :n], 0.0, ALU.mult, ALU.add),
                           [mkk, "Rld"], ["Rcum"])
                self.act(DC[:, 0:nch], Rcum[:, C - 1:n:C], AF.Exp, ["Rcum"], ["DC"])
                if smp:
                    for b_ in blk:
                        self.memset(b_[:, :, :], 0.0, ["blk"])

                def toblk(dst, fn, rkeys):
                    for hh in range(2):
                        rows = slice(64 * hh, 64 * hh + 64)
                        ov = dst[rows, 0:nch, 64 * hh:64 * hh + C]
                        fn(ov, rows, lambda x: x[rows, 0:n].rearrange("p (j s) -> p j s", s=C))
                self.act(Rt1[:, :n], Rcum[:, :n], AF.Exp, ["Rcum", "Rt1"], ["Rt1"])
                toblk(Rblk, lambda ov, rows, V: self.tt(ov, V(Rr), V(Rt1), ALU.mult, ["Rr", "Rt1"], ["blk"]), None)
                self.act(Rt1[:, :n], Rcum[:, :n], AF.Exp, ["Rcum", "Rt1", "blk"], ["Rt1"], scale=-1.0)
                self.tt(Rt2[:, :n], Rkk[:, :n], Ra[:, :n], ALU.mult, ["Rkk", "Ra", "Rt2"], ["Rt2"])
                toblk(Bblk, lambda ov, rows, V: self.tt(ov, V(Rt2), V(Rt1), ALU.mult, ["Rt2", "Rt1"], ["blk"]), None)
                toblk(Kblk, lambda ov, rows, V: self.tt(ov, V(Rk2), V(Rt1), ALU.mult, ["Rk2", "Rt1"], ["blk"]), None)
                self.tt(Rt2[:, :n], Rcum[:, :n], Rld[:, :n], ALU.subtract, ["Rcum", "Rld", "Rt2", "blk"], ["Rt2"])
                self.act(Rt2[:, :n], Rt2[:, :n], AF.Exp, ["Rt2"], ["Rt2"])
                toblk(Ablk, lambda ov, rows, V: self.stt(ov, V(Rkk), -1.0, V(Rt2), ALU.mult, ALU.mult, ["Rkk", "Rt2"], ["blk"]), None)
                toblk(Vblk, lambda ov, rows, V: self.cp("act", ov, V(Rv), ["Rv"], ["blk"]), None)
                def wk(i):
                    return "rws%d" % i

                def bc(m):
                    return m.unsqueeze(1).to_broadcast([128, nch, 128])

                def v4(ps):
                    return ps[:, 0:512].rearrange("p (j t) -> p j t", t=128)
                psa, pka = self.newps()
                psb, pkb = self.newps()
                for j in range(nch):
                    self.mm(psa[:, j * 128:(j + 1) * 128], Bblk[:, j, :], Ablk[:, j, :], True, True, ["blk"], [pka])
                    self.mm(psb[:, j * 128:(j + 1) * 128], Ablk[:, j, :], Bblk[:, j, :], True, True, ["blk"], [pkb])
                self.tt(WSk[0][:, :, :], v4(psa), bc(bd_su), ALU.mult, [pka, "cst"], [wk(0)])
                self.tt(WSk[1][:, :, :], v4(psb), bc(bd_sl), ALU.mult, [pkb, "cst"], [wk(1)])
                self.tt(WSk[4][:, :, :], WSk[0][:, :, :], bc(ident), ALU.add, [wk(0), "cst"], [wk(4)], eng="pool")
                cur = (0, 1)
                nxt = (2, 3)
                for step in range(5):
                    ia, iat = cur
                    in_, int_ = nxt
                    psa, pka = self.newps()
                    if step < 4:
                        psb, pkb = self.newps()
                    for j in range(nch):
                        self.mm(psa[:, j * 128:(j + 1) * 128], WSk[ia][:, j, :], WSk[iat][:, j, :], True, True, [wk(ia), wk(iat)], [pka])
                        if step < 4:
                            self.mm(psb[:, j * 128:(j + 1) * 128], WSk[iat][:, j, :], WSk[ia][:, j, :], True, True, [wk(ia), wk(iat)], [pkb])
                    self.cp("act", WSk[int_][:, :, :], v4(psa), [pka], [wk(int_)])
                    if step < 4:
                        self.cp("act", WSk[in_][:, :, :], v4(psb), [pkb], [wk(in_)])
                    psc, pkc = self.newps()
                    for j in range(nch):
                        self.mm(psc[:, j * 128:(j + 1) * 128], WSk[int_][:, j, :], WSk[4][:, j, :], True, True, [wk(int_), wk(4)], [pkc])
                    self.tt(WSk[4][:, :, :], v4(psc), WSk[4][:, :, :], ALU.add, [pkc, wk(4)], [wk(4)])
                    cur, nxt = nxt, cur
                ps1, pk1 = self.newps()
                ps2, pk2 = self.newps()
                ps3, pk3 = self.newps()
                for j in range(nch):
                    self.mm(ps1[:, j * 128:(j + 1) * 128], Kblk[:, j, :], Ablk[:, j, :], True, True, ["blk"], [pk1])
                    self.mm(ps2[:, j * 128:(j + 1) * 128], Bblk[:, j, :], Rblk[:, j, :], True, True, ["blk"], [pk2])
                    self.mm(ps3[:, j * 128:(j + 1) * 128], Kblk[:, j, :], Rblk[:, j, :], True, True, ["blk"], [pk3])
                self.tt(WSk[0][:, :, :], v4(ps1), bc(bd_su), ALU.mult, [pk1, "cst"], [wk(0)])
                self.tt(WSk[1][:, :, :], v4(ps2), bc(bd_iu), ALU.mult, [pk2, "cst"], [wk(1)])
                self.tt(WSk[2][:, :, :], v4(ps3), bc(bd_iu), ALU.mult, [pk3, "cst"], [wk(2)])
                ps1, pk1 = self.newps()
                ps2, pk2 = self.newps()
                ps3, pk3 = self.newps()
                for j in range(nch):
                    self.tr(ps1[:, j * 128:(j + 1) * 128], Vblk[:, j, :], ident, ["blk", "cst"], [pk1])
                    self.tr(ps2[:, j * 128:(j + 1) * 128], Bblk[:, j, :], ident, ["blk", "cst"], [pk2])
                    self.tr(ps3[:, j * 128:(j + 1) * 128], Kblk[:, j, :], ident, ["blk", "cst"], [pk3])
                self.cp("act", WSk[3][:, :, :], v4(ps1), [pk1], [wk(3)])
                self.cp("act", WSk[5][:, :, :], v4(ps2), [pk2], [wk(5)])
                self.cp("act", WSk[6][:, :, :], v4(ps3), [pk3], [wk(6)])
                Mk_ = [WSk[0][:, j, :] for j in range(nch)]
                Nb_ = [WSk[1][:, j, :] for j in range(nch)]
                Nk_ = [WSk[2][:, j, :] for j in range(nch)]
                Vt_ = [WSk[3][:, j, :] for j in range(nch)]
                P_ = [WSk[4][:, j, :] for j in range(nch)]
                Bt_ = [WSk[5][:, j, :] for j in range(nch)]
                Kt_ = [WSk[6][:, j, :] for j in range(nch)]
                XT_ = [WSk[7][:, j, :] for j in range(nch)]
                UT_ = [WSk[8][:, j, :] for j in range(nch)]

                def wkj(j, i):
                    return "rws%d" % i if i < 7 else "rws%d.%d" % (i, j)
                for j in range(nch):
                    if smp:
                        ST, STk = STb[sti % 2], "rST%d" % (sti % 2)
                        sti += 1
                        self.memset(ST[:, :], 0.0, [STk])
                        for hh in range(2):
                            self.dma(ST[64 * hh:64 * hh + 64, 64 * hh:64 * hh + 64], self.dr["st_rw"][j, 2 * p + hh], (), [STk])
                    ps, pk = self.newps()
                    self.mm(ps[:, 0:128], Ablk[:, j, :], ST[:, :], True, False, ["blk", STk], [pk])
                    self.mm(ps[:, 0:128], Mk_[j], Vt_[j], False, True, [wkj(j, 0), wkj(j, 3)], [pk])
                    self.cp("act", XT_[j], ps[:, 0:128], [pk], [wkj(j, 7)])
                    ps, pk = self.newps()
                    self.mm(ps[:, 0:128], P_[j], XT_[j], True, True, [wkj(j, 4), wkj(j, 7)], [pk])
                    self.cp("dve", UT_[j], ps[:, 0:128], [pk], [wkj(j, 8)])
                    ps, pk = self.newps()
                    self.mm(ps[:, 0:128], ST[:, :], Rblk[:, j, :], True, False, [STk, "blk"], [pk])
                    self.mm(ps[:, 0:128], UT_[j], Nb_[j], False, False, [wkj(j, 8), wkj(j, 1)], [pk])
                    self.mm(ps[:, 0:128], Vt_[j], Nk_[j], False, True, [wkj(j, 3), wkj(j, 2)], [pk])
                    for hh in range(2):
                        self.cp("act", Ry[64 * hh:64 * hh + 64, j * C:(j + 1) * C], ps[64 * hh:64 * hh + 64, 64 * hh:64 * hh + C], [pk], ["Ry"])
                    ps, pk = self.newps()
                    self.mm(ps[:, 0:128], Bt_[j], UT_[j], True, False, [wkj(j, 5), wkj(j, 8)], [pk])
                    self.mm(ps[:, 0:128], Kt_[j], Vt_[j], False, True, [wkj(j, 6), wkj(j, 3)], [pk])
                    self.tt(ST[:, :], ST[:, :], ps[:, 0:128], ALU.add, [STk, pk], [STk])
                    self.act(ST[:, :], ST[:, :], AF.Identity, [STk, "DC"], [STk], scale=DC[:, j:j + 1])
                    if smp:
                        for hh in range(2):
                            self.dma(self.dr["rw_s"][j, 2 * p + hh], ST[64 * hh:64 * hh + 64, 64 * hh:64 * hh + 64], [STk], ())
                if (not smp) and c0 + n == 2048:
                    for hh in range(2):
                        self.dma(self.dr["rw_p"][2 * p + hh], ST[64 * hh:64 * hh + 64, 64 * hh:64 * hh + 64], [STk], ())
                ps, pk = self.newps()
                self.mm(ps[:, :n], blk64, Ry[:, :n], True, True, ["cst", "Ry"], [pk])
                self.stt(Rt1[:, :n], ps[:, :n], -1.0 / 64, Ry[:, :n], ALU.mult, ALU.add, [pk, "Ry", "Rt1"], ["Rt1"])
                self.act(Rt2[:, :n], Rt1[:, :n], AF.Square, ["Rt1", "Rt2"], ["Rt2"])
                ps, pk = self.newps()
                self.mm(ps[:, :n], blk64, Rt2[:, :n], True, True, ["cst", "Rt2"], [pk])
                self.act(Rt2[:, :n], ps[:, :n], AF.Sqrt, [pk, "epsc"], ["Rt2"], scale=1.0 / 64, bias=self.epsc[:, 1:2])
                self.recip(Rt2[:, :n], Rt2[:, :n], ["Rt2"], ["Rt2"])
                self.tt(Rt1[:, :n], Rt1[:, :n], Rt2[:, :n], ALU.mult, ["Rt1", "Rt2"], ["Rt1"])
                self.ts(Rt1[:, :n], Rt1[:, :n], pvc("rw_ln_g", p), pvc("rw_ln_b", p), ALU.mult, ALU.add, ["Rt1", "pv"], ["Rt1"])
                self.tt(Rt1[:, :n], Rt1[:, :n], Rbon[:, :n], ALU.add, ["Rt1", "Rbon"], ["Rt1"])
                self.tt(yT[:, c0:c0 + n], Rt1[:, :n], Rg[:, :n], ALU.mult, ["Rt1", "Rg"], ["ryT.%d" % bi])
            for nb in range(8):
                self.linear(wo[nb][:, p:p + 1, :], 1, lambda kc, c0, n: yT[:, c0:c0 + n], lambda kc, bi: ["ryT.%d" % bi],
                            self.add_resid(nb))
        self.release(m0)

    def build(self):
        with contextlib.ExitStack() as st:
            self.setup(st)
            self.load_x()
            self.mem_prep()
            for l in range(STAGES["layers"]):
                kind = l % 3
                if kind == 0:
                    if STAGES["attn"]:
                        self.attn(l, l // 3)
                elif kind == 1:
                    if STAGES["hgrn"]:
                        self.hgrn(l)
                else:
                    if STAGES["rwkv"]:
                        self.rwkv(l)
                if STAGES["mem"]:
                    self.mem_attend(l)
                if STAGES["ffn"]:
                    self.ffn(l)
            self.store_y()
            self.P.emit()


IN_SPECS = [
    ("xp", [2048, 1024]), ("xs", [32, 1024]), ("mem", [256, 1024]),
    ("c128", [2, 4, 128, 2, 4, 128]), ("c512", [2, 4, 512, 2, 4, 128]), ("c2048", [2, 4, 2048, 2, 4, 128]),
    ("st_hg", [4, 8, 128, 128]), ("st_rw", [4, 16, 64, 64]), ("st_sh", [128, 8, 4]),
    ("st_fc", [4, 128, NJ, 4, 2]), ("cmem", [4, 4, 256, 2, 4, 256]),
    ("w_qkv", [2, 36, 128, 8, 128]), ("w_ao", [2, 8, 128, 4, 128]),
    ("w_hgin", [32, 128, 8, 128]), ("w_hgo", [8, 128, 8, 128]),
    ("w_rkv", [3, 8, 128, 8, 128]), ("rw1", [128, 8, 64]), ("ra1", [128, 8, 64]), ("rg1", [128, 8, 160]),
    ("rw2", [64, 1024]), ("ra2", [64, 1024]), ("rg2", [160, 1024]), ("w_rwo", [8, 128, 8, 128]),
    ("w_xq", [4, 8, 128, 8, 128]), ("w_xkv", [4, 16, 128, 8, 128]), ("w_xo", [4, 8, 128, 8, 128]),
    ("w_fin", [4, 2 * NJ, 128, 8, 128]), ("w_fdn", [4, 8, 128, NJ, 128]),
]
OUT_SPECS = [
    ("y_p", [2048, 1024]), ("y_s", [32, 1024]),
    ("kv128_p", [2, 128, 2, 4, 128]), ("kv512_p", [2, 512, 2, 4, 128]), ("kv2048_p", [2, 2048, 2, 4, 128]),
    ("hg_p", [8, 128, 128]), ("rw_p", [16, 64, 64]), ("sh_o", [128, 8, 5]), ("fc_o", [4, 128, NJ, 5, 2]),
    ("mkv_o", [4, 256, 2048]),
    ("kv128_s", [2, 4, 128, 2, 4, 128]), ("kv512_s", [2, 4, 512, 2, 4, 128]), ("kv2048_s", [2, 4, 2048, 2, 4, 128]),
    ("hg_s", [4, 8, 128, 128]), ("rw_s", [4, 16, 64, 64]),
]


def build_nc(pvca, cca):
    nc = bass.Bass("TRN2", target_bir_lowering=False)
    dr = {}
    for name, shape in IN_SPECS + [("pv", [128, pvca.n]), ("cst", [128, cca.n])]:
        dr[name] = nc.dram_tensor(name, shape, F32, kind="ExternalInput").ap()
    for name, shape in OUT_SPECS:
        dr[name] = nc.dram_tensor(name, shape, F32, kind="ExternalOutput").ap()
    b = Builder(nc, dr, pvca, cca)
    b.build()
    return nc


def kernel(**inp):
    inp = {k: np.asarray(v) for k, v in inp.items()}
    f = np.float32
    pv, pvca = build_pv(inp)
    cst, cca = build_cst()
    nc = build_nc(pvca, cca)
    shared = {
        "pv": pv, "cst": cst,
        "w_qkv": np.stack([tile_w(inp["attn_w_qkv"][j]) for j in range(2)]),
        "w_ao": np.stack([tile_w(inp["attn_w_o"][j]) for j in range(2)]),
        "w_hgin": tile_w(inp["hg_w_in"][0]), "w_hgo": tile_w(inp["hg_w_o"][0]),
        "w_rkv": np.stack([tile_w(inp["rw_w_rkv"][0, j]) for j in range(3)]),
        "rw1": np.ascontiguousarray(inp["rw_w1"][0].reshape(8, 128, 64).transpose(1, 0, 2)),
        "ra1": np.ascontiguousarray(inp["rw_a1"][0].reshape(8, 128, 64).transpose(1, 0, 2)),
        "rg1": np.ascontiguousarray(inp["rw_g1"][0].reshape(8, 128, 160).transpose(1, 0, 2)),
        "rw2": np.ascontiguousarray(inp["rw_w2"][0]), "ra2": np.ascontiguousarray(inp["rw_a2"][0]),
        "rg2": np.ascontiguousarray(inp["rw_g2"][0]),
        "w_rwo": tile_w(inp["rw_w_o"][0]),
        "w_xq": np.stack([tile_w(inp["xa_w_q"][l]) for l in range(4)]),
        "w_xkv": np.stack([tile_w(inp["xa_w_kv"][l]) for l in range(4)]),
        "w_xo": np.stack([tile_w(inp["xa_w_o"][l]) for l in range(4)]),
        "w_fin": np.stack([tile_w(inp["ffn_w_in"][l]) for l in range(4)]),
        "w_fdn": np.stack([tile_w(inp["ffn_w_down"][l]) for l in range(4)]),
    }
    in_maps = []
    for c in range(8):
        sl = slice(4 * c, 4 * c + 4)
        m = dict(shared)
        m["xp"] = np.ascontiguousarray(inp["x_prompt"][c])
        m["xs"] = np.ascontiguousarray(inp["x_sample"][sl].reshape(32, 1024))
        m["mem"] = np.ascontiguousarray(inp["mem_prompt"][c])
        m["c128"] = np.ascontiguousarray(inp["cache_attn_kv_w128"][:, sl])
        m["c512"] = np.ascontiguousarray(inp["cache_attn_kv_w512"][:, sl])
        m["c2048"] = np.ascontiguousarray(inp["cache_attn_kv_w2048"][:, sl])
        m["st_hg"] = np.ascontiguousarray(inp["state_hgrn"][0, sl])
        m["st_rw"] = np.ascontiguousarray(inp["state_rwkv"][0, sl].transpose(0, 1, 3, 2))
        m["st_sh"] = np.ascontiguousarray(inp["state_rwkv_shift"][0, sl].reshape(4, 8, 128).transpose(2, 1, 0))
        m["st_fc"] = np.ascontiguousarray(inp["state_ffn_conv"][:, sl].reshape(4, 4, 2, NJ, 128).transpose(0, 4, 3, 1, 2))
        m["cmem"] = np.ascontiguousarray(inp["cache_mem_kv"][:, sl])
        in_maps.append({k: np.ascontiguousarray(v, dtype=f) for k, v in m.items()})
    res = run_bass_kernel_spmd(nc, in_maps, core_ids=list(range(8)))
    R = res.results

    def cat(name, axis=0, stack=False):
        arrs = [np.asarray(R[c][name]) for c in range(8)]
        return np.stack(arrs, axis) if stack else np.concatenate(arrs, axis)

    y_p = cat("y_p", 0, True)
    y_s = cat("y_s", 0, True).reshape(32, 8, 1024)
    kvp = [cat(n, 1, True) for n in ("kv128_p", "kv512_p", "kv2048_p")]
    hg_p = cat("hg_p", 0, True)[None]
    rw_p = cat("rw_p", 0, True).transpose(0, 1, 3, 2)[None]
    sh = cat("sh_o", 0, True)
    sh = sh.transpose(0, 3, 2, 1).reshape(8, 5, 1024)
    sh_p = sh[:, 0][None]
    sh_s = sh[:, 1:5].reshape(32, 1024)[None]
    fc = cat("fc_o", 0, True)
    fc = fc.transpose(1, 0, 4, 5, 3, 2).reshape(4, 8, 5, 2, DFF)
    fc_p = np.ascontiguousarray(fc[:, :, 0])
    fc_s = np.ascontiguousarray(fc[:, :, 1:5].reshape(4, 32, 2, DFF))
    mkv = cat("mkv_o", 1, True).reshape(4, 8, 256, 2, 4, 256)
    kvs = [cat(n, 1) for n in ("kv128_s", "kv512_s", "kv2048_s")]
    hg_s = cat("hg_s", 0)[None]
    rw_s = cat("rw_s", 0).transpose(0, 1, 3, 2)[None]
    outs = (y_p, y_s, kvp[0], kvp[1], kvp[2], hg_p, rw_p, sh_p, fc_p, mkv, kvs[0], kvs[1], kvs[2], hg_s, rw_s, sh_s, fc_s)
    return tuple(np.ascontiguousarray(o, dtype=np.float32) for o in outs)
```

```python
import contextlib
import numpy as np
import concourse.bass as bass
import concourse.mybir as mybir
from concourse.bass_utils import run_bass_kernel_spmd

F32 = mybir.dt.float32
BF16 = mybir.dt.bfloat16
AF = mybir.ActivationFunctionType
ALU = mybir.AluOpType
AX = mybir.AxisListType
ENGS = ("pe", "act", "dve", "pool", "sp")
NDSEM = 48
NHW = 32

NT = 2080
TB = [(0, 512), (512, 512), (1024, 512), (1536, 512), (2048, 32)]
D = 1024
DFF = 2816
NJ = 22
EPS = 1e-6
DILS = (1, 4, 16)
WINS = (128, 512, 2048)
NWB = 6
STRICT_SAME_ENGINE = False
STAGES = {"attn": True, "hgrn": True, "rwkv": True, "mem": True, "ffn": True, "layers": 4, "a_d2d": 1, "a_kvout": 1, "a_sout": 1, "a_pu": 1, "a_su": 1, "a_groups": 3}


class Op:
    __slots__ = ("eng", "fn", "reads", "writes", "dma", "seq", "waits", "signal", "cnt", "dsem", "dval", "snap")

    def __init__(self, eng, fn, reads, writes, dma):
        self.eng, self.fn, self.reads, self.writes, self.dma = eng, fn, tuple(reads), tuple(writes), dma
        self.waits = []
        self.signal = False
        self.cnt = 0
        self.dsem = -1
        self.dval = 0
        self.snap = None


class Prog:
    def __init__(self, nc):
        self.nc = nc
        self.ops = []

    def add(self, eng, fn, reads=(), writes=(), dma=False):
        self.ops.append(Op(eng, fn, reads, writes, dma))

    def barrier(self):
        self.ops.append(None)

    def analyse(self):
        ops = self.ops
        last_w = {}
        readers = {}
        seqc = {e: 0 for e in ENGS}
        known = {e: {x: 0 for x in ENGS} for e in ENGS}
        kd = {e: set() for e in ENGS}
        dsem_last = [None] * NDSEM
        dsem_cnt = [0] * NDSEM
        nd = 0
        nds = 0
        pend = {e: [] for e in ENGS}
        last_op = {e: None for e in ENGS}
        for i, op in enumerate(ops):
            if op is None:
                for E in ENGS:
                    wl = []
                    for E2 in ENGS:
                        j = last_op[E2]
                        if j is not None and known[E][E2] < ops[j].seq:
                            ops[j].signal = True
                            wl.append(("e", E2, j))
                            known[E][E2] = ops[j].seq
                    for s_ in range(NDSEM):
                        if dsem_cnt[s_] > 0:
                            wl.append(("d", s_, 16 * dsem_cnt[s_]))
                    pend[E] = pend[E] + wl
                last_w.clear()
                readers.clear()
                dsem_last = [None] * NDSEM
                continue
            E = op.eng
            if pend[E]:
                op.waits.extend(pend[E])
                pend[E] = []
            seqc[E] += 1
            op.seq = seqc[E]
            own = op.seq - 1 if E in ("pe", "sp") else 0
            if own > known[E][E]:
                known[E][E] = own
            deps = set()
            for k in op.reads:
                j = last_w.get(k)
                if j is not None:
                    deps.add(j)
            for k in op.writes:
                j = last_w.get(k)
                if j is not None and (ops[j].dma or op.dma or ops[j].eng != E or STRICT_SAME_ENGINE):
                    deps.add(j)
                rd = readers.get(k)
                if rd:
                    for j in rd.values():
                        if ops[j].dma or op.dma or ops[j].eng != E or STRICT_SAME_ENGINE:
                            deps.add(j)
            deps.discard(i)
            if op.dma:
                if E == "pool":
                    s = NHW + (nds % (NDSEM - NHW))
                    nds += 1
                else:
                    s = nd % NHW
                    nd += 1
                if dsem_last[s] is not None:
                    deps.add(dsem_last[s])
                dsem_cnt[s] += 1
                op.dsem, op.dval = s, 16 * dsem_cnt[s]
                dsem_last[s] = i
            for j in sorted(deps):
                p = ops[j]
                if p.dma:
                    if j in kd[E]:
                        continue
                    op.waits.append(("d", p.dsem, p.dval))
                    kd[E].add(j)
                    for x in ENGS:
                        if p.snap[x] > known[E][x]:
                            known[E][x] = p.snap[x]
                else:
                    if known[E][p.eng] >= p.seq:
                        continue
                    p.signal = True
                    op.waits.append(("e", p.eng, j))
                    known[E][p.eng] = p.seq
                    for x in ENGS:
                        if p.snap[x] > known[E][x]:
                            known[E][x] = p.snap[x]
            op.snap = dict(known[E])
            if not op.dma:
                last_op[E] = i
            for k in op.writes:
                last_w[k] = i
                readers[k] = {}
            for k in op.reads:
                readers.setdefault(k, {})[("dma", i) if op.dma else E] = i
        c = {e: 0 for e in ENGS}
        for op in ops:
            if op is not None and op.signal:
                c[op.eng] += 1
                op.cnt = c[op.eng]
        self.sig_tot = c
        fin = {}
        for op in ops:
            if op is not None and op.dma:
                fin[op.dsem] = max(fin.get(op.dsem, 0), op.dval)
        self.final = fin

    def emit(self):
        nc = self.nc
        self.analyse()
        ops = self.ops
        with contextlib.ExitStack() as st:
            psem = {e: st.enter_context(nc.semaphore("prog_" + e)) for e in ENGS}
            dsem = [st.enter_context(nc.semaphore("dmas%d" % i)) for i in range(NDSEM)]
            block = st.enter_context(nc.Block())

            def stream(ename):
                def body(eng):
                    for op in ops:
                        if op is None or op.eng != ename:
                            continue
                        for w in op.waits:
                            if w[0] == "d":
                                eng.wait_ge(dsem[w[1]], w[2])
                            else:
                                eng.wait_ge(psem[w[1]], ops[w[2]].cnt)
                        ins = op.fn(eng)
                        if op.dma:
                            ins.then_inc(dsem[op.dsem], 16)
                        elif op.signal:
                            ins.then_inc(psem[ename], 1)
                    if ename == "sp":
                        for s, v in self.final.items():
                            eng.wait_ge(dsem[s], v)
                        for e2 in ENGS:
                            if e2 != "sp" and self.sig_tot[e2] > 0:
                                eng.wait_ge(psem[e2], self.sig_tot[e2])
                return body

            block.tensor(stream("pe"))
            block.scalar(stream("act"))
            block.vector(stream("dve"))
            block.gpsimd(stream("pool"))
            block.sync(stream("sp"))


class ColAlloc:
    def __init__(self):
        self.n = 0
        self.cols = {}

    def add(self, name, ncols):
        self.cols[name] = self.n
        self.n += ncols
        return self.cols[name]


def fm_vec(v):
    v = np.asarray(v, np.float32).reshape(-1)
    nc_ = v.size // 128
    return np.ascontiguousarray(v.reshape(nc_, 128).T)


def pv_layout():
    ca = ColAlloc()
    for l in range(4):
        for nm in ("norm_mix", "norm_mem", "norm_ffn", "mem_norm"):
            ca.add("%s%d" % (nm, l), 8)
        ca.add("xa_q_gain%d" % l, 2)
        ca.add("xa_k_gain%d" % l, 2)
        for t in range(3):
            ca.add("conv_w%d_%d" % (l, t), NJ)
        ca.add("conv_b%d" % l, NJ)
    for ja in range(2):
        for g in range(3):
            ca.add("aq_gain%d_%d" % (ja, g), 1)
            ca.add("ak_gain%d_%d" % (ja, g), 1)
    for l in range(4):
        ca.add("hg_lb%d" % l, 8)
    ca.add("hg_out_gain", 1)
    for j in range(6):
        ca.add("rw_mu%d" % j, 8)
    for nm in ("rw_w0", "rw_a0", "rw_k_k", "rw_k_a", "rw_r_k", "rw_ln_g", "rw_ln_b"):
        ca.add(nm, 8)
    return ca


def build_pv(inp):
    ca = pv_layout()
    pv = np.zeros((128, ca.n), np.float32)

    def put(name, v):
        a = fm_vec(v)
        pv[:, ca.cols[name]:ca.cols[name] + a.shape[1]] = a

    for l in range(4):
        for nm in ("norm_mix", "norm_mem", "norm_ffn", "mem_norm"):
            put("%s%d" % (nm, l), inp[nm][l])
        put("xa_q_gain%d" % l, inp["xa_q_gain"][l])
        put("xa_k_gain%d" % l, inp["xa_k_gain"][l])
        for t in range(3):
            put("conv_w%d_%d" % (l, t), inp["ffn_conv_w"][l, t])
        put("conv_b%d" % l, inp["ffn_conv_b"][l])
        put("hg_lb%d" % l, inp["hg_lb_logits"][l])
    for ja in range(2):
        for g in range(3):
            put("aq_gain%d_%d" % (ja, g), inp["attn_q_gain"][ja, g])
            put("ak_gain%d_%d" % (ja, g), inp["attn_k_gain"][ja, g])
    put("hg_out_gain", inp["hg_out_gain"][0])
    for j in range(6):
        put("rw_mu%d" % j, inp["rw_mu"][0, j])
    for nm in ("rw_w0", "rw_a0", "rw_k_k", "rw_k_a", "rw_r_k", "rw_ln_g", "rw_ln_b"):
        put(nm, inp[nm][0])
    return pv, ca


def cst_layout():
    ca = ColAlloc()
    ca.add("ident", 128)
    ca.add("ones", 128)
    ca.add("m_own", 128)
    ca.add("m_prev", 128)
    ca.add("blk64", 128)
    ca.add("own_s", 96)
    ca.add("bd_su", 128)
    ca.add("bd_sl", 128)
    ca.add("bd_iu", 128)
    return ca


def build_cst():
    ca = cst_layout()
    c = np.zeros((128, ca.n), np.float32)
    j = np.arange(128)[:, None]
    i = np.arange(128)[None, :]
    c[:, ca.cols["ident"]:ca.cols["ident"] + 128] = (j == i)
    c[:, ca.cols["ones"]:ca.cols["ones"] + 128] = 1.0
    c[:, ca.cols["m_own"]:ca.cols["m_own"] + 128] = (j <= i)
    c[:, ca.cols["m_prev"]:ca.cols["m_prev"] + 128] = (j >= i)
    c[:, ca.cols["blk64"]:ca.cols["blk64"] + 128] = ((j // 64) == (i // 64))
    col = ca.cols["own_s"]
    for g in range(3):
        R = min(DILS[g], 8)
        nq = 8 // R
        for s in range(4):
            for r in range(R):
                for u in range(nq):
                    for ip in range(8):
                        if ip % R == r and ip <= r + R * u:
                            c[8 * s + ip, col + u] = 1.0
                col += nq
    same = ((j // 64) == (i // 64))
    c[:, ca.cols["bd_su"]:ca.cols["bd_su"] + 128] = same & ((j % 64) < (i % 64))
    c[:, ca.cols["bd_sl"]:ca.cols["bd_sl"] + 128] = same & ((j % 64) > (i % 64))
    c[:, ca.cols["bd_iu"]:ca.cols["bd_iu"] + 128] = same & ((j % 64) <= (i % 64))
    return c, ca


def own_s_col(ca, g, s, r):
    col = ca.cols["own_s"]
    for gg in range(3):
        R = min(DILS[gg], 8)
        nq = 8 // R
        if gg == g:
            return col + (s * R + r) * nq
        col += 4 * R * nq
    raise ValueError


def tile_w(w, bw=128):
    K, N = w.shape
    return np.ascontiguousarray(w.reshape(K // 128, 128, N // bw, bw).transpose(2, 1, 0, 3))


class Builder:
    def __init__(self, nc, dr, pvca, cca):
        self.nc, self.dr, self.pvca, self.cca = nc, dr, pvca, cca
        self.P = Prog(nc)
        self.psi = 0
        self.pspool = {}
        self.wi = 0
        self.scri = 0
        self.off = 0

    def alloc(self, cols):
        a = self.arena[:, self.off:self.off + cols]
        self.off += cols
        assert self.off <= self.acols, ("SBUF arena overflow", self.off, self.acols)
        return a

    def alloc16(self, cols):
        return self.alloc((cols + 1) // 2).bitcast(BF16)[:, 0:cols]

    def mark(self):
        return self.off

    def release(self, m):
        self.P.barrier()
        self.off = m

    def newps(self, pool=None):
        if pool is None:
            i = self.psi % 8
            self.psi += 1
        else:
            lo, n = pool
            k = self.pspool.get(pool, 0)
            self.pspool[pool] = k + 1
            i = lo + k % n
        return self.ps[i], "ps%d" % i

    def scr(self):
        i = self.scri % 4
        self.scri += 1
        return self.scrb[i], "scr%d" % i

    def mm(self, out, lhsT, rhs, start, stop, r, w):
        self.P.add("pe", lambda e: e.matmul(out, lhsT, rhs, start=start, stop=stop), r, w)

    def tr(self, out, in_, ident, r, w):
        self.P.add("pe", lambda e: e.transpose(out, in_, ident), r, w)

    def act(self, out, in_, func, r, w, bias=None, scale=None, accum=None):
        kw = {}
        if bias is not None:
            kw["bias"] = bias
        if scale is not None:
            kw["scale"] = scale
        if accum is not None:
            kw["accum_out"] = accum
        self.P.add("act", lambda e: e.activation(out, in_, func, **kw), r, w)

    def cp(self, eng, out, in_, r, w):
        if eng == "act":
            self.P.add("act", lambda e: e.copy(out, in_), r, w)
        else:
            self.P.add(eng, lambda e: e.tensor_copy(out, in_), r, w)

    def tt(self, out, a, b, op, r, w, eng="dve"):
        self.P.add(eng, lambda e: e.tensor_tensor(out, a, b, op), r, w)

    def ts(self, out, a, s1, s2, op0, op1, r, w, eng="dve"):
        if s2 is None:
            self.P.add(eng, lambda e: e.tensor_scalar(out, a, s1, None, op0), r, w)
        else:
            self.P.add(eng, lambda e: e.tensor_scalar(out, a, s1, s2, op0, op1), r, w)

    def stt(self, out, a, s, b, op0, op1, r, w, eng="dve"):
        self.P.add(eng, lambda e: e.scalar_tensor_tensor(out, a, s, b, op0, op1), r, w)

    def recip(self, out, in_, r, w):
        self.P.add("dve", lambda e: e.reciprocal(out, in_), r, w)

    def memset(self, out, val, w, eng="dve"):
        self.P.add(eng, lambda e: e.memset(out, val), (), w)

    def dma(self, out, in_, r, w, q="sp", slow=False):
        if slow:
            self.P.add(q, lambda e: e.dma_start(out=out, in_=in_, allow_slow_non_contiguous=True), r, w, dma=True)
        else:
            self.P.add(q, lambda e: e.dma_start(out=out, in_=in_), r, w, dma=True)

    def pvc(self, name, k=0, n=1):
        c = self.pvca.cols[name] + k
        return self.pv[:, c:c + n]

    def cst(self, name, rows=128, n=None, k=0):
        c = self.cca.cols[name] + k
        if n is None:
            n = 128
        return self.c32[:rows, c:c + n]

    def cst16(self, name, rows=128, n=None, k=0):
        c = self.cca.cols[name] + k
        if n is None:
            n = 128
        return self.c16[:rows, c:c + n]

    def load_w(self, wdram, KC, rows=128, ncol=128):
        i = self.wi % NWB
        self.wi += 1
        wt = self.wb[i]
        key = "wb%d" % i
        self.dma(wt[:rows, :KC, :ncol], wdram, (), [key], q="pool")
        return wt, key

    def linear(self, wdram, KC, rhs_fn, rkeys_fn, cons, blocks=(0, 1, 2, 3, 4), rows=128, ncol=128):
        wt, wk = self.load_w(wdram, KC, rows, ncol)
        for bi in blocks:
            c0, n = TB[bi]
            ps, pk = self.newps()
            for kc in range(KC):
                self.mm(ps[:ncol, :n], wt[:rows, kc, :ncol], rhs_fn(kc, c0, n), kc == 0, kc == KC - 1,
                        [wk] + rkeys_fn(kc, bi), [pk])
            cons(bi, c0, n, ps, pk)

    def h_rhs(self, kc, c0, n):
        return self.hT[:, kc, c0:c0 + n]

    def h_keys(self, kc, bi):
        return ["h%d.%d" % (kc, bi)]

    def rstd_block(self, srcs, skeys, n, dim, eps, ones=None):
        if ones is None:
            ones = self.cst16("ones")
        ps, pk = self.newps()
        for ci, (ap, k) in enumerate(zip(srcs, skeys)):
            sq, sk = self.scr()
            sq16 = sq.bitcast(BF16)
            self.act(sq16[:, :n], ap, AF.Square, [k], [sk])
            self.mm(ps[:, :n], ones, sq16[:, :n], ci == 0, ci == len(srcs) - 1, [sk, "cst16"], [pk])
        rs, rk = self.scr()
        self.act(rs[:, :n], ps[:, :n], AF.Ln, [pk], [rk], scale=1.0 / dim, bias=self.epsc[:, 0:1] if eps == EPS else eps)
        self.act(rs[:, :n], rs[:, :n], AF.Exp, [rk], [rk], scale=-0.5)
        return rs, rk

    def dnorm(self, gname):
        for bi, (c0, n) in enumerate(TB):
            rs, rk = self.rstd_block([self.xT[:, c, c0:c0 + n] for c in range(8)],
                                     ["x%d.%d" % (c, bi) for c in range(8)], n, D, EPS)
            for c in range(8):
                self.stt(self.hT[:, c, c0:c0 + n], self.xT[:, c, c0:c0 + n], self.pvc(gname, c), rs[:, :n],
                         ALU.mult, ALU.mult, ["x%d.%d" % (c, bi), rk, "pv"], ["h%d.%d" % (c, bi)])

    def add_resid(self, nb):
        def cons(bi, c0, n, ps, pk):
            k = "x%d.%d" % (nb, bi)
            self.tt(self.xT[:, nb, c0:c0 + n], ps[:, :n], self.xT[:, nb, c0:c0 + n], ALU.add, [pk, k], [k])
        return cons

    def setup(self, st):
        nc = self.nc
        self.acols = 52800
        self.arena = st.enter_context(nc.sbuf_tensor("arena", [128, self.acols], F32))
        self.ps = [st.enter_context(nc.psum_tensor("psb%d" % i, [128, 512], F32)) for i in range(8)]
        self.xT = self.alloc(8 * NT).rearrange("p (c t) -> p c t", t=NT)
        self.hT = self.alloc16(8 * NT).rearrange("p (c t) -> p c t", t=NT)
        self.pv = self.alloc(self.pvca.n)
        self.c32 = self.alloc(self.cca.n)
        self.c16 = self.alloc16(self.cca.n)
        self.epsc = self.alloc(2)
        self.wb = [self.alloc16(8 * 128).rearrange("p (k n) -> p k n", n=128) for _ in range(NWB)]
        self.scrb = [self.alloc(512) for _ in range(4)]
        self.dma(self.pv, self.dr["pv"], (), ["pv"])
        self.dma(self.c32, self.dr["cst"], (), ["cst"])
        self.cp("dve", self.c16, self.c32, ["cst"], ["cst16"])
        self.memset(self.epsc[:, 0:1], EPS, ["epsc"])
        self.memset(self.epsc[:, 1:2], 64e-5, ["epsc"])
        self.P.barrier()

    def load_x(self):
        m = self.mark()
        stg = [self.alloc(1024) for _ in range(2)]
        ident = self.cst("ident")
        for tt_ in range(17):
            sb = stg[tt_ % 2]
            sk = "xstg%d" % (tt_ % 2)
            if tt_ < 16:
                rows, c0, src = 128, tt_ * 128, self.dr["xp"][tt_ * 128:(tt_ + 1) * 128, :]
            else:
                rows, c0, src = 32, 2048, self.dr["xs"]
            self.dma(sb[:rows, :], src, (), [sk])
            bi = min(c0 // 512, 4)
            for half in range(2):
                ps, pk = self.newps()
                for q in range(4):
                    c = half * 4 + q
                    self.tr(ps[:, q * 128:q * 128 + rows], sb[:rows, c * 128:(c + 1) * 128], ident[:rows, :rows],
                            [sk, "cst"], [pk])
                self.cp("act" if half == 0 else "dve",
                        self.xT[:, half * 4:half * 4 + 4, c0:c0 + rows],
                        ps[:, :].rearrange("p (q t) -> p q t", t=128)[:, :, :rows],
                        [pk], ["x%d.%d" % (half * 4 + q, bi) for q in range(4)])
        self.release(m)

    def store_y(self):
        m = self.mark()
        stg = [self.alloc(1024) for _ in range(2)]
        ident = self.cst("ident")
        for tt_ in range(17):
            sb = stg[tt_ % 2]
            sk = "ystg%d" % (tt_ % 2)
            if tt_ < 16:
                rows, c0, dst = 128, tt_ * 128, self.dr["y_p"][tt_ * 128:(tt_ + 1) * 128, :]
            else:
                rows, c0, dst = 32, 2048, self.dr["y_s"]
            bi = min(c0 // 512, 4)
            for half in range(2):
                ps, pk = self.newps()
                for q in range(4):
                    c = half * 4 + q
                    self.tr(ps[:rows, q * 128:(q + 1) * 128], self.xT[:, c, c0:c0 + rows], ident,
                            ["x%d.%d" % (c, bi), "cst"], [pk])
                self.cp("act" if half == 0 else "dve", sb[:rows, half * 512:(half + 1) * 512], ps[:rows, :], [pk], [sk])
            self.dma(dst, sb[:rows, :], [sk], ())
        self.release(m)

    def ffn(self, l):
        self.dnorm("norm_ffn%d" % l)
        m = self.mark()
        UW = 2050 + 40
        ub = [self.alloc(UW) for _ in range(2)]
        cb = [self.alloc(NT) for _ in range(2)]
        sl = cb
        GS = 8
        aT = self.alloc16(GS * NT).rearrange("p (g t) -> p g t", t=NT)
        fcst = self.alloc(NJ * 8).rearrange("p (j s t) -> p j s t", s=4, t=2)
        fco = self.alloc(NJ * 10).rearrange("p (j s t) -> p j s t", s=5, t=2)
        self.dma(fcst, self.dr["st_fc"][l], (), ["fcst"])
        for i in range(2):
            self.memset(ub[i][:, 0:2], 0.0, ["u%d" % i])
        win = self.dr["w_fin"][l]
        wdn = self.dr["w_fdn"][l]
        groups = [list(range(a, min(a + GS, NJ))) for a in range(0, NJ, GS)]
        cnt = 0
        for grp in groups:
            for gi, j in enumerate(grp):
                u = ub[cnt % 2]
                uk = "u%d" % (cnt % 2)
                c_ = cb[cnt % 2]
                ck = "c%d" % (cnt % 2)
                s_ = sl[cnt % 2]
                sk = ck
                cnt += 1
                us = u[:, 2050:2090].rearrange("p (s k) -> p s k", k=10)

                def cons_u(bi, c0, n, ps, pk, u=u, uk=uk, us=us):
                    if bi < 4:
                        self.cp("act", u[:, 2 + c0:2 + c0 + n], ps[:, :n], [pk], [uk])
                    else:
                        self.cp("act", us[:, :, 2:10], ps[:, :32].rearrange("p (s i) -> p s i", i=8), [pk], [uk])
                self.linear(win[j], 8, self.h_rhs, self.h_keys, cons_u)
                self.cp("dve", us[:, :, 0:2], fcst[:, j, :, :], ["fcst", uk], [uk])
                w0, w1, w2, bb = (self.pvc("conv_w%d_0" % l, j), self.pvc("conv_w%d_1" % l, j),
                                  self.pvc("conv_w%d_2" % l, j), self.pvc("conv_b%d" % l, j))
                self.ts(c_[:, 0:2048], u[:, 2:2050], w2, bb, ALU.mult, ALU.add, [uk, "pv"], [ck])
                self.stt(c_[:, 0:2048], u[:, 1:2049], w1, c_[:, 0:2048], ALU.mult, ALU.add, [uk, ck, "pv"], [ck])
                self.stt(c_[:, 0:2048], u[:, 0:2048], w0, c_[:, 0:2048], ALU.mult, ALU.add, [uk, ck, "pv"], [ck])
                cs = c_[:, 2048:2080].rearrange("p (s i) -> p s i", i=8)
                self.ts(cs, us[:, :, 2:10], w2, bb, ALU.mult, ALU.add, [uk, "pv"], [ck])
                self.stt(cs, us[:, :, 1:9], w1, cs, ALU.mult, ALU.add, [uk, ck, "pv"], [ck])
                self.stt(cs, us[:, :, 0:8], w0, cs, ALU.mult, ALU.add, [uk, ck, "pv"], [ck])
                self.act(s_[:, :], c_[:, :], AF.Silu, [ck], [sk])
                self.cp("act", fco[:, j, 0, :], u[:, 2048:2050], [uk], ["fco"])
                self.cp("act", fco[:, j, 1:5, :], us[:, :, 8:10], [uk], ["fco"])

                def cons_g(bi, c0, n, ps, pk, gi=gi, s_=s_, sk=sk):
                    self.tt(aT[:, gi, c0:c0 + n], ps[:, :n], s_[:, c0:c0 + n], ALU.mult, [pk, sk], ["a%d.%d" % (gi, bi)])
                self.linear(win[NJ + j], 8, self.h_rhs, self.h_keys, cons_g)
            g0, gl = grp[0], len(grp)
            for nb in range(8):
                self.linear(wdn[nb][:, g0:g0 + gl, :], gl, lambda kc, c0, n: aT[:, kc, c0:c0 + n],
                            lambda kc, bi: ["a%d.%d" % (kc, bi)], self.add_resid(nb))
        self.dma(self.dr["fc_o"][l], fco, ["fco"], ())
        self.release(m)

    def mem_prep(self):
        self.memTn = self.alloc(8 * 256).rearrange("p (c t) -> p c t", t=256)
        m = self.mark()
        stg = [self.alloc(1024) for _ in range(2)]
        raw = self.alloc(8 * 256).rearrange("p (c t) -> p c t", t=256)
        ident = self.cst("ident")
        for mb in range(2):
            self.dma(stg[mb], self.dr["mem"][mb * 128:(mb + 1) * 128, :], (), ["mstg%d" % mb])
            for half in range(2):
                ps, pk = self.newps()
                for q in range(4):
                    c = half * 4 + q
                    self.tr(ps[:, q * 128:(q + 1) * 128], stg[mb][:, c * 128:(c + 1) * 128], ident, ["mstg%d" % mb, "cst"], [pk])
                self.cp("act", raw[:, half * 4:half * 4 + 4, mb * 128:(mb + 1) * 128],
                        ps[:, :].rearrange("p (q t) -> p q t", t=128), [pk], ["mraw"])
        rs, rk = self.rstd_block([raw[:, c, :] for c in range(8)], ["mraw"] * 8, 256, D, EPS)
        for c in range(8):
            self.tt(self.memTn[:, c, :], raw[:, c, :], rs[:, :256], ALU.mult, ["mraw", rk], ["memTn"])
        self.release(m)

    def mem_kv(self, l, KTm, Vm):
        m = self.mark()
        ml = self.alloc16(8 * 256).rearrange("p (c t) -> p c t", t=256)
        kvraw = self.alloc(16 * 256).rearrange("p (b t) -> p b t", t=256)
        stage = self.alloc(2 * 2048).rearrange("p (mb c) -> p mb c", c=2048)
        for c in range(8):
            self.ts(ml[:, c, :], self.memTn[:, c, :], self.pvc("mem_norm%d" % l, c), None, ALU.mult, None,
                    ["memTn", "pv"], ["ml"])
        wkv = self.dr["w_xkv"][l]
        for blk in range(16):
            wt, wk = self.load_w(wkv[blk], 8)
            ps, pk = self.newps()
            for kc in range(8):
                self.mm(ps[:, :256], wt[:, kc, :], ml[:, kc, :], kc == 0, kc == 7, [wk, "ml"], [pk])
            self.cp("act", kvraw[:, blk, :], ps[:, :256], [pk], ["kvraw%d" % blk])
        for h in range(4):
            rs, rk = self.rstd_block([kvraw[:, 2 * h + e, :] for e in range(2)], ["kvraw%d" % (2 * h + e) for e in range(2)],
                                     256, 256, EPS)
            for e in range(2):
                b_ = 2 * h + e
                self.stt(kvraw[:, b_, :], kvraw[:, b_, :], self.pvc("xa_k_gain%d" % l, e), rs[:, :256], ALU.mult, ALU.mult,
                         ["kvraw%d" % b_, rk, "pv"], ["kvraw%d" % b_])
                self.cp("act", KTm[:, b_, :], kvraw[:, b_, :], ["kvraw%d" % b_], ["KTm"])
        ident = self.cst("ident")
        for mb in range(2):
            for q4 in range(4):
                ps, pk = self.newps()
                for q in range(4):
                    blk = q4 * 4 + q
                    self.tr(ps[:, q * 128:(q + 1) * 128], kvraw[:, blk, mb * 128:(mb + 1) * 128], ident,
                            ["kvraw%d" % blk, "cst"], [pk])
                self.cp("act" if q4 % 2 == 0 else "dve", stage[:, mb, q4 * 512:(q4 + 1) * 512], ps[:, :], [pk], ["mstage"])
        self.dma(self.dr["mkv_o"][l].rearrange("(mb m) c -> m mb c", m=128), stage, ["mstage"], ())
        self.cp("dve", Vm, stage[:, :, 1024:2048], ["mstage"], ["Vm"])
        self.release(m)

    def mem_attend(self, l):
        self.dnorm("norm_mem%d" % l)
        m0 = self.mark()
        KTm = self.alloc16(8 * 256).rearrange("p (b t) -> p b t", t=256)
        Vm = self.alloc16(2 * 1024).rearrange("p (mb c) -> p mb c", c=1024)
        self.mem_kv(l, KTm, Vm)
        qraw = self.alloc(2 * NT).rearrange("p (e t) -> p e t", t=NT)
        q16 = self.alloc16(2 * NT).rearrange("p (e t) -> p e t", t=NT)
        oT = self.alloc16(2 * NT).rearrange("p (b t) -> p b t", t=NT)
        pT = [self.alloc16(2 * 512).rearrange("p (mb t) -> p mb t", t=512) for _ in range(2)]
        rden = [self.alloc(512) for _ in range(2)]
        ckv = [self.alloc(2 * 2 * 256).rearrange("p (mb t e) -> p mb t e", t=2, e=256) for _ in range(2)]
        kTs = [self.alloc16(4 * 128).rearrange("p (q t) -> p q t", t=128) for _ in range(2)]
        vs16 = [self.alloc16(2 * 256).rearrange("p (mb e) -> p mb e", e=256) for _ in range(2)]
        pTs = [self.alloc16(16) for _ in range(2)]
        gsc = self.alloc(2)
        self.ts(gsc, self.pvc("xa_q_gain%d" % l, 0, 2), 256 ** -0.5, None, ALU.mult, None, ["pv"], ["gsc"])
        ones16 = self.cst16("ones")
        ident = self.cst("ident")
        wq = self.dr["w_xq"][l]
        it = 0
        for h in range(4):
            for e in range(2):
                def cons_q(bi, c0, n, ps, pk, e=e):
                    self.cp("act", qraw[:, e, c0:c0 + n], ps[:, :n], [pk], ["qraw%d.%d" % (e, bi)])
                self.linear(wq[2 * h + e], 8, self.h_rhs, self.h_keys, cons_q)
            for bi, (c0, n) in enumerate(TB):
                rs, rk = self.rstd_block([qraw[:, e, c0:c0 + n] for e in range(2)], ["qraw%d.%d" % (e, bi) for e in range(2)],
                                         n, 256, EPS)
                for e in range(2):
                    self.stt(q16[:, e, c0:c0 + n], qraw[:, e, c0:c0 + n], gsc[:, e:e + 1], rs[:, :n], ALU.mult, ALU.mult,
                             ["qraw%d.%d" % (e, bi), rk, "gsc"], ["q16.%d.%d" % (e, bi)])
            for bi in range(4):
                c0, n = TB[bi]
                p_ = pT[it % 2]
                pk_ = "pT%d" % (it % 2)
                rd = rden[it % 2]
                rdk = "rden%d" % (it % 2)
                it += 1
                for mb in range(2):
                    ps, pk = self.newps()
                    for e in range(2):
                        self.mm(ps[:, :n], KTm[:, 2 * h + e, mb * 128:(mb + 1) * 128], q16[:, e, c0:c0 + n], e == 0, e == 1,
                                ["KTm", "q16.%d.%d" % (e, bi)], [pk])
                    self.act(p_[:, mb, :n], ps[:, :n], AF.Exp, [pk], [pk_ + ".%d" % mb])
                psd, pkd = self.newps()
                for mb in range(2):
                    self.mm(psd[:, :n], ones16, p_[:, mb, :n], mb == 0, mb == 1, ["cst16", pk_ + ".%d" % mb], [pkd])
                self.act(rd[:, :n], psd[:, :n], AF.Ln, [pkd], [rdk])
                self.act(rd[:, :n], rd[:, :n], AF.Exp, [rdk], [rdk], scale=-1.0)
                for e in range(2):
                    pso, pko = self.newps()
                    for mb in range(2):
                        self.mm(pso[:, :n], Vm[:, mb, h * 256 + e * 128:h * 256 + (e + 1) * 128], p_[:, mb, :n], mb == 0, mb == 1,
                                ["Vm", pk_ + ".%d" % mb], [pko])
                    self.tt(oT[:, e, c0:c0 + n], pso[:, :n], rd[:, :n], ALU.mult, [pko, rdk], ["oT%d.%d" % (e, bi)])
            for s in range(4):
                ck = ckv[s % 2]
                ckk = "ckv%d" % (s % 2)
                kt = kTs[s % 2]
                ktk = "kTs%d" % (s % 2)
                v16 = vs16[s % 2]
                vk = "vs16%d" % (s % 2)
                pts = pTs[s % 2]
                ptk = "pTs%d" % (s % 2)
                for mb in range(2):
                    self.dma(ck[:, mb, :, :], self.dr["cmem"][l, s, mb * 128:(mb + 1) * 128, :, h, :], (), [ckk])
                ps, pk = self.newps()
                for e in range(2):
                    for mb in range(2):
                        q = e * 2 + mb
                        self.tr(ps[:, q * 128:(q + 1) * 128], ck[:, mb, 0, e * 128:(e + 1) * 128], ident, [ckk, "cst"], [pk])
                self.cp("act", kt, ps[:, :].rearrange("p (q t) -> p q t", t=128), [pk], [ktk])
                self.cp("dve", v16, ck[:, :, 1, :], [ckk], [vk])
                q0 = 2048 + 8 * s
                ps, pk = self.newps()
                for mb in range(2):
                    for e in range(2):
                        self.mm(ps[:, mb * 8:mb * 8 + 8], kt[:, e * 2 + mb, :], q16[:, e, q0:q0 + 8], e == 0, e == 1,
                                [ktk, "q16.%d.4" % e], [pk])
                self.act(pts[:, 0:16], ps[:, 0:16], AF.Exp, [pk], [ptk])
                pso, pko = self.newps()
                for e in range(2):
                    for mb in range(2):
                        self.mm(pso[:, e * 8:e * 8 + 8], v16[:, mb, e * 128:(e + 1) * 128], pts[:, mb * 8:mb * 8 + 8], mb == 0, mb == 1,
                                [vk, ptk], [pko])
                for mb in range(2):
                    self.mm(pso[:, 16:24], ones16, pts[:, mb * 8:mb * 8 + 8], mb == 0, mb == 1, ["cst16", ptk], [pko])
                rd, rdk = self.scr()
                self.recip(rd[:, 0:8], pso[:, 16:24], [pko], [rdk])
                for e in range(2):
                    self.tt(oT[:, e, q0:q0 + 8], pso[:, e * 8:e * 8 + 8], rd[:, 0:8], ALU.mult, [pko, rdk],
                            ["oT%d.4" % e])
            wo = self.dr["w_xo"][l]
            for nb in range(8):
                self.linear(wo[nb][:, 2 * h:2 * h + 2, :], 2, lambda kc, c0, n: oT[:, kc, c0:c0 + n],
                            lambda kc, bi: ["oT%d.%d" % (kc, bi)], self.add_resid(nb))
        self.release(m0)

    def attn_unit_a(self, q_ap, qkeys, nq, blocks, acc_cols, bufs):
        pT, ptk = bufs
        ps, pk = self.newps()
        off = 0
        offs = []
        for (kT, vt, mk, nk, keys) in blocks:
            self.mm(ps[:nk, off:off + nq], kT, q_ap, True, True, keys + qkeys, [pk])
            offs.append(off)
            off += nq
        if len(blocks) == 2 and blocks[0][3] == 128 and blocks[1][3] == 128 and nq == 128:
            self.act(pT[:, 0:256], ps[:, 0:256], AF.Exp, [pk], [ptk])
            self.tt(pT[:, 0:256], pT[:, 0:256], self.mboth, ALU.mult, [ptk, "mboth"], [ptk])
        else:
            for bi_, (kT, vt, mk, nk, keys) in enumerate(blocks):
                o_ = offs[bi_]
                self.act(pT[:nk, o_:o_ + nq], ps[:nk, o_:o_ + nq], AF.Exp, [pk], [ptk])
                self.tt(pT[:nk, o_:o_ + nq], pT[:nk, o_:o_ + nq], mk, ALU.mult, [ptk, "cst16"], [ptk])
        return (nq, blocks, offs, pT, ptk, acc_cols)

    def attn_unit_b(self, ctx):
        nq, blocks, offs, pT, ptk, acc_cols = ctx
        pso, pko = self.newps()
        nb_ = len(blocks)
        for bi_, (kT, vt, mk, nk, keys) in enumerate(blocks):
            o_ = offs[bi_]
            self.mm(pso[:, 0:nq], vt, pT[:nk, o_:o_ + nq], bi_ == 0, bi_ == nb_ - 1, keys + [ptk], [pko])
        ones16 = self.cst16("ones")
        for bi_, (kT, vt, mk, nk, keys) in enumerate(blocks):
            o_ = offs[bi_]
            self.mm(pso[:, 128:128 + nq], ones16[:nk, :], pT[:nk, o_:o_ + nq], bi_ == 0, bi_ == nb_ - 1, ["cst16", ptk], [pko])
        src = pso[:, 0:256].rearrange("p (a b) -> p a b", b=128)[:, :, 0:nq]
        self.tt(acc_cols, src, acc_cols, ALU.add, [pko, "acc"], ["acc"])

    def attn(self, l, ja):
        self.dnorm("norm_mix%d" % l)
        m0 = self.mark()
        raw = [self.alloc(NT) for _ in range(2)]
        QT = self.alloc16(NT)
        KT = self.alloc16(NT)
        tok32 = [self.alloc(4 * 128).rearrange("p (b e) -> p b e", e=128) for _ in range(4)]
        vtok = self.alloc16(16 * 128).rearrange("p (b e) -> p b e", e=128)
        acc = self.alloc(2 * NT).rearrange("p (a t) -> p a t", t=NT)
        oT = self.alloc16(NT)
        NPT = 4
        DEPTH = 2
        pTb = [self.alloc16(256) for _ in range(NPT)]
        pend = []

        def unit(q_ap, qkeys, nq, blocks, acc_cols, ui):
            pend.append(self.attn_unit_a(q_ap, qkeys, nq, blocks, acc_cols, (pTb[ui % NPT], "pTb%d" % (ui % NPT))))
            if len(pend) > DEPTH:
                self.attn_unit_b(pend.pop(0))

        def flush():
            while pend:
                self.attn_unit_b(pend.pop(0))
        self.mboth = self.alloc16(256)
        cb_ = self.alloc(8 * 2 * 128).rearrange("p (r t e) -> p r t e", t=2, e=128)
        cv_ = self.alloc16(8 * 128).rearrange("p (r e) -> p r e", e=128)
        kcT = [self.alloc16(128) for _ in range(2)]
        sstg = self.alloc(2 * 128).rearrange("p (t e) -> p t e", e=128)
        vs16 = self.alloc16(128)
        gq = self.alloc(3)
        self.cp("dve", self.mboth[:, 0:128], self.cst16("m_own"), ["cst16"], ["mboth"])
        self.cp("dve", self.mboth[:, 128:256], self.cst16("m_prev"), ["cst16"], ["mboth"])
        for g in range(3):
            self.ts(gq[:, g:g + 1], self.pvc("aq_gain%d_%d" % (ja, g)), 128 ** -0.5, None, ALU.mult, None, ["pv"], ["gq"])
        ident = self.cst("ident")
        wqkv = self.dr["w_qkv"][ja]
        wo = self.dr["w_ao"][ja]
        caches = (self.dr["c128"], self.dr["c512"], self.dr["c2048"])
        outs_p = (self.dr["kv128_p"], self.dr["kv512_p"], self.dr["kv2048_p"])
        outs_s = (self.dr["kv128_s"], self.dr["kv512_s"], self.dr["kv2048_s"])
        for g in range(3):
            L = WINS[g]
            for s in range(4):
                if STAGES["a_d2d"]:
                    self.dma(outs_s[g][ja, s, 0:L - 8], caches[g][ja, s, 8:L], (), ())
        ui = 0
        ti = 0
        for h in range(4):
            self.memset(acc[:, :, :], 0.0, ["acc"])
            for g in range(STAGES["a_groups"]):
                dil = DILS[g]
                L = WINS[g]
                nkb = (2048 // dil) // 128

                def proj(si, dst, dkey):
                    blk = (si * 3 + g) * 4 + h

                    def cons_p(bi, c0, n, ps, pk):
                        self.cp("act", dst[:, c0:c0 + n], ps[:, :n], [pk], ["%s.%d" % (dkey, bi)])
                    self.linear(wqkv[blk], 8, self.h_rhs, self.h_keys, cons_p)
                proj(0, raw[0], "raw0")
                proj(1, raw[1], "raw1")
                for bi, (c0, n) in enumerate(TB):
                    rs, rk = self.rstd_block([raw[0][:, c0:c0 + n]], ["raw0.%d" % bi], n, 128, EPS)
                    self.stt(QT[:, c0:c0 + n], raw[0][:, c0:c0 + n], gq[:, g:g + 1], rs[:, :n], ALU.mult, ALU.mult,
                             ["raw0.%d" % bi, rk, "gq"], ["QT.%d" % bi])
                    rs, rk = self.rstd_block([raw[1][:, c0:c0 + n]], ["raw1.%d" % bi], n, 128, EPS)
                    self.stt(raw[1][:, c0:c0 + n], raw[1][:, c0:c0 + n], self.pvc("ak_gain%d_%d" % (ja, g)), rs[:, :n],
                             ALU.mult, ALU.mult, ["raw1.%d" % bi, rk, "pv"], ["raw1.%d" % bi])
                    self.cp("act", KT[:, c0:c0 + n], raw[1][:, c0:c0 + n], ["raw1.%d" % bi], ["KT.%d" % bi])
                proj(2, raw[0], "raw0")
                allq = ["QT.%d" % b for b in range(5)]
                allk = ["KT.%d" % b for b in range(5)]
                o_ = outs_p[g][ja]
                for si, rsrc, rkn in ((1, raw[1], "raw1"), (2, raw[0], "raw0")):
                    rkeys = ["%s.%d" % (rkn, b) for b in range(4)]
                    for b4 in range(4):
                        tk = tok32[ti % 4]
                        tkk = "tok32_%d" % (ti % 4)
                        ti += 1
                        ps, pk = self.newps()
                        for q in range(4):
                            idx = b4 * 4 + q
                            r_, kb = idx // nkb, idx % nkb
                            st_ = r_ + dil * 128 * kb
                            self.tr(ps[:, q * 128:(q + 1) * 128], rsrc[:, st_:st_ + dil * 127 + 1:dil], ident, rkeys + ["cst"], [pk])
                        self.cp("act", tk[:, :, :], ps[:, :].rearrange("p (q e) -> p q e", e=128), [pk], [tkk])
                        if si == 2:
                            self.cp("dve", vtok[:, b4 * 4:b4 * 4 + 4, :], tk[:, :, :], [tkk], ["vtok"])
                        if not STAGES["a_kvout"]:
                            pass
                        elif g == 0:
                            if b4 == 3:
                                self.dma(o_[:, si - 1, h, :], tk[:, 3, :], [tkk], ())
                        elif g == 1:
                            dst = o_.rearrange("(j r) t h e -> j r t h e", r=dil)[:, b4, si - 1, h, :]
                            self.dma(dst, tk[:, 3, :], [tkk], ())
                        else:
                            dst = o_.rearrange("(j r) t h e -> j r t h e", r=dil)[:, b4 * 4:b4 * 4 + 4, si - 1, h, :]
                            self.dma(dst, tk[:, :, :], [tkk], ())
                    ps, pk = self.newps()
                    self.tr(ps[:32, 0:128], rsrc[:, 2048:2080], ident, ["%s.4" % rkn, "cst"], [pk])
                    self.cp("act", sstg[:32, si - 1, :], ps[:32, 0:128], [pk], ["sstg"])
                    if si == 2:
                        self.cp("dve", vs16[:32, :], sstg[:32, 1, :], ["sstg"], ["vs16"])
                for s in range(4):
                    if STAGES["a_sout"]:
                        self.dma(outs_s[g][ja, s, L - 8:L, :, h, :], sstg[8 * s:8 * s + 8, :, :], ["sstg"], ())
                for r_ in range(dil if STAGES["a_pu"] else 0):
                    for qb in range(nkb):
                        st_ = r_ + dil * 128 * qb
                        sl_ = slice(st_, st_ + dil * 127 + 1, dil)
                        blocks = [(KT[:, sl_], vtok[:, r_ * nkb + qb, :], self.cst16("m_own"), 128, allk + ["vtok"])]
                        if qb > 0:
                            sp_ = r_ + dil * 128 * (qb - 1)
                            blocks.append((KT[:, sp_:sp_ + dil * 127 + 1:dil], vtok[:, r_ * nkb + qb - 1, :], self.cst16("m_prev"),
                                           128, allk + ["vtok"]))
                        unit(QT[:, sl_], allq, 128, blocks, acc[:, :, sl_], ui)
                        ui += 1
                R = min(dil, 8)
                nq = 8 // R
                for s in range(4 if STAGES["a_su"] else 0):
                    flush()
                    src = caches[g][ja, s].rearrange("(j r) t h e -> j r t h e", r=dil)[:, 0:R, :, h, :]
                    self.dma(cb_[:, 0:R, :, :], src, (), ["cb"])
                    self.cp("dve", cv_[:, 0:R, :], cb_[:, 0:R, 1, :], ["cb"], ["cv"])
                    for r_ in range(R):
                        kc_ = kcT[ui % 2]
                        kck = "kcT%d" % (ui % 2)
                        ps, pk = self.newps()
                        self.tr(ps[:, 0:128], cb_[:, r_, 0, :], ident, ["cb", "cst"], [pk])
                        self.cp("act", kc_[:, :], ps[:, 0:128], [pk], [kck])
                        q0 = 2048 + 8 * s + r_
                        sl_ = slice(q0, q0 + R * (nq - 1) + 1, R)
                        oc = own_s_col(self.cca, g, s, r_) - self.cca.cols["own_s"]
                        blocks = [(kc_[:, :], cv_[:, r_, :], self.cst16("m_prev", 128, nq), 128, [kck, "cv"]),
                                  (KT[:, 2048:2080], vs16[:32, :], self.cst16("own_s", 32, nq, oc), 32, allk + ["vs16"])]
                        unit(QT[:, sl_], allq, nq, blocks, acc[:, :, sl_], ui)
                        ui += 1
                flush()
            self.recip(acc[:, 1, :], acc[:, 1, :], ["acc"], ["acc"])
            for bi, (c0, n) in enumerate(TB):
                self.tt(oT[:, c0:c0 + n], acc[:, 0, c0:c0 + n], acc[:, 1, c0:c0 + n], ALU.mult, ["acc"], ["ao.%d" % bi])
            for nb in range(8):
                self.linear(wo[nb][:, h:h + 1, :], 1, lambda kc, c0, n: oT[:, c0:c0 + n], lambda kc, bi: ["ao.%d" % bi],
                            self.add_resid(nb))
        self.release(m0)

    def hgrn(self, l):
        self.dnorm("norm_mix%d" % l)
        m0 = self.mark()
        Bq, Bz, Bk, Bm, Bd, Bv, Bb = [self.alloc(NT) for _ in range(7)]
        Sb = [self.alloc(128) for _ in range(2)]
        NLA = 4
        ktok = [self.alloc16(128) for _ in range(NLA)]
        vtk = [self.alloc16(128) for _ in range(NLA)]
        ATb = [self.alloc16(64) for _ in range(NLA)]
        kvb_ = [self.alloc(128) for _ in range(NLA)]
        ident16 = self.cst16("ident")
        ebC = self.alloc(36)
        ex = self.alloc(32).rearrange("p (l c) -> p l c", c=8)
        lbv = self.alloc(8)
        omlb = self.alloc(8)
        tot = self.alloc(8)
        oTh = self.alloc16(NT)

        def K(n):
            return ["%s.%d" % (n, b) for b in range(5)]
        c_lb = self.pvca.cols["hg_lb0"]
        self.act(ex[:, :, :], self.pv[:, c_lb:c_lb + 32].rearrange("p (l c) -> p l c", c=8), AF.Exp, ["pv"], ["hex"])
        self.tt(tot, ex[:, 0, :], ex[:, 1, :], ALU.add, ["hex"], ["htot"])
        self.tt(tot, tot, ex[:, 2, :], ALU.add, ["hex", "htot"], ["htot"])
        self.tt(tot, tot, ex[:, 3, :], ALU.add, ["hex", "htot"], ["htot"])
        self.cp("dve", lbv, ex[:, 1, :], ["hex"], ["hlb"])
        for l2 in range(2, l + 1):
            self.tt(lbv, lbv, ex[:, l2, :], ALU.add, ["hex", "hlb"], ["hlb"])
        self.recip(tot, tot, ["htot"], ["htot"])
        self.tt(lbv, lbv, tot, ALU.mult, ["hlb", "htot"], ["hlb"])
        self.ts(omlb, lbv, -1.0, 1.0, ALU.mult, ALU.add, ["hlb"], ["homlb"])
        ident = self.cst("ident")
        m_own = self.cst("m_own")
        win = self.dr["w_hgin"]
        wo = self.dr["w_hgo"]
        ci = 0
        si = 0
        for h in range(8):
            def proj(blk, dst, dkey):
                def cons_p(bi, c0, n, ps, pk):
                    self.cp("act", dst[:, c0:c0 + n], ps[:, :n], [pk], ["%s.%d" % (dkey, bi)])
                self.linear(win[blk], 8, self.h_rhs, self.h_keys, cons_p)
            proj(h, Bq, "hq")
            proj(8 + h, Bz, "hz")
            proj(16 + h, Bv, "hv")
            self.act(Bz[:, :], Bz[:, :], AF.Sigmoid, K("hz"), K("hz"))
            self.ts(Bz[:, :], Bz[:, :], omlb[:, h:h + 1], lbv[:, h:h + 1], ALU.mult, ALU.add, K("hz") + ["hlb", "homlb"], K("hz"))
            self.ts(Bk[:, :], Bz[:, :], -1.0, 1.0, ALU.mult, ALU.add, K("hz"), K("hk"))
            self.act(Bz[:, :], Bz[:, :], AF.Ln, K("hz"), K("hz"))
            self.memset(Bm[:, :], 1.0, K("hm"))
            self.memset(Bm[:, 0:2048:64], 0.0, K("hm"))
            self.memset(Bm[:, 2048:2080:8], 0.0, K("hm"))
            self.P.add("dve", lambda e: e.tensor_tensor_scan(Bb[:, :], Bm[:, :], Bz[:, :], 0.0, ALU.mult, ALU.add),
                       K("hm") + K("hz"), K("hb"))
            self.act(ebC[:, 0:32], Bb[:, 63:2048:64], AF.Exp, K("hb"), ["hebc"])
            self.act(ebC[:, 32:36], Bb[:, 2055:2080:8], AF.Exp, K("hb"), ["hebc"])
            self.act(Bq[:, :], Bq[:, :], AF.Silu, K("hq"), K("hq"))
            self.act(Bz[:, :], Bb[:, :], AF.Exp, K("hb") + K("hz"), K("hz"))
            self.tt(Bq[:, :], Bq[:, :], Bz[:, :], ALU.mult, K("hq") + K("hz"), K("hq"))
            Z16 = Bz[:, :].bitcast(BF16)
            qe16 = Z16[:, 0:NT]
            v16 = Z16[:, NT:2 * NT]
            self.cp("act", qe16, Bq[:, :], K("hq") + K("hz"), K("hz"))
            self.cp("dve", v16, Bv[:, :], K("hv") + K("hz"), K("hz"))
            bp = Bb[:, 0:2048].rearrange("p (n c) -> p n c", c=64)
            self.tt(Bd[:, 0:2048].rearrange("p (n c) -> p n c", c=64), bp[:, :, 63:64].to_broadcast([128, 32, 64]), bp, ALU.subtract,
                    K("hb"), K("hd"))
            bs = Bb[:, 2048:2080].rearrange("p (n c) -> p n c", c=8)
            self.tt(Bd[:, 2048:2080].rearrange("p (n c) -> p n c", c=8), bs[:, :, 7:8].to_broadcast([128, 4, 8]), bs, ALU.subtract,
                    K("hb"), K("hd"))
            self.act(Bd[:, :], Bd[:, :], AF.Exp, K("hd"), K("hd"))
            self.act(Bm[:, :], Bb[:, :], AF.Exp, K("hb") + K("hm"), K("hm"), scale=-1.0)
            B16 = Bb[:, :].bitcast(BF16)
            ke16 = B16[:, 0:NT]
            kd16 = B16[:, NT:2 * NT]
            self.tt(ke16, Bm[:, :], Bk[:, :], ALU.mult, K("hm") + K("hk") + K("hb"), K("hb"))
            self.tt(kd16, Bd[:, :], Bk[:, :], ALU.mult, K("hd") + K("hk") + K("hb"), K("hb"))
            psT = [p_[:, :].bitcast(BF16) for p_ in self.ps]

            def chunk_a(c0, C, bi):
                nonlocal ci
                kt, ktk = ktok[ci % NLA], "hkt%d" % (ci % NLA)
                vt, vtkk = vtk[ci % NLA], "hvt%d" % (ci % NLA)
                AT, atk = ATb[ci % NLA], "hat%d" % (ci % NLA)
                kvb, kvk = kvb_[ci % NLA], "hkv%d" % (ci % NLA)
                ci += 1
                i1 = self.psi % 8
                ps, pk = self.newps()
                self.tr(psT[i1][:C, 0:128], kd16[:, c0:c0 + C], ident16, K("hb") + ["cst16"], [pk])
                self.tr(psT[i1][:C, 128:256], v16[:, c0:c0 + C], ident16, K("hz") + ["cst16"], [pk])
                self.cp("act", kt[:C, :], psT[i1][:C, 0:128], [pk], [ktk])
                self.cp("act", vt[:C, :], psT[i1][:C, 128:256], [pk], [vtkk])
                ps2, pk2 = self.newps()
                self.mm(ps2[:C, :C], ke16[:, c0:c0 + C], qe16[:, c0:c0 + C], True, True, K("hb") + K("hz"), [pk2])
                self.tt(AT[:C, :C], ps2[:C, :C], m_own[:C, :C], ALU.mult, [pk2, "cst"], [atk])
                ps4, pk4 = self.newps()
                self.mm(ps4[:, 0:128], kt[:C, :], vt[:C, :], True, True, [ktk, vtkk], [pk4])
                self.cp("act", kvb[:, :], ps4[:, 0:128], [pk4], [kvk])
                return (c0, C, bi, vt, vtkk, AT, atk, kvb, kvk)

            def chunk_b(ctx, S, Sk, ecol):
                c0, C, bi, vt, vtkk, AT, atk, kvb, kvk = ctx
                ps3, pk3 = self.newps()
                self.mm(ps3[:, :C], S[:, :], Bq[:, c0:c0 + C], True, False, [Sk, "hq.%d" % bi], [pk3])
                self.mm(ps3[:, :C], vt[:C, :], AT[:C, :C], False, True, [vtkk, atk], [pk3])
                self.cp("act", Bk[:, c0:c0 + C], ps3[:, :C], [pk3], ["hk.%d" % bi])
                self.stt(S[:, :], S[:, :], ebC[:, ecol:ecol + 1], kvb[:, :], ALU.mult, ALU.add, [Sk, "hebc", kvk], [Sk])
            S, Sk = Sb[si % 2], "hS%d" % (si % 2)
            si += 1
            self.memset(S[:, :], 0.0, [Sk])
            LOOK = 2
            ctxs = {}
            for n_ in range(32 + LOOK):
                if n_ < 32:
                    ctxs[n_] = chunk_a(64 * n_, 64, n_ // 8)
                if n_ >= LOOK:
                    chunk_b(ctxs.pop(n_ - LOOK), S, Sk, n_ - LOOK)
            self.dma(self.dr["hg_p"][h], S[:, :], [Sk], ())
            sctx = [chunk_a(2048 + 8 * s, 8, 4) for s in range(2)]
            for s in range(4):
                S, Sk = Sb[si % 2], "hS%d" % (si % 2)
                si += 1
                self.dma(S[:, :], self.dr["st_hg"][s, h], (), [Sk])
                chunk_b(sctx.pop(0), S, Sk, 32 + s)
                if s + 2 < 4:
                    sctx.append(chunk_a(2048 + 8 * (s + 2), 8, 4))
                self.dma(self.dr["hg_s"][s, h], S[:, :], [Sk], ())
            proj(24 + h, Bq, "hq")
            self.act(Bq[:, :], Bq[:, :], AF.Silu, K("hq"), K("hq"))
            for bi, (c0, n) in enumerate(TB):
                rs, rk = self.rstd_block([Bk[:, c0:c0 + n]], ["hk.%d" % bi], n, 128, EPS)
                self.stt(Bk[:, c0:c0 + n], Bk[:, c0:c0 + n], self.pvc("hg_out_gain"), rs[:, :n], ALU.mult, ALU.mult,
                         ["hk.%d" % bi, rk, "pv"], ["hk.%d" % bi])
                self.tt(oTh[:, c0:c0 + n], Bk[:, c0:c0 + n], Bq[:, c0:c0 + n], ALU.mult, ["hk.%d" % bi, "hq.%d" % bi], ["hoT.%d" % bi])
            for nb in range(8):
                self.linear(wo[nb][:, h:h + 1, :], 1, lambda kc, c0, n: oTh[:, c0:c0 + n], lambda kc, bi: ["hoT.%d" % bi],
                            self.add_resid(nb))
        self.release(m0)

    def rwkv(self, l):
        self.dnorm("norm_mix%d" % l)
        m0 = self.mark()
        NCH = 4
        ident = self.cst("ident")
        blk64 = self.cst("blk64")
        pvc = self.pvc

        def hk2(c0, n, kc):
            b0 = min(max(c0 - 1, 0) // 512, 4)
            b1 = min((c0 + n - 1) // 512, 4)
            return ["h%d.%d" % (kc, b) for b in sorted({b0, b1})]
        xl = self.alloc(40).rearrange("p (c t) -> p c t", t=5)
        self.cp("dve", xl[:, :, 0:1], self.xT[:, :, 2047:2048], ["x%d.3" % c for c in range(8)], ["xl"])
        self.cp("dve", xl[:, :, 1:5], self.xT[:, :, 2055:2080:8], ["x%d.4" % c for c in range(8)], ["xl"])
        rs, rk = self.rstd_block([xl[:, c, :] for c in range(8)], ["xl"] * 8, 5, D, EPS)
        for c in range(8):
            self.stt(xl[:, c, :], xl[:, c, :], pvc("norm_mix%d" % l, c), rs[:, 0:5], ALU.mult, ALU.mult, ["xl", rk, "pv"], ["xl"])
        self.dma(self.dr["sh_o"], xl, ["xl"], ())
        hsp = self.alloc16(8 * 32).rearrange("p (c t) -> p c t", t=32)
        shs = self.alloc(32).rearrange("p (c s) -> p c s", s=4)
        self.dma(shs, self.dr["st_sh"], (), ["shs"])
        self.cp("dve", hsp[:, :, 0:32:8], shs, ["shs"], ["hsp"])
        for c in range(8):
            self.cp("dve", hsp[:, c, :].rearrange("p (s i) -> p s i", i=8)[:, :, 1:8],
                    self.hT[:, c, 2048:2080].rearrange("p (s i) -> p s i", i=8)[:, :, 0:7], ["h%d.4" % c], ["hsp"])
        omu = self.alloc(48)
        cmu = self.pvca.cols["rw_mu0"]
        mu = self.pv[:, cmu:cmu + 48]
        self.ts(omu, mu, -1.0, 1.0, ALU.mult, ALU.add, ["pv"], ["omu"])
        omka = self.alloc(8)
        self.ts(omka, pvc("rw_k_a", 0, 8), -1.0, 1.0, ALU.mult, ALU.add, ["pv"], ["omka"])
        wst = self.alloc(8 * 160)

        def proj(ps, pk, m, wA, wB, wkeys, c0, n, sample):
            for kc in range(8):
                self.mm(ps[:m, :n], wA(kc), self.hT[:, kc, c0:c0 + n], kc == 0, False, wkeys + hk2(c0, n, kc), [pk])
            for kc in range(8):
                last = kc == 7
                if sample:
                    self.mm(ps[:m, :n], wB(kc), hsp[:, kc, c0 - 2048:c0 - 2048 + n], False, last, wkeys + ["hsp"], [pk])
                elif c0 == 0:
                    self.mm(ps[:m, 1:n], wB(kc), self.hT[:, kc, 0:n - 1], False, last, wkeys + hk2(c0, n, kc), [pk])
                else:
                    self.mm(ps[:m, :n], wB(kc), self.hT[:, kc, c0 - 1:c0 - 1 + n], False, last, wkeys + hk2(c0, n, kc), [pk])

        def scaled(dstA, dstB, src, ncol, j, keyA, keyB, skey):
            mj = mu[:, 8 * j:8 * j + 8].unsqueeze(2).to_broadcast([128, 8, ncol])
            oj = omu[:, 8 * j:8 * j + 8].unsqueeze(2).to_broadcast([128, 8, ncol])
            self.tt(dstA, src, oj, ALU.mult, [skey, "omu"], [keyA])
            self.tt(dstB, src, mj, ALU.mult, [skey, "pv"], [keyB])
        TWA = self.alloc16(NT)
        TG1 = self.alloc16(NT)
        TG2 = self.alloc16(NT)
        l1A = self.alloc16(8 * 160).rearrange("p (k n) -> p k n", n=160)
        l1B = self.alloc16(8 * 160).rearrange("p (k n) -> p k n", n=160)
        w64 = wst[:, 0:8 * 64].rearrange("p (k n) -> p k n", n=64)
        for (nm, j, lo) in (("rw1", 3, 0), ("ra1", 4, 64)):
            self.dma(w64, self.dr[nm], (), ["wst"])
            scaled(l1A[:, :, lo:lo + 64], l1B[:, :, lo:lo + 64], w64, 64, j, "l1A", "l1B", "wst")
        for bi, (c0, n) in enumerate(TB):
            ps, pk = self.newps()
            proj(ps, pk, 128, lambda kc: l1A[:, kc, 0:128], lambda kc: l1B[:, kc, 0:128], ["l1A", "l1B"], c0, n, bi == 4)
            self.act(TWA[0:64, c0:c0 + n], ps[0:64, :n], AF.Tanh, [pk], ["twa.%d" % bi])
            self.cp("act", TWA[64:128, c0:c0 + n], ps[64:128, :n], [pk], ["twa.%d" % bi])
        w160 = wst[:, 0:8 * 160].rearrange("p (k n) -> p k n", n=160)
        self.dma(w160, self.dr["rg1"], (), ["wst"])
        scaled(l1A[:, :, :], l1B[:, :, :], w160, 160, 5, "l1A", "l1B", "wst")
        for bi, (c0, n) in enumerate(TB):
            ps, pk = self.newps()
            proj(ps, pk, 128, lambda kc: l1A[:, kc, 0:128], lambda kc: l1B[:, kc, 0:128], ["l1A", "l1B"], c0, n, bi == 4)
            self.act(TG1[:, c0:c0 + n], ps[:, :n], AF.Sigmoid, [pk], ["tg.%d" % bi])
            ps, pk = self.newps()
            proj(ps, pk, 32, lambda kc: l1A[:, kc, 128:160], lambda kc: l1B[:, kc, 128:160], ["l1A", "l1B"], c0, n, bi == 4)
            self.act(TG2[0:32, c0:c0 + n], ps[0:32, :n], AF.Sigmoid, [pk], ["tg.%d" % bi])
        W2A = self.alloc16(128)
        G2a = self.alloc16(128)
        G2b = self.alloc16(128)
        yT = self.alloc16(NT)
        NB_ = 128
        NCH = 2
        PA = (0, 5)
        PS_ = (5, 3)

        class Set:
            pass
        sets = []
        for si_ in range(2):
            st_ = Set()
            st_.i = si_
            (st_.Rr, st_.Rk, st_.Rv, st_.Rld, st_.Ra, st_.Rg, st_.Rkk, st_.Rk2, st_.Rcum, st_.Rt1, st_.Rt2, st_.Rbon,
             st_.Ry) = [self.alloc(NB_) for _ in range(13)]
            st_.blk = [self.alloc(NCH * 128).rearrange("p (j t) -> p j t", t=128) for _ in range(5)]
            st_.WSk = [self.alloc(NCH * 128).rearrange("p (j t) -> p j t", t=128) for _ in range(9)]
            st_.DC = self.alloc(NCH)
            for b_ in st_.blk:
                self.memset(b_[:, :, :], 0.0, ["blk%d" % si_])
            sets.append(st_)
        STb = [self.alloc(128) for _ in range(3)]
        maskp = self.alloc(NB_)
        masks = self.alloc(16)
        self.memset(maskp, 1.0, ["maskp"])
        self.memset(maskp[:, 0:NB_:64], 0.0, ["maskp"])
        self.memset(masks, 1.0, ["masks"])
        self.memset(masks[:, 0:16:8], 0.0, ["masks"])
        bd_su, bd_sl, bd_iu = self.cst("bd_su"), self.cst("bd_sl"), self.cst("bd_iu")
        RB = [(NB_ * i, NB_, False, None) for i in range(2048 // NB_)] + [(2048, 16, True, (0, 1)), (2064, 16, True, (2, 3))]
        wo = self.dr["w_rwo"]
        sti = [0]

        def run(gens):
            gens = [g for g in gens if g is not None]
            while gens:
                for g in list(gens):
                    try:
                        next(g)
                    except StopIteration:
                        gens.remove(g)

        def genA(p, c0, n, smp, S_):
            k_ = lambda nm: "%s%d" % (nm, S_.i)
            bi = min(c0 // 512, 4)
            C = 8 if smp else 64
            nch = NCH
            Ablk, Bblk, Kblk, Rblk, Vblk = S_.blk
            WSk = S_.WSk
            bk = k_("blk")
            for j, dst, dk in ((0, S_.Rr, "Rr"), (1, S_.Rk, "Rk"), (2, S_.Rv, "Rv")):
                ps, pk = self.newps(PA)
                proj(ps, pk, 128, lambda kc, j=j: self.wb[2 * j][:, kc, :], lambda kc, j=j: self.wb[2 * j + 1][:, kc, :],
                     ["wb%d" % (2 * j), "wb%d" % (2 * j + 1)], c0, n, smp)
                self.cp("act", dst[:, :n], ps[:, :n], [pk], [k_(dk)])
                yield
            ps, pk = self.newps(PA)
            self.mm(ps[:, :n], W2A[0:64, :], TWA[0:64, c0:c0 + n], True, True, ["w2a", "twa.%d" % bi], [pk])
            self.act(S_.Rld[:, :n], ps[:, :n], AF.Sigmoid, [pk, "pv"], [k_("Rld")], bias=pvc("rw_w0", p))
            self.ts(S_.Rld[:, :n], S_.Rld[:, :n], -0.6065306597126334, None, ALU.mult, None, [k_("Rld")], [k_("Rld")])
            ps, pk = self.newps(PA)
            self.mm(ps[:, :n], W2A[64:128, :], TWA[64:128, c0:c0 + n], True, True, ["w2a", "twa.%d" % bi], [pk])
            self.act(S_.Ra[:, :n], ps[:, :n], AF.Sigmoid, [pk, "pv"], [k_("Ra")], bias=pvc("rw_a0", p))
            ps, pk = self.newps(PA)
            self.mm(ps[:, :n], G2a[:, :], TG1[:, c0:c0 + n], True, False, ["g2", "tg.%d" % bi], [pk])
            self.mm(ps[:, :n], G2b[0:32, :], TG2[0:32, c0:c0 + n], False, True, ["g2", "tg.%d" % bi], [pk])
            self.cp("act", S_.Rg[:, :n], ps[:, :n], [pk], [k_("Rg")])
            yield
            Rr, Rk, Rv, Rld, Ra, Rkk, Rk2, Rcum, Rt1, Rt2, Rbon = (S_.Rr, S_.Rk, S_.Rv, S_.Rld, S_.Ra, S_.Rkk, S_.Rk2, S_.Rcum,
                                                                      S_.Rt1, S_.Rt2, S_.Rbon)
            t1, t2 = k_("Rt1"), k_("Rt2")
            self.ts(Rkk[:, :n], Rk[:, :n], pvc("rw_k_k", p), None, ALU.mult, None, [k_("Rk"), "pv"], [k_("Rkk")])
            self.act(Rt1[:, :n], Rkk[:, :n], AF.Square, [k_("Rkk")], [t1])
            ps, pk = self.newps(PA)
            self.mm(ps[:, :n], blk64, Rt1[:, :n], True, True, ["cst", t1], [pk])
            self.ts(Rt2[:, :n], ps[:, :n], 1e-24, None, ALU.max, None, [pk], [t2])
            self.act(Rt2[:, :n], Rt2[:, :n], AF.Ln, [t2], [t2])
            self.act(Rt2[:, :n], Rt2[:, :n], AF.Exp, [t2], [t2], scale=-0.5)
            self.tt(Rkk[:, :n], Rkk[:, :n], Rt2[:, :n], ALU.mult, [k_("Rkk"), t2], [k_("Rkk")])
            yield
            self.ts(Rt1[:, :n], Ra[:, :n], pvc("rw_k_a", p), omka[:, p:p + 1], ALU.mult, ALU.add, [k_("Ra"), "pv", "omka", t1], [t1])
            self.tt(Rk2[:, :n], Rk[:, :n], Rt1[:, :n], ALU.mult, [k_("Rk"), t1], [k_("Rk2")])
            self.tt(Rt1[:, :n], Rr[:, :n], Rk2[:, :n], ALU.mult, [k_("Rr"), k_("Rk2"), t1], [t1])
            self.ts(Rt1[:, :n], Rt1[:, :n], pvc("rw_r_k", p), None, ALU.mult, None, [t1, "pv"], [t1])
            ps, pk = self.newps(PA)
            self.mm(ps[:, :n], blk64, Rt1[:, :n], True, True, ["cst", t1], [pk])
            self.tt(Rbon[:, :n], ps[:, :n], Rv[:, :n], ALU.mult, [pk, k_("Rv")], [k_("Rbon")])
            yield
            mk_ = masks if smp else maskp
            mkk = "masks" if smp else "maskp"
            self.P.add("dve", lambda e, n=n, mk_=mk_: e.tensor_tensor_scan(Rcum[:, :n], mk_[:, :n], Rld[:, :n], 0.0, ALU.mult, ALU.add),
                       [mkk, k_("Rld")], [k_("Rcum")])
            self.act(S_.DC[:, 0:nch], Rcum[:, C - 1:n:C], AF.Exp, [k_("Rcum")], [k_("DC")])
            if smp:
                for b_ in S_.blk:
                    self.memset(b_[:, :, :], 0.0, [bk])

            def toblk(dst, fn):
                for hh in range(2):
                    rows = slice(64 * hh, 64 * hh + 64)
                    ov = dst[rows, 0:nch, 64 * hh:64 * hh + C]
                    fn(ov, lambda x: x[rows, 0:n].rearrange("p (j s) -> p j s", s=C))
            self.act(Rt1[:, :n], Rcum[:, :n], AF.Exp, [k_("Rcum"), t1], [t1])
            toblk(Rblk, lambda ov, V: self.tt(ov, V(Rr), V(Rt1), ALU.mult, [k_("Rr"), t1], [bk]))
            yield
            self.act(Rt1[:, :n], Rcum[:, :n], AF.Exp, [k_("Rcum"), t1, bk], [t1], scale=-1.0)
            self.tt(Rt2[:, :n], Rkk[:, :n], Ra[:, :n], ALU.mult, [k_("Rkk"), k_("Ra"), t2], [t2])
            toblk(Bblk, lambda ov, V: self.tt(ov, V(Rt2), V(Rt1), ALU.mult, [t2, t1], [bk]))
            toblk(Kblk, lambda ov, V: self.tt(ov, V(Rk2), V(Rt1), ALU.mult, [k_("Rk2"), t1], [bk]))
            yield
            self.tt(Rt2[:, :n], Rcum[:, :n], Rld[:, :n], ALU.subtract, [k_("Rcum"), k_("Rld"), t2, bk], [t2])
            self.act(Rt2[:, :n], Rt2[:, :n], AF.Exp, [t2], [t2])
            toblk(Ablk, lambda ov, V: self.stt(ov, V(Rkk), -1.0, V(Rt2), ALU.mult, ALU.mult, [k_("Rkk"), t2], [bk]))
            toblk(Vblk, lambda ov, V: self.cp("act", ov, V(Rv), [k_("Rv")], [bk]))
            yield

            def wk(i):
                return "rws%d.%d" % (S_.i, i)

            def bc(m):
                return m.unsqueeze(1).to_broadcast([128, nch, 128])

            def v4(ps):
                return ps[:, 0:nch * 128].rearrange("p (j t) -> p j t", t=128)
            psa, pka = self.newps(PA)
            psb, pkb = self.newps(PA)
            for j in range(nch):
                self.mm(psa[:, j * 128:(j + 1) * 128], Bblk[:, j, :], Ablk[:, j, :], True, True, [bk], [pka])
                self.mm(psb[:, j * 128:(j + 1) * 128], Ablk[:, j, :], Bblk[:, j, :], True, True, [bk], [pkb])
            yield
            self.tt(WSk[0][:, :, :], v4(psa), bc(bd_su), ALU.mult, [pka, "cst"], [wk(0)])
            self.tt(WSk[1][:, :, :], v4(psb), bc(bd_sl), ALU.mult, [pkb, "cst"], [wk(1)])
            self.tt(WSk[4][:, :, :], WSk[0][:, :, :], bc(ident), ALU.add, [wk(0), "cst"], [wk(4)], eng="pool")
            yield
            cur = (0, 1)
            nxt = (2, 3)
            for step in range(5):
                ia, iat = cur
                in_, int_ = nxt
                psa, pka = self.newps(PA)
                if step < 4:
                    psb, pkb = self.newps(PA)
                for j in range(nch):
                    self.mm(psa[:, j * 128:(j + 1) * 128], WSk[ia][:, j, :], WSk[iat][:, j, :], True, True, [wk(ia), wk(iat)], [pka])
                    if step < 4:
                        self.mm(psb[:, j * 128:(j + 1) * 128], WSk[iat][:, j, :], WSk[ia][:, j, :], True, True, [wk(ia), wk(iat)], [pkb])
                yield
                self.cp("act", WSk[int_][:, :, :], v4(psa), [pka], [wk(int_)])
                if step < 4:
                    self.cp("act", WSk[in_][:, :, :], v4(psb), [pkb], [wk(in_)])
                psc, pkc = self.newps(PA)
                for j in range(nch):
                    self.mm(psc[:, j * 128:(j + 1) * 128], WSk[int_][:, j, :], WSk[4][:, j, :], True, True, [wk(int_), wk(4)], [pkc])
                yield
                self.tt(WSk[4][:, :, :], v4(psc), WSk[4][:, :, :], ALU.add, [pkc, wk(4)], [wk(4)])
                cur, nxt = nxt, cur
            ps1, pk1 = self.newps(PA)
            ps2, pk2 = self.newps(PA)
            ps3, pk3 = self.newps(PA)
            for j in range(nch):
                self.mm(ps1[:, j * 128:(j + 1) * 128], Kblk[:, j, :], Ablk[:, j, :], True, True, [bk], [pk1])
                self.mm(ps2[:, j * 128:(j + 1) * 128], Bblk[:, j, :], Rblk[:, j, :], True, True, [bk], [pk2])
                self.mm(ps3[:, j * 128:(j + 1) * 128], Kblk[:, j, :], Rblk[:, j, :], True, True, [bk], [pk3])
            yield
            self.tt(WSk[0][:, :, :], v4(ps1), bc(bd_su), ALU.mult, [pk1, "cst"], [wk(0)])
            self.tt(WSk[1][:, :, :], v4(ps2), bc(bd_iu), ALU.mult, [pk2, "cst"], [wk(1)])
            self.tt(WSk[2][:, :, :], v4(ps3), bc(bd_iu), ALU.mult, [pk3, "cst"], [wk(2)])
            ps1, pk1 = self.newps(PA)
            ps2, pk2 = self.newps(PA)
            ps3, pk3 = self.newps(PA)
            for j in range(nch):
                self.tr(ps1[:, j * 128:(j + 1) * 128], Vblk[:, j, :], ident, [bk, "cst"], [pk1])
                self.tr(ps2[:, j * 128:(j + 1) * 128], Bblk[:, j, :], ident, [bk, "cst"], [pk2])
                self.tr(ps3[:, j * 128:(j + 1) * 128], Kblk[:, j, :], ident, [bk, "cst"], [pk3])
            yield
            self.cp("act", WSk[3][:, :, :], v4(ps1), [pk1], [wk(3)])
            self.cp("act", WSk[5][:, :, :], v4(ps2), [pk2], [wk(5)])
            self.cp("act", WSk[6][:, :, :], v4(ps3), [pk3], [wk(6)])
            yield

        def genS(p, c0, n, smp, seqs, S_, stref, last_prompt):
            k_ = lambda nm: "%s%d" % (nm, S_.i)
            bi = min(c0 // 512, 4)
            C = 8 if smp else 64
            nch = NCH
            Ablk, Bblk, Kblk, Rblk, Vblk = S_.blk
            WSk = S_.WSk
            bk = k_("blk")

            def wk(i, j=None):
                return "rws%d.%d" % (S_.i, i) if i < 7 else "rws%d.%d.%d" % (S_.i, i, j)
            for j in range(nch):
                if smp:
                    ST, STk = STb[sti[0] % 3], "rST%d" % (sti[0] % 3)
                    sti[0] += 1
                    self.memset(ST[:, :], 0.0, [STk])
                    for hh in range(2):
                        self.dma(ST[64 * hh:64 * hh + 64, 64 * hh:64 * hh + 64], self.dr["st_rw"][seqs[j], 2 * p + hh], (), [STk])
                else:
                    ST, STk = stref
                Mk, Nb, Nk, Vt, P_, Bt, Kt = [WSk[i][:, j, :] for i in range(7)]
                XT, UT = WSk[7][:, j, :], WSk[8][:, j, :]
                ps, pk = self.newps(PS_)
                self.mm(ps[:, 0:128], Ablk[:, j, :], ST[:, :], True, False, [bk, STk], [pk])
                self.mm(ps[:, 0:128], Mk, Vt, False, True, [wk(0), wk(3)], [pk])
                yield
                self.cp("act", XT, ps[:, 0:128], [pk], [wk(7, j)])
                ps, pk = self.newps(PS_)
                self.mm(ps[:, 0:128], P_, XT, True, True, [wk(4), wk(7, j)], [pk])
                yield
                self.cp("dve", UT, ps[:, 0:128], [pk], [wk(8, j)])
                ps, pk = self.newps(PS_)
                self.mm(ps[:, 0:128], ST[:, :], Rblk[:, j, :], True, False, [STk, bk], [pk])
                self.mm(ps[:, 0:128], UT, Nb, False, False, [wk(8, j), wk(1)], [pk])
                self.mm(ps[:, 0:128], Vt, Nk, False, True, [wk(3), wk(2)], [pk])
                ps2, pk2 = self.newps(PS_)
                self.mm(ps2[:, 0:128], Bt, UT, True, False, [wk(5), wk(8, j)], [pk2])
                self.mm(ps2[:, 0:128], Kt, Vt, False, True, [wk(6), wk(3)], [pk2])
                yield
                for hh in range(2):
                    self.cp("act", S_.Ry[64 * hh:64 * hh + 64, j * C:(j + 1) * C], ps[64 * hh:64 * hh + 64, 64 * hh:64 * hh + C], [pk], [k_("Ry")])
                self.tt(ST[:, :], ST[:, :], ps2[:, 0:128], ALU.add, [STk, pk2], [STk])
                self.act(ST[:, :], ST[:, :], AF.Identity, [STk, k_("DC")], [STk], scale=S_.DC[:, j:j + 1])
                if smp:
                    for hh in range(2):
                        self.dma(self.dr["rw_s"][seqs[j], 2 * p + hh], ST[64 * hh:64 * hh + 64, 64 * hh:64 * hh + 64], [STk], ())
                yield
            if last_prompt:
                ST, STk = stref
                for hh in range(2):
                    self.dma(self.dr["rw_p"][2 * p + hh], ST[64 * hh:64 * hh + 64, 64 * hh:64 * hh + 64], [STk], ())
            Ry, Rt1, Rt2 = S_.Ry, S_.Rt1, S_.Rt2
            t1, t2 = k_("Rt1"), k_("Rt2")
            ps, pk = self.newps(PS_)
            self.mm(ps[:, :n], blk64, Ry[:, :n], True, True, ["cst", k_("Ry")], [pk])
            yield
            self.stt(Rt1[:, :n], ps[:, :n], -1.0 / 64, Ry[:, :n], ALU.mult, ALU.add, [pk, k_("Ry"), t1], [t1])
            self.act(Rt2[:, :n], Rt1[:, :n], AF.Square, [t1, t2], [t2])
            ps, pk = self.newps(PS_)
            self.mm(ps[:, :n], blk64, Rt2[:, :n], True, True, ["cst", t2], [pk])
            yield
            self.act(Rt2[:, :n], ps[:, :n], AF.Ln, [pk, "epsc"], [t2], scale=1.0 / 64, bias=self.epsc[:, 1:2])
            self.act(Rt2[:, :n], Rt2[:, :n], AF.Exp, [t2], [t2], scale=-0.5)
            self.tt(Rt1[:, :n], Rt1[:, :n], Rt2[:, :n], ALU.mult, [t1, t2], [t1])
            self.ts(Rt1[:, :n], Rt1[:, :n], pvc("rw_ln_g", p), pvc("rw_ln_b", p), ALU.mult, ALU.add, [t1, "pv"], [t1])
            self.tt(Rt1[:, :n], Rt1[:, :n], S_.Rbon[:, :n], ALU.add, [t1, k_("Rbon")], [t1])
            self.tt(yT[:, c0:c0 + n], Rt1[:, :n], S_.Rg[:, :n], ALU.mult, [t1, k_("Rg")], ["ryT.%d" % bi])
            yield

        for p in range(8):
            w128 = wst[:, 0:1024].rearrange("p (k n) -> p k n", n=128)
            for j in range(3):
                self.dma(w128, self.dr["w_rkv"][j, p], (), ["wst"])
                scaled(self.wb[2 * j][:, :, :], self.wb[2 * j + 1][:, :, :], w128, 128, j, "wb%d" % (2 * j), "wb%d" % (2 * j + 1), "wst")
            self.dma(W2A[0:64, :], self.dr["rw2"][:, p * 128:(p + 1) * 128], (), ["w2a"], q="pool")
            self.dma(W2A[64:128, :], self.dr["ra2"][:, p * 128:(p + 1) * 128], (), ["w2a"], q="pool")
            self.dma(G2a[:, :], self.dr["rg2"][0:128, p * 128:(p + 1) * 128], (), ["g2"], q="pool")
            self.dma(G2b[0:32, :], self.dr["rg2"][128:160, p * 128:(p + 1) * 128], (), ["g2"], q="pool")
            ST, STk = STb[sti[0] % 3], "rST%d" % (sti[0] % 3)
            sti[0] += 1
            self.memset(ST[:, :], 0.0, [STk])
            prev = None
            for bidx, (c0, n, smp, seqs) in enumerate(RB):
                S_ = sets[bidx % 2]
                run([genA(p, c0, n, smp, S_), prev])
                prev = genS(p, c0, n, smp, seqs, S_, (ST, STk), (not smp) and c0 + n == 2048)
            run([prev])
            for nb in range(8):
                self.linear(wo[nb][:, p:p + 1, :], 1, lambda kc, c0, n: yT[:, c0:c0 + n], lambda kc, bi: ["ryT.%d" % bi],
                            self.add_resid(nb))
        self.release(m0)

    def build(self):
        with contextlib.ExitStack() as st:
            self.setup(st)
            self.load_x()
            self.mem_prep()
            for l in range(STAGES["layers"]):
                kind = l % 3
                if kind == 0:
                    if STAGES["attn"]:
                        self.attn(l, l // 3)
                elif kind == 1:
                    if STAGES["hgrn"]:
                        self.hgrn(l)
                else:
                    if STAGES["rwkv"]:
                        self.rwkv(l)
                if STAGES["mem"]:
                    self.mem_attend(l)
                if STAGES["ffn"]:
                    self.ffn(l)
            self.store_y()
            self.P.emit()


IN_SPECS = [
    ("xp", [2048, 1024]), ("xs", [32, 1024]), ("mem", [256, 1024]),
    ("c128", [2, 4, 128, 2, 4, 128]), ("c512", [2, 4, 512, 2, 4, 128]), ("c2048", [2, 4, 2048, 2, 4, 128]),
    ("st_hg", [4, 8, 128, 128]), ("st_rw", [4, 16, 64, 64]), ("st_sh", [128, 8, 4]),
    ("st_fc", [4, 128, NJ, 4, 2]), ("cmem", [4, 4, 256, 2, 4, 256]),
    ("w_qkv", [2, 36, 128, 8, 128]), ("w_ao", [2, 8, 128, 4, 128]),
    ("w_hgin", [32, 128, 8, 128]), ("w_hgo", [8, 128, 8, 128]),
    ("w_rkv", [3, 8, 128, 8, 128]), ("rw1", [128, 8, 64]), ("ra1", [128, 8, 64]), ("rg1", [128, 8, 160]),
    ("rw2", [64, 1024]), ("ra2", [64, 1024]), ("rg2", [160, 1024]), ("w_rwo", [8, 128, 8, 128]),
    ("w_xq", [4, 8, 128, 8, 128]), ("w_xkv", [4, 16, 128, 8, 128]), ("w_xo", [4, 8, 128, 8, 128]),
    ("w_fin", [4, 2 * NJ, 128, 8, 128]), ("w_fdn", [4, 8, 128, NJ, 128]),
]
OUT_SPECS = [
    ("y_p", [2048, 1024]), ("y_s", [32, 1024]),
    ("kv128_p", [2, 128, 2, 4, 128]), ("kv512_p", [2, 512, 2, 4, 128]), ("kv2048_p", [2, 2048, 2, 4, 128]),
    ("hg_p", [8, 128, 128]), ("rw_p", [16, 64, 64]), ("sh_o", [128, 8, 5]), ("fc_o", [4, 128, NJ, 5, 2]),
    ("mkv_o", [4, 256, 2048]),
    ("kv128_s", [2, 4, 128, 2, 4, 128]), ("kv512_s", [2, 4, 512, 2, 4, 128]), ("kv2048_s", [2, 4, 2048, 2, 4, 128]),
    ("hg_s", [4, 8, 128, 128]), ("rw_s", [4, 16, 64, 64]),
]


def build_nc(pvca, cca):
    nc = bass.Bass("TRN2", target_bir_lowering=False)
    dr = {}
    for name, shape in IN_SPECS + [("pv", [128, pvca.n]), ("cst", [128, cca.n])]:
        dr[name] = nc.dram_tensor(name, shape, F32, kind="ExternalInput").ap()
    for name, shape in OUT_SPECS:
        dr[name] = nc.dram_tensor(name, shape, F32, kind="ExternalOutput").ap()
    b = Builder(nc, dr, pvca, cca)
    b.build()
    return nc


def kernel(**inp):
    inp = {k: np.asarray(v) for k, v in inp.items()}
    f = np.float32
    pv, pvca = build_pv(inp)
    cst, cca = build_cst()
    nc = build_nc(pvca, cca)
    shared = {
        "pv": pv, "cst": cst,
        "w_qkv": np.stack([tile_w(inp["attn_w_qkv"][j]) for j in range(2)]),
        "w_ao": np.stack([tile_w(inp["attn_w_o"][j]) for j in range(2)]),
        "w_hgin": tile_w(inp["hg_w_in"][0]), "w_hgo": tile_w(inp["hg_w_o"][0]),
        "w_rkv": np.stack([tile_w(inp["rw_w_rkv"][0, j]) for j in range(3)]),
        "rw1": np.ascontiguousarray(inp["rw_w1"][0].reshape(8, 128, 64).transpose(1, 0, 2)),
        "ra1": np.ascontiguousarray(inp["rw_a1"][0].reshape(8, 128, 64).transpose(1, 0, 2)),
        "rg1": np.ascontiguousarray(inp["rw_g1"][0].reshape(8, 128, 160).transpose(1, 0, 2)),
        "rw2": np.ascontiguousarray(inp["rw_w2"][0]), "ra2": np.ascontiguousarray(inp["rw_a2"][0]),
        "rg2": np.ascontiguousarray(inp["rw_g2"][0]),
        "w_rwo": tile_w(inp["rw_w_o"][0]),
        "w_xq": np.stack([tile_w(inp["xa_w_q"][l]) for l in range(4)]),
        "w_xkv": np.stack([tile_w(inp["xa_w_kv"][l]) for l in range(4)]),
        "w_xo": np.stack([tile_w(inp["xa_w_o"][l]) for l in range(4)]),
        "w_fin": np.stack([tile_w(inp["ffn_w_in"][l]) for l in range(4)]),
        "w_fdn": np.stack([tile_w(inp["ffn_w_down"][l]) for l in range(4)]),
    }
    in_maps = []
    for c in range(8):
        sl = slice(4 * c, 4 * c + 4)
        m = dict(shared)
        m["xp"] = np.ascontiguousarray(inp["x_prompt"][c])
        m["xs"] = np.ascontiguousarray(inp["x_sample"][sl].reshape(32, 1024))
        m["mem"] = np.ascontiguousarray(inp["mem_prompt"][c])
        m["c128"] = np.ascontiguousarray(inp["cache_attn_kv_w128"][:, sl])
        m["c512"] = np.ascontiguousarray(inp["cache_attn_kv_w512"][:, sl])
        m["c2048"] = np.ascontiguousarray(inp["cache_attn_kv_w2048"][:, sl])
        m["st_hg"] = np.ascontiguousarray(inp["state_hgrn"][0, sl])
        m["st_rw"] = np.ascontiguousarray(inp["state_rwkv"][0, sl].transpose(0, 1, 3, 2))
        m["st_sh"] = np.ascontiguousarray(inp["state_rwkv_shift"][0, sl].reshape(4, 8, 128).transpose(2, 1, 0))
        m["st_fc"] = np.ascontiguousarray(inp["state_ffn_conv"][:, sl].reshape(4, 4, 2, NJ, 128).transpose(0, 4, 3, 1, 2))
        m["cmem"] = np.ascontiguousarray(inp["cache_mem_kv"][:, sl])
        in_maps.append({k: np.ascontiguousarray(v, dtype=f) for k, v in m.items()})
    res = run_bass_kernel_spmd(nc, in_maps, core_ids=list(range(8)))
    R = res.results

    def cat(name, axis=0, stack=False):
        arrs = [np.asarray(R[c][name]) for c in range(8)]
        return np.stack(arrs, axis) if stack else np.concatenate(arrs, axis)

    y_p = cat("y_p", 0, True)
    y_s = cat("y_s", 0, True).reshape(32, 8, 1024)
    kvp = [cat(n, 1, True) for n in ("kv128_p", "kv512_p", "kv2048_p")]
    hg_p = cat("hg_p", 0, True)[None]
    rw_p = cat("rw_p", 0, True).transpose(0, 1, 3, 2)[None]
    sh = cat("sh_o", 0, True)
    sh = sh.transpose(0, 3, 2, 1).reshape(8, 5, 1024)
    sh_p = sh[:, 0][None]
    sh_s = sh[:, 1:5].reshape(32, 1024)[None]
    fc = cat("fc_o", 0, True)
    fc = fc.transpose(1, 0, 4, 5, 3, 2).reshape(4, 8, 5, 2, DFF)
    fc_p = np.ascontiguousarray(fc[:, :, 0])
    fc_s = np.ascontiguousarray(fc[:, :, 1:5].reshape(4, 32, 2, DFF))
    mkv = cat("mkv_o", 1, True).reshape(4, 8, 256, 2, 4, 256)
    kvs = [cat(n, 1) for n in ("kv128_s", "kv512_s", "kv2048_s")]
    hg_s = cat("hg_s", 0)[None]
    rw_s = cat("rw_s", 0).transpose(0, 1, 3, 2)[None]
    outs = (y_p, y_s, kvp[0], kvp[1], kvp[2], hg_p, rw_p, sh_p, fc_p, mkv, kvs[0], kvs[1], kvs[2], hg_s, rw_s, sh_s, fc_s)
    return tuple(np.ascontiguousarray(o, dtype=np.float32) for o in outs)
```

```python
import contextlib
import numpy as np
import concourse.bass as bass
import concourse.mybir as mybir
from concourse.bass_utils import run_bass_kernel_spmd

F32 = mybir.dt.float32
BF16 = mybir.dt.bfloat16
AF = mybir.ActivationFunctionType
ALU = mybir.AluOpType
AX = mybir.AxisListType
ENGS = ("pe", "act", "dve", "pool", "sp")
NDSEM = 48
NHW = 32

NT = 2080
TB = [(0, 512), (512, 512), (1024, 512), (1536, 512), (2048, 32)]
D = 1024
DFF = 2816
NJ = 22
EPS = 1e-6
DILS = (1, 4, 16)
WINS = (128, 512, 2048)
NWB = 6
STRICT_SAME_ENGINE = False
STAGES = {"attn": True, "hgrn": True, "rwkv": True, "mem": True, "ffn": True, "layers": 4, "a_d2d": 1, "a_kvout": 1, "a_sout": 1, "a_pu": 1, "a_su": 1, "a_groups": 3}


class Op:
    __slots__ = ("eng", "fn", "reads", "writes", "dma", "seq", "waits", "signal", "cnt", "dsem", "dval", "snap")

    def __init__(self, eng, fn, reads, writes, dma):
        self.eng, self.fn, self.reads, self.writes, self.dma = eng, fn, tuple(reads), tuple(writes), dma
        self.waits = []
        self.signal = False
        self.cnt = 0
        self.dsem = -1
        self.dval = 0
        self.snap = None


class Prog:
    def __init__(self, nc):
        self.nc = nc
        self.ops = []

    def add(self, eng, fn, reads=(), writes=(), dma=False):
        self.ops.append(Op(eng, fn, reads, writes, dma))

    def barrier(self):
        self.ops.append(None)

    def analyse(self):
        ops = self.ops
        last_w = {}
        readers = {}
        seqc = {e: 0 for e in ENGS}
        known = {e: {x: 0 for x in ENGS} for e in ENGS}
        kd = {e: set() for e in ENGS}
        dsem_last = [None] * NDSEM
        dsem_cnt = [0] * NDSEM
        nd = 0
        nds = 0
        pend = {e: [] for e in ENGS}
        last_op = {e: None for e in ENGS}
        for i, op in enumerate(ops):
            if op is None:
                for E in ENGS:
                    wl = []
                    for E2 in ENGS:
                        j = last_op[E2]
                        if j is not None and known[E][E2] < ops[j].seq:
                            ops[j].signal = True
                            wl.append(("e", E2, j))
                            known[E][E2] = ops[j].seq
                    for s_ in range(NDSEM):
                        if dsem_cnt[s_] > 0:
                            wl.append(("d", s_, 16 * dsem_cnt[s_]))
                    pend[E] = pend[E] + wl
                last_w.clear()
                readers.clear()
                dsem_last = [None] * NDSEM
                continue
            E = op.eng
            if pend[E]:
                op.waits.extend(pend[E])
                pend[E] = []
            seqc[E] += 1
            op.seq = seqc[E]
            own = op.seq - 1 if E in ("pe", "sp") else 0
            if own > known[E][E]:
                known[E][E] = own
            deps = set()
            for k in op.reads:
                j = last_w.get(k)
                if j is not None:
                    deps.add(j)
            for k in op.writes:
                j = last_w.get(k)
                if j is not None and (ops[j].dma or op.dma or ops[j].eng != E or STRICT_SAME_ENGINE):
                    deps.add(j)
                rd = readers.get(k)
                if rd:
                    for j in rd.values():
                        if ops[j].dma or op.dma or ops[j].eng != E or STRICT_SAME_ENGINE:
                            deps.add(j)
            deps.discard(i)
            if op.dma:
                if E == "pool":
                    s = NHW + (nds % (NDSEM - NHW))
                    nds += 1
                else:
                    s = nd % NHW
                    nd += 1
                if dsem_last[s] is not None:
                    deps.add(dsem_last[s])
                dsem_cnt[s] += 1
                op.dsem, op.dval = s, 16 * dsem_cnt[s]
                dsem_last[s] = i
            for j in sorted(deps):
                p = ops[j]
                if p.dma:
                    if j in kd[E]:
                        continue
                    op.waits.append(("d", p.dsem, p.dval))
                    kd[E].add(j)
                    for x in ENGS:
                        if p.snap[x] > known[E][x]:
                            known[E][x] = p.snap[x]
                else:
                    if known[E][p.eng] >= p.seq:
                        continue
                    p.signal = True
                    op.waits.append(("e", p.eng, j))
                    known[E][p.eng] = p.seq
                    for x in ENGS:
                        if p.snap[x] > known[E][x]:
                            known[E][x] = p.snap[x]
            op.snap = dict(known[E])
            if not op.dma:
                last_op[E] = i
            for k in op.writes:
                last_w[k] = i
                readers[k] = {}
            for k in op.reads:
                readers.setdefault(k, {})[("dma", i) if op.dma else E] = i
        c = {e: 0 for e in ENGS}
        for op in ops:
            if op is not None and op.signal:
                c[op.eng] += 1
                op.cnt = c[op.eng]
        self.sig_tot = c
        fin = {}
        for op in ops:
            if op is not None and op.dma:
                fin[op.dsem] = max(fin.get(op.dsem, 0), op.dval)
        self.final = fin

    def emit(self):
        nc = self.nc
        self.analyse()
        ops = self.ops
        with contextlib.ExitStack() as st:
            psem = {e: st.enter_context(nc.semaphore("prog_" + e)) for e in ENGS}
            dsem = [st.enter_context(nc.semaphore("dmas%d" % i)) for i in range(NDSEM)]
            block = st.enter_context(nc.Block())

            def stream(ename):
                def body(eng):
                    for op in ops:
                        if op is None or op.eng != ename:
                            continue
                        for w in op.waits:
                            if w[0] == "d":
                                eng.wait_ge(dsem[w[1]], w[2])
                            else:
                                eng.wait_ge(psem[w[1]], ops[w[2]].cnt)
                        ins = op.fn(eng)
                        if op.dma:
                            ins.then_inc(dsem[op.dsem], 16)
                        elif op.signal:
                            ins.then_inc(psem[ename], 1)
                    if ename == "sp":
                        for s, v in self.final.items():
                            eng.wait_ge(dsem[s], v)
                        for e2 in ENGS:
                            if e2 != "sp" and self.sig_tot[e2] > 0:
                                eng.wait_ge(psem[e2], self.sig_tot[e2])
                return body

            block.tensor(stream("pe"))
            block.scalar(stream("act"))
            block.vector(stream("dve"))
            block.gpsimd(stream("pool"))
            block.sync(stream("sp"))


class ColAlloc:
    def __init__(self):
        self.n = 0
        self.cols = {}

    def add(self, name, ncols):
        self.cols[name] = self.n
        self.n += ncols
        return self.cols[name]


def fm_vec(v):
    v = np.asarray(v, np.float32).reshape(-1)
    nc_ = v.size // 128
    return np.ascontiguousarray(v.reshape(nc_, 128).T)


def pv_layout():
    ca = ColAlloc()
    for l in range(4):
        for nm in ("norm_mix", "norm_mem", "norm_ffn", "mem_norm"):
            ca.add("%s%d" % (nm, l), 8)
        ca.add("xa_q_gain%d" % l, 2)
        ca.add("xa_k_gain%d" % l, 2)
        for t in range(3):
            ca.add("conv_w%d_%d" % (l, t), NJ)
        ca.add("conv_b%d" % l, NJ)
    for ja in range(2):
        for g in range(3):
            ca.add("aq_gain%d_%d" % (ja, g), 1)
            ca.add("ak_gain%d_%d" % (ja, g), 1)
    for l in range(4):
        ca.add("hg_lb%d" % l, 8)
    ca.add("hg_out_gain", 1)
    for j in range(6):
        ca.add("rw_mu%d" % j, 8)
    for nm in ("rw_w0", "rw_a0", "rw_k_k", "rw_k_a", "rw_r_k", "rw_ln_g", "rw_ln_b"):
        ca.add(nm, 8)
    return ca


def build_pv(inp):
    ca = pv_layout()
    pv = np.zeros((128, ca.n), np.float32)

    def put(name, v):
        a = fm_vec(v)
        pv[:, ca.cols[name]:ca.cols[name] + a.shape[1]] = a

    for l in range(4):
        for nm in ("norm_mix", "norm_mem", "norm_ffn", "mem_norm"):
            put("%s%d" % (nm, l), inp[nm][l])
        put("xa_q_gain%d" % l, inp["xa_q_gain"][l])
        put("xa_k_gain%d" % l, inp["xa_k_gain"][l])
        for t in range(3):
            put("conv_w%d_%d" % (l, t), inp["ffn_conv_w"][l, t])
        put("conv_b%d" % l, inp["ffn_conv_b"][l])
        put("hg_lb%d" % l, inp["hg_lb_logits"][l])
    for ja in range(2):
        for g in range(3):
            put("aq_gain%d_%d" % (ja, g), inp["attn_q_gain"][ja, g])
            put("ak_gain%d_%d" % (ja, g), inp["attn_k_gain"][ja, g])
    put("hg_out_gain", inp["hg_out_gain"][0])
    for j in range(6):
        put("rw_mu%d" % j, inp["rw_mu"][0, j])
    for nm in ("rw_w0", "rw_a0", "rw_k_k", "rw_k_a", "rw_r_k", "rw_ln_g", "rw_ln_b"):
        put(nm, inp[nm][0])
    return pv, ca


def cst_layout():
    ca = ColAlloc()
    ca.add("ident", 128)
    ca.add("ones", 128)
    ca.add("m_own", 128)
    ca.add("m_prev", 128)
    ca.add("blk64", 128)
    ca.add("own_s", 96)
    ca.add("bd_su", 128)
    ca.add("bd_sl", 128)
    ca.add("bd_iu", 128)
    return ca


def build_cst():
    ca = cst_layout()
    c = np.zeros((128, ca.n), np.float32)
    j = np.arange(128)[:, None]
    i = np.arange(128)[None, :]
    c[:, ca.cols["ident"]:ca.cols["ident"] + 128] = (j == i)
    c[:, ca.cols["ones"]:ca.cols["ones"] + 128] = 1.0
    c[:, ca.cols["m_own"]:ca.cols["m_own"] + 128] = (j <= i)
    c[:, ca.cols["m_prev"]:ca.cols["m_prev"] + 128] = (j >= i)
    c[:, ca.cols["blk64"]:ca.cols["blk64"] + 128] = ((j // 64) == (i // 64))
    col = ca.cols["own_s"]
    for g in range(3):
        R = min(DILS[g], 8)
        nq = 8 // R
        for s in range(4):
            for r in range(R):
                for u in range(nq):
                    for ip in range(8):
                        if ip % R == r and ip <= r + R * u:
                            c[8 * s + ip, col + u] = 1.0
                col += nq
    same = ((j // 64) == (i // 64))
    c[:, ca.cols["bd_su"]:ca.cols["bd_su"] + 128] = same & ((j % 64) < (i % 64))
    c[:, ca.cols["bd_sl"]:ca.cols["bd_sl"] + 128] = same & ((j % 64) > (i % 64))
    c[:, ca.cols["bd_iu"]:ca.cols["bd_iu"] + 128] = same & ((j % 64) <= (i % 64))
    return c, ca


def own_s_col(ca, g, s, r):
    col = ca.cols["own_s"]
    for gg in range(3):
        R = min(DILS[gg], 8)
        nq = 8 // R
        if gg == g:
            return col + (s * R + r) * nq
        col += 4 * R * nq
    raise ValueError


def tile_w(w, bw=128):
    K, N = w.shape
    return np.ascontiguousarray(w.reshape(K // 128, 128, N // bw, bw).transpose(2, 1, 0, 3))


class Builder:
    def __init__(self, nc, dr, pvca, cca):
        self.nc, self.dr, self.pvca, self.cca = nc, dr, pvca, cca
        self.P = Prog(nc)
        self.psi = 0
        self.wi = 0
        self.scri = 0
        self.off = 0

    def alloc(self, cols):
        a = self.arena[:, self.off:self.off + cols]
        self.off += cols
        assert self.off <= self.acols, ("SBUF arena overflow", self.off, self.acols)
        return a

    def alloc16(self, cols):
        return self.alloc((cols + 1) // 2).bitcast(BF16)[:, 0:cols]

    def mark(self):
        return self.off

    def release(self, m):
        self.P.barrier()
        self.off = m

    def newps(self):
        i = self.psi % 8
        self.psi += 1
        return self.ps[i], "ps%d" % i

    def scr(self):
        i = self.scri % 4
        self.scri += 1
        return self.scrb[i], "scr%d" % i

    def mm(self, out, lhsT, rhs, start, stop, r, w):
        self.P.add("pe", lambda e: e.matmul(out, lhsT, rhs, start=start, stop=stop), r, w)

    def tr(self, out, in_, ident, r, w):
        self.P.add("pe", lambda e: e.transpose(out, in_, ident), r, w)

    def act(self, out, in_, func, r, w, bias=None, scale=None, accum=None):
        kw = {}
        if bias is not None:
            kw["bias"] = bias
        if scale is not None:
            kw["scale"] = scale
        if accum is not None:
            kw["accum_out"] = accum
        self.P.add("act", lambda e: e.activation(out, in_, func, **kw), r, w)

    def cp(self, eng, out, in_, r, w):
        if eng == "act":
            self.P.add("act", lambda e: e.copy(out, in_), r, w)
        else:
            self.P.add(eng, lambda e: e.tensor_copy(out, in_), r, w)

    def tt(self, out, a, b, op, r, w, eng="dve"):
        self.P.add(eng, lambda e: e.tensor_tensor(out, a, b, op), r, w)

    def ts(self, out, a, s1, s2, op0, op1, r, w, eng="dve"):
        if s2 is None:
            self.P.add(eng, lambda e: e.tensor_scalar(out, a, s1, None, op0), r, w)
        else:
            self.P.add(eng, lambda e: e.tensor_scalar(out, a, s1, s2, op0, op1), r, w)

    def stt(self, out, a, s, b, op0, op1, r, w, eng="dve"):
        self.P.add(eng, lambda e: e.scalar_tensor_tensor(out, a, s, b, op0, op1), r, w)

    def recip(self, out, in_, r, w):
        self.P.add("dve", lambda e: e.reciprocal(out, in_), r, w)

    def memset(self, out, val, w, eng="dve"):
        self.P.add(eng, lambda e: e.memset(out, val), (), w)

    def dma(self, out, in_, r, w, q="sp", slow=False):
        if slow:
            self.P.add(q, lambda e: e.dma_start(out=out, in_=in_, allow_slow_non_contiguous=True), r, w, dma=True)
        else:
            self.P.add(q, lambda e: e.dma_start(out=out, in_=in_), r, w, dma=True)

    def pvc(self, name, k=0, n=1):
        c = self.pvca.cols[name] + k
        return self.pv[:, c:c + n]

    def cst(self, name, rows=128, n=None, k=0):
        c = self.cca.cols[name] + k
        if n is None:
            n = 128
        return self.c32[:rows, c:c + n]

    def cst16(self, name, rows=128, n=None, k=0):
        c = self.cca.cols[name] + k
        if n is None:
            n = 128
        return self.c16[:rows, c:c + n]

    def load_w(self, wdram, KC, rows=128, ncol=128):
        i = self.wi % NWB
        self.wi += 1
        wt = self.wb[i]
        key = "wb%d" % i
        self.dma(wt[:rows, :KC, :ncol], wdram, (), [key], q="pool")
        return wt, key

    def linear(self, wdram, KC, rhs_fn, rkeys_fn, cons, blocks=(0, 1, 2, 3, 4), rows=128, ncol=128):
        wt, wk = self.load_w(wdram, KC, rows, ncol)
        for bi in blocks:
            c0, n = TB[bi]
            ps, pk = self.newps()
            for kc in range(KC):
                self.mm(ps[:ncol, :n], wt[:rows, kc, :ncol], rhs_fn(kc, c0, n), kc == 0, kc == KC - 1,
                        [wk] + rkeys_fn(kc, bi), [pk])
            cons(bi, c0, n, ps, pk)

    def h_rhs(self, kc, c0, n):
        return self.hT[:, kc, c0:c0 + n]

    def h_keys(self, kc, bi):
        return ["h%d.%d" % (kc, bi)]

    def rstd_block(self, srcs, skeys, n, dim, eps, ones=None):
        if ones is None:
            ones = self.cst16("ones")
        ps, pk = self.newps()
        for ci, (ap, k) in enumerate(zip(srcs, skeys)):
            sq, sk = self.scr()
            sq16 = sq.bitcast(BF16)
            self.act(sq16[:, :n], ap, AF.Square, [k], [sk])
            self.mm(ps[:, :n], ones, sq16[:, :n], ci == 0, ci == len(srcs) - 1, [sk, "cst16"], [pk])
        rs, rk = self.scr()
        self.act(rs[:, :n], ps[:, :n], AF.Ln, [pk], [rk], scale=1.0 / dim, bias=self.epsc[:, 0:1] if eps == EPS else eps)
        self.act(rs[:, :n], rs[:, :n], AF.Exp, [rk], [rk], scale=-0.5)
        return rs, rk

    def dnorm(self, gname):
        for bi, (c0, n) in enumerate(TB):
            rs, rk = self.rstd_block([self.xT[:, c, c0:c0 + n] for c in range(8)],
                                     ["x%d.%d" % (c, bi) for c in range(8)], n, D, EPS)
            for c in range(8):
                self.stt(self.hT[:, c, c0:c0 + n], self.xT[:, c, c0:c0 + n], self.pvc(gname, c), rs[:, :n],
                         ALU.mult, ALU.mult, ["x%d.%d" % (c, bi), rk, "pv"], ["h%d.%d" % (c, bi)])

    def add_resid(self, nb):
        def cons(bi, c0, n, ps, pk):
            k = "x%d.%d" % (nb, bi)
            self.tt(self.xT[:, nb, c0:c0 + n], ps[:, :n], self.xT[:, nb, c0:c0 + n], ALU.add, [pk, k], [k])
        return cons

    def setup(self, st):
        nc = self.nc
        self.acols = 52800
        self.arena = st.enter_context(nc.sbuf_tensor("arena", [128, self.acols], F32))
        self.ps = [st.enter_context(nc.psum_tensor("psb%d" % i, [128, 512], F32)) for i in range(8)]
        self.xT = self.alloc(8 * NT).rearrange("p (c t) -> p c t", t=NT)
        self.hT = self.alloc16(8 * NT).rearrange("p (c t) -> p c t", t=NT)
        self.pv = self.alloc(self.pvca.n)
        self.c32 = self.alloc(self.cca.n)
        self.c16 = self.alloc16(self.cca.n)
        self.epsc = self.alloc(2)
        self.wb = [self.alloc16(8 * 128).rearrange("p (k n) -> p k n", n=128) for _ in range(NWB)]
        self.scrb = [self.alloc(512) for _ in range(4)]
        self.dma(self.pv, self.dr["pv"], (), ["pv"])
        self.dma(self.c32, self.dr["cst"], (), ["cst"])
        self.cp("dve", self.c16, self.c32, ["cst"], ["cst16"])
        self.memset(self.epsc[:, 0:1], EPS, ["epsc"])
        self.memset(self.epsc[:, 1:2], 64e-5, ["epsc"])
        self.P.barrier()

    def load_x(self):
        m = self.mark()
        stg = [self.alloc(1024) for _ in range(2)]
        ident = self.cst("ident")
        for tt_ in range(17):
            sb = stg[tt_ % 2]
            sk = "xstg%d" % (tt_ % 2)
            if tt_ < 16:
                rows, c0, src = 128, tt_ * 128, self.dr["xp"][tt_ * 128:(tt_ + 1) * 128, :]
            else:
                rows, c0, src = 32, 2048, self.dr["xs"]
            self.dma(sb[:rows, :], src, (), [sk])
            bi = min(c0 // 512, 4)
            for half in range(2):
                ps, pk = self.newps()
                for q in range(4):
                    c = half * 4 + q
                    self.tr(ps[:, q * 128:q * 128 + rows], sb[:rows, c * 128:(c + 1) * 128], ident[:rows, :rows],
                            [sk, "cst"], [pk])
                self.cp("act" if half == 0 else "dve",
                        self.xT[:, half * 4:half * 4 + 4, c0:c0 + rows],
                        ps[:, :].rearrange("p (q t) -> p q t", t=128)[:, :, :rows],
                        [pk], ["x%d.%d" % (half * 4 + q, bi) for q in range(4)])
        self.release(m)

    def store_y(self):
        m = self.mark()
        stg = [self.alloc(1024) for _ in range(2)]
        ident = self.cst("ident")
        for tt_ in range(17):
            sb = stg[tt_ % 2]
            sk = "ystg%d" % (tt_ % 2)
            if tt_ < 16:
                rows, c0, dst = 128, tt_ * 128, self.dr["y_p"][tt_ * 128:(tt_ + 1) * 128, :]
            else:
                rows, c0, dst = 32, 2048, self.dr["y_s"]
            bi = min(c0 // 512, 4)
            for half in range(2):
                ps, pk = self.newps()
                for q in range(4):
                    c = half * 4 + q
                    self.tr(ps[:rows, q * 128:(q + 1) * 128], self.xT[:, c, c0:c0 + rows], ident,
                            ["x%d.%d" % (c, bi), "cst"], [pk])
                self.cp("act" if half == 0 else "dve", sb[:rows, half * 512:(half + 1) * 512], ps[:rows, :], [pk], [sk])
            self.dma(dst, sb[:rows, :], [sk], ())
        self.release(m)

    def ffn(self, l):
        self.dnorm("norm_ffn%d" % l)
        m = self.mark()
        UW = 2050 + 40
        ub = [self.alloc(UW) for _ in range(2)]
        cb = [self.alloc(NT) for _ in range(2)]
        sl = cb
        GS = 8
        aT = self.alloc16(GS * NT).rearrange("p (g t) -> p g t", t=NT)
        fcst = self.alloc(NJ * 8).rearrange("p (j s t) -> p j s t", s=4, t=2)
        fco = self.alloc(NJ * 10).rearrange("p (j s t) -> p j s t", s=5, t=2)
        self.dma(fcst, self.dr["st_fc"][l], (), ["fcst"])
        for i in range(2):
            self.memset(ub[i][:, 0:2], 0.0, ["u%d" % i])
        win = self.dr["w_fin"][l]
        wdn = self.dr["w_fdn"][l]
        groups = [list(range(a, min(a + GS, NJ))) for a in range(0, NJ, GS)]
        cnt = 0
        for grp in groups:
            for gi, j in enumerate(grp):
                u = ub[cnt % 2]
                uk = "u%d" % (cnt % 2)
                c_ = cb[cnt % 2]
                ck = "c%d" % (cnt % 2)
                s_ = sl[cnt % 2]
                sk = ck
                cnt += 1
                us = u[:, 2050:2090].rearrange("p (s k) -> p s k", k=10)

                def cons_u(bi, c0, n, ps, pk, u=u, uk=uk, us=us):
                    if bi < 4:
                        self.cp("act", u[:, 2 + c0:2 + c0 + n], ps[:, :n], [pk], [uk])
                    else:
                        self.cp("act", us[:, :, 2:10], ps[:, :32].rearrange("p (s i) -> p s i", i=8), [pk], [uk])
                self.linear(win[j], 8, self.h_rhs, self.h_keys, cons_u)
                self.cp("dve", us[:, :, 0:2], fcst[:, j, :, :], ["fcst", uk], [uk])
                w0, w1, w2, bb = (self.pvc("conv_w%d_0" % l, j), self.pvc("conv_w%d_1" % l, j),
                                  self.pvc("conv_w%d_2" % l, j), self.pvc("conv_b%d" % l, j))
                self.ts(c_[:, 0:2048], u[:, 2:2050], w2, bb, ALU.mult, ALU.add, [uk, "pv"], [ck])
                self.stt(c_[:, 0:2048], u[:, 1:2049], w1, c_[:, 0:2048], ALU.mult, ALU.add, [uk, ck, "pv"], [ck])
                self.stt(c_[:, 0:2048], u[:, 0:2048], w0, c_[:, 0:2048], ALU.mult, ALU.add, [uk, ck, "pv"], [ck])
                cs = c_[:, 2048:2080].rearrange("p (s i) -> p s i", i=8)
                self.ts(cs, us[:, :, 2:10], w2, bb, ALU.mult, ALU.add, [uk, "pv"], [ck])
                self.stt(cs, us[:, :, 1:9], w1, cs, ALU.mult, ALU.add, [uk, ck, "pv"], [ck])
                self.stt(cs, us[:, :, 0:8], w0, cs, ALU.mult, ALU.add, [uk, ck, "pv"], [ck])
                self.act(s_[:, :], c_[:, :], AF.Silu, [ck], [sk])
                self.cp("act", fco[:, j, 0, :], u[:, 2048:2050], [uk], ["fco"])
                self.cp("act", fco[:, j, 1:5, :], us[:, :, 8:10], [uk], ["fco"])

                def cons_g(bi, c0, n, ps, pk, gi=gi, s_=s_, sk=sk):
                    self.tt(aT[:, gi, c0:c0 + n], ps[:, :n], s_[:, c0:c0 + n], ALU.mult, [pk, sk], ["a%d.%d" % (gi, bi)])
                self.linear(win[NJ + j], 8, self.h_rhs, self.h_keys, cons_g)
            g0, gl = grp[0], len(grp)
            for nb in range(8):
                self.linear(wdn[nb][:, g0:g0 + gl, :], gl, lambda kc, c0, n: aT[:, kc, c0:c0 + n],
                            lambda kc, bi: ["a%d.%d" % (kc, bi)], self.add_resid(nb))
        self.dma(self.dr["fc_o"][l], fco, ["fco"], ())
        self.release(m)

    def mem_prep(self):
        self.memTn = self.alloc(8 * 256).rearrange("p (c t) -> p c t", t=256)
        m = self.mark()
        stg = [self.alloc(1024) for _ in range(2)]
        raw = self.alloc(8 * 256).rearrange("p (c t) -> p c t", t=256)
        ident = self.cst("ident")
        for mb in range(2):
            self.dma(stg[mb], self.dr["mem"][mb * 128:(mb + 1) * 128, :], (), ["mstg%d" % mb])
            for half in range(2):
                ps, pk = self.newps()
                for q in range(4):
                    c = half * 4 + q
                    self.tr(ps[:, q * 128:(q + 1) * 128], stg[mb][:, c * 128:(c + 1) * 128], ident, ["mstg%d" % mb, "cst"], [pk])
                self.cp("act", raw[:, half * 4:half * 4 + 4, mb * 128:(mb + 1) * 128],
                        ps[:, :].rearrange("p (q t) -> p q t", t=128), [pk], ["mraw"])
        rs, rk = self.rstd_block([raw[:, c, :] for c in range(8)], ["mraw"] * 8, 256, D, EPS)
        for c in range(8):
            self.tt(self.memTn[:, c, :], raw[:, c, :], rs[:, :256], ALU.mult, ["mraw", rk], ["memTn"])
        self.release(m)

    def mem_kv(self, l, KTm, Vm):
        m = self.mark()
        ml = self.alloc16(8 * 256).rearrange("p (c t) -> p c t", t=256)
        kvraw = self.alloc(16 * 256).rearrange("p (b t) -> p b t", t=256)
        stage = self.alloc(2 * 2048).rearrange("p (mb c) -> p mb c", c=2048)
        for c in range(8):
            self.ts(ml[:, c, :], self.memTn[:, c, :], self.pvc("mem_norm%d" % l, c), None, ALU.mult, None,
                    ["memTn", "pv"], ["ml"])
        wkv = self.dr["w_xkv"][l]
        for blk in range(16):
            wt, wk = self.load_w(wkv[blk], 8)
            ps, pk = self.newps()
            for kc in range(8):
                self.mm(ps[:, :256], wt[:, kc, :], ml[:, kc, :], kc == 0, kc == 7, [wk, "ml"], [pk])
            self.cp("act", kvraw[:, blk, :], ps[:, :256], [pk], ["kvraw%d" % blk])
        for h in range(4):
            rs, rk = self.rstd_block([kvraw[:, 2 * h + e, :] for e in range(2)], ["kvraw%d" % (2 * h + e) for e in range(2)],
                                     256, 256, EPS)
            for e in range(2):
                b_ = 2 * h + e
                self.stt(kvraw[:, b_, :], kvraw[:, b_, :], self.pvc("xa_k_gain%d" % l, e), rs[:, :256], ALU.mult, ALU.mult,
                         ["kvraw%d" % b_, rk, "pv"], ["kvraw%d" % b_])
                self.cp("act", KTm[:, b_, :], kvraw[:, b_, :], ["kvraw%d" % b_], ["KTm"])
        ident = self.cst("ident")
        for mb in range(2):
            for q4 in range(4):
                ps, pk = self.newps()
                for q in range(4):
                    blk = q4 * 4 + q
                    self.tr(ps[:, q * 128:(q + 1) * 128], kvraw[:, blk, mb * 128:(mb + 1) * 128], ident,
                            ["kvraw%d" % blk, "cst"], [pk])
                self.cp("act" if q4 % 2 == 0 else "dve", stage[:, mb, q4 * 512:(q4 + 1) * 512], ps[:, :], [pk], ["mstage"])
        self.dma(self.dr["mkv_o"][l].rearrange("(mb m) c -> m mb c", m=128), stage, ["mstage"], ())
        self.cp("dve", Vm, stage[:, :, 1024:2048], ["mstage"], ["Vm"])
        self.release(m)

    def mem_attend(self, l):
        self.dnorm("norm_mem%d" % l)
        m0 = self.mark()
        KTm = self.alloc16(8 * 256).rearrange("p (b t) -> p b t", t=256)
        Vm = self.alloc16(2 * 1024).rearrange("p (mb c) -> p mb c", c=1024)
        self.mem_kv(l, KTm, Vm)
        qraw = self.alloc(2 * NT).rearrange("p (e t) -> p e t", t=NT)
        q16 = self.alloc16(2 * NT).rearrange("p (e t) -> p e t", t=NT)
        oT = self.alloc16(2 * NT).rearrange("p (b t) -> p b t", t=NT)
        pT = [self.alloc16(2 * 512).rearrange("p (mb t) -> p mb t", t=512) for _ in range(2)]
        rden = [self.alloc(512) for _ in range(2)]
        ckv = [self.alloc(2 * 2 * 256).rearrange("p (mb t e) -> p mb t e", t=2, e=256) for _ in range(2)]
        kTs = [self.alloc16(4 * 128).rearrange("p (q t) -> p q t", t=128) for _ in range(2)]
        vs16 = [self.alloc16(2 * 256).rearrange("p (mb e) -> p mb e", e=256) for _ in range(2)]
        pTs = [self.alloc16(16) for _ in range(2)]
        gsc = self.alloc(2)
        self.ts(gsc, self.pvc("xa_q_gain%d" % l, 0, 2), 256 ** -0.5, None, ALU.mult, None, ["pv"], ["gsc"])
        ones16 = self.cst16("ones")
        ident = self.cst("ident")
        wq = self.dr["w_xq"][l]
        it = 0
        for h in range(4):
            for e in range(2):
                def cons_q(bi, c0, n, ps, pk, e=e):
                    self.cp("act", qraw[:, e, c0:c0 + n], ps[:, :n], [pk], ["qraw%d.%d" % (e, bi)])
                self.linear(wq[2 * h + e], 8, self.h_rhs, self.h_keys, cons_q)
            for bi, (c0, n) in enumerate(TB):
                rs, rk = self.rstd_block([qraw[:, e, c0:c0 + n] for e in range(2)], ["qraw%d.%d" % (e, bi) for e in range(2)],
                                         n, 256, EPS)
                for e in range(2):
                    self.stt(q16[:, e, c0:c0 + n], qraw[:, e, c0:c0 + n], gsc[:, e:e + 1], rs[:, :n], ALU.mult, ALU.mult,
                             ["qraw%d.%d" % (e, bi), rk, "gsc"], ["q16.%d.%d" % (e, bi)])
            for bi in range(4):
                c0, n = TB[bi]
                p_ = pT[it % 2]
                pk_ = "pT%d" % (it % 2)
                rd = rden[it % 2]
                rdk = "rden%d" % (it % 2)
                it += 1
                for mb in range(2):
                    ps, pk = self.newps()
                    for e in range(2):
                        self.mm(ps[:, :n], KTm[:, 2 * h + e, mb * 128:(mb + 1) * 128], q16[:, e, c0:c0 + n], e == 0, e == 1,
                                ["KTm", "q16.%d.%d" % (e, bi)], [pk])
                    self.act(p_[:, mb, :n], ps[:, :n], AF.Exp, [pk], [pk_ + ".%d" % mb])
                psd, pkd = self.newps()
                for mb in range(2):
                    self.mm(psd[:, :n], ones16, p_[:, mb, :n], mb == 0, mb == 1, ["cst16", pk_ + ".%d" % mb], [pkd])
                self.act(rd[:, :n], psd[:, :n], AF.Ln, [pkd], [rdk])
                self.act(rd[:, :n], rd[:, :n], AF.Exp, [rdk], [rdk], scale=-1.0)
                for e in range(2):
                    pso, pko = self.newps()
                    for mb in range(2):
                        self.mm(pso[:, :n], Vm[:, mb, h * 256 + e * 128:h * 256 + (e + 1) * 128], p_[:, mb, :n], mb == 0, mb == 1,
                                ["Vm", pk_ + ".%d" % mb], [pko])
                    self.tt(oT[:, e, c0:c0 + n], pso[:, :n], rd[:, :n], ALU.mult, [pko, rdk], ["oT%d.%d" % (e, bi)])
            for s in range(4):
                ck = ckv[s % 2]
                ckk = "ckv%d" % (s % 2)
                kt = kTs[s % 2]
                ktk = "kTs%d" % (s % 2)
                v16 = vs16[s % 2]
                vk = "vs16%d" % (s % 2)
                pts = pTs[s % 2]
                ptk = "pTs%d" % (s % 2)
                for mb in range(2):
                    self.dma(ck[:, mb, :, :], self.dr["cmem"][l, s, mb * 128:(mb + 1) * 128, :, h, :], (), [ckk])
                ps, pk = self.newps()
                for e in range(2):
                    for mb in range(2):
                        q = e * 2 + mb
                        self.tr(ps[:, q * 128:(q + 1) * 128], ck[:, mb, 0, e * 128:(e + 1) * 128], ident, [ckk, "cst"], [pk])
                self.cp("act", kt, ps[:, :].rearrange("p (q t) -> p q t", t=128), [pk], [ktk])
                self.cp("dve", v16, ck[:, :, 1, :], [ckk], [vk])
                q0 = 2048 + 8 * s
                ps, pk = self.newps()
                for mb in range(2):
                    for e in range(2):
                        self.mm(ps[:, mb * 8:mb * 8 + 8], kt[:, e * 2 + mb, :], q16[:, e, q0:q0 + 8], e == 0, e == 1,
                                [ktk, "q16.%d.4" % e], [pk])
                self.act(pts[:, 0:16], ps[:, 0:16], AF.Exp, [pk], [ptk])
                pso, pko = self.newps()
                for e in range(2):
                    for mb in range(2):
                        self.mm(pso[:, e * 8:e * 8 + 8], v16[:, mb, e * 128:(e + 1) * 128], pts[:, mb * 8:mb * 8 + 8], mb == 0, mb == 1,
                                [vk, ptk], [pko])
                for mb in range(2):
                    self.mm(pso[:, 16:24], ones16, pts[:, mb * 8:mb * 8 + 8], mb == 0, mb == 1, ["cst16", ptk], [pko])
                rd, rdk = self.scr()
                self.recip(rd[:, 0:8], pso[:, 16:24], [pko], [rdk])
                for e in range(2):
                    self.tt(oT[:, e, q0:q0 + 8], pso[:, e * 8:e * 8 + 8], rd[:, 0:8], ALU.mult, [pko, rdk],
                            ["oT%d.4" % e])
            wo = self.dr["w_xo"][l]
            for nb in range(8):
                self.linear(wo[nb][:, 2 * h:2 * h + 2, :], 2, lambda kc, c0, n: oT[:, kc, c0:c0 + n],
                            lambda kc, bi: ["oT%d.%d" % (kc, bi)], self.add_resid(nb))
        self.release(m0)

    def attn_unit_a(self, q_ap, qkeys, nq, blocks, acc_cols, bufs):
        pT, ptk = bufs
        ps, pk = self.newps()
        off = 0
        offs = []
        for (kT, vt, mk, nk, keys) in blocks:
            self.mm(ps[:nk, off:off + nq], kT, q_ap, True, True, keys + qkeys, [pk])
            offs.append(off)
            off += nq
        if len(blocks) == 2 and blocks[0][3] == 128 and blocks[1][3] == 128 and nq == 128:
            self.act(pT[:, 0:256], ps[:, 0:256], AF.Exp, [pk], [ptk])
            self.tt(pT[:, 0:256], pT[:, 0:256], self.mboth, ALU.mult, [ptk, "mboth"], [ptk])
        else:
            for bi_, (kT, vt, mk, nk, keys) in enumerate(blocks):
                o_ = offs[bi_]
                self.act(pT[:nk, o_:o_ + nq], ps[:nk, o_:o_ + nq], AF.Exp, [pk], [ptk])
                self.tt(pT[:nk, o_:o_ + nq], pT[:nk, o_:o_ + nq], mk, ALU.mult, [ptk, "cst16"], [ptk])
        return (nq, blocks, offs, pT, ptk, acc_cols)

    def attn_unit_b(self, ctx):
        nq, blocks, offs, pT, ptk, acc_cols = ctx
        pso, pko = self.newps()
        nb_ = len(blocks)
        for bi_, (kT, vt, mk, nk, keys) in enumerate(blocks):
            o_ = offs[bi_]
            self.mm(pso[:, 0:nq], vt, pT[:nk, o_:o_ + nq], bi_ == 0, bi_ == nb_ - 1, keys + [ptk], [pko])
        ones16 = self.cst16("ones")
        for bi_, (kT, vt, mk, nk, keys) in enumerate(blocks):
            o_ = offs[bi_]
            self.mm(pso[:, 128:128 + nq], ones16[:nk, :], pT[:nk, o_:o_ + nq], bi_ == 0, bi_ == nb_ - 1, ["cst16", ptk], [pko])
        src = pso[:, 0:256].rearrange("p (a b) -> p a b", b=128)[:, :, 0:nq]
        self.tt(acc_cols, src, acc_cols, ALU.add, [pko, "acc"], ["acc"])

    def attn(self, l, ja):
        self.dnorm("norm_mix%d" % l)
        m0 = self.mark()
        raw = [self.alloc(NT) for _ in range(2)]
        QT = self.alloc16(NT)
        KT = self.alloc16(NT)
        tok32 = [self.alloc(4 * 128).rearrange("p (b e) -> p b e", e=128) for _ in range(4)]
        vtok = self.alloc16(16 * 128).rearrange("p (b e) -> p b e", e=128)
        acc = self.alloc(2 * NT).rearrange("p (a t) -> p a t", t=NT)
        oT = self.alloc16(NT)
        NPT = 4
        DEPTH = 2
        pTb = [self.alloc16(256) for _ in range(NPT)]
        pend = []

        def unit(q_ap, qkeys, nq, blocks, acc_cols, ui):
            pend.append(self.attn_unit_a(q_ap, qkeys, nq, blocks, acc_cols, (pTb[ui % NPT], "pTb%d" % (ui % NPT))))
            if len(pend) > DEPTH:
                self.attn_unit_b(pend.pop(0))

        def flush():
            while pend:
                self.attn_unit_b(pend.pop(0))
        self.mboth = self.alloc16(256)
        cb_ = self.alloc(8 * 2 * 128).rearrange("p (r t e) -> p r t e", t=2, e=128)
        cv_ = self.alloc16(8 * 128).rearrange("p (r e) -> p r e", e=128)
        kcT = [self.alloc16(128) for _ in range(2)]
        sstg = self.alloc(2 * 128).rearrange("p (t e) -> p t e", e=128)
        vs16 = self.alloc16(128)
        gq = self.alloc(3)
        self.cp("dve", self.mboth[:, 0:128], self.cst16("m_own"), ["cst16"], ["mboth"])
        self.cp("dve", self.mboth[:, 128:256], self.cst16("m_prev"), ["cst16"], ["mboth"])
        for g in range(3):
            self.ts(gq[:, g:g + 1], self.pvc("aq_gain%d_%d" % (ja, g)), 128 ** -0.5, None, ALU.mult, None, ["pv"], ["gq"])
        ident = self.cst("ident")
        wqkv = self.dr["w_qkv"][ja]
        wo = self.dr["w_ao"][ja]
        caches = (self.dr["c128"], self.dr["c512"], self.dr["c2048"])
        outs_p = (self.dr["kv128_p"], self.dr["kv512_p"], self.dr["kv2048_p"])
        outs_s = (self.dr["kv128_s"], self.dr["kv512_s"], self.dr["kv2048_s"])
        for g in range(3):
            L = WINS[g]
            for s in range(4):
                if STAGES["a_d2d"]:
                    self.dma(outs_s[g][ja, s, 0:L - 8], caches[g][ja, s, 8:L], (), ())
        ui = 0
        ti = 0
        for h in range(4):
            self.memset(acc[:, :, :], 0.0, ["acc"])
            for g in range(STAGES["a_groups"]):
                dil = DILS[g]
                L = WINS[g]
                nkb = (2048 // dil) // 128

                def proj(si, dst, dkey):
                    blk = (si * 3 + g) * 4 + h

                    def cons_p(bi, c0, n, ps, pk):
                        self.cp("act", dst[:, c0:c0 + n], ps[:, :n], [pk], ["%s.%d" % (dkey, bi)])
                    self.linear(wqkv[blk], 8, self.h_rhs, self.h_keys, cons_p)
                proj(0, raw[0], "raw0")
                proj(1, raw[1], "raw1")
                for bi, (c0, n) in enumerate(TB):
                    rs, rk = self.rstd_block([raw[0][:, c0:c0 + n]], ["raw0.%d" % bi], n, 128, EPS)
                    self.stt(QT[:, c0:c0 + n], raw[0][:, c0:c0 + n], gq[:, g:g + 1], rs[:, :n], ALU.mult, ALU.mult,
                             ["raw0.%d" % bi, rk, "gq"], ["QT.%d" % bi])
                    rs, rk = self.rstd_block([raw[1][:, c0:c0 + n]], ["raw1.%d" % bi], n, 128, EPS)
                    self.stt(raw[1][:, c0:c0 + n], raw[1][:, c0:c0 + n], self.pvc("ak_gain%d_%d" % (ja, g)), rs[:, :n],
                             ALU.mult, ALU.mult, ["raw1.%d" % bi, rk, "pv"], ["raw1.%d" % bi])
                    self.cp("act", KT[:, c0:c0 + n], raw[1][:, c0:c0 + n], ["raw1.%d" % bi], ["KT.%d" % bi])
                proj(2, raw[0], "raw0")
                allq = ["QT.%d" % b for b in range(5)]
                allk = ["KT.%d" % b for b in range(5)]
                o_ = outs_p[g][ja]
                for si, rsrc, rkn in ((1, raw[1], "raw1"), (2, raw[0], "raw0")):
                    rkeys = ["%s.%d" % (rkn, b) for b in range(4)]
                    for b4 in range(4):
                        tk = tok32[ti % 4]
                        tkk = "tok32_%d" % (ti % 4)
                        ti += 1
                        ps, pk = self.newps()
                        for q in range(4):
                            idx = b4 * 4 + q
                            r_, kb = idx // nkb, idx % nkb
                            st_ = r_ + dil * 128 * kb
                            self.tr(ps[:, q * 128:(q + 1) * 128], rsrc[:, st_:st_ + dil * 127 + 1:dil], ident, rkeys + ["cst"], [pk])
                        self.cp("act", tk[:, :, :], ps[:, :].rearrange("p (q e) -> p q e", e=128), [pk], [tkk])
                        if si == 2:
                            self.cp("dve", vtok[:, b4 * 4:b4 * 4 + 4, :], tk[:, :, :], [tkk], ["vtok"])
                        if not STAGES["a_kvout"]:
                            pass
                        elif g == 0:
                            if b4 == 3:
                                self.dma(o_[:, si - 1, h, :], tk[:, 3, :], [tkk], ())
                        elif g == 1:
                            dst = o_.rearrange("(j r) t h e -> j r t h e", r=dil)[:, b4, si - 1, h, :]
                            self.dma(dst, tk[:, 3, :], [tkk], ())
                        else:
                            dst = o_.rearrange("(j r) t h e -> j r t h e", r=dil)[:, b4 * 4:b4 * 4 + 4, si - 1, h, :]
                            self.dma(dst, tk[:, :, :], [tkk], ())
                    ps, pk = self.newps()
                    self.tr(ps[:32, 0:128], rsrc[:, 2048:2080], ident, ["%s.4" % rkn, "cst"], [pk])
                    self.cp("act", sstg[:32, si - 1, :], ps[:32, 0:128], [pk], ["sstg"])
                    if si == 2:
                        self.cp("dve", vs16[:32, :], sstg[:32, 1, :], ["sstg"], ["vs16"])
                for s in range(4):
                    if STAGES["a_sout"]:
                        self.dma(outs_s[g][ja, s, L - 8:L, :, h, :], sstg[8 * s:8 * s + 8, :, :], ["sstg"], ())
                for r_ in range(dil if STAGES["a_pu"] else 0):
                    for qb in range(nkb):
                        st_ = r_ + dil * 128 * qb
                        sl_ = slice(st_, st_ + dil * 127 + 1, dil)
                        blocks = [(KT[:, sl_], vtok[:, r_ * nkb + qb, :], self.cst16("m_own"), 128, allk + ["vtok"])]
                        if qb > 0:
                            sp_ = r_ + dil * 128 * (qb - 1)
                            blocks.append((KT[:, sp_:sp_ + dil * 127 + 1:dil], vtok[:, r_ * nkb + qb - 1, :], self.cst16("m_prev"),
                                           128, allk + ["vtok"]))
                        unit(QT[:, sl_], allq, 128, blocks, acc[:, :, sl_], ui)
                        ui += 1
                R = min(dil, 8)
                nq = 8 // R
                for s in range(4 if STAGES["a_su"] else 0):
                    flush()
                    src = caches[g][ja, s].rearrange("(j r) t h e -> j r t h e", r=dil)[:, 0:R, :, h, :]
                    self.dma(cb_[:, 0:R, :, :], src, (), ["cb"])
                    self.cp("dve", cv_[:, 0:R, :], cb_[:, 0:R, 1, :], ["cb"], ["cv"])
                    for r_ in range(R):
                        kc_ = kcT[ui % 2]
                        kck = "kcT%d" % (ui % 2)
                        ps, pk = self.newps()
                        self.tr(ps[:, 0:128], cb_[:, r_, 0, :], ident, ["cb", "cst"], [pk])
                        self.cp("act", kc_[:, :], ps[:, 0:128], [pk], [kck])
                        q0 = 2048 + 8 * s + r_
                        sl_ = slice(q0, q0 + R * (nq - 1) + 1, R)
                        oc = own_s_col(self.cca, g, s, r_) - self.cca.cols["own_s"]
                        blocks = [(kc_[:, :], cv_[:, r_, :], self.cst16("m_prev", 128, nq), 128, [kck, "cv"]),
                                  (KT[:, 2048:2080], vs16[:32, :], self.cst16("own_s", 32, nq, oc), 32, allk + ["vs16"])]
                        unit(QT[:, sl_], allq, nq, blocks, acc[:, :, sl_], ui)
                        ui += 1
                flush()
            self.recip(acc[:, 1, :], acc[:, 1, :], ["acc"], ["acc"])
            for bi, (c0, n) in enumerate(TB):
                self.tt(oT[:, c0:c0 + n], acc[:, 0, c0:c0 + n], acc[:, 1, c0:c0 + n], ALU.mult, ["acc"], ["ao.%d" % bi])
            for nb in range(8):
                self.linear(wo[nb][:, h:h + 1, :], 1, lambda kc, c0, n: oT[:, c0:c0 + n], lambda kc, bi: ["ao.%d" % bi],
                            self.add_resid(nb))
        self.release(m0)

    def hgrn(self, l):
        self.dnorm("norm_mix%d" % l)
        m0 = self.mark()
        Bq, Bz, Bk, Bm, Bd, Bv, Bb = [self.alloc(NT) for _ in range(7)]
        Sb = [self.alloc(128) for _ in range(2)]
        NLA = 4
        ktok = [self.alloc16(128) for _ in range(NLA)]
        vtk = [self.alloc16(128) for _ in range(NLA)]
        ATb = [self.alloc16(64) for _ in range(NLA)]
        kvb_ = [self.alloc(128) for _ in range(NLA)]
        ident16 = self.cst16("ident")
        ebC = self.alloc(36)
        ex = self.alloc(32).rearrange("p (l c) -> p l c", c=8)
        lbv = self.alloc(8)
        omlb = self.alloc(8)
        tot = self.alloc(8)
        oTh = self.alloc16(NT)

        def K(n):
            return ["%s.%d" % (n, b) for b in range(5)]
        c_lb = self.pvca.cols["hg_lb0"]
        self.act(ex[:, :, :], self.pv[:, c_lb:c_lb + 32].rearrange("p (l c) -> p l c", c=8), AF.Exp, ["pv"], ["hex"])
        self.tt(tot, ex[:, 0, :], ex[:, 1, :], ALU.add, ["hex"], ["htot"])
        self.tt(tot, tot, ex[:, 2, :], ALU.add, ["hex", "htot"], ["htot"])
        self.tt(tot, tot, ex[:, 3, :], ALU.add, ["hex", "htot"], ["htot"])
        self.cp("dve", lbv, ex[:, 1, :], ["hex"], ["hlb"])
        for l2 in range(2, l + 1):
            self.tt(lbv, lbv, ex[:, l2, :], ALU.add, ["hex", "hlb"], ["hlb"])
        self.recip(tot, tot, ["htot"], ["htot"])
        self.tt(lbv, lbv, tot, ALU.mult, ["hlb", "htot"], ["hlb"])
        self.ts(omlb, lbv, -1.0, 1.0, ALU.mult, ALU.add, ["hlb"], ["homlb"])
        ident = self.cst("ident")
        m_own = self.cst("m_own")
        win = self.dr["w_hgin"]
        wo = self.dr["w_hgo"]
        ci = 0
        si = 0
        for h in range(8):
            def proj(blk, dst, dkey):
                def cons_p(bi, c0, n, ps, pk):
                    self.cp("act", dst[:, c0:c0 + n], ps[:, :n], [pk], ["%s.%d" % (dkey, bi)])
                self.linear(win[blk], 8, self.h_rhs, self.h_keys, cons_p)
            proj(h, Bq, "hq")
            proj(8 + h, Bz, "hz")
            proj(16 + h, Bv, "hv")
            self.act(Bz[:, :], Bz[:, :], AF.Sigmoid, K("hz"), K("hz"))
            self.ts(Bz[:, :], Bz[:, :], omlb[:, h:h + 1], lbv[:, h:h + 1], ALU.mult, ALU.add, K("hz") + ["hlb", "homlb"], K("hz"))
            self.ts(Bk[:, :], Bz[:, :], -1.0, 1.0, ALU.mult, ALU.add, K("hz"), K("hk"))
            self.act(Bz[:, :], Bz[:, :], AF.Ln, K("hz"), K("hz"))
            self.memset(Bm[:, :], 1.0, K("hm"))
            self.memset(Bm[:, 0:2048:64], 0.0, K("hm"))
            self.memset(Bm[:, 2048:2080:8], 0.0, K("hm"))
            self.P.add("dve", lambda e: e.tensor_tensor_scan(Bb[:, :], Bm[:, :], Bz[:, :], 0.0, ALU.mult, ALU.add),
                       K("hm") + K("hz"), K("hb"))
            self.act(ebC[:, 0:32], Bb[:, 63:2048:64], AF.Exp, K("hb"), ["hebc"])
            self.act(ebC[:, 32:36], Bb[:, 2055:2080:8], AF.Exp, K("hb"), ["hebc"])
            self.act(Bq[:, :], Bq[:, :], AF.Silu, K("hq"), K("hq"))
            self.act(Bz[:, :], Bb[:, :], AF.Exp, K("hb") + K("hz"), K("hz"))
            self.tt(Bq[:, :], Bq[:, :], Bz[:, :], ALU.mult, K("hq") + K("hz"), K("hq"))
            Z16 = Bz[:, :].bitcast(BF16)
            qe16 = Z16[:, 0:NT]
            v16 = Z16[:, NT:2 * NT]
            self.cp("act", qe16, Bq[:, :], K("hq") + K("hz"), K("hz"))
            self.cp("dve", v16, Bv[:, :], K("hv") + K("hz"), K("hz"))
            bp = Bb[:, 0:2048].rearrange("p (n c) -> p n c", c=64)
            self.tt(Bd[:, 0:2048].rearrange("p (n c) -> p n c", c=64), bp[:, :, 63:64].to_broadcast([128, 32, 64]), bp, ALU.subtract,
                    K("hb"), K("hd"))
            bs = Bb[:, 2048:2080].rearrange("p (n c) -> p n c", c=8)
            self.tt(Bd[:, 2048:2080].rearrange("p (n c) -> p n c", c=8), bs[:, :, 7:8].to_broadcast([128, 4, 8]), bs, ALU.subtract,
                    K("hb"), K("hd"))
            self.act(Bd[:, :], Bd[:, :], AF.Exp, K("hd"), K("hd"))
            self.act(Bm[:, :], Bb[:, :], AF.Exp, K("hb") + K("hm"), K("hm"), scale=-1.0)
            B16 = Bb[:, :].bitcast(BF16)
            ke16 = B16[:, 0:NT]
            kd16 = B16[:, NT:2 * NT]
            self.tt(ke16, Bm[:, :], Bk[:, :], ALU.mult, K("hm") + K("hk") + K("hb"), K("hb"))
            self.tt(kd16, Bd[:, :], Bk[:, :], ALU.mult, K("hd") + K("hk") + K("hb"), K("hb"))
            psT = [p_[:, :].bitcast(BF16) for p_ in self.ps]

            def chunk_a(c0, C, bi):
                nonlocal ci
                kt, ktk = ktok[ci % NLA], "hkt%d" % (ci % NLA)
                vt, vtkk = vtk[ci % NLA], "hvt%d" % (ci % NLA)
                AT, atk = ATb[ci % NLA], "hat%d" % (ci % NLA)
                kvb, kvk = kvb_[ci % NLA], "hkv%d" % (ci % NLA)
                ci += 1
                i1 = self.psi % 8
                ps, pk = self.newps()
                self.tr(psT[i1][:C, 0:128], kd16[:, c0:c0 + C], ident16, K("hb") + ["cst16"], [pk])
                self.tr(psT[i1][:C, 128:256], v16[:, c0:c0 + C], ident16, K("hz") + ["cst16"], [pk])
                self.cp("act", kt[:C, :], psT[i1][:C, 0:128], [pk], [ktk])
                self.cp("act", vt[:C, :], psT[i1][:C, 128:256], [pk], [vtkk])
                ps2, pk2 = self.newps()
                self.mm(ps2[:C, :C], ke16[:, c0:c0 + C], qe16[:, c0:c0 + C], True, True, K("hb") + K("hz"), [pk2])
                self.tt(AT[:C, :C], ps2[:C, :C], m_own[:C, :C], ALU.mult, [pk2, "cst"], [atk])
                ps4, pk4 = self.newps()
                self.mm(ps4[:, 0:128], kt[:C, :], vt[:C, :], True, True, [ktk, vtkk], [pk4])
                self.cp("act", kvb[:, :], ps4[:, 0:128], [pk4], [kvk])
                return (c0, C, bi, vt, vtkk, AT, atk, kvb, kvk)

            def chunk_b(ctx, S, Sk, ecol):
                c0, C, bi, vt, vtkk, AT, atk, kvb, kvk = ctx
                ps3, pk3 = self.newps()
                self.mm(ps3[:, :C], S[:, :], Bq[:, c0:c0 + C], True, False, [Sk, "hq.%d" % bi], [pk3])
                self.mm(ps3[:, :C], vt[:C, :], AT[:C, :C], False, True, [vtkk, atk], [pk3])
                self.cp("act", Bk[:, c0:c0 + C], ps3[:, :C], [pk3], ["hk.%d" % bi])
                self.stt(S[:, :], S[:, :], ebC[:, ecol:ecol + 1], kvb[:, :], ALU.mult, ALU.add, [Sk, "hebc", kvk], [Sk])
            S, Sk = Sb[si % 2], "hS%d" % (si % 2)
            si += 1
            self.memset(S[:, :], 0.0, [Sk])
            LOOK = 2
            ctxs = {}
            for n_ in range(32 + LOOK):
                if n_ < 32:
                    ctxs[n_] = chunk_a(64 * n_, 64, n_ // 8)
                if n_ >= LOOK:
                    chunk_b(ctxs.pop(n_ - LOOK), S, Sk, n_ - LOOK)
            self.dma(self.dr["hg_p"][h], S[:, :], [Sk], ())
            sctx = [chunk_a(2048 + 8 * s, 8, 4) for s in range(2)]
            for s in range(4):
                S, Sk = Sb[si % 2], "hS%d" % (si % 2)
                si += 1
                self.dma(S[:, :], self.dr["st_hg"][s, h], (), [Sk])
                chunk_b(sctx.pop(0), S, Sk, 32 + s)
                if s + 2 < 4:
                    sctx.append(chunk_a(2048 + 8 * (s + 2), 8, 4))
                self.dma(self.dr["hg_s"][s, h], S[:, :], [Sk], ())
            proj(24 + h, Bq, "hq")
            self.act(Bq[:, :], Bq[:, :], AF.Silu, K("hq"), K("hq"))
            for bi, (c0, n) in enumerate(TB):
                rs, rk = self.rstd_block([Bk[:, c0:c0 + n]], ["hk.%d" % bi], n, 128, EPS)
                self.stt(Bk[:, c0:c0 + n], Bk[:, c0:c0 + n], self.pvc("hg_out_gain"), rs[:, :n], ALU.mult, ALU.mult,
                         ["hk.%d" % bi, rk, "pv"], ["hk.%d" % bi])
                self.tt(oTh[:, c0:c0 + n], Bk[:, c0:c0 + n], Bq[:, c0:c0 + n], ALU.mult, ["hk.%d" % bi, "hq.%d" % bi], ["hoT.%d" % bi])
            for nb in range(8):
                self.linear(wo[nb][:, h:h + 1, :], 1, lambda kc, c0, n: oTh[:, c0:c0 + n], lambda kc, bi: ["hoT.%d" % bi],
                            self.add_resid(nb))
        self.release(m0)

    def rwkv(self, l):
        self.dnorm("norm_mix%d" % l)
        m0 = self.mark()
        NCH = 4
        ident = self.cst("ident")
        blk64 = self.cst("blk64")
        pvc = self.pvc

        def hk2(c0, n, kc):
            b0 = min(max(c0 - 1, 0) // 512, 4)
            b1 = min((c0 + n - 1) // 512, 4)
            return ["h%d.%d" % (kc, b) for b in sorted({b0, b1})]
        xl = self.alloc(40).rearrange("p (c t) -> p c t", t=5)
        self.cp("dve", xl[:, :, 0:1], self.xT[:, :, 2047:2048], ["x%d.3" % c for c in range(8)], ["xl"])
        self.cp("dve", xl[:, :, 1:5], self.xT[:, :, 2055:2080:8], ["x%d.4" % c for c in range(8)], ["xl"])
        rs, rk = self.rstd_block([xl[:, c, :] for c in range(8)], ["xl"] * 8, 5, D, EPS)
        for c in range(8):
            self.stt(xl[:, c, :], xl[:, c, :], pvc("norm_mix%d" % l, c), rs[:, 0:5], ALU.mult, ALU.mult, ["xl", rk, "pv"], ["xl"])
        self.dma(self.dr["sh_o"], xl, ["xl"], ())
        hsp = self.alloc16(8 * 32).rearrange("p (c t) -> p c t", t=32)
        shs = self.alloc(32).rearrange("p (c s) -> p c s", s=4)
        self.dma(shs, self.dr["st_sh"], (), ["shs"])
        self.cp("dve", hsp[:, :, 0:32:8], shs, ["shs"], ["hsp"])
        for c in range(8):
            self.cp("dve", hsp[:, c, :].rearrange("p (s i) -> p s i", i=8)[:, :, 1:8],
                    self.hT[:, c, 2048:2080].rearrange("p (s i) -> p s i", i=8)[:, :, 0:7], ["h%d.4" % c], ["hsp"])
        omu = self.alloc(48)
        cmu = self.pvca.cols["rw_mu0"]
        mu = self.pv[:, cmu:cmu + 48]
        self.ts(omu, mu, -1.0, 1.0, ALU.mult, ALU.add, ["pv"], ["omu"])
        omka = self.alloc(8)
        self.ts(omka, pvc("rw_k_a", 0, 8), -1.0, 1.0, ALU.mult, ALU.add, ["pv"], ["omka"])
        wst = self.alloc(8 * 160)

        def proj(ps, pk, m, wA, wB, wkeys, c0, n, sample):
            for kc in range(8):
                self.mm(ps[:m, :n], wA(kc), self.hT[:, kc, c0:c0 + n], kc == 0, False, wkeys + hk2(c0, n, kc), [pk])
            for kc in range(8):
                last = kc == 7
                if sample:
                    self.mm(ps[:m, :n], wB(kc), hsp[:, kc, :], False, last, wkeys + ["hsp"], [pk])
                elif c0 == 0:
                    self.mm(ps[:m, 1:n], wB(kc), self.hT[:, kc, 0:n - 1], False, last, wkeys + hk2(c0, n, kc), [pk])
                else:
                    self.mm(ps[:m, :n], wB(kc), self.hT[:, kc, c0 - 1:c0 - 1 + n], False, last, wkeys + hk2(c0, n, kc), [pk])

        def scaled(dstA, dstB, src, ncol, j, keyA, keyB, skey):
            mj = mu[:, 8 * j:8 * j + 8].unsqueeze(2).to_broadcast([128, 8, ncol])
            oj = omu[:, 8 * j:8 * j + 8].unsqueeze(2).to_broadcast([128, 8, ncol])
            self.tt(dstA, src, oj, ALU.mult, [skey, "omu"], [keyA])
            self.tt(dstB, src, mj, ALU.mult, [skey, "pv"], [keyB])
        TWA = self.alloc16(NT)
        TG1 = self.alloc16(NT)
        TG2 = self.alloc16(NT)
        l1A = self.alloc16(8 * 160).rearrange("p (k n) -> p k n", n=160)
        l1B = self.alloc16(8 * 160).rearrange("p (k n) -> p k n", n=160)
        w64 = wst[:, 0:8 * 64].rearrange("p (k n) -> p k n", n=64)
        for (nm, j, lo) in (("rw1", 3, 0), ("ra1", 4, 64)):
            self.dma(w64, self.dr[nm], (), ["wst"])
            scaled(l1A[:, :, lo:lo + 64], l1B[:, :, lo:lo + 64], w64, 64, j, "l1A", "l1B", "wst")
        for bi, (c0, n) in enumerate(TB):
            ps, pk = self.newps()
            proj(ps, pk, 128, lambda kc: l1A[:, kc, 0:128], lambda kc: l1B[:, kc, 0:128], ["l1A", "l1B"], c0, n, bi == 4)
            self.act(TWA[0:64, c0:c0 + n], ps[0:64, :n], AF.Tanh, [pk], ["twa.%d" % bi])
            self.cp("act", TWA[64:128, c0:c0 + n], ps[64:128, :n], [pk], ["twa.%d" % bi])
        w160 = wst[:, 0:8 * 160].rearrange("p (k n) -> p k n", n=160)
        self.dma(w160, self.dr["rg1"], (), ["wst"])
        scaled(l1A[:, :, :], l1B[:, :, :], w160, 160, 5, "l1A", "l1B", "wst")
        for bi, (c0, n) in enumerate(TB):
            ps, pk = self.newps()
            proj(ps, pk, 128, lambda kc: l1A[:, kc, 0:128], lambda kc: l1B[:, kc, 0:128], ["l1A", "l1B"], c0, n, bi == 4)
            self.act(TG1[:, c0:c0 + n], ps[:, :n], AF.Sigmoid, [pk], ["tg.%d" % bi])
            ps, pk = self.newps()
            proj(ps, pk, 32, lambda kc: l1A[:, kc, 128:160], lambda kc: l1B[:, kc, 128:160], ["l1A", "l1B"], c0, n, bi == 4)
            self.act(TG2[0:32, c0:c0 + n], ps[0:32, :n], AF.Sigmoid, [pk], ["tg.%d" % bi])
        W2A = self.alloc16(128)
        G2a = self.alloc16(128)
        G2b = self.alloc16(128)
        yT = self.alloc16(NT)
        NB_ = 256
        Rr, Rk, Rv, Rld, Ra, Rg, Rkk, Rk2, Rcum, Rt1, Rt2, Rbon, Ry = [self.alloc(NB_) for _ in range(13)]
        blk = [self.alloc(NCH * 128).rearrange("p (j t) -> p j t", t=128) for _ in range(5)]
        Ablk, Bblk, Kblk, Rblk, Vblk = blk
        for b_ in blk:
            self.memset(b_[:, :, :], 0.0, ["blk"])
        WSk = [self.alloc(NCH * 128).rearrange("p (j t) -> p j t", t=128) for _ in range(9)]
        STb = [self.alloc(128) for _ in range(2)]
        DC = self.alloc(NCH)
        maskp = self.alloc(NB_)
        masks = self.alloc(32)
        self.memset(maskp, 1.0, ["maskp"])
        self.memset(maskp[:, 0:NB_:64], 0.0, ["maskp"])
        self.memset(masks, 1.0, ["masks"])
        self.memset(masks[:, 0:32:8], 0.0, ["masks"])
        bd_su, bd_sl, bd_iu = self.cst("bd_su"), self.cst("bd_sl"), self.cst("bd_iu")
        RB = [(NB_ * i, NB_, False) for i in range(2048 // NB_)] + [(2048, 32, True)]
        sti = 0
        wo = self.dr["w_rwo"]
        for p in range(8):
            w128 = wst[:, 0:1024].rearrange("p (k n) -> p k n", n=128)
            for j in range(3):
                self.dma(w128, self.dr["w_rkv"][j, p], (), ["wst"])
                scaled(self.wb[2 * j][:, :, :], self.wb[2 * j + 1][:, :, :], w128, 128, j, "wb%d" % (2 * j), "wb%d" % (2 * j + 1), "wst")
            self.dma(W2A[0:64, :], self.dr["rw2"][:, p * 128:(p + 1) * 128], (), ["w2a"], q="pool")
            self.dma(W2A[64:128, :], self.dr["ra2"][:, p * 128:(p + 1) * 128], (), ["w2a"], q="pool")
            self.dma(G2a[:, :], self.dr["rg2"][0:128, p * 128:(p + 1) * 128], (), ["g2"], q="pool")
            self.dma(G2b[0:32, :], self.dr["rg2"][128:160, p * 128:(p + 1) * 128], (), ["g2"], q="pool")
            ST, STk = STb[sti % 2], "rST%d" % (sti % 2)
            sti += 1
            self.memset(ST[:, :], 0.0, [STk])
            for (c0, n, smp) in RB:
                bi = min(c0 // 512, 4)
                C = 8 if smp else 64
                nch = 4
                for j, dst, dk in ((0, Rr, "Rr"), (1, Rk, "Rk"), (2, Rv, "Rv")):
                    ps, pk = self.newps()
                    proj(ps, pk, 128, lambda kc, j=j: self.wb[2 * j][:, kc, :], lambda kc, j=j: self.wb[2 * j + 1][:, kc, :],
                         ["wb%d" % (2 * j), "wb%d" % (2 * j + 1)], c0, n, smp)
                    self.cp("act", dst[:, :n], ps[:, :n], [pk], [dk])
                ps, pk = self.newps()
                self.mm(ps[:, :n], W2A[0:64, :], TWA[0:64, c0:c0 + n], True, True, ["w2a", "twa.%d" % bi], [pk])
                self.act(Rld[:, :n], ps[:, :n], AF.Sigmoid, [pk, "pv"], ["Rld"], bias=pvc("rw_w0", p))
                self.ts(Rld[:, :n], Rld[:, :n], -0.6065306597126334, None, ALU.mult, None, ["Rld"], ["Rld"])
                ps, pk = self.newps()
                self.mm(ps[:, :n], W2A[64:128, :], TWA[64:128, c0:c0 + n], True, True, ["w2a", "twa.%d" % bi], [pk])
                self.act(Ra[:, :n], ps[:, :n], AF.Sigmoid, [pk, "pv"], ["Ra"], bias=pvc("rw_a0", p))
                ps, pk = self.newps()
                self.mm(ps[:, :n], G2a[:, :], TG1[:, c0:c0 + n], True, False, ["g2", "tg.%d" % bi], [pk])
                self.mm(ps[:, :n], G2b[0:32, :], TG2[0:32, c0:c0 + n], False, True, ["g2", "tg.%d" % bi], [pk])
                self.cp("act", Rg[:, :n], ps[:, :n], [pk], ["Rg"])
                self.ts(Rkk[:, :n], Rk[:, :n], pvc("rw_k_k", p), None, ALU.mult, None, ["Rk", "pv"], ["Rkk"])
                self.act(Rt1[:, :n], Rkk[:, :n], AF.Square, ["Rkk"], ["Rt1"])
                ps, pk = self.newps()
                self.mm(ps[:, :n], blk64, Rt1[:, :n], True, True, ["cst", "Rt1"], [pk])
                self.act(Rt2[:, :n], ps[:, :n], AF.Sqrt, [pk], ["Rt2"])
                self.ts(Rt2[:, :n], Rt2[:, :n], 1e-12, None, ALU.max, None, ["Rt2"], ["Rt2"])
                self.recip(Rt2[:, :n], Rt2[:, :n], ["Rt2"], ["Rt2"])
                self.tt(Rkk[:, :n], Rkk[:, :n], Rt2[:, :n], ALU.mult, ["Rkk", "Rt2"], ["Rkk"])
                self.ts(Rt1[:, :n], Ra[:, :n], pvc("rw_k_a", p), omka[:, p:p + 1], ALU.mult, ALU.add, ["Ra", "pv", "omka", "Rt1"], ["Rt1"])
                self.tt(Rk2[:, :n], Rk[:, :n], Rt1[:, :n], ALU.mult, ["Rk", "Rt1"], ["Rk2"])
                self.tt(Rt1[:, :n], Rr[:, :n], Rk2[:, :n], ALU.mult, ["Rr", "Rk2", "Rt1"], ["Rt1"])
                self.ts(Rt1[:, :n], Rt1[:, :n], pvc("rw_r_k", p), None, ALU.mult, None, ["Rt1", "pv"], ["Rt1"])
                ps, pk = self.newps()
                self.mm(ps[:, :n], blk64, Rt1[:, :n], True, True, ["cst", "Rt1"], [pk])
                self.tt(Rbon[:, :n], ps[:, :n], Rv[:, :n], ALU.mult, [pk, "Rv"], ["Rbon"])
                mk_ = masks if smp else maskp
                mkk = "masks" if smp else "maskp"
                self.P.add("dve", lambda e, n=n, mk_=mk_: e.tensor_tensor_scan(Rcum[:, :n], mk_[:, :n], Rld[:, :n], 0.0, ALU.mult, ALU.add),
                           [mkk, "Rld"], ["Rcum"])
                self.act(DC[:, 0:nch], Rcum[:, C - 1:n:C], AF.Exp, ["Rcum"], ["DC"])
                if smp:
                    for b_ in blk:
                        self.memset(b_[:, :, :], 0.0, ["blk"])

                def toblk(dst, fn, rkeys):
                    for hh in range(2):
                        rows = slice(64 * hh, 64 * hh + 64)
                        ov = dst[rows, 0:nch, 64 * hh:64 * hh + C]
                        fn(ov, rows, lambda x: x[rows, 0:n].rearrange("p (j s) -> p j s", s=C))
                self.act(Rt1[:, :n], Rcum[:, :n], AF.Exp, ["Rcum", "Rt1"], ["Rt1"])
                toblk(Rblk, lambda ov, rows, V: self.tt(ov, V(Rr), V(Rt1), ALU.mult, ["Rr", "Rt1"], ["blk"]), None)
                self.act(Rt1[:, :n], Rcum[:, :n], AF.Exp, ["Rcum", "Rt1", "blk"], ["Rt1"], scale=-1.0)
                self.tt(Rt2[:, :n], Rkk[:, :n], Ra[:, :n], ALU.mult, ["Rkk", "Ra", "Rt2"], ["Rt2"])
                toblk(Bblk, lambda ov, rows, V: self.tt(ov, V(Rt2), V(Rt1), ALU.mult, ["Rt2", "Rt1"], ["blk"]), None)
                toblk(Kblk, lambda ov, rows, V: self.tt(ov, V(Rk2), V(Rt1), ALU.mult, ["Rk2", "Rt1"], ["blk"]), None)
                self.tt(Rt2[:, :n], Rcum[:, :n], Rld[:, :n], ALU.subtract, ["Rcum", "Rld", "Rt2", "blk"], ["Rt2"])
                self.act(Rt2[:, :n], Rt2[:, :n], AF.Exp, ["Rt2"], ["Rt2"])
                toblk(Ablk, lambda ov, rows, V: self.stt(ov, V(Rkk), -1.0, V(Rt2), ALU.mult, ALU.mult, ["Rkk", "Rt2"], ["blk"]), None)
                toblk(Vblk, lambda ov, rows, V: self.cp("act", ov, V(Rv), ["Rv"], ["blk"]), None)
                def half(hf):
                    j0 = 2 * hf
                    js = (j0, j0 + 1)

                    def wk(i):
                        return "rws%d.%d" % (i, hf)

                    def bc(m):
                        return m.unsqueeze(1).to_broadcast([128, 2, 128])

                    def v2(ps):
                        return ps[:, 0:256].rearrange("p (j t) -> p j t", t=128)

                    def W(i):
                        return WSk[i][:, j0:j0 + 2, :]
                    psa, pka = self.newps()
                    psb, pkb = self.newps()
                    for q, j in enumerate(js):
                        self.mm(psa[:, q * 128:(q + 1) * 128], Bblk[:, j, :], Ablk[:, j, :], True, True, ["blk"], [pka])
                        self.mm(psb[:, q * 128:(q + 1) * 128], Ablk[:, j, :], Bblk[:, j, :], True, True, ["blk"], [pkb])
                    yield
                    self.tt(W(0), v2(psa), bc(bd_su), ALU.mult, [pka, "cst"], [wk(0)])
                    self.tt(W(1), v2(psb), bc(bd_sl), ALU.mult, [pkb, "cst"], [wk(1)])
                    self.tt(W(4), W(0), bc(ident), ALU.add, [wk(0), "cst"], [wk(4)], eng="pool")
                    yield
                    cur = (0, 1)
                    nxt = (2, 3)
                    for step in range(5):
                        ia, iat = cur
                        in_, int_ = nxt
                        psa, pka = self.newps()
                        if step < 4:
                            psb, pkb = self.newps()
                        for q, j in enumerate(js):
                            self.mm(psa[:, q * 128:(q + 1) * 128], WSk[ia][:, j, :], WSk[iat][:, j, :], True, True, [wk(ia), wk(iat)], [pka])
                            if step < 4:
                                self.mm(psb[:, q * 128:(q + 1) * 128], WSk[iat][:, j, :], WSk[ia][:, j, :], True, True, [wk(ia), wk(iat)], [pkb])
                        yield
                        self.cp("act", W(int_), v2(psa), [pka], [wk(int_)])
                        if step < 4:
                            self.cp("act", W(in_), v2(psb), [pkb], [wk(in_)])
                        psc, pkc = self.newps()
                        for q, j in enumerate(js):
                            self.mm(psc[:, q * 128:(q + 1) * 128], WSk[int_][:, j, :], WSk[4][:, j, :], True, True, [wk(int_), wk(4)], [pkc])
                        yield
                        self.tt(W(4), v2(psc), W(4), ALU.add, [pkc, wk(4)], [wk(4)])
                        cur, nxt = nxt, cur
                    ps1, pk1 = self.newps()
                    for q, j in enumerate(js):
                        self.mm(ps1[:, q * 128:(q + 1) * 128], Kblk[:, j, :], Ablk[:, j, :], True, True, ["blk"], [pk1])
                        self.mm(ps1[:, 256 + q * 128:256 + (q + 1) * 128], Bblk[:, j, :], Rblk[:, j, :], True, True, ["blk"], [pk1])
                    ps3, pk3 = self.newps()
                    for q, j in enumerate(js):
                        self.mm(ps3[:, q * 128:(q + 1) * 128], Kblk[:, j, :], Rblk[:, j, :], True, True, ["blk"], [pk3])
                    yield
                    self.tt(W(0), v2(ps1), bc(bd_su), ALU.mult, [pk1, "cst"], [wk(0)])
                    self.tt(W(1), ps1[:, 256:512].rearrange("p (j t) -> p j t", t=128), bc(bd_iu), ALU.mult, [pk1, "cst"], [wk(1)])
                    self.tt(W(2), v2(ps3), bc(bd_iu), ALU.mult, [pk3, "cst"], [wk(2)])
                    ps1, pk1 = self.newps()
                    for q, j in enumerate(js):
                        self.tr(ps1[:, q * 128:(q + 1) * 128], Vblk[:, j, :], ident, ["blk", "cst"], [pk1])
                        self.tr(ps1[:, 256 + q * 128:256 + (q + 1) * 128], Bblk[:, j, :], ident, ["blk", "cst"], [pk1])
                    ps3, pk3 = self.newps()
                    for q, j in enumerate(js):
                        self.tr(ps3[:, q * 128:(q + 1) * 128], Kblk[:, j, :], ident, ["blk", "cst"], [pk3])
                    yield
                    self.cp("act", W(3), v2(ps1), [pk1], [wk(3)])
                    self.cp("act", W(5), ps1[:, 256:512].rearrange("p (j t) -> p j t", t=128), [pk1], [wk(5)])
                    self.cp("act", W(6), v2(ps3), [pk3], [wk(6)])
                    yield
                gens = [half(0), half(1)]
                while gens:
                    for g_ in list(gens):
                        try:
                            next(g_)
                        except StopIteration:
                            gens.remove(g_)
                Mk_ = [WSk[0][:, j, :] for j in range(nch)]
                Nb_ = [WSk[1][:, j, :] for j in range(nch)]
                Nk_ = [WSk[2][:, j, :] for j in range(nch)]
                Vt_ = [WSk[3][:, j, :] for j in range(nch)]
                P_ = [WSk[4][:, j, :] for j in range(nch)]
                Bt_ = [WSk[5][:, j, :] for j in range(nch)]
                Kt_ = [WSk[6][:, j, :] for j in range(nch)]
                XT_ = [WSk[7][:, j, :] for j in range(nch)]
                UT_ = [WSk[8][:, j, :] for j in range(nch)]

                def wkj(j, i):
                    return "rws%d.%d" % (i, j // 2) if i < 7 else "rws%d.j%d" % (i, j)
                for j in range(nch):
                    if smp:
                        ST, STk = STb[sti % 2], "rST%d" % (sti % 2)
                        sti += 1
                        self.memset(ST[:, :], 0.0, [STk])
                        for hh in range(2):
                            self.dma(ST[64 * hh:64 * hh + 64, 64 * hh:64 * hh + 64], self.dr["st_rw"][j, 2 * p + hh], (), [STk])
                    ps, pk = self.newps()
                    self.mm(ps[:, 0:128], Ablk[:, j, :], ST[:, :], True, False, ["blk", STk], [pk])
                    self.mm(ps[:, 0:128], Mk_[j], Vt_[j], False, True, [wkj(j, 0), wkj(j, 3)], [pk])
                    self.cp("act", XT_[j], ps[:, 0:128], [pk], [wkj(j, 7)])
                    ps, pk = self.newps()
                    self.mm(ps[:, 0:128], P_[j], XT_[j], True, True, [wkj(j, 4), wkj(j, 7)], [pk])
                    self.cp("dve", UT_[j], ps[:, 0:128], [pk], [wkj(j, 8)])
                    ps, pk = self.newps()
                    self.mm(ps[:, 0:128], ST[:, :], Rblk[:, j, :], True, False, [STk, "blk"], [pk])
                    self.mm(ps[:, 0:128], UT_[j], Nb_[j], False, False, [wkj(j, 8), wkj(j, 1)], [pk])
                    self.mm(ps[:, 0:128], Vt_[j], Nk_[j], False, True, [wkj(j, 3), wkj(j, 2)], [pk])
                    for hh in range(2):
                        self.cp("act", Ry[64 * hh:64 * hh + 64, j * C:(j + 1) * C], ps[64 * hh:64 * hh + 64, 64 * hh:64 * hh + C], [pk], ["Ry"])
                    ps, pk = self.newps()
                    self.mm(ps[:, 0:128], Bt_[j], UT_[j], True, False, [wkj(j, 5), wkj(j, 8)], [pk])
                    self.mm(ps[:, 0:128], Kt_[j], Vt_[j], False, True, [wkj(j, 6), wkj(j, 3)], [pk])
                    self.tt(ST[:, :], ST[:, :], ps[:, 0:128], ALU.add, [STk, pk], [STk])
                    self.act(ST[:, :], ST[:, :], AF.Identity, [STk, "DC"], [STk], scale=DC[:, j:j + 1])
                    if smp:
                        for hh in range(2):
                            self.dma(self.dr["rw_s"][j, 2 * p + hh], ST[64 * hh:64 * hh + 64, 64 * hh:64 * hh + 64], [STk], ())
                if (not smp) and c0 + n == 2048:
                    for hh in range(2):
                        self.dma(self.dr["rw_p"][2 * p + hh], ST[64 * hh:64 * hh + 64, 64 * hh:64 * hh + 64], [STk], ())
                ps, pk = self.newps()
                self.mm(ps[:, :n], blk64, Ry[:, :n], True, True, ["cst", "Ry"], [pk])
                self.stt(Rt1[:, :n], ps[:, :n], -1.0 / 64, Ry[:, :n], ALU.mult, ALU.add, [pk, "Ry", "Rt1"], ["Rt1"])
                self.act(Rt2[:, :n], Rt1[:, :n], AF.Square, ["Rt1", "Rt2"], ["Rt2"])
                ps, pk = self.newps()
                self.mm(ps[:, :n], blk64, Rt2[:, :n], True, True, ["cst", "Rt2"], [pk])
                self.act(Rt2[:, :n], ps[:, :n], AF.Sqrt, [pk, "epsc"], ["Rt2"], scale=1.0 / 64, bias=self.epsc[:, 1:2])
                self.recip(Rt2[:, :n], Rt2[:, :n], ["Rt2"], ["Rt2"])
                self.tt(Rt1[:, :n], Rt1[:, :n], Rt2[:, :n], ALU.mult, ["Rt1", "Rt2"], ["Rt1"])
                self.ts(Rt1[:, :n], Rt1[:, :n], pvc("rw_ln_g", p), pvc("rw_ln_b", p), ALU.mult, ALU.add, ["Rt1", "pv"], ["Rt1"])
                self.tt(Rt1[:, :n], Rt1[:, :n], Rbon[:, :n], ALU.add, ["Rt1", "Rbon"], ["Rt1"])
                self.tt(yT[:, c0:c0 + n], Rt1[:, :n], Rg[:, :n], ALU.mult, ["Rt1", "Rg"], ["ryT.%d" % bi])
            for nb in range(8):
                self.linear(wo[nb][:, p:p + 1, :], 1, lambda kc, c0, n: yT[:, c0:c0 + n], lambda kc, bi: ["ryT.%d" % bi],
                            self.add_resid(nb))
        self.release(m0)

    def build(self):
        with contextlib.ExitStack() as st:
            self.setup(st)
            self.load_x()
            self.mem_prep()
            for l in range(STAGES["layers"]):
                kind = l % 3
                if kind == 0:
                    if STAGES["attn"]:
                        self.attn(l, l // 3)
                elif kind == 1:
                    if STAGES["hgrn"]:
                        self.hgrn(l)
                else:
                    if STAGES["rwkv"]:
                        self.rwkv(l)
                if STAGES["mem"]:
                    self.mem_attend(l)
                if STAGES["ffn"]:
                    self.ffn(l)
            self.store_y()
            self.P.emit()


IN_SPECS = [
    ("xp", [2048, 1024]), ("xs", [32, 1024]), ("mem", [256, 1024]),
    ("c128", [2, 4, 128, 2, 4, 128]), ("c512", [2, 4, 512, 2, 4, 128]), ("c2048", [2, 4, 2048, 2, 4, 128]),
    ("st_hg", [4, 8, 128, 128]), ("st_rw", [4, 16, 64, 64]), ("st_sh", [128, 8, 4]),
    ("st_fc", [4, 128, NJ, 4, 2]), ("cmem", [4, 4, 256, 2, 4, 256]),
    ("w_qkv", [2, 36, 128, 8, 128]), ("w_ao", [2, 8, 128, 4, 128]),
    ("w_hgin", [32, 128, 8, 128]), ("w_hgo", [8, 128, 8, 128]),
    ("w_rkv", [3, 8, 128, 8, 128]), ("rw1", [128, 8, 64]), ("ra1", [128, 8, 64]), ("rg1", [128, 8, 160]),
    ("rw2", [64, 1024]), ("ra2", [64, 1024]), ("rg2", [160, 1024]), ("w_rwo", [8, 128, 8, 128]),
    ("w_xq", [4, 8, 128, 8, 128]), ("w_xkv", [4, 16, 128, 8, 128]), ("w_xo", [4, 8, 128, 8, 128]),
    ("w_fin", [4, 2 * NJ, 128, 8, 128]), ("w_fdn", [4, 8, 128, NJ, 128]),
]
OUT_SPECS = [
    ("y_p", [2048, 1024]), ("y_s", [32, 1024]),
    ("kv128_p", [2, 128, 2, 4, 128]), ("kv512_p", [2, 512, 2, 4, 128]), ("kv2048_p", [2, 2048, 2, 4, 128]),
    ("hg_p", [8, 128, 128]), ("rw_p", [16, 64, 64]), ("sh_o", [128, 8, 5]), ("fc_o", [4, 128, NJ, 5, 2]),
    ("mkv_o", [4, 256, 2048]),
    ("kv128_s", [2, 4, 128, 2, 4, 128]), ("kv512_s", [2, 4, 512, 2, 4, 128]), ("kv2048_s", [2, 4, 2048, 2, 4, 128]),
    ("hg_s", [4, 8, 128, 128]), ("rw_s", [4, 16, 64, 64]),
]


def build_nc(pvca, cca):
    nc = bass.Bass("TRN2", target_bir_lowering=False)
    dr = {}
    for name, shape in IN_SPECS + [("pv", [128, pvca.n]), ("cst", [128, cca.n])]:
        dr[name] = nc.dram_tensor(name, shape, F32, kind="ExternalInput").ap()
    for name, shape in OUT_SPECS:
        dr[name] = nc.dram_tensor(name, shape, F32, kind="ExternalOutput").ap()
    b = Builder(nc, dr, pvca, cca)
    b.build()
    return nc


def kernel(**inp):
    inp = {k: np.asarray(v) for k, v in inp.items()}
    f = np.float32
    pv, pvca = build_pv(inp)
    cst, cca = build_cst()
    nc = build_nc(pvca, cca)
    shared = {
        "pv": pv, "cst": cst,
        "w_qkv": np.stack([tile_w(inp["attn_w_qkv"][j]) for j in range(2)]),
        "w_ao": np.stack([tile_w(inp["attn_w_o"][j]) for j in range(2)]),
        "w_hgin": tile_w(inp["hg_w_in"][0]), "w_hgo": tile_w(inp["hg_w_o"][0]),
        "w_rkv": np.stack([tile_w(inp["rw_w_rkv"][0, j]) for j in range(3)]),
        "rw1": np.ascontiguousarray(inp["rw_w1"][0].reshape(8, 128, 64).transpose(1, 0, 2)),
        "ra1": np.ascontiguousarray(inp["rw_a1"][0].reshape(8, 128, 64).transpose(1, 0, 2)),
        "rg1": np.ascontiguousarray(inp["rw_g1"][0].reshape(8, 128, 160).transpose(1, 0, 2)),
        "rw2": np.ascontiguousarray(inp["rw_w2"][0]), "ra2": np.ascontiguousarray(inp["rw_a2"][0]),
        "rg2": np.ascontiguousarray(inp["rw_g2"][0]),
        "w_rwo": tile_w(inp["rw_w_o"][0]),
        "w_xq": np.stack([tile_w(inp["xa_w_q"][l]) for l in range(4)]),
        "w_xkv": np.stack([tile_w(inp["xa_w_kv"][l]) for l in range(4)]),
        "w_xo": np.stack([tile_w(inp["xa_w_o"][l]) for l in range(4)]),
        "w_fin": np.stack([tile_w(inp["ffn_w_in"][l]) for l in range(4)]),
        "w_fdn": np.stack([tile_w(inp["ffn_w_down"][l]) for l in range(4)]),
    }
    in_maps = []
    for c in range(8):
        sl = slice(4 * c, 4 * c + 4)
        m = dict(shared)
        m["xp"] = np.ascontiguousarray(inp["x_prompt"][c])
        m["xs"] = np.ascontiguousarray(inp["x_sample"][sl].reshape(32, 1024))
        m["mem"] = np.ascontiguousarray(inp["mem_prompt"][c])
        m["c128"] = np.ascontiguousarray(inp["cache_attn_kv_w128"][:, sl])
        m["c512"] = np.ascontiguousarray(inp["cache_attn_kv_w512"][:, sl])
        m["c2048"] = np.ascontiguousarray(inp["cache_attn_kv_w2048"][:, sl])
        m["st_hg"] = np.ascontiguousarray(inp["state_hgrn"][0, sl])
        m["st_rw"] = np.ascontiguousarray(inp["state_rwkv"][0, sl].transpose(0, 1, 3, 2))
        m["st_sh"] = np.ascontiguousarray(inp["state_rwkv_shift"][0, sl].reshape(4, 8, 128).transpose(2, 1, 0))
        m["st_fc"] = np.ascontiguousarray(inp["state_ffn_conv"][:, sl].reshape(4, 4, 2, NJ, 128).transpose(0, 4, 3, 1, 2))
        m["cmem"] = np.ascontiguousarray(inp["cache_mem_kv"][:, sl])
        in_maps.append({k: np.ascontiguousarray(v, dtype=f) for k, v in m.items()})
    res = run_bass_kernel_spmd(nc, in_maps, core_ids=list(range(8)))
    R = res.results

    def cat(name, axis=0, stack=False):
        arrs = [np.asarray(R[c][name]) for c in range(8)]
        return np.stack(arrs, axis) if stack else np.concatenate(arrs, axis)

    y_p = cat("y_p", 0, True)
    y_s = cat("y_s", 0, True).reshape(32, 8, 1024)
    kvp = [cat(n, 1, True) for n in ("kv128_p", "kv512_p", "kv2048_p")]
    hg_p = cat("hg_p", 0, True)[None]
    rw_p = cat("rw_p", 0, True).transpose(0, 1, 3, 2)[None]
    sh = cat("sh_o", 0, True)
    sh = sh.transpose(0, 3, 2, 1).reshape(8, 5, 1024)
    sh_p = sh[:, 0][None]
    sh_s = sh[:, 1:5].reshape(32, 1024)[None]
    fc = cat("fc_o", 0, True)
    fc = fc.transpose(1, 0, 4, 5, 3, 2).reshape(4, 8, 5, 2, DFF)
    fc_p = np.ascontiguousarray(fc[:, :, 0])
    fc_s = np.ascontiguousarray(fc[:, :, 1:5].reshape(4, 32, 2, DFF))
    mkv = cat("mkv_o", 1, True).reshape(4, 8, 256, 2, 4, 256)
    kvs = [cat(n, 1) for n in ("kv128_s", "kv512_s", "kv2048_s")]
    hg_s = cat("hg_s", 0)[None]
    rw_s = cat("rw_s", 0).transpose(0, 1, 3, 2)[None]
    outs = (y_p, y_s, kvp[0], kvp[1], kvp[2], hg_p, rw_p, sh_p, fc_p, mkv, kvs[0], kvs[1], kvs[2], hg_s, rw_s, sh_s, fc_s)
    return tuple(np.ascontiguousarray(o, dtype=np.float32) for o in outs)
```
